# Optimizing a Trainium2 kernel written in Bass

```python
import jax, jax.numpy as jnp
from jax import lax
import numpy as np

D_MODEL = 1024
BATCH = 4
SEQ = 4096
DEPTH = 1
DEC_BATCH = 8
DEC_SEQ = 32
PAST_LEN = 2048

CHUNK = 64
N_META = 16
Q_BLOCK = 128
SB_HEADS = 8
SB_HEAD_DIM = 64
SB_WIDTH = SB_HEADS * SB_HEAD_DIM
HG_HEADS = 4
HG_HEAD_DIM = 128
HG_WIDTH = HG_HEADS * HG_HEAD_DIM
D_FF = 2816
EPS = 1e-6
IN_SPLITS = [SB_WIDTH, SB_WIDTH, SB_WIDTH, HG_WIDTH, HG_WIDTH, HG_WIDTH, HG_WIDTH, D_MODEL, D_MODEL]
N_IN = sum(IN_SPLITS)

kernel_name = 'streaming_stickbreak_hgrn2_macaron'


def rmsnorm(x, g):
    xf = x.astype(jnp.float32)
    y = xf * lax.rsqrt(jnp.mean(xf * xf, axis=-1, keepdims=True) + EPS)
    return (y * g.astype(jnp.float32)).astype(x.dtype)


def half_ffn(x, g, w_gate, w_up, w_down):
    h = rmsnorm(x, g)
    return x + 0.5 * ((jax.nn.silu(h @ w_gate) * (h @ w_up)) @ w_down)


def sb_block(q, k, v, q_idx, k_idx):
    z = jnp.einsum('bqhd,bkhd->bhqk', q.astype(jnp.float32), k.astype(jnp.float32)) * (SB_HEAD_DIM ** -0.5)
    strict = k_idx[None, :] < q_idx[:, None]
    log_stay = jnp.where(strict, jax.nn.log_sigmoid(-z), 0.0)
    after = lax.cumsum(log_stay, axis=3, reverse=True) - log_stay
    a = jnp.where(strict, jnp.exp(jax.nn.log_sigmoid(z) + after), 0.0)
    return jnp.einsum('bhqk,bkhd->bqhd', a, v.astype(jnp.float32))


def sb_prompt(q, k, v):
    B, L, H, D = q.shape
    Lp = -(-L // Q_BLOCK) * Q_BLOCK
    pad = ((0, 0), (0, Lp - L), (0, 0), (0, 0))
    qp, kp, vp = jnp.pad(q, pad), jnp.pad(k, pad), jnp.pad(v, pad)
    n = Lp // Q_BLOCK
    qb = jnp.moveaxis(qp.reshape(B, n, Q_BLOCK, H, D), 1, 0)
    starts = jnp.arange(n, dtype=jnp.int32) * Q_BLOCK
    k_idx = jnp.arange(Lp, dtype=jnp.int32)
    out = lax.map(lambda a: sb_block(a[0], kp, vp, a[1] + jnp.arange(Q_BLOCK, dtype=jnp.int32), k_idx), (qb, starts))
    return jnp.moveaxis(out, 0, 1).reshape(B, Lp, H, D)[:, :L]


def hgrn_chunk(S, q, k, iv, logf):
    T = q.shape[1]
    b = jnp.cumsum(logf, axis=1)
    causal = jnp.arange(T)[:, None] >= jnp.arange(T)[None, :]
    diff = b[:, :, None] - b[:, None, :]
    decay = jnp.exp(jnp.where(causal[None, :, :, None, None], diff, -jnp.inf))
    scores = jnp.einsum('bthk,bshk,btshk->bhts', q, k, decay)
    o = jnp.einsum('bhts,bshv->bthv', scores, iv) + jnp.einsum('bthk,bhkv->bthv', q * jnp.exp(b), S)
    b_last = b[:, -1]
    S_new = jnp.exp(b_last)[..., None] * S + jnp.einsum('bshk,bshv->bhkv', k * jnp.exp(b_last[:, None] - b), iv)
    return o, S_new


def hgrn_prompt(q, k, iv, logf):
    B, L, H, K = q.shape
    V = iv.shape[-1]
    S0 = jnp.zeros((B, H, K, V), jnp.float32)
    o_meta, S = hgrn_chunk(S0, q[:, :N_META], k[:, :N_META], iv[:, :N_META], logf[:, :N_META])
    n = (L - N_META) // CHUNK
    blk = lambda a: jnp.moveaxis(a[:, N_META:].reshape(B, n, CHUNK, H, a.shape[-1]), 1, 0)

    def step(S, xs):
        o, S2 = hgrn_chunk(S, *xs)
        return S2, o

    S, outs = lax.scan(step, S, (blk(q), blk(k), blk(iv), blk(logf)))
    o_rest = jnp.moveaxis(outs, 0, 1).reshape(B, n * CHUNK, H, V)
    return jnp.concatenate([o_meta, o_rest], axis=1), S


def hgrn_lower_bounds(logits):
    p = jax.nn.softmax(logits.astype(jnp.float32), axis=0)
    return jnp.cumsum(p, axis=0) - p[0]


def setup_inputs(seed: int = 0) -> dict:
    key = jax.random.key(seed)
    ks = jax.random.split(key, 32)
    nrm = lambda k, s, sc: jax.random.normal(k, s, jnp.float32) * sc
    gain = lambda k, s: 1.0 + 0.02 * jax.random.normal(k, s, jnp.float32)
    L_cache = N_META + PAST_LEN
    return {
        'x_prompt': nrm(ks[0], (BATCH, SEQ, D_MODEL), 1.0),
        'x_sample': nrm(ks[1], (DEC_BATCH, DEC_SEQ, D_MODEL), 1.0),
        'cache_sb_k': nrm(ks[2], (DEPTH, DEC_BATCH, L_cache, SB_HEADS, SB_HEAD_DIM), 1.0),
        'cache_sb_v': nrm(ks[3], (DEPTH, DEC_BATCH, L_cache, SB_HEADS, SB_HEAD_DIM), 1.0),
        'state_hgrn': nrm(ks[4], (DEPTH, DEC_BATCH, HG_HEADS, HG_HEAD_DIM, HG_HEAD_DIM), 0.5),
        'meta_tokens': nrm(ks[5], (N_META, D_MODEL), 1.0),
        'ffn1_norm': gain(ks[6], (DEPTH, D_MODEL)),
        'ffn1_w_gate': nrm(ks[7], (DEPTH, D_MODEL, D_FF), D_MODEL ** -0.5),
        'ffn1_w_up': nrm(ks[8], (DEPTH, D_MODEL, D_FF), D_MODEL ** -0.5),
        'ffn1_w_down': nrm(ks[9], (DEPTH, D_FF, D_MODEL), D_FF ** -0.5),
        'mix_norm': gain(ks[10], (DEPTH, D_MODEL)),
        'w_in': nrm(ks[11], (DEPTH, D_MODEL, N_IN), D_MODEL ** -0.5),
        'b_gate': nrm(ks[12], (DEPTH, 2 * D_MODEL), 0.02),
        'hg_lb_logits': nrm(ks[13], (DEPTH + 1, HG_WIDTH), 0.1),
        'hg_out_norm': gain(ks[14], (DEPTH, HG_WIDTH)),
        'w_branch_a': nrm(ks[15], (DEPTH, SB_WIDTH, D_MODEL), SB_WIDTH ** -0.5),
        'w_branch_b': nrm(ks[16], (DEPTH, HG_WIDTH, D_MODEL), HG_WIDTH ** -0.5),
        'w_out': nrm(ks[17], (DEPTH, D_MODEL, D_MODEL), D_MODEL ** -0.5),
        'ffn2_norm': gain(ks[18], (DEPTH, D_MODEL)),
        'ffn2_w_gate': nrm(ks[19], (DEPTH, D_MODEL, D_FF), D_MODEL ** -0.5),
        'ffn2_w_up': nrm(ks[20], (DEPTH, D_MODEL, D_FF), D_MODEL ** -0.5),
        'ffn2_w_down': nrm(ks[21], (DEPTH, D_FF, D_MODEL), D_FF ** -0.5),
        'final_norm': gain(ks[22], (D_MODEL,)),
    }


def reference(x_prompt, x_sample, cache_sb_k, cache_sb_v, state_hgrn, meta_tokens,
              ffn1_norm, ffn1_w_gate, ffn1_w_up, ffn1_w_down, mix_norm, w_in, b_gate,
              hg_lb_logits, hg_out_norm, w_branch_a, w_branch_b, w_out,
              ffn2_norm, ffn2_w_gate, ffn2_w_up, ffn2_w_down, final_norm):
    lower = hgrn_lower_bounds(hg_lb_logits)
    split_at = np.cumsum(IN_SPLITS)[:-1].tolist()

    def pre_mix(x, l):
        x = half_ffn(x, ffn1_norm[l], ffn1_w_gate[l], ffn1_w_up[l], ffn1_w_down[l])
        h = rmsnorm(x, mix_norm[l])
        B, L, _ = h.shape
        qa, ka, va, zf, ih, qh, g, ga, gb = jnp.split(h @ w_in[l], split_at, axis=-1)
        heads_a = lambda a: a.reshape(B, L, SB_HEADS, SB_HEAD_DIM)
        heads_b = lambda a: a.astype(jnp.float32).reshape(B, L, HG_HEADS, HG_HEAD_DIM)
        lb = lower[l + 1]
        f = lb + (1.0 - lb) * jax.nn.sigmoid(zf.astype(jnp.float32))
        hg = (heads_b(qh), heads_b(1.0 - f), heads_b(ih), heads_b(jnp.log(f)))
        gates = jax.nn.sigmoid((jnp.concatenate([ga, gb], -1) + b_gate[l]).astype(jnp.float32))
        return x, (heads_a(qa), heads_a(ka), heads_a(va)), hg, g, gates

    def post_mix(x, l, oa, ob, g, gates):
        B, L, _ = x.shape
        on = ob * lax.rsqrt(jnp.mean(ob * ob, axis=-1, keepdims=True) + EPS) * hg_out_norm[l].astype(jnp.float32).reshape(HG_HEADS, HG_HEAD_DIM)
        ob = (on * jax.nn.silu(g.astype(jnp.float32).reshape(B, L, HG_HEADS, HG_HEAD_DIM))).reshape(B, L, HG_WIDTH).astype(x.dtype)
        oa = oa.reshape(B, L, SB_WIDTH).astype(x.dtype)
        gates = gates.astype(x.dtype)
        merged = gates[..., :D_MODEL] * (oa @ w_branch_a[l]) + gates[..., D_MODEL:] * (ob @ w_branch_b[l])
        x = x + merged @ w_out[l]
        return half_ffn(x, ffn2_norm[l], ffn2_w_gate[l], ffn2_w_up[l], ffn2_w_down[l])

    Bp = x_prompt.shape[0]
    xp = jnp.concatenate([jnp.broadcast_to(meta_tokens.astype(x_prompt.dtype)[None], (Bp, N_META, D_MODEL)), x_prompt], axis=1)
    pk, pv, ps = [], [], []
    for l in range(DEPTH):
        xp, (qa, ka, va), hg, g, gates = pre_mix(xp, l)
        oa = sb_prompt(qa, ka, va)
        ob, S = hgrn_prompt(*hg)
        xp = post_mix(xp, l, oa, ob, g, gates)
        pk.append(ka)
        pv.append(va)
        ps.append(S)
    y_prompt = rmsnorm(xp, final_norm)[:, N_META:]

    xs = x_sample
    T = xs.shape[1]
    Lc = cache_sb_k.shape[2]
    q_idx = Lc + jnp.arange(T, dtype=jnp.int32)
    k_idx = jnp.arange(Lc + T, dtype=jnp.int32)
    sk, sv, ss = [], [], []
    for l in range(DEPTH):
        xs, (qa, ka, va), hg, g, gates = pre_mix(xs, l)
        kk = jnp.concatenate([cache_sb_k[l], ka.astype(cache_sb_k.dtype)], axis=1)
        vv = jnp.concatenate([cache_sb_v[l], va.astype(cache_sb_v.dtype)], axis=1)
        oa = sb_block(qa, kk, vv, q_idx, k_idx)
        ob, S = hgrn_chunk(state_hgrn[l].astype(jnp.float32), *hg)
        xs = post_mix(xs, l, oa, ob, g, gates)
        sk.append(ka)
        sv.append(va)
        ss.append(S)
    y_sample = rmsnorm(xs, final_norm)

    return (y_prompt, y_sample, jnp.stack(pk), jnp.stack(pv), jnp.stack(ps), jnp.stack(sk), jnp.stack(sv), jnp.stack(ss))
```

```python
import numpy as np
from contextlib import ExitStack
import concourse.bass as bass
import concourse.mybir as mybir
from concourse.bass_utils import run_bass_kernel_spmd

F32 = mybir.dt.float32
BF16 = mybir.dt.bfloat16
AF = mybir.ActivationFunctionType
ALU = mybir.AluOpType

D = 1024
DFF = 2816
NMETA = 16
TS = 32
LC = 2064
EPS = 1e-6
NT = 512
ENGS = ['pe', 'act', 'dve', 'pool', 'sp']

V_N1, V_NM, V_N2, V_NF, V_BGA, V_BGB, V_HGN, V_LB0, V_LB1, V_LBV, V_OML = 0, 8, 16, 24, 32, 40, 48, 52, 56, 60, 64
NVEC = 68


class Sched:
    def __init__(self):
        self.prog = {e: [] for e in ENGS}
        self.cnt = {e: 0 for e in ENGS}
        self.dma_cnt = {}
        self.last_w = {}
        self.readers = {}
        self.known = {e: {} for e in ENGS}
        self.fence = {}
        self.touched = set()
        self.nwaits = 0
        import os
        self.glimit = int(os.environ.get('KOPS', '100000000'))

    def _deps(self, reads, writes):
        need = {}

        def add(src, val):
            if need.get(src, 0) < val:
                need[src] = val
        for k in list(reads) + list(writes):
            if isinstance(k, tuple) and k[0] == 'SH' and k not in self.touched:
                self.touched.add(k)
                for s, v in self.fence.items():
                    add(s, v)
        for k in reads:
            lw = self.last_w.get(k)
            if lw:
                add(*lw)
        for k in writes:
            lw = self.last_w.get(k)
            if lw:
                add(*lw)
            for s, v in self.readers.get(k, {}).items():
                add(s, v)
        return need

    def _waits(self, eng, need):
        waits = []
        for src, val in need.items():
            if src == eng and eng == 'pe':
                continue
            if self.known[eng].get(src, 0) >= val:
                continue
            self.known[eng][src] = val
            waits.append((src, val))
        self.nwaits += len(waits)
        return waits

    def _mark(self, src, val, reads, writes):
        for k in writes:
            self.last_w[k] = (src, val)
            self.readers[k] = {}
        for k in reads:
            if k in writes:
                continue
            self.readers.setdefault(k, {})[src] = val

    def op(self, eng, fn, reads=(), writes=()):
        psr = [k for k in reads if isinstance(k, tuple) and k[0] == 'ps']
        if psr:
            reads = [k for k in reads if k not in psr]
            writes = list(writes) + [k for k in psr if k not in writes]
        self.gcount = getattr(self, 'gcount', 0) + 1
        if self.gcount > self.glimit:
            return
        import sys as _sys, os as _os
        if _os.environ.get('KTRACE'):
            lo, hi = [int(x) for x in _os.environ['KTRACE'].split(',')]
            if lo <= self.gcount <= hi:
                fr = _sys._getframe(2)
                print('OP', self.gcount, eng, 'line', fr.f_lineno, 'from', fr.f_back.f_lineno, 'reads', list(reads)[:3], 'writes', list(writes))
        need = self._deps(reads, writes)
        waits = self._waits(eng, need)
        self.cnt[eng] += 1
        self._mark(eng, self.cnt[eng], reads, writes)
        self.prog[eng].append(('op', waits, fn))

    def dma(self, q, out, in_, reads, writes, sem):
        self.gcount = getattr(self, 'gcount', 0) + 1
        if self.gcount > self.glimit:
            return
        import sys as _sys, os as _os
        if _os.environ.get('KTRACE'):
            lo, hi = [int(x) for x in _os.environ['KTRACE'].split(',')]
            if lo <= self.gcount <= hi:
                fr = _sys._getframe(1)
                print('DMA', self.gcount, q, 'line', fr.f_lineno, 'sem', sem, 'writes', list(writes))
        need = self._deps(reads, writes)
        waits = self._waits(q, need)
        self.dma_cnt[sem] = self.dma_cnt.get(sem, 0) + 16
        self._mark(sem, self.dma_cnt[sem], reads, writes)
        self.prog[q].append(('dma', waits, (out, in_, sem)))

    def phase_switch(self):
        f = dict(self.fence)

        def add(s, v):
            if f.get(s, 0) < v:
                f[s] = v
        for k in list(self.last_w.keys()):
            if isinstance(k, tuple) and k[0] == 'SH':
                add(*self.last_w[k])
                del self.last_w[k]
        for k in list(self.readers.keys()):
            if isinstance(k, tuple) and k[0] == 'SH':
                for s, v in self.readers[k].items():
                    add(s, v)
                del self.readers[k]
        self.fence = f
        self.touched = set()

    def final_wait(self, q):
        waits = [(s, v) for s, v in self.dma_cnt.items()]
        self.prog[q].append(('wait', waits, None))


class Builder:
    def __init__(self, nft):
        self.nft = nft
        self.FR = nft * NT
        self.KTW = max(NMETA + self.FR, NMETA + LC + TS + 128)
        self.NBLK = max(1 + 4 * nft, 19)
        self.s = Sched()
        self.ring_i = 0
        self.kv_i = 0
        self.xtok_i = 0

    def mm(self, out, lhsT, rhs, start, stop, reads, writes, sgc=False):
        self.s.op('pe', lambda e: e.matmul(out, lhsT=lhsT, rhs=rhs, start=start, stop=stop, skip_group_check=sgc), reads, writes)

    def tr(self, out, in_, n, reads, writes):
        ident = self.ident
        self.s.op('pe', lambda e: e.transpose(out=out, in_=in_, identity=ident[0:n, 0:n]), reads, writes)

    def act(self, out, in_, func, reads, writes, bias=None, scale=None):
        kw = {}
        if bias is not None:
            kw['bias'] = bias
        if scale is not None:
            kw['scale'] = scale
        self.s.op('act', lambda e: e.activation(out=out, in_=in_, func=func, **kw), reads, writes)

    def tt(self, eng, out, in0, in1, op, reads, writes):
        self.s.op(eng, lambda e: e.tensor_tensor(out=out, in0=in0, in1=in1, op=op), reads, writes)

    def tsc(self, eng, out, in0, s1, s2, op0, op1, reads, writes):
        if s2 is None:
            self.s.op(eng, lambda e: e.tensor_scalar(out=out, in0=in0, scalar1=s1, scalar2=None, op0=op0), reads, writes)
        else:
            self.s.op(eng, lambda e: e.tensor_scalar(out=out, in0=in0, scalar1=s1, scalar2=s2, op0=op0, op1=op1), reads, writes)

    def stt(self, eng, out, in0, scalar, in1, op0, op1, reads, writes):
        self.s.op(eng, lambda e: e.scalar_tensor_tensor(out=out, in0=in0, scalar=scalar, in1=in1, op0=op0, op1=op1), reads, writes)

    def cp(self, eng, out, in_, reads, writes):
        if eng == 'act':
            self.act(out, in_, AF.Copy, reads, writes)
        else:
            self.s.op(eng, lambda e: e.tensor_copy(out=out, in_=in_), reads, writes)

    def ring_load(self, dram_ap, nelem, rdkey):
        slot = self.ring_i % 3
        self.ring_i += 1
        out = self.ring[:, slot, 0:nelem]
        self.s.dma('sp', out, dram_ap, [rdkey], [('ring', slot)], 'ring%d' % slot)
        return slot

    def build(self):
        nc = bass.Bass("TRN2", target_bir_lowering=False)
        self.nc = nc
        nft, FR = self.nft, self.FR
        dt_in = {}

        def din(name, shape):
            dt_in[name] = nc.dram_tensor(name, list(shape), F32, kind="ExternalInput").ap()
            return dt_in[name]

        def dout(name, shape):
            return nc.dram_tensor(name, list(shape), F32, kind="ExternalOutput").ap()
        self.xp = din("xp", [FR, D])
        self.xs = din("xs", [TS, D])
        self.meta = din("meta", [NMETA, D])
        self.ck = din("ck", [LC, 512])
        self.cv = din("cv", [LC, 512])
        self.st = din("st", [4, 128, 128])
        self.vecs_d = din("vecs", [128, NVEC])
        self.lbrep_d = din("lbrep", [128, 2, 512])
        self.cst_d = din("cst", [128, 5 * 128 + 4 * 512])
        self.w32 = {}
        self.wsz = {'gu1': 22 * 128 * 2048, 'gu2': 22 * 128 * 2048, 'd1': 2 * 8 * 128 * 1408, 'd2': 2 * 8 * 128 * 1408,
                    'wfm': 52 * 128 * 1024, 'wtm': 4 * 128 * 4096}
        for n, sz in self.wsz.items():
            self.w32[n] = din(n, [sz // 2048, 2048])
        self.yp = dout("yp", [FR, D])
        self.ys = dout("ys", [TS, D])
        self.pk = dout("pk", [NMETA + FR, 512])
        self.pv = dout("pv", [NMETA + FR, 512])
        self.ph = dout("ph", [4, 128, 128])
        self.sk = dout("sk", [TS, 512])
        self.sv = dout("sv", [TS, 512])
        self.sh = dout("sh", [4, 128, 128])
        self.wbf = {n: nc.dram_tensor(n + "_bf", [sz], BF16).ap() for n, sz in self.wsz.items()}

        with ExitStack() as es:
            es.enter_context(nc.allow_low_precision("bf16 matmul operands, fp32 accumulation"))

            def sb(name, shape, dt):
                return es.enter_context(nc.sbuf_tensor(name, list(shape), dt))

            def ps(name):
                return es.enter_context(nc.psum_tensor(name, [128, 512], F32))
            self.ident = sb("ident", [128, 128], F32)
            self.onesb = sb("onesb", [128, 128], BF16)
            self.negU = sb("negU", [128, 128], BF16)
            self.negL = sb("negL", [128, 128], BF16)
            self.tri = sb("tri", [128, 128], F32)
            self.masks = sb("masks", [128, 4, 512], BF16)
            self.vecs = sb("vecs_s", [128, NVEC], F32)
            self.lb_t = sb("lb_t", [128, 2, 512], F32)
            self.xT = sb("xT", [128, 8, NT], F32)
            self.hT = sb("hT", [128, 8, NT + 64], BF16)
            self.rstd = sb("rstd", [128, NT], F32)
            self.lnt = sb("lnt", [128, NT], F32)
            self.KT = sb("KT", [128, 4, self.KTW], BF16)
            self.Vp = sb("Vp", [128, self.NBLK, 512], BF16)
            self.ring = sb("ring", [128, 3, 4096], BF16)
            self.S = sb("S", [128, 2, 4, 128], F32)
            self.Sb = sb("Sb", [128, 2, 4, 128], BF16)
            self.kvst = sb("kvst", [128, 2, 512], F32)
            self.qT = sb("qT", [128, 4, NT], BF16)
            self.obT = sb("obT", [128, 4, NT], F32)
            self.oaT = sb("oaT", [128, 4, NT], BF16)
            SHW = 11264
            self.SH = sb("SH", [128, SHW], F32)
            self.bank = [ps("bank%d" % i) for i in range(8)]
            self._views()

            self._emit_all()

            sems = {}
            names = [e for e in ENGS if e != 'sp'] + sorted(self.s.dma_cnt.keys())
            for n in names:
                sems[n] = es.enter_context(nc.semaphore("s_" + n))
            block = es.enter_context(nc.Block())
            prog = self.s.prog

            def run(eng_obj, ename):
                for kind, waits, payload in prog[ename]:
                    for src, val in waits:
                        eng_obj.wait_ge(sems[src], val)
                    if kind == 'op':
                        ins = payload(eng_obj)
                        ins.then_inc(sems[ename], 1)
                    elif kind == 'dma':
                        out, in_, sem = payload
                        eng_obj.dma_start(out=out, in_=in_).then_inc(sems[sem], 16)

            @block.tensor
            def _(e):
                run(e, 'pe')

            @block.scalar
            def _(e):
                run(e, 'act')

            @block.vector
            def _(e):
                run(e, 'dve')

            @block.gpsimd
            def _(e):
                run(e, 'pool')

            @block.sync
            def _(e):
                run(e, 'sp')
        return nc

    def _views(self):
        SH = self.SH

        def v(off_b, nbytes, dt, pattern=None, **kw):
            a = SH[:, off_b // 4:(off_b + nbytes) // 4]
            if dt is BF16:
                a = a.bitcast(BF16)
            if pattern:
                a = a.rearrange(pattern, **kw)
            return a
        self.hid = v(0, 11264, BF16, "p (c n) -> p c n", c=11)
        self.sg = v(11264, 4096, F32, "p (c n) -> p c n", c=2)
        self.x_tok = v(15360, 8192, F32, "p (c n) -> p c n", c=2)
        self.omfT = v(0, 8192, F32, "p (c n) -> p c n", c=4)
        self.qhT = v(8192, 8192, F32, "p (c n) -> p c n", c=4)
        self.logf = v(16384, 4096, F32, "p (c n) -> p c n", c=2)
        self.omf_tm = v(20480, 4096, F32, "p (c n) -> p c n", c=2)
        self.iv = v(24576, 2048, BF16, "p (c n) -> p c n", c=2)
        self.eb = v(26624, 2048, F32, "p (a h t) -> p a h t", a=2, h=4)
        self.enb = v(28672, 2048, F32, "p (a h t) -> p a h t", a=2, h=4)
        self.qe = v(30720, 1024, BF16, "p (a h t) -> p a h t", a=2, h=4)
        self.ke = v(31744, 1024, BF16, "p (a h t) -> p a h t", a=2, h=4)
        self.ke_flat = v(31744, 1152, BF16)
        self.enb_tm = v(32768, 4096, F32, "p (c n) -> p c n", c=2)
        self.ke_tm = v(36864, 2048, BF16, "p (c n) -> p c n", c=2)
        self.sc = v(38912, 1024, BF16, "p (a h t) -> p a h t", a=2, h=4)
        self.tmpS = v(39936, 2048, F32, "p (h t) -> p h t", h=4)
        self.e_ = v(0, 12288, F32, "p (c n) -> p c n", c=6)
        self.sp_ = v(12288, 6144, BF16, "p (c n) -> p c n", c=6)
        self.e2_ = v(18432, 6144, F32, "p (c n) -> p c n", c=3)
        self.at_ = v(24576, 6144, BF16, "p (c n) -> p c n", c=6)
        self.ckv = v(30720, 8192, F32, "p (c n) -> p c n", c=2)
        self.sgT = v(0, 8192, F32, "p (c n) -> p c n", c=4)
        self.gA = v(8192, 8192, BF16, "p (c n) -> p c n", c=8)
        self.gB = v(16384, 8192, BF16, "p (c n) -> p c n", c=8)
        self.t1 = v(24576, 4096, F32, "p (c n) -> p c n", c=2)
        self.t2 = v(28672, 4096, F32, "p (c n) -> p c n", c=2)
        self.mT = v(32768, 8192, BF16, "p (c n) -> p c n", c=8)
        self.obn = v(40960, 4096, BF16, "p (c n) -> p c n", c=4)

    def _emit_all(self):
        import os
        self.kstop = int(os.environ.get('KSTOP', '99'))
        self.prologue()
        if self.kstop >= 1:
            self.tile(-1)
        if self.kstop >= 20:
            for f in range(self.nft):
                self.tile(f)
        self.s.final_wait('sp')

    def prologue(self):
        s = self.s
        c = self.cst_d
        o = 0
        s.dma('pool', self.ident[:], c[:, o:o + 128], [], ['ident'], 'cst'); o += 128
        s.dma('pool', self.onesb[:], c[:, o:o + 128], [], ['onesb'], 'cst'); o += 128
        s.dma('pool', self.negU[:], c[:, o:o + 128], [], ['negU'], 'cst'); o += 128
        s.dma('pool', self.tri[:], c[:, o:o + 128], [], ['tri'], 'cst'); o += 128
        s.dma('pool', self.negL[:], c[:, o:o + 128], [], ['negL'], 'cst'); o += 128
        s.dma('pool', self.masks[:], c[:, o:o + 2048].rearrange("p (d n) -> p d n", d=4), [], ['masks'], 'cst'); o += 2048
        s.dma('pool', self.vecs[:], self.vecs_d, [], ['vecs'], 'cst')
        s.dma('pool', self.lb_t[:], self.lbrep_d, [], ['lb_t'], 'cst')
        allc = ['ident', 'onesb', 'negU', 'negL', 'tri', 'masks', 'vecs', 'lb_t']
        tot = s.dma_cnt['cst']
        for k in allc:
            s.last_w[k] = ('cst', tot)
        pieces = [('gu1', 0, 1408), ('d1', 0, 704), ('gu1', 1408, 1408), ('d1', 704, 704),
                  ('wfm', 0, 1024), ('wtm', 0, 1024), ('wfm', 1024, 2304),
                  ('gu2', 0, 1408), ('d2', 0, 704), ('gu2', 1408, 1408), ('d2', 704, 704)]
        for i, (n, r0, nr) in enumerate(pieces):
            dst = self.wbf[n].rearrange("(r c) -> r c", c=2048)[r0:r0 + nr, :]
            s.dma('pool', dst, self.w32[n][r0:r0 + nr, :], [], [('scr', n, r0)], 'cast%d' % i)
        self.s.op('dve', lambda e: e.memset(self.hT[:, :, :], 0.0), [], [('hT', kc) for kc in range(8)])
        self.s.op('pool', lambda e: e.memset(self.KT[:, :, :], 0.0), [], [('KT', c) for c in range(4)])
        self.s.op('dve', lambda e: e.memset(self.xT[:, :, :], 0.0), [], [('xT', kc) for kc in range(8)])
        self.s.op('pool', lambda e: e.memset(self.SH[:, :], 0.0), [], [('SH', 'all')])
        vv = self.vecs
        self.tt('dve', vv[:, V_LBV:V_LBV + 4], vv[:, V_LB1:V_LB1 + 4], vv[:, V_LB0:V_LB0 + 4], ALU.subtract, ['vecs'], ['vecs'])
        self.act(vv[:, V_LBV:V_LBV + 4], vv[:, V_LBV:V_LBV + 4], AF.Sigmoid, ['vecs'], ['vecs'])
        self.tsc('dve', vv[:, V_OML:V_OML + 4], vv[:, V_LBV:V_LBV + 4], -1.0, 1.0, ALU.mult, ALU.add, ['vecs'], ['vecs'])
        lt = self.lb_t
        self.tt('dve', lt[:, 0, :], lt[:, 1, :], lt[:, 0, :], ALU.subtract, ['lb_t'], ['lb_t'])
        self.act(lt[:, 0, :], lt[:, 0, :], AF.Sigmoid, ['lb_t'], ['lb_t'])
        self.tsc('dve', lt[:, 1, :], lt[:, 0, :], -1.0, 1.0, ALU.mult, ALU.add, ['lb_t'], ['lb_t'])

    def scr_key(self, n, row2048):
        bounds = {'gu1': [0, 1408], 'gu2': [0, 1408], 'd1': [0, 704], 'd2': [0, 704], 'wfm': [0, 1024], 'wtm': [0]}[n]
        r0 = max(b for b in bounds if b <= row2048)
        return ('scr', n, r0)

    def load_gu(self, which, c):
        n = 'gu%d' % which
        ap = self.wbf[n].rearrange("(c p f) -> c p f", p=128, f=2048)[c]
        return self.ring_load(ap, 2048, self.scr_key(n, c * 128))

    def load_d(self, which, half, o):
        n = 'd%d' % which
        ap = self.wbf[n].rearrange("(h o p f) -> h o p f", h=2, o=8, p=128)[half, o]
        return self.ring_load(ap, 1408, self.scr_key(n, (half * 8 + o) * 128 * 1408 // 2048))

    def load_fm(self, c0, ncnk):
        slot = self.ring_i % 3
        self.ring_i += 1
        src = self.wbf['wfm'].rearrange("(c p f) -> c p f", p=128, f=1024)
        for i in range(ncnk):
            self.s.dma('sp', self.ring[:, slot, i * 1024:(i + 1) * 1024], src[c0 + i], [self.scr_key('wfm', c0 * 64)], [('ring', slot)], 'ring%d' % slot)
        return slot

    def load_tm(self, g):
        ap = self.wbf['wtm'].rearrange("(g p f) -> g p f", p=128, f=4096)[g]
        return self.ring_load(ap, 4096, ('scr', 'wtm', 0))

    def tile(self, f):
        if f < 0:
            n = TS + NMETA
            segs = [('sample', 0, TS), ('meta', TS, NMETA)]
        else:
            n = NT
            segs = [('frames', 0, NT)]
        self.n = n
        self.f = f
        self.segs = segs
        s = self.s
        ks = self.kstop if f < 0 else 99
        s.phase_switch()
        self.load_x()
        if ks < 2: return
        self.norm(V_N1)
        if ks < 3: return
        self.ffn(1)
        if ks < 4: return
        self.norm(V_NM)
        s.phase_switch()
        self.w_in()
        if ks < 5 or ks == 41: return
        self.hgrn()
        if ks < 6: return
        s.phase_switch()
        self.attn()
        if ks < 7: return
        s.phase_switch()
        self.post()
        if ks < 8: return
        s.phase_switch()
        self.norm(V_N2)
        self.ffn(2)
        self.final()

    def load_x(self):
        n, f = self.n, self.f
        if f < 0:
            blocks = [(0, n)]
        else:
            blocks = [(j * 128, 128) for j in range(4)]
        for bi, (c0, nb) in enumerate(blocks):
            slot = self.xtok_i % 2
            self.xtok_i += 1
            xk = ('SH', 'x_tok', slot)
            if f < 0:
                self.s.dma('pool', self.x_tok[0:TS, slot, :], self.xs, [], [xk], 'xin%d' % slot)
                self.s.dma('pool', self.x_tok[TS:n, slot, :], self.meta, [], [xk], 'xin%d' % slot)
            else:
                r0 = f * NT + c0
                self.s.dma('pool', self.x_tok[:, slot, :], self.xp[r0:r0 + 128, :], [], [xk], 'xin%d' % slot)
            for kc in range(8):
                bk = self.bank[kc % 2]
                self.tr(bk[:, 0:nb], self.x_tok[0:nb, slot, kc * 128:(kc + 1) * 128], nb, [xk, 'ident'], [('ps', kc % 2)])
                eng = 'dve' if kc % 2 == 0 else 'act'
                self.cp(eng, self.xT[:, kc, c0:c0 + nb], bk[:, 0:nb], [('ps', kc % 2)], [('xT', kc)])

    def norm(self, gcol, inplace=False):
        n = self.n
        for kc in range(8):
            self.act(self.hT[:, kc, 0:n], self.xT[:, kc, 0:n], AF.Square, [('xT', kc)], [('hT', kc)])
        bk = self.bank[7]
        for kc in range(8):
            self.mm(bk[:, 0:n], self.onesb[:, :], self.hT[:, kc, 0:n], kc == 0, kc == 7, [('hT', kc), 'onesb'], [('ps', 7)])
        self.act(self.lnt[:, 0:n], bk[:, 0:n], AF.Ln, [('ps', 7)], ['lnt'], bias=EPS, scale=1.0 / D)
        self.act(self.rstd[:, 0:n], self.lnt[:, 0:n], AF.Exp, ['lnt'], ['rstd'], scale=-0.5)
        for kc in range(8):
            eng = 'dve'
            if inplace:
                self.stt(eng, self.xT[:, kc, 0:n], self.xT[:, kc, 0:n], self.vecs[:, gcol + kc:gcol + kc + 1], self.rstd[:, 0:n],
                         ALU.mult, ALU.mult, [('xT', kc), 'rstd', 'vecs'], [('xT', kc)])
            else:
                self.stt(eng, self.hT[:, kc, 0:n], self.xT[:, kc, 0:n], self.vecs[:, gcol + kc:gcol + kc + 1], self.rstd[:, 0:n],
                         ALU.mult, ALU.mult, [('xT', kc), 'rstd', 'vecs'], [('hT', kc)])

    def ffn(self, which):
        n = self.n
        hk = [('hT', kc) for kc in range(8)]
        it = 0
        for half in range(2):
            for cc in range(11):
                c = half * 11 + cc
                slot = self.load_gu(which, c)
                w = self.ring[:, slot, 0:2048].rearrange("p (g k n) -> p g k n", g=2, k=8)
                gb, ub = it % 2, 2 + it % 2
                for kc in range(8):
                    self.mm(self.bank[gb][:, 0:n], w[:, 0, kc, :], self.hT[:, kc, 0:n], kc == 0, kc == 7, hk + [('ring', slot)], [('ps', gb)])
                for kc in range(8):
                    self.mm(self.bank[ub][:, 0:n], w[:, 1, kc, :], self.hT[:, kc, 0:n], kc == 0, kc == 7, hk + [('ring', slot)], [('ps', ub)])
                sgk = ('SH', 'sg', it % 2)
                self.act(self.sg[:, it % 2, 0:n], self.bank[gb][:, 0:n], AF.Silu, [('ps', gb)], [sgk])
                self.tt('dve', self.hid[:, cc, 0:n], self.sg[:, it % 2, 0:n], self.bank[ub][:, 0:n], ALU.mult, [sgk, ('ps', ub)], [('SH', 'hid', cc)])
                it += 1
            for o in range(8):
                slot = self.load_d(which, half, o)
                w = self.ring[:, slot, 0:1408].rearrange("p (c n) -> p c n", c=11)
                db = 4 + o % 2
                for cc in range(11):
                    self.mm(self.bank[db][:, 0:n], w[:, cc, :], self.hid[:, cc, 0:n], cc == 0, cc == 10, [('SH', 'hid', cc), ('ring', slot)], [('ps', db)])
                self.stt('dve', self.xT[:, o, 0:n], self.bank[db][:, 0:n], 0.5, self.xT[:, o, 0:n], ALU.mult, ALU.add, [('ps', db), ('xT', o)], [('xT', o)])

    def kcol(self, seg):
        if seg == 'sample':
            return NMETA + LC
        if seg == 'meta':
            return 0
        return NMETA + self.f * NT

    def w_in(self):
        n = self.n
        hk = [('hT', kc) for kc in range(8)]
        bi = 0
        for g4 in range(4):
            slot = self.load_fm(g4 * 4, 4)
            w = self.ring[:, slot, 0:4096].rearrange("p (c k n) -> p c k n", c=4, k=8)
            for ci in range(4):
                b = bi % 4
                bi += 1
                bk = self.bank[b]
                for kc in range(8):
                    self.mm(bk[:, 0:n], w[:, ci, kc, :], self.hT[:, kc, 0:n], kc == 0, kc == 7, hk + [('ring', slot)], [('ps', b)])
                if g4 == 0:
                    self.act(self.qT[:, ci, 0:n], bk[:, 0:n], AF.Copy, [('ps', b)], [('qT', ci)], scale=0.125)
                elif g4 == 1:
                    for (sname, c0, ns) in self.segs:
                        kc0 = self.kcol(sname)
                        self.cp('dve', self.KT[:, ci, kc0:kc0 + ns], bk[:, c0:c0 + ns], [('ps', b)], [('KT', ci)])
                elif g4 == 2:
                    self.act(self.lnt[:, 0:n], bk[:, 0:n], AF.Sigmoid, [('ps', b)], ['lnt'], scale=-1.0)
                    self.tsc('dve', self.omfT[:, ci, 0:n], self.lnt[:, 0:n], self.vecs[:, V_OML + ci:V_OML + ci + 1], None, ALU.mult, None,
                             ['lnt', 'vecs'], [('SH', 'omfT', ci)])
                else:
                    self.cp('act', self.qhT[:, ci, 0:n], bk[:, 0:n], [('ps', b)], [('SH', 'qhT', ci)])
        if self.kstop == 41:
            return
        if self.f < 0:
            blocks = [('sample', 0, TS, self.sk, self.sv, 0, 18), ('meta', TS, NMETA, self.pk, self.pv, 0, 0)]
        else:
            blocks = [('frames', j * 128, 128, self.pk, self.pv, NMETA + self.f * NT + j * 128, 1 + 4 * self.f + j) for j in range(4)]
        for g in range(2):
            slot = self.load_tm(g)
            w = self.ring[:, slot, 0:4096].rearrange("p (k n) -> p k n", k=8)
            for (sname, c0, nb, okd, ovd, r0, vblk) in blocks:
                b = 4 + bi % 4
                bi += 1
                bk = self.bank[b]
                for kc in range(8):
                    self.mm(bk[:, :], self.hT[:, kc, c0:c0 + 128], w[:, kc, :], kc == 0, kc == 7, hk + [('ring', slot)], [('ps', b)])
                ks = self.kv_i % 2
                self.kv_i += 1
                self.cp('dve', self.kvst[0:nb, ks, :], bk[0:nb, :], [('ps', b)], [('kvst', ks)])
                dst = (okd if g == 0 else ovd)[r0:r0 + nb, :]
                import os
                if not os.environ.get('NOKV'):
                    self.s.dma(os.environ.get('KVQ', 'pool'), dst, self.kvst[0:nb, ks, :], [('kvst', ks)], [], 'kvo%d' % ks)
                if g == 1:
                    self.cp('act', self.Vp[0:nb, vblk, :], bk[0:nb, :], [('ps', b)], [('Vp', vblk)])

    def hgrn(self):
        s = self.s
        hk = [('hT', kc) for kc in range(8)]
        slz = self.load_tm(2)
        sli = self.load_tm(3)
        wz = self.ring[:, slz, 0:4096].rearrange("p (k n) -> p k n", k=8)
        wi = self.ring[:, sli, 0:4096].rearrange("p (k n) -> p k n", k=8)
        if self.f < 0:
            chunks = [(0, TS, 1), (TS, NMETA, 0)]
            self.s.dma('pool', self.S[:, 1, :, :], self.st.rearrange("h k v -> k h v"), [], [('S', 1)], 'stin')
            for hh in range(4):
                self.cp('pool', self.Sb[:, 1, hh, :], self.S[:, 1, hh, :], [('S', 1)], [('Sb', 1, hh)])
            self.s.op('dve', lambda e: e.memset(self.S[:, 0, :, :], 0.0), [], [('S', 0)])
            self.s.op('dve', lambda e: e.memset(self.Sb[:, 0, :, :], 0.0), [], [('Sb', 0, hh) for hh in range(4)])
        else:
            chunks = [(i * 64, 64, 0) for i in range(8)]
        for ci, (c0, T, si) in enumerate(chunks):
            par = ci % 2
            bA, bB, bC, bD, bE, bF = self.bank[0], self.bank[1], self.bank[2 + par], self.bank[4], self.bank[5], self.bank[6 + par]
            kC, kF = ('ps', 2 + par), ('ps', 6 + par)
            for kc in range(8):
                self.mm(bA[:, :], self.hT[:, kc, c0:c0 + 128], wz[:, kc, :], kc == 0, kc == 7, hk + [('ring', slz)], [('ps', 0)])
            for kc in range(8):
                self.mm(bB[:, :], self.hT[:, kc, c0:c0 + 128], wi[:, kc, :], kc == 0, kc == 7, hk + [('ring', sli)], [('ps', 1)])
            k_omf, k_logf, k_iv = ('SH', 'omf_tm', par), ('SH', 'logf', par), ('SH', 'iv', par)
            self.act(self.omf_tm[0:T, par, :], bA[0:T, :], AF.Sigmoid, [('ps', 0)], [k_omf], scale=-1.0)
            self.tt('dve', self.omf_tm[0:T, par, :], self.omf_tm[0:T, par, :], self.lb_t[0:T, 1, :], ALU.mult, [k_omf, 'lb_t'], [k_omf])
            self.act(self.logf[0:T, par, :], self.omf_tm[0:T, par, :], AF.Ln, [k_omf], [k_logf], bias=1.0, scale=-1.0)
            self.cp('act', self.iv[0:T, par, :], bB[0:T, :], [('ps', 1)], [k_iv])
            for hh in range(4):
                self.mm(bC[:, hh * 64:hh * 64 + T], self.logf[0:T, par, hh * 128:(hh + 1) * 128], self.tri[0:T, 0:T], True, True,
                        [k_logf, 'tri'], [kC])
            self.mm(bD[:, :], self.tri[0:T, 0:128], self.logf[0:T, par, :], True, True, [k_logf, 'tri'], [('ps', 4)])
            k_enbt, k_ket = ('SH', 'enb_tm', par), ('SH', 'ke_tm', par)
            self.act(self.enb_tm[0:T, par, :], bD[0:T, :], AF.Exp, [('ps', 4)], [k_enbt], scale=-1.0)
            self.tt('dve', self.ke_tm[0:T, par, :], self.omf_tm[0:T, par, :], self.enb_tm[0:T, par, :], ALU.mult, [k_omf, k_enbt], [k_ket])
            for hh in range(4):
                k_eb, k_enb, k_qe, k_ke = ('SH', 'eb', par, hh), ('SH', 'enb', par, hh), ('SH', 'qe', par, hh), ('SH', 'ke', par, hh)
                self.act(self.eb[:, par, hh, 0:T], bC[:, hh * 64:hh * 64 + T], AF.Exp, [kC], [k_eb])
                self.act(self.enb[:, par, hh, 0:T], bC[:, hh * 64:hh * 64 + T], AF.Exp, [kC], [k_enb], scale=-1.0)
                self.tt('dve', self.qe[:, par, hh, 0:T], self.qhT[:, hh, c0:c0 + T], self.eb[:, par, hh, 0:T], ALU.mult,
                        [('SH', 'qhT', hh), k_eb], [k_qe])
                self.tt('pool', self.ke[:, par, hh, 0:T], self.omfT[:, hh, c0:c0 + T], self.enb[:, par, hh, 0:T], ALU.mult,
                        [('SH', 'omfT', hh), k_enb], [k_ke])
            for hh in range(4):
                k_qe, k_ke = ('SH', 'qe', par, hh), ('SH', 'ke', par, hh)
                self.mm(bC[:, 256 + hh * 64:256 + hh * 64 + T], self.ke_flat[:, (par * 4 + hh) * 64:(par * 4 + hh) * 64 + 128], self.qe[:, par, hh, 0:T], True, True,
                        [k_qe, k_ke], [kC])
            for hh in range(4):
                k_sc = ('SH', 'sc', par, hh)
                self.tt('dve', self.sc[0:T, par, hh, 0:T], bC[0:T, 256 + hh * 64:256 + hh * 64 + T], self.tri[0:T, 0:T], ALU.mult,
                        [kC, 'tri'], [k_sc])
            for hh in range(4):
                k_sc, k_qe = ('SH', 'sc', par, hh), ('SH', 'qe', par, hh)
                self.mm(bE[:, hh * 64:hh * 64 + T], self.iv[0:T, par, hh * 128:(hh + 1) * 128], self.sc[0:T, par, hh, 0:T], True, False,
                        [k_iv, k_sc], [('ps', 5)])
                self.mm(bE[:, hh * 64:hh * 64 + T], self.Sb[:, si, hh, :], self.qe[:, par, hh, 0:T], False, True,
                        [('Sb', si, hh), k_qe], [('ps', 5)])
            for hh in range(4):
                self.cp('act' if hh % 2 == 0 else 'dve', self.obT[:, hh, c0:c0 + T], bE[:, hh * 64:hh * 64 + T], [('ps', 5)], [('obT', hh)])
            for hh in range(4):
                self.mm(bF[:, hh * 128:(hh + 1) * 128], self.ke_tm[0:T, par, hh * 128:(hh + 1) * 128], self.iv[0:T, par, hh * 128:(hh + 1) * 128],
                        True, True, [k_ket, k_iv], [kF])
            for hh in range(4):
                k_eb = ('SH', 'eb', par, hh)
                ebl = self.eb[:, par, hh, T - 1:T]
                self.tsc('dve', self.tmpS[:, hh, :], self.S[:, si, hh, :], ebl, None, ALU.mult, None, [('S', si), k_eb], [('SH', 'tmpS', hh)])
                self.stt('dve', self.S[:, si, hh, :], bF[:, hh * 128:(hh + 1) * 128], ebl, self.tmpS[:, hh, :], ALU.mult, ALU.add,
                         [kF, k_eb, ('SH', 'tmpS', hh)], [('S', si)])
                self.cp('pool', self.Sb[:, si, hh, :], self.S[:, si, hh, :], [('S', si)], [('Sb', si, hh)])
            if self.f < 0 and si == 1:
                self.s.dma('pool', self.sh.rearrange("h k v -> k h v"), self.S[:, 1, :, :], [('S', 1)], [], 'sout')
            if self.f == self.nft - 1 and ci == len(chunks) - 1:
                self.s.dma('pool', self.ph.rearrange("h k v -> k h v"), self.S[:, 0, :, :], [('S', 0)], [], 'sout')

    def attn(self):
        f = self.f
        if f < 0:
            nb_c = (LC + 127) // 128
            for b in range(nb_c):
                r0 = b * 128
                kn = min(128, LC - r0)
                slot = b % 2
                ck_key = ('SH', 'ckv', slot)
                self.s.dma('pool', self.ckv[0:kn, slot, 0:512], self.ck[r0:r0 + kn, :], [], [ck_key], 'ckin%d' % slot)
                self.s.dma('pool', self.ckv[0:kn, slot, 512:1024], self.cv[r0:r0 + kn, :], [], [ck_key], 'ckin%d' % slot)
                bk = self.bank[b % 2]
                for ch in range(4):
                    self.tr(bk[:, ch * 128:ch * 128 + kn], self.ckv[0:kn, slot, ch * 128:(ch + 1) * 128], kn, [ck_key, 'ident'], [('ps', b % 2)])
                for ch in range(4):
                    self.cp('dve' if ch % 2 == 0 else 'act', self.KT[:, ch, NMETA + r0:NMETA + r0 + kn], bk[:, ch * 128:ch * 128 + kn],
                            [('ps', b % 2)], [('KT', ch)])
                self.cp('pool', self.Vp[0:kn, 1 + b, :], self.ckv[0:kn, slot, 512:1024], [ck_key], [('Vp', 1 + b)])
            blocks = [(18, TS, NMETA + LC, 0)]
            blocks.append((1 + nb_c - 1, LC - 128 * (nb_c - 1), NMETA + 128 * (nb_c - 1), None))
            for b in range(nb_c - 2, -1, -1):
                blocks.append((1 + b, 128, NMETA + 128 * b, None))
            self.attn_job(0, TS, blocks)
            self.attn_job(TS, NMETA, [(0, NMETA, 0, 0)])
        else:
            blocks = []
            for m in range(3, -1, -1):
                fb = 4 * f + m
                blocks.append((1 + fb, 128, NMETA + 128 * fb, 128 * m))
            for fb in range(4 * f - 1, -1, -1):
                blocks.append((1 + fb, 128, NMETA + 128 * fb, None))
            blocks.append((0, NMETA, 0, None))
            self.attn_job(0, NT, blocks)

    def attn_job(self, q0, nq, blocks):
        nblk = len(blocks)
        for grp in ([0, 1, 2], [3, 4, 5], [6, 7]):
            S_ = len(grp)
            nseq = nblk * S_

            def emitZ(idx):
                k, si = divmod(idx, S_)
                h = grp[si]
                vblk, kn, kcol, md = blocks[k]
                ch, pb = h // 2, 64 * (h % 2)
                zb = idx % 2
                self.mm(self.bank[zb][:, 0:nq], self.KT[pb:pb + 64, ch, kcol:kcol + 128], self.qT[pb:pb + 64, ch, q0:q0 + nq], True, True,
                        [('KT', ch), ('qT', ch)], [('ps', zb)])
            for idx in range(min(2, nseq)):
                emitZ(idx)
            for k in range(nblk):
                vblk, kn, kcol, md = blocks[k]
                par = k % 2
                for si, h in enumerate(grp):
                    idx = k * S_ + si
                    zb = idx % 2
                    ke_, ks_ = ('SH', 'e', si, par), ('SH', 'sp', si, par)
                    self.act(self.e_[0:kn, si * 2 + par, 0:nq], self.bank[zb][0:kn, 0:nq], AF.Exp, [('ps', zb)], [ke_])
                    if idx + 2 < nseq:
                        emitZ(idx + 2)
                    if md is not None:
                        self.tt('pool', self.e_[0:kn, si * 2 + par, 0:nq], self.e_[0:kn, si * 2 + par, 0:nq], self.masks[0:kn, md // 128, 0:nq],
                                ALU.mult, [ke_, 'masks'], [ke_])
                    self.act(self.sp_[0:kn, si * 2 + par, 0:nq], self.e_[0:kn, si * 2 + par, 0:nq], AF.Ln, [ke_], [ks_], bias=1.0)
                for si, h in enumerate(grp):
                    ks_ = ('SH', 'sp', si, par)
                    self.mm(self.bank[2 + si][:, 0:nq], self.negU[0:kn, :], self.sp_[0:kn, si * 2 + par, 0:nq], k == 0, False,
                            [ks_, 'negU'], [('ps', 2 + si)], sgc=True)
                for si, h in enumerate(grp):
                    ke_, k2_, ka_ = ('SH', 'e', si, par), ('SH', 'e2', si), ('SH', 'at', si, par)
                    self.act(self.e2_[0:kn, si, 0:nq], self.bank[2 + si][0:kn, 0:nq], AF.Exp, [('ps', 2 + si)], [k2_])
                    self.tt('dve', self.at_[0:kn, si * 2 + par, 0:nq], self.e2_[0:kn, si, 0:nq], self.e_[0:kn, si * 2 + par, 0:nq], ALU.mult,
                            [k2_, ke_], [ka_])
                for si, h in enumerate(grp):
                    ks_, ka_ = ('SH', 'sp', si, par), ('SH', 'at', si, par)
                    self.mm(self.bank[2 + si][:, 0:nq], self.negL[0:kn, :], self.sp_[0:kn, si * 2 + par, 0:nq], False, k == nblk - 1,
                            [ks_, 'negL'], [('ps', 2 + si)], sgc=True)
                    vlo = h * 64 if h % 2 == 0 else (h - 1) * 64
                    self.mm(self.bank[5 + si][:, 0:nq], self.Vp[0:kn, vblk, vlo:vlo + 128], self.at_[0:kn, si * 2 + par, 0:nq], k == 0, k == nblk - 1,
                            [ka_, ('Vp', vblk)], [('ps', 5 + si)])
            for si, h in enumerate(grp):
                ch, pb = h // 2, 64 * (h % 2)
                self.cp('dve' if si % 2 == 0 else 'act', self.oaT[pb:pb + 64, ch, q0:q0 + nq], self.bank[5 + si][pb:pb + 64, 0:nq],
                        [('ps', 5 + si)], [('oaT', ch, pb)])

    def post(self):
        n = self.n
        hk = [('hT', kc) for kc in range(8)]
        bi = 0
        for ld in range(5):
            slot = self.load_fm(16 + ld * 4, 4)
            w = self.ring[:, slot, 0:4096].rearrange("p (c k n) -> p c k n", c=4, k=8)
            for ci in range(4):
                cidx = ld * 4 + ci
                b = bi % 4
                bi += 1
                bk = self.bank[b]
                for kc in range(8):
                    self.mm(bk[:, 0:n], w[:, ci, kc, :], self.hT[:, kc, 0:n], kc == 0, kc == 7, hk + [('ring', slot)], [('ps', b)])
                if cidx < 4:
                    self.act(self.sgT[:, cidx, 0:n], bk[:, 0:n], AF.Silu, [('ps', b)], [('SH', 'sgT', cidx)])
                elif cidx < 12:
                    o = cidx - 4
                    self.act(self.gA[:, o, 0:n], bk[:, 0:n], AF.Sigmoid, [('ps', b), 'vecs'], [('SH', 'gA', o)], bias=self.vecs[:, V_BGA + o:V_BGA + o + 1])
                else:
                    o = cidx - 12
                    self.act(self.gB[:, o, 0:n], bk[:, 0:n], AF.Sigmoid, [('ps', b), 'vecs'], [('SH', 'gB', o)], bias=self.vecs[:, V_BGB + o:V_BGB + o + 1])
        for hh in range(4):
            kq = ('SH', 'mT', hh)
            self.act(self.mT[:, hh, 0:n], self.obT[:, hh, 0:n], AF.Square, [('obT', hh)], [kq])
            self.mm(self.bank[4][:, 0:n], self.onesb[:, :], self.mT[:, hh, 0:n], True, True, [kq, 'onesb'], [('ps', 4)])
            self.act(self.lnt[:, 0:n], self.bank[4][:, 0:n], AF.Ln, [('ps', 4)], ['lnt'], bias=EPS, scale=1.0 / 128)
            self.act(self.rstd[:, 0:n], self.lnt[:, 0:n], AF.Exp, ['lnt'], ['rstd'], scale=-0.5)
            k1 = ('SH', 't1', hh % 2)
            self.stt('dve', self.t1[:, hh % 2, 0:n], self.obT[:, hh, 0:n], self.vecs[:, V_HGN + hh:V_HGN + hh + 1], self.rstd[:, 0:n], ALU.mult, ALU.mult,
                     [('obT', hh), 'vecs', 'rstd'], [k1])
            self.tt('pool', self.obn[:, hh, 0:n], self.t1[:, hh % 2, 0:n], self.sgT[:, hh, 0:n], ALU.mult, [k1, ('SH', 'sgT', hh)], [('SH', 'obn', hh)])
        oak = [('oaT', c, pb) for c in range(4) for pb in (0, 64)]
        obk = [('SH', 'obn', c) for c in range(4)]
        for ld in range(2):
            slot = self.load_fm(36 + ld * 4, 4)
            w = self.ring[:, slot, 0:4096].rearrange("p (c k n) -> p c k n", c=4, k=8)
            for ci in range(4):
                o = ld * 4 + ci
                ba, bb = o % 2, 2 + o % 2
                for c in range(4):
                    self.mm(self.bank[ba][:, 0:n], w[:, ci, c, :], self.oaT[:, c, 0:n], c == 0, c == 3, oak + [('ring', slot)], [('ps', ba)])
                for c in range(4):
                    self.mm(self.bank[bb][:, 0:n], w[:, ci, 4 + c, :], self.obn[:, c, 0:n], c == 0, c == 3, obk + [('ring', slot)], [('ps', bb)])
                k1, k2 = ('SH', 't1', o % 2), ('SH', 't2', o % 2)
                self.tt('dve', self.t1[:, o % 2, 0:n], self.gA[:, o, 0:n], self.bank[ba][:, 0:n], ALU.mult, [('SH', 'gA', o), ('ps', ba)], [k1])
                self.tt('dve', self.t2[:, o % 2, 0:n], self.gB[:, o, 0:n], self.bank[bb][:, 0:n], ALU.mult, [('SH', 'gB', o), ('ps', bb)], [k2])
                self.tt('pool', self.mT[:, o, 0:n], self.t1[:, o % 2, 0:n], self.t2[:, o % 2, 0:n], ALU.add, [k1, k2], [('SH', 'mT', o)])
        mk = [('SH', 'mT', c) for c in range(8)]
        for ld in range(2):
            slot = self.load_fm(44 + ld * 4, 4)
            w = self.ring[:, slot, 0:4096].rearrange("p (c k n) -> p c k n", c=4, k=8)
            for ci in range(4):
                o = ld * 4 + ci
                b = 4 + o % 2
                for c in range(8):
                    self.mm(self.bank[b][:, 0:n], w[:, ci, c, :], self.mT[:, c, 0:n], c == 0, c == 7, mk + [('ring', slot)], [('ps', b)])
                self.tt('dve', self.xT[:, o, 0:n], self.xT[:, o, 0:n], self.bank[b][:, 0:n], ALU.add, [('xT', o), ('ps', b)], [('xT', o)])

    def final(self):
        n, f = self.n, self.f
        self.norm(V_NF, inplace=True)
        if f < 0:
            blocks = [(0, n, self.ys, 0, TS)]
        else:
            blocks = [(j * 128, 128, self.yp, f * NT + j * 128, 128) for j in range(4)]
        for (c0, nb, dst, r0, nout) in blocks:
            slot = self.xtok_i % 2
            self.xtok_i += 1
            xk = ('SH', 'x_tok', slot)
            for half in range(2):
                bk = self.bank[half]
                for q in range(4):
                    kc = half * 4 + q
                    self.tr(bk[:, q * 128:(q + 1) * 128], self.xT[:, kc, c0:c0 + 128], 128, [('xT', kc), 'ident'], [('ps', half)])
                self.cp('dve' if half == 0 else 'act', self.x_tok[0:nb, slot, half * 512:(half + 1) * 512], bk[0:nb, :], [('ps', half)], [xk])
            self.s.dma('pool', dst[r0:r0 + nout, :], self.x_tok[0:nout, slot, :], [xk], [], 'yout%d' % slot)


def _fix_kt_keys(b):
    pass


def _layouts(inp):
    f32 = np.float32

    def fm_vec(v):
        v = np.asarray(v, f32).reshape(-1, 128)
        return v.T

    def gu(wg, wu):
        a = np.asarray(wg, f32).reshape(8, 128, 22, 128).transpose(2, 1, 0, 3)
        b = np.asarray(wu, f32).reshape(8, 128, 22, 128).transpose(2, 1, 0, 3)
        return np.ascontiguousarray(np.stack([a, b], axis=2)).reshape(-1, 2048)

    def dn(wd):
        a = np.asarray(wd, f32).reshape(2, 11, 128, 8, 128).transpose(0, 3, 2, 1, 4)
        return np.ascontiguousarray(a).reshape(-1, 2048)
    w_in = np.asarray(inp['w_in'][0], f32)
    cols = np.concatenate([np.arange(0, 512), np.arange(512, 1024), np.arange(1536, 2048), np.arange(2560, 3072),
                           np.arange(3072, 3584), np.arange(3584, 4608), np.arange(4608, 5632)])
    fm = w_in[:, cols].reshape(8, 128, 36, 128).transpose(2, 1, 0, 3)
    wa = np.asarray(inp['w_branch_a'][0], f32).reshape(4, 128, 8, 128).transpose(2, 1, 0, 3)
    wb = np.asarray(inp['w_branch_b'][0], f32).reshape(4, 128, 8, 128).transpose(2, 1, 0, 3)
    wab = np.concatenate([wa, wb], axis=2)
    wo = np.asarray(inp['w_out'][0], f32).reshape(8, 128, 8, 128).transpose(2, 1, 0, 3)
    wfm = np.ascontiguousarray(np.concatenate([fm, wab, wo], axis=0)).reshape(-1, 2048)
    wtm = np.ascontiguousarray(w_in[:, 512:2560].reshape(8, 128, 4, 512).transpose(2, 1, 0, 3)).reshape(-1, 2048)
    vecs = np.zeros((128, NVEC), f32)
    vecs[:, V_N1:V_N1 + 8] = fm_vec(inp['ffn1_norm'][0])
    vecs[:, V_NM:V_NM + 8] = fm_vec(inp['mix_norm'][0])
    vecs[:, V_N2:V_N2 + 8] = fm_vec(inp['ffn2_norm'][0])
    vecs[:, V_NF:V_NF + 8] = fm_vec(inp['final_norm'])
    vecs[:, V_BGA:V_BGA + 16] = fm_vec(inp['b_gate'][0])
    vecs[:, V_HGN:V_HGN + 4] = fm_vec(inp['hg_out_norm'][0])
    vecs[:, V_LB0:V_LB0 + 4] = fm_vec(inp['hg_lb_logits'][0])
    vecs[:, V_LB1:V_LB1 + 4] = fm_vec(inp['hg_lb_logits'][1])
    lbrep = np.ascontiguousarray(np.broadcast_to(np.asarray(inp['hg_lb_logits'], f32)[None], (128, 2, 512)))
    p = np.arange(128)[:, None]
    j = np.arange(128)[None, :]
    ident = (p == j).astype(f32)
    ones = np.ones((128, 128), f32)
    negU = -(p >= j).astype(f32)
    negL = -(p < j).astype(f32)
    tri = (p <= j).astype(f32)
    cc = np.arange(512)[None, :]
    masks = np.concatenate([((p + d) < cc).astype(f32) for d in (0, 128, 256, 384)], axis=1)
    cst = np.ascontiguousarray(np.concatenate([ident, ones, negU, tri, negL, masks], axis=1))
    return dict(gu1=gu(inp['ffn1_w_gate'][0], inp['ffn1_w_up'][0]), d1=dn(inp['ffn1_w_down'][0]),
                gu2=gu(inp['ffn2_w_gate'][0], inp['ffn2_w_up'][0]), d2=dn(inp['ffn2_w_down'][0]),
                wfm=wfm, wtm=wtm, vecs=vecs, lbrep=lbrep, cst=cst)


_NC_CACHE = {}


def run(inp, nft):
    f32 = np.float32
    shared = _layouts(inp)
    if nft not in _NC_CACHE:
        _NC_CACHE[nft] = Builder(nft).build()
    nc = _NC_CACHE[nft]
    xp = np.asarray(inp['x_prompt'], f32)
    xs = np.asarray(inp['x_sample'], f32)
    ck = np.asarray(inp['cache_sb_k'], f32)
    cv = np.asarray(inp['cache_sb_v'], f32)
    st = np.asarray(inp['state_hgrn'], f32)
    meta = np.ascontiguousarray(np.asarray(inp['meta_tokens'], f32))
    B = xp.shape[0]
    in_maps = []
    for c in range(8):
        m = dict(shared)
        m['xp'] = np.ascontiguousarray(xp[c // 2])
        m['xs'] = np.ascontiguousarray(xs[c])
        m['meta'] = meta
        m['ck'] = np.ascontiguousarray(ck[0, c].reshape(LC, 512))
        m['cv'] = np.ascontiguousarray(cv[0, c].reshape(LC, 512))
        m['st'] = np.ascontiguousarray(st[0, c])
        in_maps.append(m)
    res = run_bass_kernel_spmd(nc, in_maps, core_ids=list(range(8)))
    r = res.results
    FR = nft * NT
    y_prompt = np.stack([r[2 * b]['yp'] for b in range(B)]).astype(f32)
    y_sample = np.stack([r[c]['ys'] for c in range(8)]).astype(f32)
    pk = np.stack([r[2 * b]['pk'].reshape(NMETA + FR, 8, 64) for b in range(B)])[None].astype(f32)
    pv = np.stack([r[2 * b]['pv'].reshape(NMETA + FR, 8, 64) for b in range(B)])[None].astype(f32)
    ph = np.stack([r[2 * b]['ph'] for b in range(B)])[None].astype(f32)
    sk = np.stack([r[c]['sk'].reshape(TS, 8, 64) for c in range(8)])[None].astype(f32)
    sv = np.stack([r[c]['sv'].reshape(TS, 8, 64) for c in range(8)])[None].astype(f32)
    sh = np.stack([r[c]['sh'] for c in range(8)])[None].astype(f32)
    return (y_prompt, y_sample, pk, pv, ph, sk, sv, sh)


def kernel(**inputs):
    nft = np.asarray(inputs['x_prompt']).shape[1] // NT
    return run(inputs, nft)
```

```python
import numpy as np
from contextlib import ExitStack
import concourse.bass as bass
import concourse.mybir as mybir
from concourse.bass_utils import run_bass_kernel_spmd

F32 = mybir.dt.float32
BF16 = mybir.dt.bfloat16
AF = mybir.ActivationFunctionType
ALU = mybir.AluOpType

D = 1024
DFF = 2816
NMETA = 16
TS = 32
LC = 2064
EPS = 1e-6
NT = 512
ENGS = ['pe', 'act', 'dve', 'pool', 'sp']

V_N1, V_NM, V_N2, V_NF, V_BGA, V_BGB, V_HGN, V_LB0, V_LB1, V_LBV, V_OML, V_FA, V_FB = 0, 8, 16, 24, 32, 40, 48, 52, 56, 60, 64, 68, 69
NVEC = 70


class Sched:
    def __init__(self):
        self.prog = {e: [] for e in ENGS}
        self.cnt = {e: 0 for e in ENGS}
        self.dma_cnt = {}
        self.last_w = {}
        self.readers = {}
        self.known = {e: {} for e in ENGS}
        self.fence = {}
        self.touched = set()
        self.nwaits = 0
        import os
        self.glimit = int(os.environ.get('KOPS', '100000000'))

    def _deps(self, reads, writes):
        need = {}

        def add(src, val):
            if need.get(src, 0) < val:
                need[src] = val
        for k in list(reads) + list(writes):
            if isinstance(k, tuple) and k[0] == 'SH' and k not in self.touched:
                self.touched.add(k)
                for s, v in self.fence.items():
                    add(s, v)
        for k in reads:
            lw = self.last_w.get(k)
            if lw:
                add(*lw)
        for k in writes:
            lw = self.last_w.get(k)
            if lw:
                add(*lw)
            for s, v in self.readers.get(k, {}).items():
                add(s, v)
        return need

    def _waits(self, eng, need):
        waits = []
        for src, val in need.items():
            if src == eng and eng == 'pe':
                continue
            if self.known[eng].get(src, 0) >= val:
                continue
            self.known[eng][src] = val
            waits.append((src, val))
        self.nwaits += len(waits)
        return waits

    def _mark(self, src, val, reads, writes):
        for k in writes:
            self.last_w[k] = (src, val)
            self.readers[k] = {}
        for k in reads:
            if k in writes:
                continue
            self.readers.setdefault(k, {})[src] = val

    def op(self, eng, fn, reads=(), writes=()):
        psr = [k for k in reads if isinstance(k, tuple) and k[0] == 'ps']
        if psr:
            reads = [k for k in reads if k not in psr]
            writes = list(writes) + [k for k in psr if k not in writes]
        self.gcount = getattr(self, 'gcount', 0) + 1
        if self.gcount > self.glimit:
            return
        import sys as _sys, os as _os
        if _os.environ.get('KTRACE'):
            lo, hi = [int(x) for x in _os.environ['KTRACE'].split(',')]
            if lo <= self.gcount <= hi:
                fr = _sys._getframe(2)
                print('OP', self.gcount, eng, 'line', fr.f_lineno, 'from', fr.f_back.f_lineno, 'reads', list(reads)[:3], 'writes', list(writes))
        need = self._deps(reads, writes)
        waits = self._waits(eng, need)
        self.cnt[eng] += 1
        self._mark(eng, self.cnt[eng], reads, writes)
        self.prog[eng].append(('op', waits, fn))

    def dma(self, q, out, in_, reads, writes, sem):
        self.gcount = getattr(self, 'gcount', 0) + 1
        if self.gcount > self.glimit:
            return
        import sys as _sys, os as _os
        if _os.environ.get('KTRACE'):
            lo, hi = [int(x) for x in _os.environ['KTRACE'].split(',')]
            if lo <= self.gcount <= hi:
                fr = _sys._getframe(1)
                print('DMA', self.gcount, q, 'line', fr.f_lineno, 'sem', sem, 'writes', list(writes))
        need = self._deps(reads, writes)
        waits = self._waits(q, need)
        self.dma_cnt[sem] = self.dma_cnt.get(sem, 0) + 16
        self._mark(sem, self.dma_cnt[sem], reads, writes)
        self.prog[q].append(('dma', waits, (out, in_, sem)))

    def phase_switch(self):
        f = dict(self.fence)

        def add(s, v):
            if f.get(s, 0) < v:
                f[s] = v
        for k in list(self.last_w.keys()):
            if isinstance(k, tuple) and k[0] == 'SH':
                add(*self.last_w[k])
                del self.last_w[k]
        for k in list(self.readers.keys()):
            if isinstance(k, tuple) and k[0] == 'SH':
                for s, v in self.readers[k].items():
                    add(s, v)
                del self.readers[k]
        self.fence = f
        self.touched = set()

    def final_wait(self, q):
        waits = [(s, v) for s, v in self.dma_cnt.items()]
        self.prog[q].append(('wait', waits, None))


class Builder:
    def __init__(self, npre, nmain):
        self.npre, self.nmain = npre, nmain
        nft = npre + nmain
        self.nft = nft
        self.FR = nft * NT
        self.KTW = max(NMETA + self.FR, NMETA + LC + TS + 128)
        self.NBLK = max(1 + 4 * nft, 19)
        self.s = Sched()
        self.ring_i = 0
        self.kv_i = 0
        self.xtok_i = 0

    def mm(self, out, lhsT, rhs, start, stop, reads, writes, sgc=False):
        self.s.op('pe', lambda e: e.matmul(out, lhsT=lhsT, rhs=rhs, start=start, stop=stop, skip_group_check=sgc), reads, writes)

    def tr(self, out, in_, n, reads, writes):
        ident = self.ident
        self.s.op('pe', lambda e: e.transpose(out=out, in_=in_, identity=ident[0:n, 0:n]), reads, writes)

    def act(self, out, in_, func, reads, writes, bias=None, scale=None):
        kw = {}
        if bias is not None:
            kw['bias'] = bias
        if scale is not None:
            kw['scale'] = scale
        self.s.op('act', lambda e: e.activation(out=out, in_=in_, func=func, **kw), reads, writes)

    def tt(self, eng, out, in0, in1, op, reads, writes):
        self.s.op(eng, lambda e: e.tensor_tensor(out=out, in0=in0, in1=in1, op=op), reads, writes)

    def tsc(self, eng, out, in0, s1, s2, op0, op1, reads, writes):
        if s2 is None:
            self.s.op(eng, lambda e: e.tensor_scalar(out=out, in0=in0, scalar1=s1, scalar2=None, op0=op0), reads, writes)
        else:
            self.s.op(eng, lambda e: e.tensor_scalar(out=out, in0=in0, scalar1=s1, scalar2=s2, op0=op0, op1=op1), reads, writes)

    def stt(self, eng, out, in0, scalar, in1, op0, op1, reads, writes):
        self.s.op(eng, lambda e: e.scalar_tensor_tensor(out=out, in0=in0, scalar=scalar, in1=in1, op0=op0, op1=op1), reads, writes)

    def cp(self, eng, out, in_, reads, writes):
        if eng == 'act':
            self.act(out, in_, AF.Copy, reads, writes)
        else:
            self.s.op(eng, lambda e: e.tensor_copy(out=out, in_=in_), reads, writes)

    def ring_load(self, dram_ap, nelem, rdkey):
        slot = self.ring_i % 3
        self.ring_i += 1
        out = self.ring[:, slot, 0:nelem]
        self.s.dma('sp', out, dram_ap, [rdkey], [('ring', slot)], 'ring%d' % slot)
        return slot

    def build(self):
        nc = bass.Bass("TRN2", target_bir_lowering=False)
        self.nc = nc
        nft, FR = self.nft, self.FR
        FM_ = self.nmain * NT
        FP_ = max(self.npre, 1) * NT
        dt_in = {}

        def din(name, shape):
            dt_in[name] = nc.dram_tensor(name, list(shape), F32, kind="ExternalInput").ap()
            return dt_in[name]

        def dout(name, shape):
            return nc.dram_tensor(name, list(shape), F32, kind="ExternalOutput").ap()
        self.xp = din("xp", [FM_, D])
        self.xpre = din("xpre", [FP_, D])
        self.flag_d = din("flag", [128, 512])
        self.xs = din("xs", [TS, D])
        self.meta = din("meta", [NMETA, D])
        self.ck = din("ck", [LC, 512])
        self.cv = din("cv", [LC, 512])
        self.st = din("st", [4, 128, 128])
        self.vecs_d = din("vecs", [128, NVEC])
        self.lbrep_d = din("lbrep", [128, 2, 512])
        self.cst_d = din("cst", [128, 5 * 128 + 4 * 512])
        self.w32 = {}
        self.wsz = {'gu1': 22 * 128 * 2048, 'gu2': 22 * 128 * 2048, 'd1': 2 * 8 * 128 * 1408, 'd2': 2 * 8 * 128 * 1408,
                    'wfm': 52 * 128 * 1024, 'wtm': 4 * 128 * 4096}
        for n, sz in self.wsz.items():
            self.w32[n] = din(n, [sz // 2048, 2048])
        self.yp = dout("yp", [FM_, D])
        self.ys = dout("ys", [TS, D])
        self.pk = dout("pk", [NMETA + FM_, 512])
        self.pv = dout("pv", [NMETA + FM_, 512])
        self.ph = dout("ph", [4, 128, 128])
        self.sk = dout("sk", [TS, 512])
        self.sv = dout("sv", [TS, 512])
        self.sh = dout("sh", [4, 128, 128])
        self.wbf = {n: nc.dram_tensor(n + "_bf", [sz], BF16).ap() for n, sz in self.wsz.items()}

        with ExitStack() as es:
            es.enter_context(nc.allow_low_precision("bf16 matmul operands, fp32 accumulation"))

            def sb(name, shape, dt):
                return es.enter_context(nc.sbuf_tensor(name, list(shape), dt))

            def ps(name):
                return es.enter_context(nc.psum_tensor(name, [128, 512], F32))
            self.ident = sb("ident", [128, 128], F32)
            self.onesb = sb("onesb", [128, 128], BF16)
            self.negU = sb("negU", [128, 128], BF16)
            self.negL = sb("negL", [128, 128], BF16)
            self.tri = sb("tri", [128, 128], F32)
            self.masks = sb("masks", [128, 4, 512], BF16)
            self.vecs = sb("vecs_s", [128, NVEC], F32)
            self.lb_t = sb("lb_t", [128, 2, 512], F32)
            self.xT = sb("xT", [128, 8, NT], F32)
            self.hT = sb("hT", [128, 8, NT + 64], BF16)
            self.rstd = sb("rstd", [128, NT], F32)
            self.lnt = sb("lnt", [128, NT], F32)
            self.KT = sb("KT", [128, 4, self.KTW], BF16)
            self.Vp = sb("Vp", [128, self.NBLK, 512], BF16)
            self.ring = sb("ring", [128, 3, 4096], BF16)
            self.S = sb("S", [128, 2, 4, 128], F32)
            self.Sb = sb("Sb", [128, 2, 4, 128], BF16)
            self.kvst = sb("kvst", [128, 2, 512], F32)
            self.Ssave = sb("Ssave", [128, 4, 128], F32)
            self.flag_t = sb("flag_t", [128, 512], BF16)
            self.qT = sb("qT", [128, 4, NT], BF16)
            self.obT = sb("obT", [128, 4, NT], F32)
            self.oaT = sb("oaT", [128, 4, NT], BF16)
            SHW = 11264
            self.SH = sb("SH", [128, SHW], F32)
            self.bank = [ps("bank%d" % i) for i in range(8)]
            self._views()

            self._emit_all()

            sems = {}
            names = [e for e in ENGS if e != 'sp'] + sorted(self.s.dma_cnt.keys())
            for n in names:
                sems[n] = es.enter_context(nc.semaphore("s_" + n))
            block = es.enter_context(nc.Block())
            prog = self.s.prog

            def run(eng_obj, ename):
                for kind, waits, payload in prog[ename]:
                    for src, val in waits:
                        eng_obj.wait_ge(sems[src], val)
                    if kind == 'op':
                        ins = payload(eng_obj)
                        ins.then_inc(sems[ename], 1)
                    elif kind == 'dma':
                        out, in_, sem = payload
                        eng_obj.dma_start(out=out, in_=in_).then_inc(sems[sem], 16)

            @block.tensor
            def _(e):
                run(e, 'pe')

            @block.scalar
            def _(e):
                run(e, 'act')

            @block.vector
            def _(e):
                run(e, 'dve')

            @block.gpsimd
            def _(e):
                run(e, 'pool')

            @block.sync
            def _(e):
                run(e, 'sp')
        return nc

    def _views(self):
        SH = self.SH

        def v(off_b, nbytes, dt, pattern=None, **kw):
            a = SH[:, off_b // 4:(off_b + nbytes) // 4]
            if dt is BF16:
                a = a.bitcast(BF16)
            if pattern:
                a = a.rearrange(pattern, **kw)
            return a
        self.hid = v(0, 11264, BF16, "p (c n) -> p c n", c=11)
        self.sg = v(11264, 4096, F32, "p (c n) -> p c n", c=2)
        self.x_tok = v(15360, 8192, F32, "p (c n) -> p c n", c=2)
        self.omfT = v(0, 8192, F32, "p (c n) -> p c n", c=4)
        self.qhT = v(8192, 8192, F32, "p (c n) -> p c n", c=4)
        self.logf = v(16384, 4096, F32, "p (c n) -> p c n", c=2)
        self.omf_tm = v(20480, 4096, F32, "p (c n) -> p c n", c=2)
        self.iv = v(24576, 2048, BF16, "p (c n) -> p c n", c=2)
        self.eb = v(26624, 2048, F32, "p (a h t) -> p a h t", a=2, h=4)
        self.enb = v(28672, 2048, F32, "p (a h t) -> p a h t", a=2, h=4)
        self.qe = v(30720, 1024, BF16, "p (a h t) -> p a h t", a=2, h=4)
        self.ke = v(41984, 2048, BF16, "p (a h t) -> p a h t", a=2, h=4)
        self.enb_tm = v(32768, 4096, F32, "p (c n) -> p c n", c=2)
        self.ke_tm = v(36864, 2048, BF16, "p (c n) -> p c n", c=2)
        self.sc = v(38912, 1024, BF16, "p (a h t) -> p a h t", a=2, h=4)
        self.tmpS = v(39936, 2048, F32, "p (h t) -> p h t", h=4)
        self.e_ = v(0, 12288, F32, "p (c n) -> p c n", c=6)
        self.sp_ = v(12288, 6144, BF16, "p (c n) -> p c n", c=6)
        self.e2_ = v(18432, 6144, F32, "p (c n) -> p c n", c=3)
        self.at_ = v(24576, 6144, BF16, "p (c n) -> p c n", c=6)
        self.ckv = v(30720, 8192, F32, "p (c n) -> p c n", c=2)
        self.sgT = v(0, 8192, F32, "p (c n) -> p c n", c=4)
        self.gA = v(8192, 8192, BF16, "p (c n) -> p c n", c=8)
        self.gB = v(16384, 8192, BF16, "p (c n) -> p c n", c=8)
        self.t1 = v(24576, 4096, F32, "p (c n) -> p c n", c=2)
        self.t2 = v(28672, 4096, F32, "p (c n) -> p c n", c=2)
        self.mT = v(32768, 8192, BF16, "p (c n) -> p c n", c=8)
        self.obn = v(40960, 4096, BF16, "p (c n) -> p c n", c=4)

    def _emit_all(self):
        import os
        self.kstop = int(os.environ.get('KSTOP', '99'))
        self.prologue()
        if self.kstop >= 1:
            self.tile(-1)
        if self.kstop >= 20:
            for p in range(self.npre):
                self.tile(p, 'pre')
            if self.npre > 0:
                self.state_select()
            for m in range(self.nmain):
                self.tile(self.npre + m, 'main')
        self.s.final_wait('sp')

    def prologue(self):
        s = self.s
        c = self.cst_d
        o = 0
        s.dma('pool', self.ident[:], c[:, o:o + 128], [], ['ident'], 'cst'); o += 128
        s.dma('pool', self.onesb[:], c[:, o:o + 128], [], ['onesb'], 'cst'); o += 128
        s.dma('pool', self.negU[:], c[:, o:o + 128], [], ['negU'], 'cst'); o += 128
        s.dma('pool', self.tri[:], c[:, o:o + 128], [], ['tri'], 'cst'); o += 128
        s.dma('pool', self.negL[:], c[:, o:o + 128], [], ['negL'], 'cst'); o += 128
        s.dma('pool', self.masks[:], c[:, o:o + 2048].rearrange("p (d n) -> p d n", d=4), [], ['masks'], 'cst'); o += 2048
        s.dma('pool', self.vecs[:], self.vecs_d, [], ['vecs'], 'cst')
        s.dma('pool', self.lb_t[:], self.lbrep_d, [], ['lb_t'], 'cst')
        s.dma('pool', self.flag_t[:], self.flag_d, [], ['flag_t'], 'cst')
        allc = ['ident', 'onesb', 'negU', 'negL', 'tri', 'masks', 'vecs', 'lb_t', 'flag_t']
        tot = s.dma_cnt['cst']
        for k in allc:
            s.last_w[k] = ('cst', tot)
        pieces = [('gu1', 0, 1408), ('d1', 0, 704), ('gu1', 1408, 1408), ('d1', 704, 704),
                  ('wfm', 0, 1024), ('wtm', 0, 1024), ('wfm', 1024, 2304),
                  ('gu2', 0, 1408), ('d2', 0, 704), ('gu2', 1408, 1408), ('d2', 704, 704)]
        for i, (n, r0, nr) in enumerate(pieces):
            dst = self.wbf[n].rearrange("(r c) -> r c", c=2048)[r0:r0 + nr, :]
            s.dma('pool', dst, self.w32[n][r0:r0 + nr, :], [], [('scr', n, r0)], 'cast%d' % i)
        self.s.op('dve', lambda e: e.memset(self.hT[:, :, :], 0.0), [], [('hT', kc) for kc in range(8)])
        self.s.op('pool', lambda e: e.memset(self.KT[:, :, :], 0.0), [], [('KT', c) for c in range(4)])
        self.s.op('dve', lambda e: e.memset(self.xT[:, :, :], 0.0), [], [('xT', kc) for kc in range(8)])
        self.s.op('pool', lambda e: e.memset(self.SH[:, :], 0.0), [], [('SH', 'all')])
        vv = self.vecs
        self.tt('dve', vv[:, V_LBV:V_LBV + 4], vv[:, V_LB1:V_LB1 + 4], vv[:, V_LB0:V_LB0 + 4], ALU.subtract, ['vecs'], ['vecs'])
        self.act(vv[:, V_LBV:V_LBV + 4], vv[:, V_LBV:V_LBV + 4], AF.Sigmoid, ['vecs'], ['vecs'])
        self.tsc('dve', vv[:, V_OML:V_OML + 4], vv[:, V_LBV:V_LBV + 4], -1.0, 1.0, ALU.mult, ALU.add, ['vecs'], ['vecs'])
        lt = self.lb_t
        self.tt('dve', lt[:, 0, :], lt[:, 1, :], lt[:, 0, :], ALU.subtract, ['lb_t'], ['lb_t'])
        self.act(lt[:, 0, :], lt[:, 0, :], AF.Sigmoid, ['lb_t'], ['lb_t'])
        self.tsc('dve', lt[:, 1, :], lt[:, 0, :], -1.0, 1.0, ALU.mult, ALU.add, ['lb_t'], ['lb_t'])

    def scr_key(self, n, row2048):
        bounds = {'gu1': [0, 1408], 'gu2': [0, 1408], 'd1': [0, 704], 'd2': [0, 704], 'wfm': [0, 1024], 'wtm': [0]}[n]
        r0 = max(b for b in bounds if b <= row2048)
        return ('scr', n, r0)

    def load_gu(self, which, c):
        n = 'gu%d' % which
        ap = self.wbf[n].rearrange("(c p f) -> c p f", p=128, f=2048)[c]
        return self.ring_load(ap, 2048, self.scr_key(n, c * 128))

    def load_d(self, which, half, o):
        n = 'd%d' % which
        ap = self.wbf[n].rearrange("(h o p f) -> h o p f", h=2, o=8, p=128)[half, o]
        return self.ring_load(ap, 1408, self.scr_key(n, (half * 8 + o) * 128 * 1408 // 2048))

    def load_fm(self, c0, ncnk):
        slot = self.ring_i % 3
        self.ring_i += 1
        src = self.wbf['wfm'].rearrange("(c p f) -> c p f", p=128, f=1024)
        for i in range(ncnk):
            self.s.dma('sp', self.ring[:, slot, i * 1024:(i + 1) * 1024], src[c0 + i], [self.scr_key('wfm', c0 * 64)], [('ring', slot)], 'ring%d' % slot)
        return slot

    def load_tm(self, g):
        ap = self.wbf['wtm'].rearrange("(g p f) -> g p f", p=128, f=4096)[g]
        return self.ring_load(ap, 4096, ('scr', 'wtm', 0))

    def state_select(self):
        S0 = self.S[:, 0, :, :]
        self.tsc('dve', self.Ssave[:, :, :], self.Ssave[:, :, :], self.vecs[:, V_FA:V_FA + 1], None, ALU.mult, None, ['Ssave', 'vecs'], ['Ssave'])
        self.stt('dve', S0, S0, self.vecs[:, V_FB:V_FB + 1], self.Ssave[:, :, :], ALU.mult, ALU.add, [('S', 0), 'Ssave', 'vecs'], [('S', 0)])
        for hh in range(4):
            self.cp('pool', self.Sb[:, 0, hh, :], self.S[:, 0, hh, :], [('S', 0)], [('Sb', 0, hh)])

    def tile(self, f, mode='extra'):
        self.mode = mode
        self.m = f - self.npre if mode == 'main' else f
        if f < 0:
            n = TS + NMETA
            segs = [('sample', 0, TS), ('meta', TS, NMETA)]
        else:
            n = NT
            segs = [('frames', 0, NT)]
        self.n = n
        self.f = f
        self.segs = segs
        s = self.s
        ks = self.kstop if f < 0 else 99
        s.phase_switch()
        self.load_x()
        if mode == 'pre':
            self.norm(V_N1)
            self.ffn(1)
            self.norm(V_NM)
            s.phase_switch()
            self.w_in()
            self.hgrn()
            return
        if ks < 2: return
        self.norm(V_N1)
        if ks < 3: return
        self.ffn(1)
        if ks < 4: return
        self.norm(V_NM)
        s.phase_switch()
        self.w_in()
        if ks < 5 or ks == 41: return
        self.hgrn()
        if ks < 6: return
        s.phase_switch()
        self.attn()
        if ks < 7: return
        s.phase_switch()
        self.post()
        if ks < 8: return
        s.phase_switch()
        self.norm(V_N2)
        self.ffn(2)
        self.final()

    def load_x(self):
        n, f = self.n, self.f
        if f < 0:
            blocks = [(0, n)]
        else:
            blocks = [(j * 128, 128) for j in range(4)]
        for bi, (c0, nb) in enumerate(blocks):
            slot = self.xtok_i % 2
            self.xtok_i += 1
            xk = ('SH', 'x_tok', slot)
            if f < 0:
                self.s.dma('pool', self.x_tok[0:TS, slot, :], self.xs, [], [xk], 'xin%d' % slot)
                self.s.dma('pool', self.x_tok[TS:n, slot, :], self.meta, [], [xk], 'xin%d' % slot)
            else:
                r0 = self.m * NT + c0
                srcx = self.xpre if self.mode == 'pre' else self.xp
                self.s.dma('pool', self.x_tok[:, slot, :], srcx[r0:r0 + 128, :], [], [xk], 'xin%d' % slot)
            for kc in range(8):
                bk = self.bank[kc % 2]
                self.tr(bk[:, 0:nb], self.x_tok[0:nb, slot, kc * 128:(kc + 1) * 128], nb, [xk, 'ident'], [('ps', kc % 2)])
                eng = 'dve' if kc % 2 == 0 else 'act'
                self.cp(eng, self.xT[:, kc, c0:c0 + nb], bk[:, 0:nb], [('ps', kc % 2)], [('xT', kc)])

    def norm(self, gcol, inplace=False):
        n = self.n
        for kc in range(8):
            self.act(self.hT[:, kc, 0:n], self.xT[:, kc, 0:n], AF.Square, [('xT', kc)], [('hT', kc)])
        bk = self.bank[7]
        for kc in range(8):
            self.mm(bk[:, 0:n], self.onesb[:, :], self.hT[:, kc, 0:n], kc == 0, kc == 7, [('hT', kc), 'onesb'], [('ps', 7)])
        self.act(self.lnt[:, 0:n], bk[:, 0:n], AF.Ln, [('ps', 7)], ['lnt'], bias=EPS, scale=1.0 / D)
        self.act(self.rstd[:, 0:n], self.lnt[:, 0:n], AF.Exp, ['lnt'], ['rstd'], scale=-0.5)
        for kc in range(8):
            eng = 'dve'
            if inplace:
                self.stt(eng, self.xT[:, kc, 0:n], self.xT[:, kc, 0:n], self.vecs[:, gcol + kc:gcol + kc + 1], self.rstd[:, 0:n],
                         ALU.mult, ALU.mult, [('xT', kc), 'rstd', 'vecs'], [('xT', kc)])
            else:
                self.stt(eng, self.hT[:, kc, 0:n], self.xT[:, kc, 0:n], self.vecs[:, gcol + kc:gcol + kc + 1], self.rstd[:, 0:n],
                         ALU.mult, ALU.mult, [('xT', kc), 'rstd', 'vecs'], [('hT', kc)])

    def ffn(self, which):
        n = self.n
        hk = [('hT', kc) for kc in range(8)]
        it = 0
        for half in range(2):
            for cc in range(11):
                c = half * 11 + cc
                slot = self.load_gu(which, c)
                w = self.ring[:, slot, 0:2048].rearrange("p (g k n) -> p g k n", g=2, k=8)
                gb, ub = it % 2, 2 + it % 2
                for kc in range(8):
                    self.mm(self.bank[gb][:, 0:n], w[:, 0, kc, :], self.hT[:, kc, 0:n], kc == 0, kc == 7, hk + [('ring', slot)], [('ps', gb)])
                for kc in range(8):
                    self.mm(self.bank[ub][:, 0:n], w[:, 1, kc, :], self.hT[:, kc, 0:n], kc == 0, kc == 7, hk + [('ring', slot)], [('ps', ub)])
                sgk = ('SH', 'sg', it % 2)
                self.act(self.sg[:, it % 2, 0:n], self.bank[gb][:, 0:n], AF.Silu, [('ps', gb)], [sgk])
                self.tt('dve', self.hid[:, cc, 0:n], self.sg[:, it % 2, 0:n], self.bank[ub][:, 0:n], ALU.mult, [sgk, ('ps', ub)], [('SH', 'hid', cc)])
                it += 1
            for o in range(8):
                slot = self.load_d(which, half, o)
                w = self.ring[:, slot, 0:1408].rearrange("p (c n) -> p c n", c=11)
                db = 4 + o % 2
                for cc in range(11):
                    self.mm(self.bank[db][:, 0:n], w[:, cc, :], self.hid[:, cc, 0:n], cc == 0, cc == 10, [('SH', 'hid', cc), ('ring', slot)], [('ps', db)])
                self.stt('dve', self.xT[:, o, 0:n], self.bank[db][:, 0:n], 0.5, self.xT[:, o, 0:n], ALU.mult, ALU.add, [('ps', db), ('xT', o)], [('xT', o)])

    def kcol(self, seg):
        if seg == 'sample':
            return NMETA + LC
        if seg == 'meta':
            return 0
        return NMETA + self.f * NT

    def w_in(self):
        n = self.n
        hk = [('hT', kc) for kc in range(8)]
        bi = 0
        for g4 in ([1] if self.mode == 'pre' else range(4)):
            slot = self.load_fm(g4 * 4, 4)
            w = self.ring[:, slot, 0:4096].rearrange("p (c k n) -> p c k n", c=4, k=8)
            for ci in range(4):
                b = bi % 4
                bi += 1
                bk = self.bank[b]
                for kc in range(8):
                    self.mm(bk[:, 0:n], w[:, ci, kc, :], self.hT[:, kc, 0:n], kc == 0, kc == 7, hk + [('ring', slot)], [('ps', b)])
                if g4 == 0:
                    self.act(self.qT[:, ci, 0:n], bk[:, 0:n], AF.Copy, [('ps', b)], [('qT', ci)], scale=0.125)
                elif g4 == 1:
                    for (sname, c0, ns) in self.segs:
                        kc0 = self.kcol(sname)
                        self.cp('dve', self.KT[:, ci, kc0:kc0 + ns], bk[:, c0:c0 + ns], [('ps', b)], [('KT', ci)])
                elif g4 == 2:
                    self.act(self.lnt[:, 0:n], bk[:, 0:n], AF.Sigmoid, [('ps', b)], ['lnt'], scale=-1.0)
                    self.tsc('dve', self.omfT[:, ci, 0:n], self.lnt[:, 0:n], self.vecs[:, V_OML + ci:V_OML + ci + 1], None, ALU.mult, None,
                             ['lnt', 'vecs'], [('SH', 'omfT', ci)])
                else:
                    self.cp('act', self.qhT[:, ci, 0:n], bk[:, 0:n], [('ps', b)], [('SH', 'qhT', ci)])
        if self.kstop == 41:
            return
        if self.f < 0:
            blocks = [('sample', 0, TS, self.sk, self.sv, 0, 18), ('meta', TS, NMETA, self.pk, self.pv, 0, 0)]
        else:
            blocks = [('frames', j * 128, 128, self.pk, self.pv, NMETA + self.m * NT + j * 128, 1 + 4 * self.f + j) for j in range(4)]
        for g in ([1] if self.mode == 'pre' else range(2)):
            slot = self.load_tm(g)
            w = self.ring[:, slot, 0:4096].rearrange("p (k n) -> p k n", k=8)
            for (sname, c0, nb, okd, ovd, r0, vblk) in blocks:
                b = 4 + bi % 4
                bi += 1
                bk = self.bank[b]
                for kc in range(8):
                    self.mm(bk[:, :], self.hT[:, kc, c0:c0 + 128], w[:, kc, :], kc == 0, kc == 7, hk + [('ring', slot)], [('ps', b)])
                if self.mode == 'pre':
                    self.cp('act', self.Vp[0:nb, vblk, :], bk[0:nb, :], [('ps', b)], [('Vp', vblk)])
                    continue
                ks = self.kv_i % 2
                self.kv_i += 1
                self.cp('dve', self.kvst[0:nb, ks, :], bk[0:nb, :], [('ps', b)], [('kvst', ks)])
                dst = (okd if g == 0 else ovd)[r0:r0 + nb, :]
                import os
                if not os.environ.get('NOKV'):
                    self.s.dma(os.environ.get('KVQ', 'pool'), dst, self.kvst[0:nb, ks, :], [('kvst', ks)], [], 'kvo%d' % ks)
                if g == 1:
                    self.cp('act', self.Vp[0:nb, vblk, :], bk[0:nb, :], [('ps', b)], [('Vp', vblk)])

    def hgrn(self):
        s = self.s
        hk = [('hT', kc) for kc in range(8)]
        slz = self.load_tm(2)
        sli = self.load_tm(3)
        wz = self.ring[:, slz, 0:4096].rearrange("p (k n) -> p k n", k=8)
        wi = self.ring[:, sli, 0:4096].rearrange("p (k n) -> p k n", k=8)
        if self.f < 0:
            chunks = [(0, TS, 1), (TS, NMETA, 0)]
            self.s.dma('pool', self.S[:, 1, :, :], self.st.rearrange("h k v -> k h v"), [], [('S', 1)], 'stin')
            for hh in range(4):
                self.cp('pool', self.Sb[:, 1, hh, :], self.S[:, 1, hh, :], [('S', 1)], [('Sb', 1, hh)])
            self.s.op('dve', lambda e: e.memset(self.S[:, 0, :, :], 0.0), [], [('S', 0)])
            self.s.op('dve', lambda e: e.memset(self.Sb[:, 0, :, :], 0.0), [], [('Sb', 0, hh) for hh in range(4)])
        else:
            chunks = [(i * 64, 64, 0) for i in range(8)]
        for ci, (c0, T, si) in enumerate(chunks):
            par = ci % 2
            bA, bB, bC, bD, bE, bF = self.bank[0], self.bank[1], self.bank[2 + par], self.bank[4], self.bank[5], self.bank[6 + par]
            kC, kF = ('ps', 2 + par), ('ps', 6 + par)
            for kc in range(8):
                self.mm(bA[:, :], self.hT[:, kc, c0:c0 + 128], wz[:, kc, :], kc == 0, kc == 7, hk + [('ring', slz)], [('ps', 0)])
            for kc in range(8):
                self.mm(bB[:, :], self.hT[:, kc, c0:c0 + 128], wi[:, kc, :], kc == 0, kc == 7, hk + [('ring', sli)], [('ps', 1)])
            k_omf, k_logf, k_iv = ('SH', 'omf_tm', par), ('SH', 'logf', par), ('SH', 'iv', par)
            self.act(self.omf_tm[0:T, par, :], bA[0:T, :], AF.Sigmoid, [('ps', 0)], [k_omf], scale=-1.0)
            self.tt('dve', self.omf_tm[0:T, par, :], self.omf_tm[0:T, par, :], self.lb_t[0:T, 1, :], ALU.mult, [k_omf, 'lb_t'], [k_omf])
            self.act(self.logf[0:T, par, :], self.omf_tm[0:T, par, :], AF.Ln, [k_omf], [k_logf], bias=1.0, scale=-1.0)
            self.cp('act', self.iv[0:T, par, :], bB[0:T, :], [('ps', 1)], [k_iv])
            cheap = self.mode == 'pre'
            if not cheap:
                for hh in range(4):
                    self.mm(bC[:, hh * 64:hh * 64 + T], self.logf[0:T, par, hh * 128:(hh + 1) * 128], self.tri[0:T, 0:T], True, True,
                            [k_logf, 'tri'], [kC])
            self.mm(bD[:, :], self.tri[0:T, 0:128], self.logf[0:T, par, :], True, True, [k_logf, 'tri'], [('ps', 4)])
            k_enbt, k_ket = ('SH', 'enb_tm', par), ('SH', 'ke_tm', par)
            self.act(self.enb_tm[0:T, par, :], bD[0:T, :], AF.Exp, [('ps', 4)], [k_enbt], scale=-1.0)
            self.tt('dve', self.ke_tm[0:T, par, :], self.omf_tm[0:T, par, :], self.enb_tm[0:T, par, :], ALU.mult, [k_omf, k_enbt], [k_ket])
            if cheap:
                for hh in range(4):
                    self.mm(bC[:, hh * 64:hh * 64 + 2], self.logf[0:T, par, hh * 128:(hh + 1) * 128], self.tri[0:T, 126:128], True, True,
                            [k_logf, 'tri'], [kC])
                for hh in range(4):
                    self.act(self.eb[:, par, hh, T - 1:T], bC[:, hh * 64:hh * 64 + 1], AF.Exp, [kC], [('SH', 'eb', par, hh)])
            else:
                for hh in range(4):
                    k_eb, k_enb, k_qe, k_ke = ('SH', 'eb', par, hh), ('SH', 'enb', par, hh), ('SH', 'qe', par, hh), ('SH', 'ke', par, hh)
                    self.act(self.eb[:, par, hh, 0:T], bC[:, hh * 64:hh * 64 + T], AF.Exp, [kC], [k_eb])
                    self.act(self.enb[:, par, hh, 0:T], bC[:, hh * 64:hh * 64 + T], AF.Exp, [kC], [k_enb], scale=-1.0)
                    self.tt('dve', self.qe[:, par, hh, 0:T], self.qhT[:, hh, c0:c0 + T], self.eb[:, par, hh, 0:T], ALU.mult,
                            [('SH', 'qhT', hh), k_eb], [k_qe])
                    self.tt('pool', self.ke[:, par, hh, 0:T], self.omfT[:, hh, c0:c0 + T], self.enb[:, par, hh, 0:T], ALU.mult,
                            [('SH', 'omfT', hh), k_enb], [k_ke])
                for hh in range(4):
                    k_qe, k_ke = ('SH', 'qe', par, hh), ('SH', 'ke', par, hh)
                    self.mm(bC[:, 256 + hh * 64:256 + hh * 64 + T], self.ke[:, par, hh, 0:128], self.qe[:, par, hh, 0:T], True, True,
                            [k_qe, k_ke], [kC])
                for hh in range(4):
                    k_sc = ('SH', 'sc', par, hh)
                    self.tt('dve', self.sc[0:T, par, hh, 0:T], bC[0:T, 256 + hh * 64:256 + hh * 64 + T], self.tri[0:T, 0:T], ALU.mult,
                            [kC, 'tri'], [k_sc])
                for hh in range(4):
                    k_sc, k_qe = ('SH', 'sc', par, hh), ('SH', 'qe', par, hh)
                    self.mm(bE[:, hh * 64:hh * 64 + T], self.iv[0:T, par, hh * 128:(hh + 1) * 128], self.sc[0:T, par, hh, 0:T], True, False,
                            [k_iv, k_sc], [('ps', 5)])
                    self.mm(bE[:, hh * 64:hh * 64 + T], self.Sb[:, si, hh, :], self.qe[:, par, hh, 0:T], False, True,
                            [('Sb', si, hh), k_qe], [('ps', 5)])
                for hh in range(4):
                    self.cp('act' if hh % 2 == 0 else 'dve', self.obT[:, hh, c0:c0 + T], bE[:, hh * 64:hh * 64 + T], [('ps', 5)], [('obT', hh)])
            for hh in range(4):
                self.mm(bF[:, hh * 128:(hh + 1) * 128], self.ke_tm[0:T, par, hh * 128:(hh + 1) * 128], self.iv[0:T, par, hh * 128:(hh + 1) * 128],
                        True, True, [k_ket, k_iv], [kF])
            for hh in range(4):
                k_eb = ('SH', 'eb', par, hh)
                ebl = self.eb[:, par, hh, T - 1:T]
                self.tsc('dve', self.tmpS[:, hh, :], self.S[:, si, hh, :], ebl, None, ALU.mult, None, [('S', si), k_eb], [('SH', 'tmpS', hh)])
                self.stt('dve', self.S[:, si, hh, :], bF[:, hh * 128:(hh + 1) * 128], ebl, self.tmpS[:, hh, :], ALU.mult, ALU.add,
                         [kF, k_eb, ('SH', 'tmpS', hh)], [('S', si)])
                self.cp('pool', self.Sb[:, si, hh, :], self.S[:, si, hh, :], [('S', si)], [('Sb', si, hh)])
            if self.f < 0 and si == 1:
                self.s.dma('pool', self.sh.rearrange("h k v -> k h v"), self.S[:, 1, :, :], [('S', 1)], [], 'sout')
            if self.f < 0 and si == 0 and self.npre > 0:
                self.cp('pool', self.Ssave[:, :, :], self.S[:, 0, :, :], [('S', 0)], ['Ssave'])
            if self.mode == 'main' and self.f == self.nft - 1 and ci == len(chunks) - 1:
                self.s.dma('pool', self.ph.rearrange("h k v -> k h v"), self.S[:, 0, :, :], [('S', 0)], [], 'sout')

    def attn(self):
        f = self.f
        if f < 0:
            nb_c = (LC + 127) // 128
            for b in range(nb_c):
                r0 = b * 128
                kn = min(128, LC - r0)
                slot = b % 2
                ck_key = ('SH', 'ckv', slot)
                self.s.dma('pool', self.ckv[0:kn, slot, 0:512], self.ck[r0:r0 + kn, :], [], [ck_key], 'ckin%d' % slot)
                self.s.dma('pool', self.ckv[0:kn, slot, 512:1024], self.cv[r0:r0 + kn, :], [], [ck_key], 'ckin%d' % slot)
                bk = self.bank[b % 2]
                for ch in range(4):
                    self.tr(bk[:, ch * 128:ch * 128 + kn], self.ckv[0:kn, slot, ch * 128:(ch + 1) * 128], kn, [ck_key, 'ident'], [('ps', b % 2)])
                for ch in range(4):
                    self.cp('dve' if ch % 2 == 0 else 'act', self.KT[:, ch, NMETA + r0:NMETA + r0 + kn], bk[:, ch * 128:ch * 128 + kn],
                            [('ps', b % 2)], [('KT', ch)])
                self.cp('pool', self.Vp[0:kn, 1 + b, :], self.ckv[0:kn, slot, 512:1024], [ck_key], [('Vp', 1 + b)])
            blocks = [(18, TS, NMETA + LC, 0)]
            blocks.append((1 + nb_c - 1, LC - 128 * (nb_c - 1), NMETA + 128 * (nb_c - 1), None))
            for b in range(nb_c - 2, -1, -1):
                blocks.append((1 + b, 128, NMETA + 128 * b, None))
            self.attn_job(0, TS, blocks)
            self.attn_job(TS, NMETA, [(0, NMETA, 0, 0)])
        else:
            blocks = []
            for m in range(3, -1, -1):
                fb = 4 * f + m
                blocks.append((1 + fb, 128, NMETA + 128 * fb, 128 * m))
            for fb in range(4 * f - 1, -1, -1):
                blocks.append((1 + fb, 128, NMETA + 128 * fb, 'flag' if fb < 4 * self.npre else None))
            blocks.append((0, NMETA, 0, None))
            self.attn_job(0, NT, blocks)

    def attn_job(self, q0, nq, blocks):
        nblk = len(blocks)
        for grp in ([0, 1, 2], [3, 4, 5], [6, 7]):
            S_ = len(grp)
            nseq = nblk * S_

            def emitZ(idx):
                k, si = divmod(idx, S_)
                h = grp[si]
                vblk, kn, kcol, md = blocks[k]
                ch, pb = h // 2, 64 * (h % 2)
                zb = idx % 2
                self.mm(self.bank[zb][:, 0:nq], self.KT[pb:pb + 64, ch, kcol:kcol + 128], self.qT[pb:pb + 64, ch, q0:q0 + nq], True, True,
                        [('KT', ch), ('qT', ch)], [('ps', zb)])
            for idx in range(min(2, nseq)):
                emitZ(idx)
            for k in range(nblk):
                vblk, kn, kcol, md = blocks[k]
                par = k % 2
                for si, h in enumerate(grp):
                    idx = k * S_ + si
                    zb = idx % 2
                    ke_, ks_ = ('SH', 'e', si, par), ('SH', 'sp', si, par)
                    self.act(self.e_[0:kn, si * 2 + par, 0:nq], self.bank[zb][0:kn, 0:nq], AF.Exp, [('ps', zb)], [ke_])
                    if idx + 2 < nseq:
                        emitZ(idx + 2)
                    if md == 'flag':
                        self.tt('pool', self.e_[0:kn, si * 2 + par, 0:nq], self.e_[0:kn, si * 2 + par, 0:nq], self.flag_t[0:kn, 0:nq],
                                ALU.mult, [ke_, 'flag_t'], [ke_])
                    elif md is not None:
                        self.tt('pool', self.e_[0:kn, si * 2 + par, 0:nq], self.e_[0:kn, si * 2 + par, 0:nq], self.masks[0:kn, md // 128, 0:nq],
                                ALU.mult, [ke_, 'masks'], [ke_])
                    self.act(self.sp_[0:kn, si * 2 + par, 0:nq], self.e_[0:kn, si * 2 + par, 0:nq], AF.Ln, [ke_], [ks_], bias=1.0)
                for si, h in enumerate(grp):
                    ks_ = ('SH', 'sp', si, par)
                    self.mm(self.bank[2 + si][:, 0:nq], self.negU[0:kn, :], self.sp_[0:kn, si * 2 + par, 0:nq], k == 0, False,
                            [ks_, 'negU'], [('ps', 2 + si)], sgc=True)
                for si, h in enumerate(grp):
                    ke_, k2_, ka_ = ('SH', 'e', si, par), ('SH', 'e2', si), ('SH', 'at', si, par)
                    self.act(self.e2_[0:kn, si, 0:nq], self.bank[2 + si][0:kn, 0:nq], AF.Exp, [('ps', 2 + si)], [k2_])
                    self.tt('dve', self.at_[0:kn, si * 2 + par, 0:nq], self.e2_[0:kn, si, 0:nq], self.e_[0:kn, si * 2 + par, 0:nq], ALU.mult,
                            [k2_, ke_], [ka_])
                for si, h in enumerate(grp):
                    ks_, ka_ = ('SH', 'sp', si, par), ('SH', 'at', si, par)
                    self.mm(self.bank[2 + si][:, 0:nq], self.negL[0:kn, :], self.sp_[0:kn, si * 2 + par, 0:nq], False, k == nblk - 1,
                            [ks_, 'negL'], [('ps', 2 + si)], sgc=True)
                    vlo = h * 64 if h % 2 == 0 else (h - 1) * 64
                    self.mm(self.bank[5 + si][:, 0:nq], self.Vp[0:kn, vblk, vlo:vlo + 128], self.at_[0:kn, si * 2 + par, 0:nq], k == 0, k == nblk - 1,
                            [ka_, ('Vp', vblk)], [('ps', 5 + si)])
            for si, h in enumerate(grp):
                ch, pb = h // 2, 64 * (h % 2)
                self.cp('dve' if si % 2 == 0 else 'act', self.oaT[pb:pb + 64, ch, q0:q0 + nq], self.bank[5 + si][pb:pb + 64, 0:nq],
                        [('ps', 5 + si)], [('oaT', ch, pb)])

    def post(self):
        n = self.n
        hk = [('hT', kc) for kc in range(8)]
        bi = 0
        for ld in range(5):
            slot = self.load_fm(16 + ld * 4, 4)
            w = self.ring[:, slot, 0:4096].rearrange("p (c k n) -> p c k n", c=4, k=8)
            for ci in range(4):
                cidx = ld * 4 + ci
                b = bi % 4
                bi += 1
                bk = self.bank[b]
                for kc in range(8):
                    self.mm(bk[:, 0:n], w[:, ci, kc, :], self.hT[:, kc, 0:n], kc == 0, kc == 7, hk + [('ring', slot)], [('ps', b)])
                if cidx < 4:
                    self.act(self.sgT[:, cidx, 0:n], bk[:, 0:n], AF.Silu, [('ps', b)], [('SH', 'sgT', cidx)])
                elif cidx < 12:
                    o = cidx - 4
                    self.act(self.gA[:, o, 0:n], bk[:, 0:n], AF.Sigmoid, [('ps', b), 'vecs'], [('SH', 'gA', o)], bias=self.vecs[:, V_BGA + o:V_BGA + o + 1])
                else:
                    o = cidx - 12
                    self.act(self.gB[:, o, 0:n], bk[:, 0:n], AF.Sigmoid, [('ps', b), 'vecs'], [('SH', 'gB', o)], bias=self.vecs[:, V_BGB + o:V_BGB + o + 1])
        for hh in range(4):
            kq = ('SH', 'mT', hh)
            self.act(self.mT[:, hh, 0:n], self.obT[:, hh, 0:n], AF.Square, [('obT', hh)], [kq])
            self.mm(self.bank[4][:, 0:n], self.onesb[:, :], self.mT[:, hh, 0:n], True, True, [kq, 'onesb'], [('ps', 4)])
            self.act(self.lnt[:, 0:n], self.bank[4][:, 0:n], AF.Ln, [('ps', 4)], ['lnt'], bias=EPS, scale=1.0 / 128)
            self.act(self.rstd[:, 0:n], self.lnt[:, 0:n], AF.Exp, ['lnt'], ['rstd'], scale=-0.5)
            k1 = ('SH', 't1', hh % 2)
            self.stt('dve', self.t1[:, hh % 2, 0:n], self.obT[:, hh, 0:n], self.vecs[:, V_HGN + hh:V_HGN + hh + 1], self.rstd[:, 0:n], ALU.mult, ALU.mult,
                     [('obT', hh), 'vecs', 'rstd'], [k1])
            self.tt('pool', self.obn[:, hh, 0:n], self.t1[:, hh % 2, 0:n], self.sgT[:, hh, 0:n], ALU.mult, [k1, ('SH', 'sgT', hh)], [('SH', 'obn', hh)])
        oak = [('oaT', c, pb) for c in range(4) for pb in (0, 64)]
        obk = [('SH', 'obn', c) for c in range(4)]
        for ld in range(2):
            slot = self.load_fm(36 + ld * 4, 4)
            w = self.ring[:, slot, 0:4096].rearrange("p (c k n) -> p c k n", c=4, k=8)
            for ci in range(4):
                o = ld * 4 + ci
                ba, bb = o % 2, 2 + o % 2
                for c in range(4):
                    self.mm(self.bank[ba][:, 0:n], w[:, ci, c, :], self.oaT[:, c, 0:n], c == 0, c == 3, oak + [('ring', slot)], [('ps', ba)])
                for c in range(4):
                    self.mm(self.bank[bb][:, 0:n], w[:, ci, 4 + c, :], self.obn[:, c, 0:n], c == 0, c == 3, obk + [('ring', slot)], [('ps', bb)])
                k1, k2 = ('SH', 't1', o % 2), ('SH', 't2', o % 2)
                self.tt('dve', self.t1[:, o % 2, 0:n], self.gA[:, o, 0:n], self.bank[ba][:, 0:n], ALU.mult, [('SH', 'gA', o), ('ps', ba)], [k1])
                self.tt('dve', self.t2[:, o % 2, 0:n], self.gB[:, o, 0:n], self.bank[bb][:, 0:n], ALU.mult, [('SH', 'gB', o), ('ps', bb)], [k2])
                self.tt('pool', self.mT[:, o, 0:n], self.t1[:, o % 2, 0:n], self.t2[:, o % 2, 0:n], ALU.add, [k1, k2], [('SH', 'mT', o)])
        mk = [('SH', 'mT', c) for c in range(8)]
        for ld in range(2):
            slot = self.load_fm(44 + ld * 4, 4)
            w = self.ring[:, slot, 0:4096].rearrange("p (c k n) -> p c k n", c=4, k=8)
            for ci in range(4):
                o = ld * 4 + ci
                b = 4 + o % 2
                for c in range(8):
                    self.mm(self.bank[b][:, 0:n], w[:, ci, c, :], self.mT[:, c, 0:n], c == 0, c == 7, mk + [('ring', slot)], [('ps', b)])
                self.tt('dve', self.xT[:, o, 0:n], self.xT[:, o, 0:n], self.bank[b][:, 0:n], ALU.add, [('xT', o), ('ps', b)], [('xT', o)])

    def final(self):
        n, f = self.n, self.f
        self.norm(V_NF, inplace=True)
        if f < 0:
            blocks = [(0, n, self.ys, 0, TS)]
        else:
            blocks = [(j * 128, 128, self.yp, self.m * NT + j * 128, 128) for j in range(4)]
        for (c0, nb, dst, r0, nout) in blocks:
            slot = self.xtok_i % 2
            self.xtok_i += 1
            xk = ('SH', 'x_tok', slot)
            for half in range(2):
                bk = self.bank[half]
                for q in range(4):
                    kc = half * 4 + q
                    self.tr(bk[:, q * 128:(q + 1) * 128], self.xT[:, kc, c0:c0 + 128], 128, [('xT', kc), 'ident'], [('ps', half)])
                self.cp('dve' if half == 0 else 'act', self.x_tok[0:nb, slot, half * 512:(half + 1) * 512], bk[0:nb, :], [('ps', half)], [xk])
            self.s.dma('pool', dst[r0:r0 + nout, :], self.x_tok[0:nout, slot, :], [xk], [], 'yout%d' % slot)


def _fix_kt_keys(b):
    pass


def _layouts(inp):
    f32 = np.float32

    def fm_vec(v):
        v = np.asarray(v, f32).reshape(-1, 128)
        return v.T

    def gu(wg, wu):
        a = np.asarray(wg, f32).reshape(8, 128, 22, 128).transpose(2, 1, 0, 3)
        b = np.asarray(wu, f32).reshape(8, 128, 22, 128).transpose(2, 1, 0, 3)
        return np.ascontiguousarray(np.stack([a, b], axis=2)).reshape(-1, 2048)

    def dn(wd):
        a = np.asarray(wd, f32).reshape(2, 11, 128, 8, 128).transpose(0, 3, 2, 1, 4)
        return np.ascontiguousarray(a).reshape(-1, 2048)
    w_in = np.asarray(inp['w_in'][0], f32)
    cols = np.concatenate([np.arange(0, 512), np.arange(512, 1024), np.arange(1536, 2048), np.arange(2560, 3072),
                           np.arange(3072, 3584), np.arange(3584, 4608), np.arange(4608, 5632)])
    fm = w_in[:, cols].reshape(8, 128, 36, 128).transpose(2, 1, 0, 3)
    wa = np.asarray(inp['w_branch_a'][0], f32).reshape(4, 128, 8, 128).transpose(2, 1, 0, 3)
    wb = np.asarray(inp['w_branch_b'][0], f32).reshape(4, 128, 8, 128).transpose(2, 1, 0, 3)
    wab = np.concatenate([wa, wb], axis=2)
    wo = np.asarray(inp['w_out'][0], f32).reshape(8, 128, 8, 128).transpose(2, 1, 0, 3)
    wfm = np.ascontiguousarray(np.concatenate([fm, wab, wo], axis=0)).reshape(-1, 2048)
    wtm = np.ascontiguousarray(w_in[:, 512:2560].reshape(8, 128, 4, 512).transpose(2, 1, 0, 3)).reshape(-1, 2048)
    vecs = np.zeros((128, NVEC), f32)
    vecs[:, V_N1:V_N1 + 8] = fm_vec(inp['ffn1_norm'][0])
    vecs[:, V_NM:V_NM + 8] = fm_vec(inp['mix_norm'][0])
    vecs[:, V_N2:V_N2 + 8] = fm_vec(inp['ffn2_norm'][0])
    vecs[:, V_NF:V_NF + 8] = fm_vec(inp['final_norm'])
    vecs[:, V_BGA:V_BGA + 16] = fm_vec(inp['b_gate'][0])
    vecs[:, V_HGN:V_HGN + 4] = fm_vec(inp['hg_out_norm'][0])
    vecs[:, V_LB0:V_LB0 + 4] = fm_vec(inp['hg_lb_logits'][0])
    vecs[:, V_LB1:V_LB1 + 4] = fm_vec(inp['hg_lb_logits'][1])
    lbrep = np.ascontiguousarray(np.broadcast_to(np.asarray(inp['hg_lb_logits'], f32)[None], (128, 2, 512)))
    p = np.arange(128)[:, None]
    j = np.arange(128)[None, :]
    ident = (p == j).astype(f32)
    ones = np.ones((128, 128), f32)
    negU = -(p >= j).astype(f32)
    negL = -(p < j).astype(f32)
    tri = (p <= j).astype(f32)
    cc = np.arange(512)[None, :]
    masks = np.concatenate([((p + d) < cc).astype(f32) for d in (0, 128, 256, 384)], axis=1)
    cst = np.ascontiguousarray(np.concatenate([ident, ones, negU, tri, negL, masks], axis=1))
    return dict(gu1=gu(inp['ffn1_w_gate'][0], inp['ffn1_w_up'][0]), d1=dn(inp['ffn1_w_down'][0]),
                gu2=gu(inp['ffn2_w_gate'][0], inp['ffn2_w_up'][0]), d2=dn(inp['ffn2_w_down'][0]),
                wfm=wfm, wtm=wtm, vecs=vecs, lbrep=lbrep, cst=cst)


_NC_CACHE = {}


def run(inp, nft):
    f32 = np.float32
    shared = _layouts(inp)
    if nft % 2 == 0:
        npre = nmain = nft // 2
    else:
        npre, nmain = 0, nft
    key = (npre, nmain)
    if key not in _NC_CACHE:
        _NC_CACHE[key] = Builder(npre, nmain).build()
    nc = _NC_CACHE[key]
    xp = np.asarray(inp['x_prompt'], f32)
    xs = np.asarray(inp['x_sample'], f32)
    ck = np.asarray(inp['cache_sb_k'], f32)
    cv = np.asarray(inp['cache_sb_v'], f32)
    st = np.asarray(inp['state_hgrn'], f32)
    meta = np.ascontiguousarray(np.asarray(inp['meta_tokens'], f32))
    B = xp.shape[0]
    H = nmain * NT
    in_maps = []
    for c in range(8):
        m = dict(shared)
        b, half = c // 2, c % 2
        vecs = shared['vecs'].copy()
        if npre == 0:
            m['xp'] = np.ascontiguousarray(xp[b])
            m['xpre'] = np.zeros((NT, D), f32)
            m['flag'] = np.ones((128, 512), f32)
            vecs[:, V_FA], vecs[:, V_FB] = 0.0, 1.0
        elif half == 0:
            m['xp'] = np.ascontiguousarray(xp[b, 0:H])
            m['xpre'] = np.zeros((npre * NT, D), f32)
            m['flag'] = np.zeros((128, 512), f32)
            vecs[:, V_FA], vecs[:, V_FB] = 1.0, 0.0
        else:
            m['xp'] = np.ascontiguousarray(xp[b, H:2 * H])
            m['xpre'] = np.ascontiguousarray(xp[b, 0:H])
            m['flag'] = np.ones((128, 512), f32)
            vecs[:, V_FA], vecs[:, V_FB] = 0.0, 1.0
        m['vecs'] = vecs
        m['xs'] = np.ascontiguousarray(xs[c])
        m['meta'] = meta
        m['ck'] = np.ascontiguousarray(ck[0, c].reshape(LC, 512))
        m['cv'] = np.ascontiguousarray(cv[0, c].reshape(LC, 512))
        m['st'] = np.ascontiguousarray(st[0, c])
        in_maps.append(m)
    res = run_bass_kernel_spmd(nc, in_maps, core_ids=list(range(8)))
    r = res.results
    if npre == 0:
        y_prompt = np.stack([r[2 * b]['yp'] for b in range(B)])
        pk = np.stack([r[2 * b]['pk'] for b in range(B)])
        pv = np.stack([r[2 * b]['pv'] for b in range(B)])
        ph = np.stack([r[2 * b]['ph'] for b in range(B)])
    else:
        y_prompt = np.stack([np.concatenate([r[2 * b]['yp'], r[2 * b + 1]['yp']], axis=0) for b in range(B)])
        pk = np.stack([np.concatenate([r[2 * b]['pk'], r[2 * b + 1]['pk'][NMETA:]], axis=0) for b in range(B)])
        pv = np.stack([np.concatenate([r[2 * b]['pv'], r[2 * b + 1]['pv'][NMETA:]], axis=0) for b in range(B)])
        ph = np.stack([r[2 * b + 1]['ph'] for b in range(B)])
    y_prompt = y_prompt.astype(f32)
    L = pk.shape[1]
    pk = pk.reshape(B, L, 8, 64)[None].astype(f32)
    pv = pv.reshape(B, L, 8, 64)[None].astype(f32)
    ph = ph[None].astype(f32)
    y_sample = np.stack([r[c]['ys'] for c in range(8)]).astype(f32)
    sk = np.stack([r[c]['sk'].reshape(TS, 8, 64) for c in range(8)])[None].astype(f32)
    sv = np.stack([r[c]['sv'].reshape(TS, 8, 64) for c in range(8)])[None].astype(f32)
    sh = np.stack([r[c]['sh'] for c in range(8)])[None].astype(f32)
    return (y_prompt, y_sample, pk, pv, ph, sk, sv, sh)


def kernel(**inputs):
    nft = np.asarray(inputs['x_prompt']).shape[1] // NT
    return run(inputs, nft)
```

```python
import numpy as np
from contextlib import ExitStack
import concourse.bass as bass
import concourse.mybir as mybir
from concourse.bass_utils import run_bass_kernel_spmd

F32 = mybir.dt.float32
BF16 = mybir.dt.bfloat16
AF = mybir.ActivationFunctionType
ALU = mybir.AluOpType

D = 1024
DFF = 2816
NMETA = 16
TS = 32
LC = 2064
EPS = 1e-6
NT = 512
ENGS = ['pe', 'act', 'dve', 'pool', 'sp']

V_N1, V_NM, V_N2, V_NF, V_BGA, V_BGB, V_HGN, V_LB0, V_LB1, V_LBV, V_OML, V_FA, V_FB, V_DEAD = 0, 8, 16, 24, 32, 40, 48, 52, 56, 60, 64, 68, 69, 70
NVEC = 71


class Sched:
    def __init__(self):
        self.prog = {e: [] for e in ENGS}
        self.cnt = {e: 0 for e in ENGS}
        self.dma_cnt = {}
        self.last_w = {}
        self.readers = {}
        self.known = {e: {} for e in ENGS}
        self.fence = {}
        self.touched = set()
        self.nwaits = 0
        import os
        self.glimit = int(os.environ.get('KOPS', '100000000'))

    def _deps(self, reads, writes):
        need = {}

        def add(src, val):
            if need.get(src, 0) < val:
                need[src] = val
        for k in list(reads) + list(writes):
            if isinstance(k, tuple) and k[0] == 'SH' and k not in self.touched:
                self.touched.add(k)
                for s, v in self.fence.items():
                    add(s, v)
        for k in reads:
            lw = self.last_w.get(k)
            if lw:
                add(*lw)
        for k in writes:
            lw = self.last_w.get(k)
            if lw:
                add(*lw)
            for s, v in self.readers.get(k, {}).items():
                add(s, v)
        return need

    def _waits(self, eng, need):
        waits = []
        for src, val in need.items():
            if src == eng and eng == 'pe':
                continue
            if self.known[eng].get(src, 0) >= val:
                continue
            self.known[eng][src] = val
            waits.append((src, val))
        self.nwaits += len(waits)
        return waits

    def _mark(self, src, val, reads, writes):
        for k in writes:
            self.last_w[k] = (src, val)
            self.readers[k] = {}
        for k in reads:
            if k in writes:
                continue
            self.readers.setdefault(k, {})[src] = val

    def op(self, eng, fn, reads=(), writes=()):
        psr = [k for k in reads if isinstance(k, tuple) and k[0] == 'ps']
        if psr:
            reads = [k for k in reads if k not in psr]
            writes = list(writes) + [k for k in psr if k not in writes]
        self.gcount = getattr(self, 'gcount', 0) + 1
        if self.gcount > self.glimit:
            return
        import sys as _sys, os as _os
        if _os.environ.get('KTRACE'):
            lo, hi = [int(x) for x in _os.environ['KTRACE'].split(',')]
            if lo <= self.gcount <= hi:
                fr = _sys._getframe(2)
                print('OP', self.gcount, eng, 'line', fr.f_lineno, 'from', fr.f_back.f_lineno, 'reads', list(reads)[:3], 'writes', list(writes))
        need = self._deps(reads, writes)
        waits = self._waits(eng, need)
        self.cnt[eng] += 1
        self._mark(eng, self.cnt[eng], reads, writes)
        self.prog[eng].append(('op', waits, fn))

    def dma(self, q, out, in_, reads, writes, sem):
        self.gcount = getattr(self, 'gcount', 0) + 1
        if self.gcount > self.glimit:
            return
        import sys as _sys, os as _os
        if _os.environ.get('KTRACE'):
            lo, hi = [int(x) for x in _os.environ['KTRACE'].split(',')]
            if lo <= self.gcount <= hi:
                fr = _sys._getframe(1)
                print('DMA', self.gcount, q, 'line', fr.f_lineno, 'sem', sem, 'writes', list(writes))
        need = self._deps(reads, writes)
        waits = self._waits(q, need)
        self.dma_cnt[sem] = self.dma_cnt.get(sem, 0) + 16
        self._mark(sem, self.dma_cnt[sem], reads, writes)
        self.prog[q].append(('dma', waits, (out, in_, sem)))

    def phase_switch(self):
        f = dict(self.fence)

        def add(s, v):
            if f.get(s, 0) < v:
                f[s] = v
        for k in list(self.last_w.keys()):
            if isinstance(k, tuple) and k[0] == 'SH':
                add(*self.last_w[k])
                del self.last_w[k]
        for k in list(self.readers.keys()):
            if isinstance(k, tuple) and k[0] == 'SH':
                for s, v in self.readers[k].items():
                    add(s, v)
                del self.readers[k]
        self.fence = f
        self.touched = set()

    def final_wait(self, q):
        waits = [(s, v) for s, v in self.dma_cnt.items()]
        self.prog[q].append(('wait', waits, None))


class Builder:
    def __init__(self, npre, nmain):
        self.npre, self.nmain = npre, nmain
        nft = npre + nmain
        self.nft = nft
        self.FR = nft * NT
        self.KTW = max(NMETA + self.FR, NMETA + LC + TS + 128)
        self.NBLK = max(1 + 4 * nft, 19)
        self.s = Sched()
        self.ring_i = 0
        self.kv_i = 0
        self.xtok_i = 0

    def mm(self, out, lhsT, rhs, start, stop, reads, writes, sgc=False):
        self.s.op('pe', lambda e: e.matmul(out, lhsT=lhsT, rhs=rhs, start=start, stop=stop, skip_group_check=sgc), reads, writes)

    def tr(self, out, in_, n, reads, writes):
        ident = self.ident
        self.s.op('pe', lambda e: e.transpose(out=out, in_=in_, identity=ident[0:n, 0:n]), reads, writes)

    def act(self, out, in_, func, reads, writes, bias=None, scale=None):
        kw = {}
        if bias is not None:
            kw['bias'] = bias
        if scale is not None:
            kw['scale'] = scale
        self.s.op('act', lambda e: e.activation(out=out, in_=in_, func=func, **kw), reads, writes)

    def tt(self, eng, out, in0, in1, op, reads, writes):
        self.s.op(eng, lambda e: e.tensor_tensor(out=out, in0=in0, in1=in1, op=op), reads, writes)

    def tsc(self, eng, out, in0, s1, s2, op0, op1, reads, writes):
        if s2 is None:
            self.s.op(eng, lambda e: e.tensor_scalar(out=out, in0=in0, scalar1=s1, scalar2=None, op0=op0), reads, writes)
        else:
            self.s.op(eng, lambda e: e.tensor_scalar(out=out, in0=in0, scalar1=s1, scalar2=s2, op0=op0, op1=op1), reads, writes)

    def stt(self, eng, out, in0, scalar, in1, op0, op1, reads, writes):
        self.s.op(eng, lambda e: e.scalar_tensor_tensor(out=out, in0=in0, scalar=scalar, in1=in1, op0=op0, op1=op1), reads, writes)

    def cp(self, eng, out, in_, reads, writes):
        if eng == 'act':
            self.act(out, in_, AF.Copy, reads, writes)
        else:
            self.s.op(eng, lambda e: e.tensor_copy(out=out, in_=in_), reads, writes)

    def ring_load(self, dram_ap, nelem, rdkey):
        slot = self.ring_i % 3
        self.ring_i += 1
        out = self.ring[:, slot, 0:nelem]
        self.s.dma('sp', out, dram_ap, [rdkey], [('ring', slot)], 'ring%d' % slot)
        return slot

    def build(self):
        nc = bass.Bass("TRN2", target_bir_lowering=False)
        self.nc = nc
        nft, FR = self.nft, self.FR
        FM_ = self.nmain * NT
        FP_ = max(self.npre, 1) * NT
        dt_in = {}

        def din(name, shape):
            dt_in[name] = nc.dram_tensor(name, list(shape), F32, kind="ExternalInput").ap()
            return dt_in[name]

        def dout(name, shape):
            return nc.dram_tensor(name, list(shape), F32, kind="ExternalOutput").ap()
        self.xp = din("xp", [FM_, D])
        self.xpre = din("xpre", [FP_, D])
        self.flag_d = din("flag", [128, 512])
        self.xs = din("xs", [TS, D])
        self.meta = din("meta", [NMETA, D])
        self.ck = din("ck", [LC, 512])
        self.cv = din("cv", [LC, 512])
        self.st = din("st", [4, 128, 128])
        self.vecs_d = din("vecs", [128, NVEC])
        self.lbrep_d = din("lbrep", [128, 2, 512])
        self.cst_d = din("cst", [128, 5 * 128 + 4 * 512])
        self.w32 = {}
        self.wsz = {'gu1': 22 * 128 * 2048, 'gu2': 22 * 128 * 2048, 'd1': 2 * 8 * 128 * 1408, 'd2': 2 * 8 * 128 * 1408,
                    'wfm': 52 * 128 * 1024, 'wtm': 4 * 128 * 4096}
        for n, sz in self.wsz.items():
            self.w32[n] = din(n, [sz // 2048, 2048])
        self.yp = dout("yp", [FM_, D])
        self.ys = dout("ys", [TS, D])
        self.pk = dout("pk", [NMETA + FM_, 512])
        self.pv = dout("pv", [NMETA + FM_, 512])
        self.ph = dout("ph", [4, 128, 128])
        self.sk = dout("sk", [TS, 512])
        self.sv = dout("sv", [TS, 512])
        self.sh = dout("sh", [4, 128, 128])
        self.wbf = {n: nc.dram_tensor(n + "_bf", [sz], BF16).ap() for n, sz in self.wsz.items()}

        with ExitStack() as es:
            es.enter_context(nc.allow_low_precision("bf16 matmul operands, fp32 accumulation"))

            def sb(name, shape, dt):
                return es.enter_context(nc.sbuf_tensor(name, list(shape), dt))

            def ps(name):
                return es.enter_context(nc.psum_tensor(name, [128, 512], F32))
            self.ident = sb("ident", [128, 128], F32)
            self.onesb = sb("onesb", [128, 128], BF16)
            self.negU = sb("negU", [128, 128], BF16)
            self.negL = sb("negL", [128, 128], BF16)
            self.tri = sb("tri", [128, 128], F32)
            self.masks = sb("masks", [128, 4, 512], BF16)
            self.vecs = sb("vecs_s", [128, NVEC], F32)
            self.lb_t = sb("lb_t", [128, 2, 512], F32)
            self.xT = sb("xT", [128, 8, NT], F32)
            self.hT = sb("hT", [128, 8, NT + 64], BF16)
            self.rstd = sb("rstd", [128, NT], F32)
            self.lnt = sb("lnt", [128, NT], F32)
            self.KT = sb("KT", [128, 4, self.KTW], BF16)
            self.Vp = sb("Vp", [128, self.NBLK, 512], BF16)
            self.ring = sb("ring", [128, 3, 4096], BF16)
            self.S = sb("S", [128, 2, 4, 128], F32)
            self.Sb = sb("Sb", [128, 2, 4, 128], BF16)
            self.kvst = sb("kvst", [128, 2, 512], F32)
            self.Ssave = sb("Ssave", [128, 4, 128], F32)
            self.flag_t = sb("flag_t", [128, 512], BF16)
            self.qT = sb("qT", [128, 4, NT], BF16)
            self.obT = sb("obT", [128, 4, NT], F32)
            self.oaT = sb("oaT", [128, 4, NT], BF16)
            SHW = 11264
            self.SH = sb("SH", [128, SHW], F32)
            self.bank = [ps("bank%d" % i) for i in range(8)]
            self._views()

            self._emit_all()

            sems = {}
            names = [e for e in ENGS if e != 'sp'] + sorted(self.s.dma_cnt.keys())
            for n in names:
                sems[n] = es.enter_context(nc.semaphore("s_" + n))
            block = es.enter_context(nc.Block())
            prog = self.s.prog

            def run(eng_obj, ename):
                for kind, waits, payload in prog[ename]:
                    for src, val in waits:
                        eng_obj.wait_ge(sems[src], val)
                    if kind == 'op':
                        ins = payload(eng_obj)
                        ins.then_inc(sems[ename], 1)
                    elif kind == 'dma':
                        out, in_, sem = payload
                        eng_obj.dma_start(out=out, in_=in_).then_inc(sems[sem], 16)

            @block.tensor
            def _(e):
                run(e, 'pe')

            @block.scalar
            def _(e):
                run(e, 'act')

            @block.vector
            def _(e):
                run(e, 'dve')

            @block.gpsimd
            def _(e):
                run(e, 'pool')

            @block.sync
            def _(e):
                run(e, 'sp')
        return nc

    def _views(self):
        SH = self.SH

        def v(off_b, nbytes, dt, pattern=None, **kw):
            a = SH[:, off_b // 4:(off_b + nbytes) // 4]
            if dt is BF16:
                a = a.bitcast(BF16)
            if pattern:
                a = a.rearrange(pattern, **kw)
            return a
        self.hid = v(0, 11264, BF16, "p (c n) -> p c n", c=11)
        self.sg = v(11264, 4096, F32, "p (c n) -> p c n", c=2)
        self.x_tok = v(15360, 8192, F32, "p (c n) -> p c n", c=2)
        self.omfT = v(0, 8192, F32, "p (c n) -> p c n", c=4)
        self.qhT = v(8192, 8192, F32, "p (c n) -> p c n", c=4)
        self.logf = v(16384, 4096, F32, "p (c n) -> p c n", c=2)
        self.omf_tm = v(20480, 4096, F32, "p (c n) -> p c n", c=2)
        self.iv = v(24576, 2048, BF16, "p (c n) -> p c n", c=2)
        self.eb = v(26624, 2048, F32, "p (a h t) -> p a h t", a=2, h=4)
        self.enb = v(28672, 2048, F32, "p (a h t) -> p a h t", a=2, h=4)
        self.qe = v(30720, 1024, BF16, "p (a h t) -> p a h t", a=2, h=4)
        self.ke = v(41984, 2048, BF16, "p (a h t) -> p a h t", a=2, h=4)
        self.enb_tm = v(32768, 4096, F32, "p (c n) -> p c n", c=2)
        self.ke_tm = v(36864, 2048, BF16, "p (c n) -> p c n", c=2)
        self.sc = v(38912, 1024, BF16, "p (a h t) -> p a h t", a=2, h=4)
        self.tmpS = v(39936, 2048, F32, "p (h t) -> p h t", h=4)
        self.e_ = v(0, 12288, F32, "p (c n) -> p c n", c=6)
        self.sp_ = v(12288, 6144, BF16, "p (c n) -> p c n", c=6)
        self.e2_ = v(18432, 6144, F32, "p (c n) -> p c n", c=3)
        self.at_ = v(24576, 6144, BF16, "p (c n) -> p c n", c=6)
        self.ckv = v(30720, 8192, F32, "p (c n) -> p c n", c=2)
        self.sgT = v(0, 8192, F32, "p (c n) -> p c n", c=4)
        self.gA = v(8192, 8192, BF16, "p (c n) -> p c n", c=8)
        self.gB = v(16384, 8192, BF16, "p (c n) -> p c n", c=8)
        self.t1 = v(24576, 4096, F32, "p (c n) -> p c n", c=2)
        self.t2 = v(28672, 4096, F32, "p (c n) -> p c n", c=2)
        self.mT = v(32768, 8192, BF16, "p (c n) -> p c n", c=8)
        self.obn = v(40960, 4096, BF16, "p (c n) -> p c n", c=4)

    def _emit_all(self):
        import os
        self.kstop = int(os.environ.get('KSTOP', '99'))
        self.prologue()
        self.tile(-1, 'meta')
        for p in range(self.npre):
            self.tile(p, 'pre')
        if self.npre > 0:
            self.state_select()
        for m in range(self.nmain):
            self.tile(self.npre + m, 'main')
        self.tile(-1, 'sample')
        self.s.final_wait('sp')

    def prologue(self):
        s = self.s
        c = self.cst_d
        o = 0
        s.dma('pool', self.ident[:], c[:, o:o + 128], [], ['ident'], 'cst'); o += 128
        s.dma('pool', self.onesb[:], c[:, o:o + 128], [], ['onesb'], 'cst'); o += 128
        s.dma('pool', self.negU[:], c[:, o:o + 128], [], ['negU'], 'cst'); o += 128
        s.dma('pool', self.tri[:], c[:, o:o + 128], [], ['tri'], 'cst'); o += 128
        s.dma('pool', self.negL[:], c[:, o:o + 128], [], ['negL'], 'cst'); o += 128
        s.dma('pool', self.masks[:], c[:, o:o + 2048].rearrange("p (d n) -> p d n", d=4), [], ['masks'], 'cst'); o += 2048
        s.dma('pool', self.vecs[:], self.vecs_d, [], ['vecs'], 'cst')
        s.dma('pool', self.lb_t[:], self.lbrep_d, [], ['lb_t'], 'cst')
        s.dma('pool', self.flag_t[:], self.flag_d, [], ['flag_t'], 'cst')
        allc = ['ident', 'onesb', 'negU', 'negL', 'tri', 'masks', 'vecs', 'lb_t', 'flag_t']
        tot = s.dma_cnt['cst']
        for k in allc:
            s.last_w[k] = ('cst', tot)
        pieces = [('gu1', 0, 1408), ('d1', 0, 704), ('gu1', 1408, 1408), ('d1', 704, 704),
                  ('wfm', 0, 1024), ('wtm', 0, 1024), ('wfm', 1024, 2304),
                  ('gu2', 0, 1408), ('d2', 0, 704), ('gu2', 1408, 1408), ('d2', 704, 704)]
        for i, (n, r0, nr) in enumerate(pieces):
            dst = self.wbf[n].rearrange("(r c) -> r c", c=2048)[r0:r0 + nr, :]
            s.dma('pool', dst, self.w32[n][r0:r0 + nr, :], [], [('scr', n, r0)], 'cast%d' % i)
        self.s.op('dve', lambda e: e.memset(self.hT[:, :, :], 0.0), [], [('hT', kc) for kc in range(8)])
        self.s.op('pool', lambda e: e.memset(self.KT[:, :, :], 0.0), [], [('KT', c) for c in range(4)])
        self.s.op('dve', lambda e: e.memset(self.xT[:, :, :], 0.0), [], [('xT', kc) for kc in range(8)])
        self.s.op('pool', lambda e: e.memset(self.SH[:, :], 0.0), [], [('SH', 'all')])
        vv = self.vecs
        self.tt('dve', vv[:, V_LBV:V_LBV + 4], vv[:, V_LB1:V_LB1 + 4], vv[:, V_LB0:V_LB0 + 4], ALU.subtract, ['vecs'], ['vecs'])
        self.act(vv[:, V_LBV:V_LBV + 4], vv[:, V_LBV:V_LBV + 4], AF.Sigmoid, ['vecs'], ['vecs'])
        self.tsc('dve', vv[:, V_OML:V_OML + 4], vv[:, V_LBV:V_LBV + 4], -1.0, 1.0, ALU.mult, ALU.add, ['vecs'], ['vecs'])
        lt = self.lb_t
        self.tt('dve', lt[:, 0, :], lt[:, 1, :], lt[:, 0, :], ALU.subtract, ['lb_t'], ['lb_t'])
        self.act(lt[:, 0, :], lt[:, 0, :], AF.Sigmoid, ['lb_t'], ['lb_t'])
        self.tsc('dve', lt[:, 1, :], lt[:, 0, :], -1.0, 1.0, ALU.mult, ALU.add, ['lb_t'], ['lb_t'])

    def scr_key(self, n, row2048):
        bounds = {'gu1': [0, 1408], 'gu2': [0, 1408], 'd1': [0, 704], 'd2': [0, 704], 'wfm': [0, 1024], 'wtm': [0]}[n]
        r0 = max(b for b in bounds if b <= row2048)
        return ('scr', n, r0)

    def load_gu(self, which, c):
        n = 'gu%d' % which
        ap = self.wbf[n].rearrange("(c p f) -> c p f", p=128, f=2048)[c]
        return self.ring_load(ap, 2048, self.scr_key(n, c * 128))

    def load_d(self, which, half, o):
        n = 'd%d' % which
        ap = self.wbf[n].rearrange("(h o p f) -> h o p f", h=2, o=8, p=128)[half, o]
        return self.ring_load(ap, 1408, self.scr_key(n, (half * 8 + o) * 128 * 1408 // 2048))

    def load_fm(self, c0, ncnk):
        slot = self.ring_i % 3
        self.ring_i += 1
        src = self.wbf['wfm'].rearrange("(c p f) -> c p f", p=128, f=1024)
        for i in range(ncnk):
            self.s.dma('sp', self.ring[:, slot, i * 1024:(i + 1) * 1024], src[c0 + i], [self.scr_key('wfm', c0 * 64)], [('ring', slot)], 'ring%d' % slot)
        return slot

    def load_tm(self, g):
        ap = self.wbf['wtm'].rearrange("(g p f) -> g p f", p=128, f=4096)[g]
        return self.ring_load(ap, 4096, ('scr', 'wtm', 0))

    def state_select(self):
        S0 = self.S[:, 0, :, :]
        self.tsc('dve', self.Ssave[:, :, :], self.Ssave[:, :, :], self.vecs[:, V_FA:V_FA + 1], None, ALU.mult, None, ['Ssave', 'vecs'], ['Ssave'])
        self.stt('dve', S0, S0, self.vecs[:, V_FB:V_FB + 1], self.Ssave[:, :, :], ALU.mult, ALU.add, [('S', 0), 'Ssave', 'vecs'], [('S', 0)])
        for hh in range(4):
            self.cp('pool', self.Sb[:, 0, hh, :], self.S[:, 0, hh, :], [('S', 0)], [('Sb', 0, hh)])

    def tile(self, f, mode='extra'):
        self.mode = mode
        self.m = f - self.npre if mode == 'main' else f
        if mode == 'meta':
            n = NMETA
            segs = [('meta', 0, NMETA)]
        elif mode == 'sample':
            n = TS
            segs = [('sample', 0, TS)]
        else:
            n = NT
            segs = [('frames', 0, NT)]
        self.n = n
        self.f = f
        self.segs = segs
        s = self.s
        ks = self.kstop if f < 0 else 99
        s.phase_switch()
        self.load_x()
        if mode in ('pre', 'meta'):
            self.norm(V_N1)
            self.ffn(1)
            self.norm(V_NM)
            s.phase_switch()
            self.w_in()
            self.hgrn()
            return
        if ks < 2: return
        self.norm(V_N1)
        if ks < 3: return
        self.ffn(1)
        if ks < 4: return
        self.norm(V_NM)
        s.phase_switch()
        self.w_in()
        if ks < 5 or ks == 41: return
        self.hgrn()
        if ks < 6: return
        s.phase_switch()
        self.attn()
        if ks < 7: return
        s.phase_switch()
        self.post()
        if ks < 8: return
        s.phase_switch()
        self.norm(V_N2)
        self.ffn(2)
        self.final()

    def load_x(self):
        n, f = self.n, self.f
        if f < 0:
            blocks = [(0, n)]
        else:
            blocks = [(j * 128, 128) for j in range(4)]
        for bi, (c0, nb) in enumerate(blocks):
            slot = self.xtok_i % 2
            self.xtok_i += 1
            xk = ('SH', 'x_tok', slot)
            if f < 0:
                self.s.dma('pool', self.x_tok[0:n, slot, :], self.xs if self.mode == 'sample' else self.meta, [], [xk], 'xin%d' % slot)
            else:
                r0 = self.m * NT + c0
                srcx = self.xpre if self.mode == 'pre' else self.xp
                self.s.dma('pool', self.x_tok[:, slot, :], srcx[r0:r0 + 128, :], [], [xk], 'xin%d' % slot)
            for kc in range(8):
                bk = self.bank[kc % 2]
                self.tr(bk[:, 0:nb], self.x_tok[0:nb, slot, kc * 128:(kc + 1) * 128], nb, [xk, 'ident'], [('ps', kc % 2)])
                eng = 'dve' if kc % 2 == 0 else 'act'
                self.cp(eng, self.xT[:, kc, c0:c0 + nb], bk[:, 0:nb], [('ps', kc % 2)], [('xT', kc)])

    def norm(self, gcol, inplace=False):
        n = self.n
        for kc in range(8):
            self.act(self.hT[:, kc, 0:n], self.xT[:, kc, 0:n], AF.Square, [('xT', kc)], [('hT', kc)])
        bk = self.bank[7]
        for kc in range(8):
            self.mm(bk[:, 0:n], self.onesb[:, :], self.hT[:, kc, 0:n], kc == 0, kc == 7, [('hT', kc), 'onesb'], [('ps', 7)])
        self.act(self.lnt[:, 0:n], bk[:, 0:n], AF.Ln, [('ps', 7)], ['lnt'], bias=EPS, scale=1.0 / D)
        self.act(self.rstd[:, 0:n], self.lnt[:, 0:n], AF.Exp, ['lnt'], ['rstd'], scale=-0.5)
        for kc in range(8):
            eng = 'dve'
            if inplace:
                self.stt(eng, self.xT[:, kc, 0:n], self.xT[:, kc, 0:n], self.vecs[:, gcol + kc:gcol + kc + 1], self.rstd[:, 0:n],
                         ALU.mult, ALU.mult, [('xT', kc), 'rstd', 'vecs'], [('xT', kc)])
            else:
                self.stt(eng, self.hT[:, kc, 0:n], self.xT[:, kc, 0:n], self.vecs[:, gcol + kc:gcol + kc + 1], self.rstd[:, 0:n],
                         ALU.mult, ALU.mult, [('xT', kc), 'rstd', 'vecs'], [('hT', kc)])

    def ffn(self, which):
        n = self.n
        hk = [('hT', kc) for kc in range(8)]
        it = 0
        for half in range(2):
            for cc in range(11):
                c = half * 11 + cc
                slot = self.load_gu(which, c)
                w = self.ring[:, slot, 0:2048].rearrange("p (g k n) -> p g k n", g=2, k=8)
                gb, ub = it % 2, 2 + it % 2
                for kc in range(8):
                    self.mm(self.bank[gb][:, 0:n], w[:, 0, kc, :], self.hT[:, kc, 0:n], kc == 0, kc == 7, hk + [('ring', slot)], [('ps', gb)])
                for kc in range(8):
                    self.mm(self.bank[ub][:, 0:n], w[:, 1, kc, :], self.hT[:, kc, 0:n], kc == 0, kc == 7, hk + [('ring', slot)], [('ps', ub)])
                sgk = ('SH', 'sg', it % 2)
                self.act(self.sg[:, it % 2, 0:n], self.bank[gb][:, 0:n], AF.Silu, [('ps', gb)], [sgk])
                self.tt('dve', self.hid[:, cc, 0:n], self.sg[:, it % 2, 0:n], self.bank[ub][:, 0:n], ALU.mult, [sgk, ('ps', ub)], [('SH', 'hid', cc)])
                it += 1
            for o in range(8):
                slot = self.load_d(which, half, o)
                w = self.ring[:, slot, 0:1408].rearrange("p (c n) -> p c n", c=11)
                db = 4 + o % 2
                for cc in range(11):
                    self.mm(self.bank[db][:, 0:n], w[:, cc, :], self.hid[:, cc, 0:n], cc == 0, cc == 10, [('SH', 'hid', cc), ('ring', slot)], [('ps', db)])
                self.stt('dve', self.xT[:, o, 0:n], self.bank[db][:, 0:n], 0.5, self.xT[:, o, 0:n], ALU.mult, ALU.add, [('ps', db), ('xT', o)], [('xT', o)])

    def kcol(self, seg):
        if seg == 'sample':
            return NMETA + LC
        if seg == 'meta':
            return 0
        return NMETA + self.f * NT

    def w_in(self):
        n = self.n
        hk = [('hT', kc) for kc in range(8)]
        bi = 0
        for g4 in ([1] if self.mode in ('pre', 'meta') else range(4)):
            slot = self.load_fm(g4 * 4, 4)
            w = self.ring[:, slot, 0:4096].rearrange("p (c k n) -> p c k n", c=4, k=8)
            for ci in range(4):
                b = bi % 4
                bi += 1
                bk = self.bank[b]
                for kc in range(8):
                    self.mm(bk[:, 0:n], w[:, ci, kc, :], self.hT[:, kc, 0:n], kc == 0, kc == 7, hk + [('ring', slot)], [('ps', b)])
                if g4 == 0:
                    self.act(self.qT[:, ci, 0:n], bk[:, 0:n], AF.Copy, [('ps', b)], [('qT', ci)], scale=0.125)
                elif g4 == 1:
                    for (sname, c0, ns) in self.segs:
                        kc0 = self.kcol(sname)
                        self.cp('dve', self.KT[:, ci, kc0:kc0 + ns], bk[:, c0:c0 + ns], [('ps', b)], [('KT', ci)])
                elif g4 == 2:
                    self.act(self.lnt[:, 0:n], bk[:, 0:n], AF.Sigmoid, [('ps', b)], ['lnt'], scale=-1.0)
                    self.tsc('dve', self.omfT[:, ci, 0:n], self.lnt[:, 0:n], self.vecs[:, V_OML + ci:V_OML + ci + 1], None, ALU.mult, None,
                             ['lnt', 'vecs'], [('SH', 'omfT', ci)])
                else:
                    self.cp('act', self.qhT[:, ci, 0:n], bk[:, 0:n], [('ps', b)], [('SH', 'qhT', ci)])
        if self.kstop == 41:
            return
        if self.mode == 'sample':
            blocks = [('sample', 0, TS, self.sk, self.sv, 0, 18)]
        elif self.mode == 'meta':
            blocks = [('meta', 0, NMETA, self.pk, self.pv, 0, 0)]
        else:
            blocks = [('frames', j * 128, 128, self.pk, self.pv, NMETA + self.m * NT + j * 128, 1 + 4 * self.f + j) for j in range(4)]
        for g in ([1] if self.mode == 'pre' else range(2)):
            slot = self.load_tm(g)
            w = self.ring[:, slot, 0:4096].rearrange("p (k n) -> p k n", k=8)
            for (sname, c0, nb, okd, ovd, r0, vblk) in blocks:
                b = 4 + bi % 4
                bi += 1
                bk = self.bank[b]
                for kc in range(8):
                    self.mm(bk[:, :], self.hT[:, kc, c0:c0 + 128], w[:, kc, :], kc == 0, kc == 7, hk + [('ring', slot)], [('ps', b)])
                if self.mode == 'pre':
                    self.cp('act', self.Vp[0:nb, vblk, :], bk[0:nb, :], [('ps', b)], [('Vp', vblk)])
                    continue
                ks = self.kv_i % 2
                self.kv_i += 1
                self.cp('dve', self.kvst[0:nb, ks, :], bk[0:nb, :], [('ps', b)], [('kvst', ks)])
                dst = (okd if g == 0 else ovd)[r0:r0 + nb, :]
                import os
                if not os.environ.get('NOKV'):
                    self.s.dma(os.environ.get('KVQ', 'pool'), dst, self.kvst[0:nb, ks, :], [('kvst', ks)], [], 'kvo%d' % ks)
                if g == 1:
                    self.cp('act', self.Vp[0:nb, vblk, :], bk[0:nb, :], [('ps', b)], [('Vp', vblk)])

    def hgrn(self):
        s = self.s
        hk = [('hT', kc) for kc in range(8)]
        slz = self.load_tm(2)
        sli = self.load_tm(3)
        wz = self.ring[:, slz, 0:4096].rearrange("p (k n) -> p k n", k=8)
        wi = self.ring[:, sli, 0:4096].rearrange("p (k n) -> p k n", k=8)
        if self.mode == 'sample':
            chunks = [(0, TS, 1)]
            self.s.dma('pool', self.S[:, 1, :, :], self.st.rearrange("h k v -> k h v"), [], [('S', 1)], 'stin')
            for hh in range(4):
                self.cp('pool', self.Sb[:, 1, hh, :], self.S[:, 1, hh, :], [('S', 1)], [('Sb', 1, hh)])
        elif self.mode == 'meta':
            chunks = [(0, NMETA, 0)]
            self.s.op('dve', lambda e: e.memset(self.S[:, 0, :, :], 0.0), [], [('S', 0)])
            self.s.op('dve', lambda e: e.memset(self.Sb[:, 0, :, :], 0.0), [], [('Sb', 0, hh) for hh in range(4)])
        else:
            chunks = [(i * 64, 64, 0) for i in range(8)]
        for ci, (c0, T, si) in enumerate(chunks):
            par = ci % 2
            bA, bB, bC, bD, bE, bF = self.bank[0], self.bank[1], self.bank[2 + par], self.bank[4], self.bank[5], self.bank[6 + par]
            kC, kF = ('ps', 2 + par), ('ps', 6 + par)
            for kc in range(8):
                self.mm(bA[:, :], self.hT[:, kc, c0:c0 + 128], wz[:, kc, :], kc == 0, kc == 7, hk + [('ring', slz)], [('ps', 0)])
            for kc in range(8):
                self.mm(bB[:, :], self.hT[:, kc, c0:c0 + 128], wi[:, kc, :], kc == 0, kc == 7, hk + [('ring', sli)], [('ps', 1)])
            k_omf, k_logf, k_iv = ('SH', 'omf_tm', par), ('SH', 'logf', par), ('SH', 'iv', par)
            self.act(self.omf_tm[0:T, par, :], bA[0:T, :], AF.Sigmoid, [('ps', 0)], [k_omf], scale=-1.0)
            self.tt('dve', self.omf_tm[0:T, par, :], self.omf_tm[0:T, par, :], self.lb_t[0:T, 1, :], ALU.mult, [k_omf, 'lb_t'], [k_omf])
            self.act(self.logf[0:T, par, :], self.omf_tm[0:T, par, :], AF.Ln, [k_omf], [k_logf], bias=1.0, scale=-1.0)
            self.cp('act', self.iv[0:T, par, :], bB[0:T, :], [('ps', 1)], [k_iv])
            cheap = self.mode in ('pre', 'meta')
            if not cheap:
                for hh in range(4):
                    self.mm(bC[:, hh * 64:hh * 64 + T], self.logf[0:T, par, hh * 128:(hh + 1) * 128], self.tri[0:T, 0:T], True, True,
                            [k_logf, 'tri'], [kC])
            self.mm(bD[:, :], self.tri[0:T, 0:128], self.logf[0:T, par, :], True, True, [k_logf, 'tri'], [('ps', 4)])
            k_enbt, k_ket = ('SH', 'enb_tm', par), ('SH', 'ke_tm', par)
            self.act(self.enb_tm[0:T, par, :], bD[0:T, :], AF.Exp, [('ps', 4)], [k_enbt], scale=-1.0)
            self.tt('dve', self.ke_tm[0:T, par, :], self.omf_tm[0:T, par, :], self.enb_tm[0:T, par, :], ALU.mult, [k_omf, k_enbt], [k_ket])
            if cheap:
                for hh in range(4):
                    self.mm(bC[:, hh * 64:hh * 64 + 2], self.logf[0:T, par, hh * 128:(hh + 1) * 128], self.tri[0:T, 126:128], True, True,
                            [k_logf, 'tri'], [kC])
                for hh in range(4):
                    self.act(self.eb[:, par, hh, T - 1:T], bC[:, hh * 64:hh * 64 + 1], AF.Exp, [kC], [('SH', 'eb', par, hh)])
            else:
                for hh in range(4):
                    k_eb, k_enb, k_qe, k_ke = ('SH', 'eb', par, hh), ('SH', 'enb', par, hh), ('SH', 'qe', par, hh), ('SH', 'ke', par, hh)
                    self.act(self.eb[:, par, hh, 0:T], bC[:, hh * 64:hh * 64 + T], AF.Exp, [kC], [k_eb])
                    self.act(self.enb[:, par, hh, 0:T], bC[:, hh * 64:hh * 64 + T], AF.Exp, [kC], [k_enb], scale=-1.0)
                    self.tt('dve', self.qe[:, par, hh, 0:T], self.qhT[:, hh, c0:c0 + T], self.eb[:, par, hh, 0:T], ALU.mult,
                            [('SH', 'qhT', hh), k_eb], [k_qe])
                    self.tt('pool', self.ke[:, par, hh, 0:T], self.omfT[:, hh, c0:c0 + T], self.enb[:, par, hh, 0:T], ALU.mult,
                            [('SH', 'omfT', hh), k_enb], [k_ke])
                for hh in range(4):
                    k_qe, k_ke = ('SH', 'qe', par, hh), ('SH', 'ke', par, hh)
                    self.mm(bC[:, 256 + hh * 64:256 + hh * 64 + T], self.ke[:, par, hh, 0:128], self.qe[:, par, hh, 0:T], True, True,
                            [k_qe, k_ke], [kC])
                for hh in range(4):
                    k_sc = ('SH', 'sc', par, hh)
                    self.tt('dve', self.sc[0:T, par, hh, 0:T], bC[0:T, 256 + hh * 64:256 + hh * 64 + T], self.tri[0:T, 0:T], ALU.mult,
                            [kC, 'tri'], [k_sc])
                for hh in range(4):
                    k_sc, k_qe = ('SH', 'sc', par, hh), ('SH', 'qe', par, hh)
                    self.mm(bE[:, hh * 64:hh * 64 + T], self.iv[0:T, par, hh * 128:(hh + 1) * 128], self.sc[0:T, par, hh, 0:T], True, False,
                            [k_iv, k_sc], [('ps', 5)])
                    self.mm(bE[:, hh * 64:hh * 64 + T], self.Sb[:, si, hh, :], self.qe[:, par, hh, 0:T], False, True,
                            [('Sb', si, hh), k_qe], [('ps', 5)])
                for hh in range(4):
                    self.cp('act' if hh % 2 == 0 else 'dve', self.obT[:, hh, c0:c0 + T], bE[:, hh * 64:hh * 64 + T], [('ps', 5)], [('obT', hh)])
            for hh in range(4):
                self.mm(bF[:, hh * 128:(hh + 1) * 128], self.ke_tm[0:T, par, hh * 128:(hh + 1) * 128], self.iv[0:T, par, hh * 128:(hh + 1) * 128],
                        True, True, [k_ket, k_iv], [kF])
            for hh in range(4):
                k_eb = ('SH', 'eb', par, hh)
                ebl = self.eb[:, par, hh, T - 1:T]
                self.tsc('dve', self.tmpS[:, hh, :], self.S[:, si, hh, :], ebl, None, ALU.mult, None, [('S', si), k_eb], [('SH', 'tmpS', hh)])
                self.stt('dve', self.S[:, si, hh, :], bF[:, hh * 128:(hh + 1) * 128], ebl, self.tmpS[:, hh, :], ALU.mult, ALU.add,
                         [kF, k_eb, ('SH', 'tmpS', hh)], [('S', si)])
                self.cp('pool', self.Sb[:, si, hh, :], self.S[:, si, hh, :], [('S', si)], [('Sb', si, hh)])
            if self.f < 0 and si == 1:
                self.s.dma('pool', self.sh.rearrange("h k v -> k h v"), self.S[:, 1, :, :], [('S', 1)], [], 'sout')
            if self.f < 0 and si == 0 and self.npre > 0:
                self.cp('pool', self.Ssave[:, :, :], self.S[:, 0, :, :], [('S', 0)], ['Ssave'])
            if self.mode == 'main' and self.f == self.nft - 1 and ci == len(chunks) - 1:
                self.s.dma('pool', self.ph.rearrange("h k v -> k h v"), self.S[:, 0, :, :], [('S', 0)], [], 'sout')

    def attn(self):
        f = self.f
        if f < 0:
            nb_c = (LC + 127) // 128
            for b in range(nb_c):
                r0 = b * 128
                kn = min(128, LC - r0)
                slot = b % 2
                ck_key = ('SH', 'ckv', slot)
                self.s.dma('pool', self.ckv[0:kn, slot, 0:512], self.ck[r0:r0 + kn, :], [], [ck_key], 'ckin%d' % slot)
                self.s.dma('pool', self.ckv[0:kn, slot, 512:1024], self.cv[r0:r0 + kn, :], [], [ck_key], 'ckin%d' % slot)
                bk = self.bank[b % 2]
                for ch in range(4):
                    self.tr(bk[:, ch * 128:ch * 128 + kn], self.ckv[0:kn, slot, ch * 128:(ch + 1) * 128], kn, [ck_key, 'ident'], [('ps', b % 2)])
                for ch in range(4):
                    self.cp('dve' if ch % 2 == 0 else 'act', self.KT[:, ch, NMETA + r0:NMETA + r0 + kn], bk[:, ch * 128:ch * 128 + kn],
                            [('ps', b % 2)], [('KT', ch)])
                self.cp('pool', self.Vp[0:kn, 1 + b, :], self.ckv[0:kn, slot, 512:1024], [ck_key], [('Vp', 1 + b)])
            blocks = [(18, TS, NMETA + LC, 0)]
            blocks.append((1 + nb_c - 1, LC - 128 * (nb_c - 1), NMETA + 128 * (nb_c - 1), None))
            for b in range(nb_c - 2, -1, -1):
                blocks.append((1 + b, 128, NMETA + 128 * b, None))
            self.attn_job(0, TS, blocks)
        else:
            blocks = []
            for m in range(3, -1, -1):
                fb = 4 * f + m
                blocks.append((1 + fb, 128, NMETA + 128 * fb, 128 * m))
            for fb in range(4 * f - 1, -1, -1):
                blocks.append((1 + fb, 128, NMETA + 128 * fb, 'flag' if fb < 4 * self.npre else None))
            blocks.append((0, NMETA, 0, None))
            self.attn_job(0, NT, blocks)

    def attn_job(self, q0, nq, blocks):
        nblk = len(blocks)
        for grp in ([0, 1, 2], [3, 4, 5], [6, 7]):
            S_ = len(grp)
            nseq = nblk * S_

            def emitZ(idx):
                k, si = divmod(idx, S_)
                h = grp[si]
                vblk, kn, kcol, md = blocks[k]
                ch, pb = h // 2, 64 * (h % 2)
                zb = idx % 2
                self.mm(self.bank[zb][:, 0:nq], self.KT[pb:pb + 64, ch, kcol:kcol + 128], self.qT[pb:pb + 64, ch, q0:q0 + nq], True, True,
                        [('KT', ch), ('qT', ch)], [('ps', zb)])
            for idx in range(min(2, nseq)):
                emitZ(idx)
            for k in range(nblk):
                vblk, kn, kcol, md = blocks[k]
                par = k % 2
                for si, h in enumerate(grp):
                    idx = k * S_ + si
                    zb = idx % 2
                    ke_, ks_ = ('SH', 'e', si, par), ('SH', 'sp', si, par)
                    if md == 'flag':
                        self.act(self.e_[0:kn, si * 2 + par, 0:nq], self.bank[zb][0:kn, 0:nq], AF.Exp, [('ps', zb), 'vecs'], [ke_],
                                 bias=self.vecs[0:kn, V_DEAD:V_DEAD + 1])
                    else:
                        self.act(self.e_[0:kn, si * 2 + par, 0:nq], self.bank[zb][0:kn, 0:nq], AF.Exp, [('ps', zb)], [ke_])
                    if idx + 2 < nseq:
                        emitZ(idx + 2)
                    if md is not None and md != 'flag':
                        self.tt('pool', self.e_[0:kn, si * 2 + par, 0:nq], self.e_[0:kn, si * 2 + par, 0:nq], self.masks[0:kn, md // 128, 0:nq],
                                ALU.mult, [ke_, 'masks'], [ke_])
                    self.act(self.sp_[0:kn, si * 2 + par, 0:nq], self.e_[0:kn, si * 2 + par, 0:nq], AF.Ln, [ke_], [ks_], bias=1.0)
                for si, h in enumerate(grp):
                    ks_ = ('SH', 'sp', si, par)
                    self.mm(self.bank[2 + si][:, 0:nq], self.negU[0:kn, :], self.sp_[0:kn, si * 2 + par, 0:nq], k == 0, False,
                            [ks_, 'negU'], [('ps', 2 + si)], sgc=True)
                for si, h in enumerate(grp):
                    ke_, k2_, ka_ = ('SH', 'e', si, par), ('SH', 'e2', si), ('SH', 'at', si, par)
                    self.act(self.e2_[0:kn, si, 0:nq], self.bank[2 + si][0:kn, 0:nq], AF.Exp, [('ps', 2 + si)], [k2_])
                    self.tt('dve', self.at_[0:kn, si * 2 + par, 0:nq], self.e2_[0:kn, si, 0:nq], self.e_[0:kn, si * 2 + par, 0:nq], ALU.mult,
                            [k2_, ke_], [ka_])
                for si, h in enumerate(grp):
                    ks_, ka_ = ('SH', 'sp', si, par), ('SH', 'at', si, par)
                    self.mm(self.bank[2 + si][:, 0:nq], self.negL[0:kn, :], self.sp_[0:kn, si * 2 + par, 0:nq], False, k == nblk - 1,
                            [ks_, 'negL'], [('ps', 2 + si)], sgc=True)
                    vlo = h * 64 if h % 2 == 0 else (h - 1) * 64
                    self.mm(self.bank[5 + si][:, 0:nq], self.Vp[0:kn, vblk, vlo:vlo + 128], self.at_[0:kn, si * 2 + par, 0:nq], k == 0, k == nblk - 1,
                            [ka_, ('Vp', vblk)], [('ps', 5 + si)])
            for si, h in enumerate(grp):
                ch, pb = h // 2, 64 * (h % 2)
                self.cp('dve' if si % 2 == 0 else 'act', self.oaT[pb:pb + 64, ch, q0:q0 + nq], self.bank[5 + si][pb:pb + 64, 0:nq],
                        [('ps', 5 + si)], [('oaT', ch, pb)])

    def post(self):
        n = self.n
        hk = [('hT', kc) for kc in range(8)]
        bi = 0
        for ld in range(5):
            slot = self.load_fm(16 + ld * 4, 4)
            w = self.ring[:, slot, 0:4096].rearrange("p (c k n) -> p c k n", c=4, k=8)
            for ci in range(4):
                cidx = ld * 4 + ci
                b = bi % 4
                bi += 1
                bk = self.bank[b]
                for kc in range(8):
                    self.mm(bk[:, 0:n], w[:, ci, kc, :], self.hT[:, kc, 0:n], kc == 0, kc == 7, hk + [('ring', slot)], [('ps', b)])
                if cidx < 4:
                    self.act(self.sgT[:, cidx, 0:n], bk[:, 0:n], AF.Silu, [('ps', b)], [('SH', 'sgT', cidx)])
                elif cidx < 12:
                    o = cidx - 4
                    self.act(self.gA[:, o, 0:n], bk[:, 0:n], AF.Sigmoid, [('ps', b), 'vecs'], [('SH', 'gA', o)], bias=self.vecs[:, V_BGA + o:V_BGA + o + 1])
                else:
                    o = cidx - 12
                    self.act(self.gB[:, o, 0:n], bk[:, 0:n], AF.Sigmoid, [('ps', b), 'vecs'], [('SH', 'gB', o)], bias=self.vecs[:, V_BGB + o:V_BGB + o + 1])
        for hh in range(4):
            kq = ('SH', 'mT', hh)
            self.act(self.mT[:, hh, 0:n], self.obT[:, hh, 0:n], AF.Square, [('obT', hh)], [kq])
            self.mm(self.bank[4][:, 0:n], self.onesb[:, :], self.mT[:, hh, 0:n], True, True, [kq, 'onesb'], [('ps', 4)])
            self.act(self.lnt[:, 0:n], self.bank[4][:, 0:n], AF.Ln, [('ps', 4)], ['lnt'], bias=EPS, scale=1.0 / 128)
            self.act(self.rstd[:, 0:n], self.lnt[:, 0:n], AF.Exp, ['lnt'], ['rstd'], scale=-0.5)
            k1 = ('SH', 't1', hh % 2)
            self.stt('dve', self.t1[:, hh % 2, 0:n], self.obT[:, hh, 0:n], self.vecs[:, V_HGN + hh:V_HGN + hh + 1], self.rstd[:, 0:n], ALU.mult, ALU.mult,
                     [('obT', hh), 'vecs', 'rstd'], [k1])
            self.tt('pool', self.obn[:, hh, 0:n], self.t1[:, hh % 2, 0:n], self.sgT[:, hh, 0:n], ALU.mult, [k1, ('SH', 'sgT', hh)], [('SH', 'obn', hh)])
        oak = [('oaT', c, pb) for c in range(4) for pb in (0, 64)]
        obk = [('SH', 'obn', c) for c in range(4)]
        for ld in range(2):
            slot = self.load_fm(36 + ld * 4, 4)
            w = self.ring[:, slot, 0:4096].rearrange("p (c k n) -> p c k n", c=4, k=8)
            for ci in range(4):
                o = ld * 4 + ci
                ba, bb = o % 2, 2 + o % 2
                for c in range(4):
                    self.mm(self.bank[ba][:, 0:n], w[:, ci, c, :], self.oaT[:, c, 0:n], c == 0, c == 3, oak + [('ring', slot)], [('ps', ba)])
                for c in range(4):
                    self.mm(self.bank[bb][:, 0:n], w[:, ci, 4 + c, :], self.obn[:, c, 0:n], c == 0, c == 3, obk + [('ring', slot)], [('ps', bb)])
                k1, k2 = ('SH', 't1', o % 2), ('SH', 't2', o % 2)
                self.tt('dve', self.t1[:, o % 2, 0:n], self.gA[:, o, 0:n], self.bank[ba][:, 0:n], ALU.mult, [('SH', 'gA', o), ('ps', ba)], [k1])
                self.tt('dve', self.t2[:, o % 2, 0:n], self.gB[:, o, 0:n], self.bank[bb][:, 0:n], ALU.mult, [('SH', 'gB', o), ('ps', bb)], [k2])
                self.tt('pool', self.mT[:, o, 0:n], self.t1[:, o % 2, 0:n], self.t2[:, o % 2, 0:n], ALU.add, [k1, k2], [('SH', 'mT', o)])
        mk = [('SH', 'mT', c) for c in range(8)]
        for ld in range(2):
            slot = self.load_fm(44 + ld * 4, 4)
            w = self.ring[:, slot, 0:4096].rearrange("p (c k n) -> p c k n", c=4, k=8)
            for ci in range(4):
                o = ld * 4 + ci
                b = 4 + o % 2
                for c in range(8):
                    self.mm(self.bank[b][:, 0:n], w[:, ci, c, :], self.mT[:, c, 0:n], c == 0, c == 7, mk + [('ring', slot)], [('ps', b)])
                self.tt('dve', self.xT[:, o, 0:n], self.xT[:, o, 0:n], self.bank[b][:, 0:n], ALU.add, [('xT', o), ('ps', b)], [('xT', o)])

    def final(self):
        n, f = self.n, self.f
        self.norm(V_NF, inplace=True)
        if f < 0:
            blocks = [(0, n, self.ys, 0, TS)]
            assert self.mode == 'sample'
        else:
            blocks = [(j * 128, 128, self.yp, self.m * NT + j * 128, 128) for j in range(4)]
        for (c0, nb, dst, r0, nout) in blocks:
            slot = self.xtok_i % 2
            self.xtok_i += 1
            xk = ('SH', 'x_tok', slot)
            for half in range(2):
                bk = self.bank[half]
                for q in range(4):
                    kc = half * 4 + q
                    self.tr(bk[:, q * 128:(q + 1) * 128], self.xT[:, kc, c0:c0 + 128], 128, [('xT', kc), 'ident'], [('ps', half)])
                self.cp('dve' if half == 0 else 'act', self.x_tok[0:nb, slot, half * 512:(half + 1) * 512], bk[0:nb, :], [('ps', half)], [xk])
            self.s.dma('pool', dst[r0:r0 + nout, :], self.x_tok[0:nout, slot, :], [xk], [], 'yout%d' % slot)


def _fix_kt_keys(b):
    pass


def _layouts(inp):
    f32 = np.float32

    def fm_vec(v):
        v = np.asarray(v, f32).reshape(-1, 128)
        return v.T

    def gu(wg, wu):
        a = np.asarray(wg, f32).reshape(8, 128, 22, 128).transpose(2, 1, 0, 3)
        b = np.asarray(wu, f32).reshape(8, 128, 22, 128).transpose(2, 1, 0, 3)
        return np.ascontiguousarray(np.stack([a, b], axis=2)).reshape(-1, 2048)

    def dn(wd):
        a = np.asarray(wd, f32).reshape(2, 11, 128, 8, 128).transpose(0, 3, 2, 1, 4)
        return np.ascontiguousarray(a).reshape(-1, 2048)
    w_in = np.asarray(inp['w_in'][0], f32)
    cols = np.concatenate([np.arange(0, 512), np.arange(512, 1024), np.arange(1536, 2048), np.arange(2560, 3072),
                           np.arange(3072, 3584), np.arange(3584, 4608), np.arange(4608, 5632)])
    fm = w_in[:, cols].reshape(8, 128, 36, 128).transpose(2, 1, 0, 3)
    wa = np.asarray(inp['w_branch_a'][0], f32).reshape(4, 128, 8, 128).transpose(2, 1, 0, 3)
    wb = np.asarray(inp['w_branch_b'][0], f32).reshape(4, 128, 8, 128).transpose(2, 1, 0, 3)
    wab = np.concatenate([wa, wb], axis=2)
    wo = np.asarray(inp['w_out'][0], f32).reshape(8, 128, 8, 128).transpose(2, 1, 0, 3)
    wfm = np.ascontiguousarray(np.concatenate([fm, wab, wo], axis=0)).reshape(-1, 2048)
    wtm = np.ascontiguousarray(w_in[:, 512:2560].reshape(8, 128, 4, 512).transpose(2, 1, 0, 3)).reshape(-1, 2048)
    vecs = np.zeros((128, NVEC), f32)
    vecs[:, V_N1:V_N1 + 8] = fm_vec(inp['ffn1_norm'][0])
    vecs[:, V_NM:V_NM + 8] = fm_vec(inp['mix_norm'][0])
    vecs[:, V_N2:V_N2 + 8] = fm_vec(inp['ffn2_norm'][0])
    vecs[:, V_NF:V_NF + 8] = fm_vec(inp['final_norm'])
    vecs[:, V_BGA:V_BGA + 16] = fm_vec(inp['b_gate'][0])
    vecs[:, V_HGN:V_HGN + 4] = fm_vec(inp['hg_out_norm'][0])
    vecs[:, V_LB0:V_LB0 + 4] = fm_vec(inp['hg_lb_logits'][0])
    vecs[:, V_LB1:V_LB1 + 4] = fm_vec(inp['hg_lb_logits'][1])
    lbrep = np.ascontiguousarray(np.broadcast_to(np.asarray(inp['hg_lb_logits'], f32)[None], (128, 2, 512)))
    p = np.arange(128)[:, None]
    j = np.arange(128)[None, :]
    ident = (p == j).astype(f32)
    ones = np.ones((128, 128), f32)
    negU = -(p >= j).astype(f32)
    negL = -(p < j).astype(f32)
    tri = (p <= j).astype(f32)
    cc = np.arange(512)[None, :]
    masks = np.concatenate([((p + d) < cc).astype(f32) for d in (0, 128, 256, 384)], axis=1)
    cst = np.ascontiguousarray(np.concatenate([ident, ones, negU, tri, negL, masks], axis=1))
    return dict(gu1=gu(inp['ffn1_w_gate'][0], inp['ffn1_w_up'][0]), d1=dn(inp['ffn1_w_down'][0]),
                gu2=gu(inp['ffn2_w_gate'][0], inp['ffn2_w_up'][0]), d2=dn(inp['ffn2_w_down'][0]),
                wfm=wfm, wtm=wtm, vecs=vecs, lbrep=lbrep, cst=cst)


_NC_CACHE = {}


def run(inp, nft):
    f32 = np.float32
    shared = _layouts(inp)
    if nft % 2 == 0:
        npre = nmain = nft // 2
    else:
        npre, nmain = 0, nft
    key = (npre, nmain)
    if key not in _NC_CACHE:
        _NC_CACHE[key] = Builder(npre, nmain).build()
    nc = _NC_CACHE[key]
    xp = np.asarray(inp['x_prompt'], f32)
    xs = np.asarray(inp['x_sample'], f32)
    ck = np.asarray(inp['cache_sb_k'], f32)
    cv = np.asarray(inp['cache_sb_v'], f32)
    st = np.asarray(inp['state_hgrn'], f32)
    meta = np.ascontiguousarray(np.asarray(inp['meta_tokens'], f32))
    B = xp.shape[0]
    H = nmain * NT
    in_maps = []
    for c in range(8):
        m = dict(shared)
        b, half = c // 2, c % 2
        vecs = shared['vecs'].copy()
        if npre == 0:
            m['xp'] = np.ascontiguousarray(xp[b])
            m['xpre'] = np.zeros((NT, D), f32)
            m['flag'] = np.ones((128, 512), f32)
            vecs[:, V_FA], vecs[:, V_FB], vecs[:, V_DEAD] = 0.0, 1.0, 0.0
        elif half == 0:
            m['xp'] = np.ascontiguousarray(xp[b, 0:H])
            m['xpre'] = np.zeros((npre * NT, D), f32)
            m['flag'] = np.zeros((128, 512), f32)
            vecs[:, V_FA], vecs[:, V_FB], vecs[:, V_DEAD] = 1.0, 0.0, -30000.0
        else:
            m['xp'] = np.ascontiguousarray(xp[b, H:2 * H])
            m['xpre'] = np.ascontiguousarray(xp[b, 0:H])
            m['flag'] = np.ones((128, 512), f32)
            vecs[:, V_FA], vecs[:, V_FB], vecs[:, V_DEAD] = 0.0, 1.0, 0.0
        m['vecs'] = vecs
        m['xs'] = np.ascontiguousarray(xs[c])
        m['meta'] = meta
        m['ck'] = np.ascontiguousarray(ck[0, c].reshape(LC, 512))
        m['cv'] = np.ascontiguousarray(cv[0, c].reshape(LC, 512))
        m['st'] = np.ascontiguousarray(st[0, c])
        in_maps.append(m)
    res = run_bass_kernel_spmd(nc, in_maps, core_ids=list(range(8)))
    r = res.results
    if npre == 0:
        y_prompt = np.stack([r[2 * b]['yp'] for b in range(B)])
        pk = np.stack([r[2 * b]['pk'] for b in range(B)])
        pv = np.stack([r[2 * b]['pv'] for b in range(B)])
        ph = np.stack([r[2 * b]['ph'] for b in range(B)])
    else:
        y_prompt = np.stack([np.concatenate([r[2 * b]['yp'], r[2 * b + 1]['yp']], axis=0) for b in range(B)])
        pk = np.stack([np.concatenate([r[2 * b]['pk'], r[2 * b + 1]['pk'][NMETA:]], axis=0) for b in range(B)])
        pv = np.stack([np.concatenate([r[2 * b]['pv'], r[2 * b + 1]['pv'][NMETA:]], axis=0) for b in range(B)])
        ph = np.stack([r[2 * b + 1]['ph'] for b in range(B)])
    y_prompt = y_prompt.astype(f32)
    L = pk.shape[1]
    pk = pk.reshape(B, L, 8, 64)[None].astype(f32)
    pv = pv.reshape(B, L, 8, 64)[None].astype(f32)
    ph = ph[None].astype(f32)
    y_sample = np.stack([r[c]['ys'] for c in range(8)]).astype(f32)
    sk = np.stack([r[c]['sk'].reshape(TS, 8, 64) for c in range(8)])[None].astype(f32)
    sv = np.stack([r[c]['sv'].reshape(TS, 8, 64) for c in range(8)])[None].astype(f32)
    sh = np.stack([r[c]['sh'] for c in range(8)])[None].astype(f32)
    return (y_prompt, y_sample, pk, pv, ph, sk, sv, sh)


def kernel(**inputs):
    nft = np.asarray(inputs['x_prompt']).shape[1] // NT
    return run(inputs, nft)
```

```python
import numpy as np
from contextlib import ExitStack
import concourse.bass as bass
import concourse.mybir as mybir
from concourse.bass_utils import run_bass_kernel_spmd

F32 = mybir.dt.float32
BF16 = mybir.dt.bfloat16
AF = mybir.ActivationFunctionType
ALU = mybir.AluOpType

D = 1024
DFF = 2816
NMETA = 16
TS = 32
LC = 2064
EPS = 1e-6
NT = 512
ENGS = ['pe', 'act', 'dve', 'pool', 'sp']

V_N1, V_NM, V_N2, V_NF, V_BGA, V_BGB, V_HGN, V_LB0, V_LB1, V_LBV, V_OML, V_FA, V_FB, V_DEAD = 0, 8, 16, 24, 32, 40, 48, 52, 56, 60, 64, 68, 69, 70
NVEC = 71


class Sched:
    def __init__(self):
        self.prog = {e: [] for e in ENGS}
        self.cnt = {e: 0 for e in ENGS}
        self.dma_cnt = {}
        self.last_w = {}
        self.readers = {}
        self.known = {e: {} for e in ENGS}
        self.fence = {}
        self.touched = set()
        self.nwaits = 0
        import os
        self.glimit = int(os.environ.get('KOPS', '100000000'))

    def _deps(self, reads, writes):
        need = {}

        def add(src, val):
            if need.get(src, 0) < val:
                need[src] = val
        for k in list(reads) + list(writes):
            if isinstance(k, tuple) and k[0] == 'SH' and k not in self.touched:
                self.touched.add(k)
                for s, v in self.fence.items():
                    add(s, v)
        for k in reads:
            lw = self.last_w.get(k)
            if lw:
                add(*lw)
        for k in writes:
            lw = self.last_w.get(k)
            if lw:
                add(*lw)
            for s, v in self.readers.get(k, {}).items():
                add(s, v)
        return need

    def _waits(self, eng, need):
        waits = []
        for src, val in need.items():
            if src == eng and eng == 'pe':
                continue
            if self.known[eng].get(src, 0) >= val:
                continue
            self.known[eng][src] = val
            waits.append((src, val))
        self.nwaits += len(waits)
        return waits

    def _mark(self, src, val, reads, writes):
        for k in writes:
            self.last_w[k] = (src, val)
            self.readers[k] = {}
        for k in reads:
            if k in writes:
                continue
            self.readers.setdefault(k, {})[src] = val

    def op(self, eng, fn, reads=(), writes=()):
        psr = [k for k in reads if isinstance(k, tuple) and k[0] == 'ps']
        if psr:
            reads = [k for k in reads if k not in psr]
            writes = list(writes) + [k for k in psr if k not in writes]
        self.gcount = getattr(self, 'gcount', 0) + 1
        if self.gcount > self.glimit:
            return
        import sys as _sys, os as _os
        if _os.environ.get('KTRACE'):
            lo, hi = [int(x) for x in _os.environ['KTRACE'].split(',')]
            if lo <= self.gcount <= hi:
                fr = _sys._getframe(2)
                print('OP', self.gcount, eng, 'line', fr.f_lineno, 'from', fr.f_back.f_lineno, 'reads', list(reads)[:3], 'writes', list(writes))
        need = self._deps(reads, writes)
        waits = self._waits(eng, need)
        self.cnt[eng] += 1
        self._mark(eng, self.cnt[eng], reads, writes)
        self.prog[eng].append(('op', waits, fn))

    def dma(self, q, out, in_, reads, writes, sem):
        self.gcount = getattr(self, 'gcount', 0) + 1
        if self.gcount > self.glimit:
            return
        import sys as _sys, os as _os
        if _os.environ.get('KTRACE'):
            lo, hi = [int(x) for x in _os.environ['KTRACE'].split(',')]
            if lo <= self.gcount <= hi:
                fr = _sys._getframe(1)
                print('DMA', self.gcount, q, 'line', fr.f_lineno, 'sem', sem, 'writes', list(writes))
        need = self._deps(reads, writes)
        waits = self._waits(q, need)
        self.dma_cnt[sem] = self.dma_cnt.get(sem, 0) + 16
        self._mark(sem, self.dma_cnt[sem], reads, writes)
        self.prog[q].append(('dma', waits, (out, in_, sem)))

    def phase_switch(self):
        f = dict(self.fence)

        def add(s, v):
            if f.get(s, 0) < v:
                f[s] = v
        for k in list(self.last_w.keys()):
            if isinstance(k, tuple) and k[0] == 'SH':
                add(*self.last_w[k])
                del self.last_w[k]
        for k in list(self.readers.keys()):
            if isinstance(k, tuple) and k[0] == 'SH':
                for s, v in self.readers[k].items():
                    add(s, v)
                del self.readers[k]
        self.fence = f
        self.touched = set()

    def final_wait(self, q):
        waits = [(s, v) for s, v in self.dma_cnt.items()]
        self.prog[q].append(('wait', waits, None))


class Builder:
    def __init__(self, npre, nmain):
        self.npre, self.nmain = npre, nmain
        nft = npre + nmain
        self.nft = nft
        self.FR = nft * NT
        self.KTW = max(NMETA + self.FR, NMETA + LC + TS + 128)
        self.NBLK = max(1 + 4 * nft, 19)
        self.s = Sched()
        self.ring_i = 0
        self.kv_i = 0
        self.xtok_i = 0

    def mm(self, out, lhsT, rhs, start, stop, reads, writes, sgc=False):
        self.s.op('pe', lambda e: e.matmul(out, lhsT=lhsT, rhs=rhs, start=start, stop=stop, skip_group_check=sgc), reads, writes)

    def tr(self, out, in_, n, reads, writes):
        ident = self.ident
        self.s.op('pe', lambda e: e.transpose(out=out, in_=in_, identity=ident[0:n, 0:n]), reads, writes)

    def act(self, out, in_, func, reads, writes, bias=None, scale=None):
        kw = {}
        if bias is not None:
            kw['bias'] = bias
        if scale is not None:
            kw['scale'] = scale
        self.s.op('act', lambda e: e.activation(out=out, in_=in_, func=func, **kw), reads, writes)

    def tt(self, eng, out, in0, in1, op, reads, writes):
        self.s.op(eng, lambda e: e.tensor_tensor(out=out, in0=in0, in1=in1, op=op), reads, writes)

    def tsc(self, eng, out, in0, s1, s2, op0, op1, reads, writes):
        if s2 is None:
            self.s.op(eng, lambda e: e.tensor_scalar(out=out, in0=in0, scalar1=s1, scalar2=None, op0=op0), reads, writes)
        else:
            self.s.op(eng, lambda e: e.tensor_scalar(out=out, in0=in0, scalar1=s1, scalar2=s2, op0=op0, op1=op1), reads, writes)

    def stt(self, eng, out, in0, scalar, in1, op0, op1, reads, writes):
        self.s.op(eng, lambda e: e.scalar_tensor_tensor(out=out, in0=in0, scalar=scalar, in1=in1, op0=op0, op1=op1), reads, writes)

    def cp(self, eng, out, in_, reads, writes):
        if eng == 'act':
            self.act(out, in_, AF.Copy, reads, writes)
        else:
            self.s.op(eng, lambda e: e.tensor_copy(out=out, in_=in_), reads, writes)

    def ring_load(self, dram_ap, nelem, rdkey):
        slot = self.ring_i % 4
        self.ring_i += 1
        out = self.ring[:, slot, 0:nelem]
        self.s.dma('sp', out, dram_ap, [rdkey], [('ring', slot)], 'ring%d' % slot)
        return slot

    def build(self):
        nc = bass.Bass("TRN2", target_bir_lowering=False)
        self.nc = nc
        nft, FR = self.nft, self.FR
        FM_ = self.nmain * NT
        FP_ = max(self.npre, 1) * NT
        dt_in = {}

        def din(name, shape):
            dt_in[name] = nc.dram_tensor(name, list(shape), F32, kind="ExternalInput").ap()
            return dt_in[name]

        def dout(name, shape):
            return nc.dram_tensor(name, list(shape), F32, kind="ExternalOutput").ap()
        self.xp = din("xp", [FM_, D])
        self.xpre = din("xpre", [FP_, D])
        self.flag_d = din("flag", [128, 512])
        self.xs = din("xs", [TS, D])
        self.meta = din("meta", [NMETA, D])
        self.ck = din("ck", [LC, 512])
        self.cv = din("cv", [LC, 512])
        self.st = din("st", [4, 128, 128])
        self.vecs_d = din("vecs", [128, NVEC])
        self.lbrep_d = din("lbrep", [128, 2, 512])
        self.cst_d = din("cst", [128, 5 * 128 + 4 * 512])
        self.w32 = {}
        self.wsz = {'gu1': 22 * 128 * 2048, 'gu2': 22 * 128 * 2048, 'd1': 2 * 8 * 128 * 1408, 'd2': 2 * 8 * 128 * 1408,
                    'wfm': 52 * 128 * 1024, 'wtm': 4 * 128 * 4096}
        for n, sz in self.wsz.items():
            self.w32[n] = din(n, [sz // 2048, 2048])
        self.yp = dout("yp", [FM_, D])
        self.ys = dout("ys", [TS, D])
        self.pk = dout("pk", [NMETA + FM_, 512])
        self.pv = dout("pv", [NMETA + FM_, 512])
        self.ph = dout("ph", [4, 128, 128])
        self.sk = dout("sk", [TS, 512])
        self.sv = dout("sv", [TS, 512])
        self.sh = dout("sh", [4, 128, 128])
        self.wbf = {n: nc.dram_tensor(n + "_bf", [sz], BF16).ap() for n, sz in self.wsz.items()}

        with ExitStack() as es:
            es.enter_context(nc.allow_low_precision("bf16 matmul operands, fp32 accumulation"))

            def sb(name, shape, dt):
                return es.enter_context(nc.sbuf_tensor(name, list(shape), dt))

            def ps(name):
                return es.enter_context(nc.psum_tensor(name, [128, 512], F32))
            self.ident = sb("ident", [128, 128], F32)
            self.onesb = sb("onesb", [128, 128], BF16)
            self.negU = sb("negU", [128, 128], BF16)
            self.negL = sb("negL", [128, 128], BF16)
            self.tri = sb("tri", [128, 128], F32)
            self.masks = sb("masks", [128, 4, 512], BF16)
            self.vecs = sb("vecs_s", [128, NVEC], F32)
            self.lb_t = sb("lb_t", [128, 2, 512], F32)
            self.xT = sb("xT", [128, 8, NT], F32)
            self.hT = sb("hT", [128, 8, NT + 64], BF16)
            self.rstd = sb("rstd", [128, NT], F32)
            self.lnt = sb("lnt", [128, NT], F32)
            self.KT = sb("KT", [128, 4, self.KTW], BF16)
            self.Vp = sb("Vp", [128, self.NBLK, 512], BF16)
            self.ring = sb("ring", [128, 4, 4096], BF16)
            self.S = sb("S", [128, 2, 4, 128], F32)
            self.Sb = sb("Sb", [128, 2, 4, 128], BF16)
            self.kvst = sb("kvst", [128, 2, 512], F32)
            self.Ssave = sb("Ssave", [128, 4, 128], F32)
            self.qT = sb("qT", [128, 4, NT], BF16)
            self.obT = sb("obT", [128, 4, NT], F32)
            self.oaT = sb("oaT", [128, 4, NT], BF16)
            SHW = 11008
            self.SH = sb("SH", [128, SHW], F32)
            self.bank = [ps("bank%d" % i) for i in range(8)]
            self.zerob = sb("zerob", [128, 128], BF16)
            self._views()
            import os
            if os.environ.get('KMEM'):
                print('sbuf bytes remaining', nc.sbuf_bytes_remaining)

            self._emit_all()

            sems = {}
            names = [e for e in ENGS if e != 'sp'] + sorted(self.s.dma_cnt.keys())
            for n in names:
                sems[n] = es.enter_context(nc.semaphore("s_" + n))
            block = es.enter_context(nc.Block())
            prog = self.s.prog

            def run(eng_obj, ename):
                for kind, waits, payload in prog[ename]:
                    for src, val in waits:
                        eng_obj.wait_ge(sems[src], val)
                    if kind == 'op':
                        ins = payload(eng_obj)
                        ins.then_inc(sems[ename], 1)
                    elif kind == 'dma':
                        out, in_, sem = payload
                        eng_obj.dma_start(out=out, in_=in_).then_inc(sems[sem], 16)

            @block.tensor
            def _(e):
                run(e, 'pe')

            @block.scalar
            def _(e):
                run(e, 'act')

            @block.vector
            def _(e):
                run(e, 'dve')

            @block.gpsimd
            def _(e):
                run(e, 'pool')

            @block.sync
            def _(e):
                run(e, 'sp')
        return nc

    def _views(self):
        SH = self.SH

        def v(off_b, nbytes, dt, pattern=None, **kw):
            a = SH[:, off_b // 4:(off_b + nbytes) // 4]
            if dt is BF16:
                a = a.bitcast(BF16)
            if pattern:
                a = a.rearrange(pattern, **kw)
            return a
        self.hid = v(0, 11264, BF16, "p (c n) -> p c n", c=11)
        self.sg = v(11264, 4096, F32, "p (c n) -> p c n", c=2)
        self.x_tok = v(15360, 8192, F32, "p (c n) -> p c n", c=2)
        self.omfT = v(0, 8192, F32, "p (c n) -> p c n", c=4)
        self.qhT = v(8192, 8192, F32, "p (c n) -> p c n", c=4)
        self.logf = v(16384, 4096, F32, "p (c n) -> p c n", c=2)
        self.omf_tm = v(20480, 4096, F32, "p (c n) -> p c n", c=2)
        self.iv = v(24576, 2048, BF16, "p (c n) -> p c n", c=2)
        self.eb = v(26624, 2048, F32, "p (a h t) -> p a h t", a=2, h=4)
        self.enb = v(28672, 2048, F32, "p (a h t) -> p a h t", a=2, h=4)
        self.qe = v(30720, 1024, BF16, "p (a h t) -> p a h t", a=2, h=4)
        self.ke = v(41984, 2048, BF16, "p (a h t) -> p a h t", a=2, h=4)
        self.enb_tm = v(32768, 4096, F32, "p (c n) -> p c n", c=2)
        self.ke_tm = v(36864, 2048, BF16, "p (c n) -> p c n", c=2)
        self.sc = v(38912, 1024, BF16, "p (a h t) -> p a h t", a=2, h=4)
        self.tmpS = v(39936, 2048, F32, "p (h t) -> p h t", h=4)
        self.e_ = v(0, 12288, F32, "p (c n) -> p c n", c=6)
        self.sp_ = v(12288, 6144, BF16, "p (c n) -> p c n", c=6)
        self.e2_ = v(18432, 6144, F32, "p (c n) -> p c n", c=3)
        self.at_ = v(24576, 6144, BF16, "p (c n) -> p c n", c=6)
        self.ckv = v(30720, 8192, F32, "p (c n) -> p c n", c=2)
        self.sgT = v(0, 8192, F32, "p (c n) -> p c n", c=4)
        self.gA = v(8192, 8192, BF16, "p (c n) -> p c n", c=8)
        self.gB = v(16384, 8192, BF16, "p (c n) -> p c n", c=8)
        self.t1 = v(24576, 2048, F32, "p (c n) -> p c n", c=1)
        self.t2 = v(26624, 2048, F32, "p (c n) -> p c n", c=1)
        self.mT = v(28672, 8192, BF16, "p (c n) -> p c n", c=8)
        self.obn = v(36864, 4096, BF16, "p (c n) -> p c n", c=4)

    def _emit_all(self):
        import os
        self.kstop = int(os.environ.get('KSTOP', '99'))
        self.prologue()
        self.tile(-1, 'meta')
        for p in range(self.npre):
            self.tile(p, 'pre')
        if self.npre > 0:
            self.state_select()
        for m in range(self.nmain):
            self.tile(self.npre + m, 'main')
        self.tile(-1, 'sample')
        self.s.final_wait('sp')

    def prologue(self):
        s = self.s
        c = self.cst_d
        o = 0
        s.dma('pool', self.ident[:], c[:, o:o + 128], [], ['ident'], 'cst'); o += 128
        s.dma('pool', self.onesb[:], c[:, o:o + 128], [], ['onesb'], 'cst'); o += 128
        s.dma('pool', self.negU[:], c[:, o:o + 128], [], ['negU'], 'cst'); o += 128
        s.dma('pool', self.tri[:], c[:, o:o + 128], [], ['tri'], 'cst'); o += 128
        s.dma('pool', self.negL[:], c[:, o:o + 128], [], ['negL'], 'cst'); o += 128
        s.dma('pool', self.masks[:], c[:, o:o + 2048].rearrange("p (d n) -> p d n", d=4), [], ['masks'], 'cst'); o += 2048
        s.dma('pool', self.vecs[:], self.vecs_d, [], ['vecs'], 'cst')
        s.dma('pool', self.lb_t[:], self.lbrep_d, [], ['lb_t'], 'cst')
        allc = ['ident', 'onesb', 'negU', 'negL', 'tri', 'masks', 'vecs', 'lb_t']
        tot = s.dma_cnt['cst']
        for k in allc:
            s.last_w[k] = ('cst', tot)
        pieces = [('gu1', 0, 1408), ('d1', 0, 704), ('gu1', 1408, 1408), ('d1', 704, 704),
                  ('wfm', 0, 1024), ('wtm', 0, 1024), ('wfm', 1024, 2304),
                  ('gu2', 0, 1408), ('d2', 0, 704), ('gu2', 1408, 1408), ('d2', 704, 704)]
        for i, (n, r0, nr) in enumerate(pieces):
            dst = self.wbf[n].rearrange("(r c) -> r c", c=2048)[r0:r0 + nr, :]
            s.dma('pool', dst, self.w32[n][r0:r0 + nr, :], [], [('scr', n, r0)], 'cast%d' % i)
        self.s.op('dve', lambda e: e.memset(self.hT[:, :, :], 0.0), [], [('hT', kc) for kc in range(8)])
        self.s.op('pool', lambda e: e.memset(self.KT[:, :, :], 0.0), [], [('KT', c) for c in range(4)])
        self.s.op('dve', lambda e: e.memset(self.xT[:, :, :], 0.0), [], [('xT', kc) for kc in range(8)])
        self.s.op('pool', lambda e: e.memset(self.SH[:, :], 0.0), [], [('SH', 'all')])
        self.s.op('pool', lambda e: e.memset(self.zerob[:, :], 0.0), [], ['zerob'])
        vv = self.vecs
        self.tt('dve', vv[:, V_LBV:V_LBV + 4], vv[:, V_LB1:V_LB1 + 4], vv[:, V_LB0:V_LB0 + 4], ALU.subtract, ['vecs'], ['vecs'])
        self.act(vv[:, V_LBV:V_LBV + 4], vv[:, V_LBV:V_LBV + 4], AF.Sigmoid, ['vecs'], ['vecs'])
        self.tsc('dve', vv[:, V_OML:V_OML + 4], vv[:, V_LBV:V_LBV + 4], -1.0, 1.0, ALU.mult, ALU.add, ['vecs'], ['vecs'])
        lt = self.lb_t
        self.tt('dve', lt[:, 0, :], lt[:, 1, :], lt[:, 0, :], ALU.subtract, ['lb_t'], ['lb_t'])
        self.act(lt[:, 0, :], lt[:, 0, :], AF.Sigmoid, ['lb_t'], ['lb_t'])
        self.tsc('dve', lt[:, 1, :], lt[:, 0, :], -1.0, 1.0, ALU.mult, ALU.add, ['lb_t'], ['lb_t'])

    def scr_key(self, n, row2048):
        bounds = {'gu1': [0, 1408], 'gu2': [0, 1408], 'd1': [0, 704], 'd2': [0, 704], 'wfm': [0, 1024], 'wtm': [0]}[n]
        r0 = max(b for b in bounds if b <= row2048)
        return ('scr', n, r0)

    def load_gu(self, which, c):
        n = 'gu%d' % which
        ap = self.wbf[n].rearrange("(c p f) -> c p f", p=128, f=2048)[c]
        return self.ring_load(ap, 2048, self.scr_key(n, c * 128))

    def load_d(self, which, half, o):
        n = 'd%d' % which
        ap = self.wbf[n].rearrange("(h o p f) -> h o p f", h=2, o=8, p=128)[half, o]
        return self.ring_load(ap, 1408, self.scr_key(n, (half * 8 + o) * 128 * 1408 // 2048))

    def load_fm(self, c0, ncnk):
        slot = self.ring_i % 4
        self.ring_i += 1
        src = self.wbf['wfm'].rearrange("(c p f) -> c p f", p=128, f=1024)
        for i in range(ncnk):
            self.s.dma('sp', self.ring[:, slot, i * 1024:(i + 1) * 1024], src[c0 + i], [self.scr_key('wfm', c0 * 64)], [('ring', slot)], 'ring%d' % slot)
        return slot

    def load_tm(self, g):
        ap = self.wbf['wtm'].rearrange("(g p f) -> g p f", p=128, f=4096)[g]
        return self.ring_load(ap, 4096, ('scr', 'wtm', 0))

    def state_select(self):
        S0 = self.S[:, 0, :, :]
        self.tsc('dve', self.Ssave[:, :, :], self.Ssave[:, :, :], self.vecs[:, V_FA:V_FA + 1], None, ALU.mult, None, ['Ssave', 'vecs'], ['Ssave'])
        self.stt('dve', S0, S0, self.vecs[:, V_FB:V_FB + 1], self.Ssave[:, :, :], ALU.mult, ALU.add, [('S', 0), 'Ssave', 'vecs'], [('S', 0)])
        for hh in range(4):
            self.cp('pool', self.Sb[:, 0, hh, :], self.S[:, 0, hh, :], [('S', 0)], [('Sb', 0, hh)])

    def tile(self, f, mode='extra'):
        self.mode = mode
        self.m = f - self.npre if mode == 'main' else f
        if mode == 'meta':
            n = NMETA
            segs = [('meta', 0, NMETA)]
        elif mode == 'sample':
            n = TS
            segs = [('sample', 0, TS)]
        else:
            n = NT
            segs = [('frames', 0, NT)]
        self.n = n
        self.f = f
        self.segs = segs
        s = self.s
        ks = self.kstop if f < 0 else 99
        s.phase_switch()
        self.load_x()
        if mode in ('pre', 'meta'):
            self.norm(V_N1)
            self.ffn(1)
            self.norm(V_NM)
            s.phase_switch()
            self.w_in()
            self.hgrn()
            return
        if ks < 2: return
        self.norm(V_N1)
        if ks < 3: return
        self.ffn(1)
        if ks < 4: return
        self.norm(V_NM)
        s.phase_switch()
        self.w_in()
        if ks < 5 or ks == 41: return
        self.hgrn()
        if ks < 6: return
        s.phase_switch()
        self.attn()
        if ks < 7: return
        s.phase_switch()
        self.post()
        if ks < 8: return
        s.phase_switch()
        self.norm(V_N2)
        self.ffn(2)
        self.final()

    def load_x(self):
        n, f = self.n, self.f
        if f < 0:
            blocks = [(0, n)]
        else:
            blocks = [(j * 128, 128) for j in range(4)]
        for bi, (c0, nb) in enumerate(blocks):
            slot = self.xtok_i % 2
            self.xtok_i += 1
            xk = ('SH', 'x_tok', slot)
            if f < 0:
                self.s.dma('pool', self.x_tok[0:n, slot, :], self.xs if self.mode == 'sample' else self.meta, [], [xk], 'xin%d' % slot)
            else:
                r0 = self.m * NT + c0
                srcx = self.xpre if self.mode == 'pre' else self.xp
                self.s.dma('pool', self.x_tok[:, slot, :], srcx[r0:r0 + 128, :], [], [xk], 'xin%d' % slot)
            for kc in range(8):
                bk = self.bank[kc % 2]
                self.tr(bk[:, 0:nb], self.x_tok[0:nb, slot, kc * 128:(kc + 1) * 128], nb, [xk, 'ident'], [('ps', kc % 2)])
                eng = 'dve' if kc % 2 == 0 else 'act'
                self.cp(eng, self.xT[:, kc, c0:c0 + nb], bk[:, 0:nb], [('ps', kc % 2)], [('xT', kc)])

    def norm(self, gcol, inplace=False):
        n = self.n
        for kc in range(8):
            self.act(self.hT[:, kc, 0:n], self.xT[:, kc, 0:n], AF.Square, [('xT', kc)], [('hT', kc)])
        bk = self.bank[7]
        for kc in range(8):
            self.mm(bk[:, 0:n], self.onesb[:, :], self.hT[:, kc, 0:n], kc == 0, kc == 7, [('hT', kc), 'onesb'], [('ps', 7)])
        self.act(self.lnt[:, 0:n], bk[:, 0:n], AF.Ln, [('ps', 7)], ['lnt'], bias=EPS, scale=1.0 / D)
        self.act(self.rstd[:, 0:n], self.lnt[:, 0:n], AF.Exp, ['lnt'], ['rstd'], scale=-0.5)
        for kc in range(8):
            eng = 'dve'
            if inplace:
                self.stt(eng, self.xT[:, kc, 0:n], self.xT[:, kc, 0:n], self.vecs[:, gcol + kc:gcol + kc + 1], self.rstd[:, 0:n],
                         ALU.mult, ALU.mult, [('xT', kc), 'rstd', 'vecs'], [('xT', kc)])
            else:
                self.stt(eng, self.hT[:, kc, 0:n], self.xT[:, kc, 0:n], self.vecs[:, gcol + kc:gcol + kc + 1], self.rstd[:, 0:n],
                         ALU.mult, ALU.mult, [('xT', kc), 'rstd', 'vecs'], [('hT', kc)])

    def ffn(self, which):
        n = self.n
        hk = [('hT', kc) for kc in range(8)]
        it = 0
        for half in range(2):
            for cc in range(11):
                c = half * 11 + cc
                slot = self.load_gu(which, c)
                w = self.ring[:, slot, 0:2048].rearrange("p (g k n) -> p g k n", g=2, k=8)
                gb, ub = it % 2, 2 + it % 2
                for kc in range(8):
                    self.mm(self.bank[gb][:, 0:n], w[:, 0, kc, :], self.hT[:, kc, 0:n], kc == 0, kc == 7, hk + [('ring', slot)], [('ps', gb)])
                for kc in range(8):
                    self.mm(self.bank[ub][:, 0:n], w[:, 1, kc, :], self.hT[:, kc, 0:n], kc == 0, kc == 7, hk + [('ring', slot)], [('ps', ub)])
                sgk = ('SH', 'sg', it % 2)
                self.act(self.sg[:, it % 2, 0:n], self.bank[gb][:, 0:n], AF.Silu, [('ps', gb)], [sgk])
                self.tt('dve', self.hid[:, cc, 0:n], self.sg[:, it % 2, 0:n], self.bank[ub][:, 0:n], ALU.mult, [sgk, ('ps', ub)], [('SH', 'hid', cc)])
                it += 1
            for o in range(8):
                slot = self.load_d(which, half, o)
                w = self.ring[:, slot, 0:1408].rearrange("p (c n) -> p c n", c=11)
                db = 4 + o % 2
                for cc in range(11):
                    self.mm(self.bank[db][:, 0:n], w[:, cc, :], self.hid[:, cc, 0:n], cc == 0, cc == 10, [('SH', 'hid', cc), ('ring', slot)], [('ps', db)])
                self.stt('dve', self.xT[:, o, 0:n], self.bank[db][:, 0:n], 0.5, self.xT[:, o, 0:n], ALU.mult, ALU.add, [('ps', db), ('xT', o)], [('xT', o)])

    def kcol(self, seg):
        if seg == 'sample':
            return NMETA + LC
        if seg == 'meta':
            return 0
        return NMETA + self.f * NT

    def w_in(self):
        n = self.n
        hk = [('hT', kc) for kc in range(8)]
        bi = 0
        for g4 in ([1] if self.mode in ('pre', 'meta') else range(4)):
            slot = self.load_fm(g4 * 4, 4)
            w = self.ring[:, slot, 0:4096].rearrange("p (c k n) -> p c k n", c=4, k=8)
            for ci in range(4):
                b = bi % 4
                bi += 1
                bk = self.bank[b]
                for kc in range(8):
                    self.mm(bk[:, 0:n], w[:, ci, kc, :], self.hT[:, kc, 0:n], kc == 0, kc == 7, hk + [('ring', slot)], [('ps', b)])
                if g4 == 0:
                    self.act(self.qT[:, ci, 0:n], bk[:, 0:n], AF.Copy, [('ps', b)], [('qT', ci)], scale=0.125)
                elif g4 == 1:
                    for (sname, c0, ns) in self.segs:
                        kc0 = self.kcol(sname)
                        self.cp('dve', self.KT[:, ci, kc0:kc0 + ns], bk[:, c0:c0 + ns], [('ps', b)], [('KT', ci)])
                elif g4 == 2:
                    self.act(self.lnt[:, 0:n], bk[:, 0:n], AF.Sigmoid, [('ps', b)], ['lnt'], scale=-1.0)
                    self.tsc('dve', self.omfT[:, ci, 0:n], self.lnt[:, 0:n], self.vecs[:, V_OML + ci:V_OML + ci + 1], None, ALU.mult, None,
                             ['lnt', 'vecs'], [('SH', 'omfT', ci)])
                else:
                    self.cp('act', self.qhT[:, ci, 0:n], bk[:, 0:n], [('ps', b)], [('SH', 'qhT', ci)])
        if self.kstop == 41:
            return
        if self.mode == 'sample':
            blocks = [('sample', 0, TS, self.sk, self.sv, 0, 18)]
        elif self.mode == 'meta':
            blocks = [('meta', 0, NMETA, self.pk, self.pv, 0, 0)]
        else:
            blocks = [('frames', j * 128, 128, self.pk, self.pv, NMETA + self.m * NT + j * 128, 1 + 4 * self.f + j) for j in range(4)]
        for g in ([1] if self.mode == 'pre' else range(2)):
            slot = self.load_tm(g)
            w = self.ring[:, slot, 0:4096].rearrange("p (k n) -> p k n", k=8)
            for (sname, c0, nb, okd, ovd, r0, vblk) in blocks:
                b = 4 + bi % 4
                bi += 1
                bk = self.bank[b]
                for kc in range(8):
                    self.mm(bk[:, :], self.hT[:, kc, c0:c0 + 128], w[:, kc, :], kc == 0, kc == 7, hk + [('ring', slot)], [('ps', b)])
                if self.mode == 'pre':
                    self.cp('act', self.Vp[0:nb, vblk, :], bk[0:nb, :], [('ps', b)], [('Vp', vblk)])
                    continue
                ks = self.kv_i % 2
                self.kv_i += 1
                self.cp('dve', self.kvst[0:nb, ks, :], bk[0:nb, :], [('ps', b)], [('kvst', ks)])
                dst = (okd if g == 0 else ovd)[r0:r0 + nb, :]
                import os
                if not os.environ.get('NOKV'):
                    self.s.dma(os.environ.get('KVQ', 'pool'), dst, self.kvst[0:nb, ks, :], [('kvst', ks)], [], 'kvo%d' % ks)
                if g == 1:
                    self.cp('act', self.Vp[0:nb, vblk, :], bk[0:nb, :], [('ps', b)], [('Vp', vblk)])

    def hgrn(self):
        s = self.s
        hk = [('hT', kc) for kc in range(8)]
        slz = self.load_tm(2)
        sli = self.load_tm(3)
        wz = self.ring[:, slz, 0:4096].rearrange("p (k n) -> p k n", k=8)
        wi = self.ring[:, sli, 0:4096].rearrange("p (k n) -> p k n", k=8)
        if self.mode == 'sample':
            chunks = [(0, TS, 1)]
            self.s.dma('pool', self.S[:, 1, :, :], self.st.rearrange("h k v -> k h v"), [], [('S', 1)], 'stin')
            for hh in range(4):
                self.cp('pool', self.Sb[:, 1, hh, :], self.S[:, 1, hh, :], [('S', 1)], [('Sb', 1, hh)])
        elif self.mode == 'meta':
            chunks = [(0, NMETA, 0)]
            self.s.op('dve', lambda e: e.memset(self.S[:, 0, :, :], 0.0), [], [('S', 0)])
            self.s.op('dve', lambda e: e.memset(self.Sb[:, 0, :, :], 0.0), [], [('Sb', 0, hh) for hh in range(4)])
        else:
            chunks = [(i * 64, 64, 0) for i in range(8)]
        for ci, (c0, T, si) in enumerate(chunks):
            par = ci % 2
            bA, bB, bC, bD, bE, bF = self.bank[0], self.bank[1], self.bank[2 + par], self.bank[4], self.bank[5], self.bank[6 + par]
            kC, kF = ('ps', 2 + par), ('ps', 6 + par)
            for kc in range(8):
                self.mm(bA[:, :], self.hT[:, kc, c0:c0 + 128], wz[:, kc, :], kc == 0, kc == 7, hk + [('ring', slz)], [('ps', 0)])
            for kc in range(8):
                self.mm(bB[:, :], self.hT[:, kc, c0:c0 + 128], wi[:, kc, :], kc == 0, kc == 7, hk + [('ring', sli)], [('ps', 1)])
            k_omf, k_logf, k_iv = ('SH', 'omf_tm', par), ('SH', 'logf', par), ('SH', 'iv', par)
            self.act(self.omf_tm[0:T, par, :], bA[0:T, :], AF.Sigmoid, [('ps', 0)], [k_omf], scale=-1.0)
            self.tt('dve', self.omf_tm[0:T, par, :], self.omf_tm[0:T, par, :], self.lb_t[0:T, 1, :], ALU.mult, [k_omf, 'lb_t'], [k_omf])
            self.act(self.logf[0:T, par, :], self.omf_tm[0:T, par, :], AF.Ln, [k_omf], [k_logf], bias=1.0, scale=-1.0)
            self.cp('act', self.iv[0:T, par, :], bB[0:T, :], [('ps', 1)], [k_iv])
            cheap = self.mode in ('pre', 'meta')
            if not cheap:
                for hh in range(4):
                    self.mm(bC[:, hh * 64:hh * 64 + T], self.logf[0:T, par, hh * 128:(hh + 1) * 128], self.tri[0:T, 0:T], True, True,
                            [k_logf, 'tri'], [kC])
            self.mm(bD[:, :], self.tri[0:T, 0:128], self.logf[0:T, par, :], True, True, [k_logf, 'tri'], [('ps', 4)])
            k_enbt, k_ket = ('SH', 'enb_tm', par), ('SH', 'ke_tm', par)
            self.act(self.enb_tm[0:T, par, :], bD[0:T, :], AF.Exp, [('ps', 4)], [k_enbt], scale=-1.0)
            self.tt('dve', self.ke_tm[0:T, par, :], self.omf_tm[0:T, par, :], self.enb_tm[0:T, par, :], ALU.mult, [k_omf, k_enbt], [k_ket])
            if cheap:
                for hh in range(4):
                    self.mm(bC[:, hh * 64:hh * 64 + 2], self.logf[0:T, par, hh * 128:(hh + 1) * 128], self.tri[0:T, 126:128], True, True,
                            [k_logf, 'tri'], [kC])
                for hh in range(4):
                    self.act(self.eb[:, par, hh, T - 1:T], bC[:, hh * 64:hh * 64 + 1], AF.Exp, [kC], [('SH', 'eb', par, hh)])
            else:
                for hh in range(4):
                    k_eb, k_enb, k_qe, k_ke = ('SH', 'eb', par, hh), ('SH', 'enb', par, hh), ('SH', 'qe', par, hh), ('SH', 'ke', par, hh)
                    self.act(self.eb[:, par, hh, 0:T], bC[:, hh * 64:hh * 64 + T], AF.Exp, [kC], [k_eb])
                    self.act(self.enb[:, par, hh, 0:T], bC[:, hh * 64:hh * 64 + T], AF.Exp, [kC], [k_enb], scale=-1.0)
                    self.tt('dve', self.qe[:, par, hh, 0:T], self.qhT[:, hh, c0:c0 + T], self.eb[:, par, hh, 0:T], ALU.mult,
                            [('SH', 'qhT', hh), k_eb], [k_qe])
                    self.tt('pool', self.ke[:, par, hh, 0:T], self.omfT[:, hh, c0:c0 + T], self.enb[:, par, hh, 0:T], ALU.mult,
                            [('SH', 'omfT', hh), k_enb], [k_ke])
                for hh in range(4):
                    k_qe, k_ke = ('SH', 'qe', par, hh), ('SH', 'ke', par, hh)
                    self.mm(bC[:, 256 + hh * 64:256 + hh * 64 + T], self.ke[:, par, hh, 0:128], self.qe[:, par, hh, 0:T], True, True,
                            [k_qe, k_ke], [kC])
                for hh in range(4):
                    k_sc = ('SH', 'sc', par, hh)
                    self.tt('dve', self.sc[0:T, par, hh, 0:T], bC[0:T, 256 + hh * 64:256 + hh * 64 + T], self.tri[0:T, 0:T], ALU.mult,
                            [kC, 'tri'], [k_sc])
                for hh in range(4):
                    k_sc, k_qe = ('SH', 'sc', par, hh), ('SH', 'qe', par, hh)
                    self.mm(bE[:, hh * 64:hh * 64 + T], self.iv[0:T, par, hh * 128:(hh + 1) * 128], self.sc[0:T, par, hh, 0:T], True, False,
                            [k_iv, k_sc], [('ps', 5)])
                    self.mm(bE[:, hh * 64:hh * 64 + T], self.Sb[:, si, hh, :], self.qe[:, par, hh, 0:T], False, True,
                            [('Sb', si, hh), k_qe], [('ps', 5)])
                for hh in range(4):
                    self.cp('act' if hh % 2 == 0 else 'dve', self.obT[:, hh, c0:c0 + T], bE[:, hh * 64:hh * 64 + T], [('ps', 5)], [('obT', hh)])
            for hh in range(4):
                self.mm(bF[:, hh * 128:(hh + 1) * 128], self.ke_tm[0:T, par, hh * 128:(hh + 1) * 128], self.iv[0:T, par, hh * 128:(hh + 1) * 128],
                        True, True, [k_ket, k_iv], [kF])
            for hh in range(4):
                k_eb = ('SH', 'eb', par, hh)
                ebl = self.eb[:, par, hh, T - 1:T]
                self.tsc('dve', self.tmpS[:, hh, :], self.S[:, si, hh, :], ebl, None, ALU.mult, None, [('S', si), k_eb], [('SH', 'tmpS', hh)])
                self.stt('dve', self.S[:, si, hh, :], bF[:, hh * 128:(hh + 1) * 128], ebl, self.tmpS[:, hh, :], ALU.mult, ALU.add,
                         [kF, k_eb, ('SH', 'tmpS', hh)], [('S', si)])
                self.cp('pool', self.Sb[:, si, hh, :], self.S[:, si, hh, :], [('S', si)], [('Sb', si, hh)])
            if self.f < 0 and si == 1:
                self.s.dma('pool', self.sh.rearrange("h k v -> k h v"), self.S[:, 1, :, :], [('S', 1)], [], 'sout')
            if self.f < 0 and si == 0 and self.npre > 0:
                self.cp('pool', self.Ssave[:, :, :], self.S[:, 0, :, :], [('S', 0)], ['Ssave'])
            if self.mode == 'main' and self.f == self.nft - 1 and ci == len(chunks) - 1:
                self.s.dma('pool', self.ph.rearrange("h k v -> k h v"), self.S[:, 0, :, :], [('S', 0)], [], 'sout')

    def attn(self):
        f = self.f
        if f < 0:
            nb_c = (LC + 127) // 128
            for b in range(nb_c):
                r0 = b * 128
                kn = min(128, LC - r0)
                slot = b % 2
                ck_key = ('SH', 'ckv', slot)
                self.s.dma('pool', self.ckv[0:kn, slot, 0:512], self.ck[r0:r0 + kn, :], [], [ck_key], 'ckin%d' % slot)
                self.s.dma('pool', self.ckv[0:kn, slot, 512:1024], self.cv[r0:r0 + kn, :], [], [ck_key], 'ckin%d' % slot)
                bk = self.bank[b % 2]
                for ch in range(4):
                    self.tr(bk[:, ch * 128:ch * 128 + kn], self.ckv[0:kn, slot, ch * 128:(ch + 1) * 128], kn, [ck_key, 'ident'], [('ps', b % 2)])
                for ch in range(4):
                    self.cp('dve' if ch % 2 == 0 else 'act', self.KT[:, ch, NMETA + r0:NMETA + r0 + kn], bk[:, ch * 128:ch * 128 + kn],
                            [('ps', b % 2)], [('KT', ch)])
                self.cp('pool', self.Vp[0:kn, 1 + b, :], self.ckv[0:kn, slot, 512:1024], [ck_key], [('Vp', 1 + b)])
            blocks = [(18, TS, NMETA + LC, 0)]
            blocks.append((1 + nb_c - 1, LC - 128 * (nb_c - 1), NMETA + 128 * (nb_c - 1), None))
            for b in range(nb_c - 2, -1, -1):
                blocks.append((1 + b, 128, NMETA + 128 * b, None))
            self.attn_job(0, TS, blocks)
        else:
            blocks = []
            for m in range(3, -1, -1):
                fb = 4 * f + m
                blocks.append((1 + fb, 128, NMETA + 128 * fb, 128 * m))
            for fb in range(4 * f - 1, -1, -1):
                blocks.append((1 + fb, 128, NMETA + 128 * fb, 'flag' if fb < 4 * self.npre else None))
            blocks.append((0, NMETA, 0, None))
            self.attn_job(0, NT, blocks)

    def attn_job(self, q0, nq, blocks):
        nblk = len(blocks)
        for grp in ([0, 1, 2], [3, 4, 5], [6, 7]):
            S_ = len(grp)
            nseq = nblk * S_

            def emitZ(idx):
                k, si = divmod(idx, S_)
                h = grp[si]
                vblk, kn, kcol, md = blocks[k]
                ch, pb = h // 2, 64 * (h % 2)
                zb = idx % 2
                self.mm(self.bank[zb][:, 0:nq], self.KT[pb:pb + 64, ch, kcol:kcol + 128], self.qT[pb:pb + 64, ch, q0:q0 + nq], True, True,
                        [('KT', ch), ('qT', ch)], [('ps', zb)])
            for idx in range(min(2, nseq)):
                emitZ(idx)
            for k in range(nblk):
                vblk, kn, kcol, md = blocks[k]
                par = k % 2
                for si, h in enumerate(grp):
                    idx = k * S_ + si
                    zb = idx % 2
                    ke_, ks_ = ('SH', 'e', si, par), ('SH', 'sp', si, par)
                    if md == 'flag':
                        self.act(self.e_[0:kn, si * 2 + par, 0:nq], self.bank[zb][0:kn, 0:nq], AF.Exp, [('ps', zb), 'vecs'], [ke_],
                                 bias=self.vecs[0:kn, V_DEAD:V_DEAD + 1])
                    else:
                        self.act(self.e_[0:kn, si * 2 + par, 0:nq], self.bank[zb][0:kn, 0:nq], AF.Exp, [('ps', zb)], [ke_])
                    if idx + 2 < nseq:
                        emitZ(idx + 2)
                    if md is not None and md != 'flag':
                        self.tt('pool', self.e_[0:kn, si * 2 + par, 0:nq], self.e_[0:kn, si * 2 + par, 0:nq], self.masks[0:kn, md // 128, 0:nq],
                                ALU.mult, [ke_, 'masks'], [ke_])
                    self.act(self.sp_[0:kn, si * 2 + par, 0:nq], self.e_[0:kn, si * 2 + par, 0:nq], AF.Ln, [ke_], [ks_], bias=1.0)
                for si, h in enumerate(grp):
                    ks_ = ('SH', 'sp', si, par)
                    self.mm(self.bank[2 + si][:, 0:nq], self.negU[0:kn, :], self.sp_[0:kn, si * 2 + par, 0:nq], k == 0, False,
                            [ks_, 'negU'], [('ps', 2 + si)], sgc=True)
                for si, h in enumerate(grp):
                    ke_, k2_, ka_ = ('SH', 'e', si, par), ('SH', 'e2', si), ('SH', 'at', si, par)
                    self.act(self.e2_[0:kn, si, 0:nq], self.bank[2 + si][0:kn, 0:nq], AF.Exp, [('ps', 2 + si)], [k2_])
                    self.tt('dve', self.at_[0:kn, si * 2 + par, 0:nq], self.e2_[0:kn, si, 0:nq], self.e_[0:kn, si * 2 + par, 0:nq], ALU.mult,
                            [k2_, ke_], [ka_])
                for si, h in enumerate(grp):
                    ks_, ka_ = ('SH', 'sp', si, par), ('SH', 'at', si, par)
                    self.mm(self.bank[2 + si][:, 0:nq], self.negL[0:kn, :], self.sp_[0:kn, si * 2 + par, 0:nq], False, k == nblk - 1,
                            [ks_, 'negL'], [('ps', 2 + si)], sgc=True)
                    vlo = h * 64 if h % 2 == 0 else (h - 1) * 64
                    self.mm(self.bank[5 + si][:, 0:nq], self.Vp[0:kn, vblk, vlo:vlo + 128], self.at_[0:kn, si * 2 + par, 0:nq], k == 0, k == nblk - 1,
                            [ka_, ('Vp', vblk)], [('ps', 5 + si)])
            for si, h in enumerate(grp):
                ch, pb = h // 2, 64 * (h % 2)
                self.cp('dve' if si % 2 == 0 else 'act', self.oaT[pb:pb + 64, ch, q0:q0 + nq], self.bank[5 + si][pb:pb + 64, 0:nq],
                        [('ps', 5 + si)], [('oaT', ch, pb)])

    def post(self):
        n = self.n
        hk = [('hT', kc) for kc in range(8)]
        bi = 0
        for ld in range(5):
            slot = self.load_fm(16 + ld * 4, 4)
            w = self.ring[:, slot, 0:4096].rearrange("p (c k n) -> p c k n", c=4, k=8)
            for ci in range(4):
                cidx = ld * 4 + ci
                b = bi % 4
                bi += 1
                bk = self.bank[b]
                for kc in range(8):
                    self.mm(bk[:, 0:n], w[:, ci, kc, :], self.hT[:, kc, 0:n], kc == 0, kc == 7, hk + [('ring', slot)], [('ps', b)])
                if cidx < 4:
                    self.act(self.sgT[:, cidx, 0:n], bk[:, 0:n], AF.Silu, [('ps', b)], [('SH', 'sgT', cidx)])
                elif cidx < 12:
                    o = cidx - 4
                    self.act(self.gA[:, o, 0:n], bk[:, 0:n], AF.Sigmoid, [('ps', b), 'vecs'], [('SH', 'gA', o)], bias=self.vecs[:, V_BGA + o:V_BGA + o + 1])
                else:
                    o = cidx - 12
                    self.act(self.gB[:, o, 0:n], bk[:, 0:n], AF.Sigmoid, [('ps', b), 'vecs'], [('SH', 'gB', o)], bias=self.vecs[:, V_BGB + o:V_BGB + o + 1])
        for hh in range(4):
            kq = ('SH', 'mT', hh)
            self.act(self.mT[:, hh, 0:n], self.obT[:, hh, 0:n], AF.Square, [('obT', hh)], [kq])
            self.mm(self.bank[4][:, 0:n], self.onesb[:, :], self.mT[:, hh, 0:n], True, True, [kq, 'onesb'], [('ps', 4)])
            self.act(self.lnt[:, 0:n], self.bank[4][:, 0:n], AF.Ln, [('ps', 4)], ['lnt'], bias=EPS, scale=1.0 / 128)
            self.act(self.rstd[:, 0:n], self.lnt[:, 0:n], AF.Exp, ['lnt'], ['rstd'], scale=-0.5)
            k1 = ('SH', 't1', 0)
            self.stt('dve', self.t1[:, 0, 0:n], self.obT[:, hh, 0:n], self.vecs[:, V_HGN + hh:V_HGN + hh + 1], self.rstd[:, 0:n], ALU.mult, ALU.mult,
                     [('obT', hh), 'vecs', 'rstd'], [k1])
            self.tt('pool', self.obn[:, hh, 0:n], self.t1[:, 0, 0:n], self.sgT[:, hh, 0:n], ALU.mult, [k1, ('SH', 'sgT', hh)], [('SH', 'obn', hh)])
        oak = [('oaT', c, pb) for c in range(4) for pb in (0, 64)]
        obk = [('SH', 'obn', c) for c in range(4)]
        for ld in range(2):
            slot = self.load_fm(36 + ld * 4, 4)
            w = self.ring[:, slot, 0:4096].rearrange("p (c k n) -> p c k n", c=4, k=8)
            for ci in range(4):
                o = ld * 4 + ci
                ba, bb = o % 2, 2 + o % 2
                for c in range(4):
                    self.mm(self.bank[ba][:, 0:n], w[:, ci, c, :], self.oaT[:, c, 0:n], c == 0, c == 3, oak + [('ring', slot)], [('ps', ba)])
                for c in range(4):
                    self.mm(self.bank[bb][:, 0:n], w[:, ci, 4 + c, :], self.obn[:, c, 0:n], c == 0, c == 3, obk + [('ring', slot)], [('ps', bb)])
                k1, k2 = ('SH', 't1', 0), ('SH', 't2', 0)
                self.tt('dve', self.t1[:, 0, 0:n], self.gA[:, o, 0:n], self.bank[ba][:, 0:n], ALU.mult, [('SH', 'gA', o), ('ps', ba)], [k1])
                self.tt('dve', self.t2[:, 0, 0:n], self.gB[:, o, 0:n], self.bank[bb][:, 0:n], ALU.mult, [('SH', 'gB', o), ('ps', bb)], [k2])
                self.tt('pool', self.mT[:, o, 0:n], self.t1[:, 0, 0:n], self.t2[:, 0, 0:n], ALU.add, [k1, k2], [('SH', 'mT', o)])
        mk = [('SH', 'mT', c) for c in range(8)]
        for ld in range(2):
            slot = self.load_fm(44 + ld * 4, 4)
            w = self.ring[:, slot, 0:4096].rearrange("p (c k n) -> p c k n", c=4, k=8)
            for ci in range(4):
                o = ld * 4 + ci
                b = 4 + o % 2
                for c in range(8):
                    self.mm(self.bank[b][:, 0:n], w[:, ci, c, :], self.mT[:, c, 0:n], c == 0, c == 7, mk + [('ring', slot)], [('ps', b)])
                self.tt('dve', self.xT[:, o, 0:n], self.xT[:, o, 0:n], self.bank[b][:, 0:n], ALU.add, [('xT', o), ('ps', b)], [('xT', o)])

    def final(self):
        n, f = self.n, self.f
        self.norm(V_NF, inplace=True)
        if f < 0:
            blocks = [(0, n, self.ys, 0, TS)]
            assert self.mode == 'sample'
        else:
            blocks = [(j * 128, 128, self.yp, self.m * NT + j * 128, 128) for j in range(4)]
        for (c0, nb, dst, r0, nout) in blocks:
            slot = self.xtok_i % 2
            self.xtok_i += 1
            xk = ('SH', 'x_tok', slot)
            for half in range(2):
                bk = self.bank[half]
                for q in range(4):
                    kc = half * 4 + q
                    self.tr(bk[:, q * 128:(q + 1) * 128], self.xT[:, kc, c0:c0 + 128], 128, [('xT', kc), 'ident'], [('ps', half)])
                self.cp('dve' if half == 0 else 'act', self.x_tok[0:nb, slot, half * 512:(half + 1) * 512], bk[0:nb, :], [('ps', half)], [xk])
            self.s.dma('pool', dst[r0:r0 + nout, :], self.x_tok[0:nout, slot, :], [xk], [], 'yout%d' % slot)


def _fix_kt_keys(b):
    pass


def _layouts(inp):
    f32 = np.float32

    def fm_vec(v):
        v = np.asarray(v, f32).reshape(-1, 128)
        return v.T

    def gu(wg, wu):
        a = np.asarray(wg, f32).reshape(8, 128, 22, 128).transpose(2, 1, 0, 3)
        b = np.asarray(wu, f32).reshape(8, 128, 22, 128).transpose(2, 1, 0, 3)
        return np.ascontiguousarray(np.stack([a, b], axis=2)).reshape(-1, 2048)

    def dn(wd):
        a = np.asarray(wd, f32).reshape(2, 11, 128, 8, 128).transpose(0, 3, 2, 1, 4)
        return np.ascontiguousarray(a).reshape(-1, 2048)
    w_in = np.asarray(inp['w_in'][0], f32)
    cols = np.concatenate([np.arange(0, 512), np.arange(512, 1024), np.arange(1536, 2048), np.arange(2560, 3072),
                           np.arange(3072, 3584), np.arange(3584, 4608), np.arange(4608, 5632)])
    fm = w_in[:, cols].reshape(8, 128, 36, 128).transpose(2, 1, 0, 3)
    wa = np.asarray(inp['w_branch_a'][0], f32).reshape(4, 128, 8, 128).transpose(2, 1, 0, 3)
    wb = np.asarray(inp['w_branch_b'][0], f32).reshape(4, 128, 8, 128).transpose(2, 1, 0, 3)
    wab = np.concatenate([wa, wb], axis=2)
    wo = np.asarray(inp['w_out'][0], f32).reshape(8, 128, 8, 128).transpose(2, 1, 0, 3)
    wfm = np.ascontiguousarray(np.concatenate([fm, wab, wo], axis=0)).reshape(-1, 2048)
    wtm = np.ascontiguousarray(w_in[:, 512:2560].reshape(8, 128, 4, 512).transpose(2, 1, 0, 3)).reshape(-1, 2048)
    vecs = np.zeros((128, NVEC), f32)
    vecs[:, V_N1:V_N1 + 8] = fm_vec(inp['ffn1_norm'][0])
    vecs[:, V_NM:V_NM + 8] = fm_vec(inp['mix_norm'][0])
    vecs[:, V_N2:V_N2 + 8] = fm_vec(inp['ffn2_norm'][0])
    vecs[:, V_NF:V_NF + 8] = fm_vec(inp['final_norm'])
    vecs[:, V_BGA:V_BGA + 16] = fm_vec(inp['b_gate'][0])
    vecs[:, V_HGN:V_HGN + 4] = fm_vec(inp['hg_out_norm'][0])
    vecs[:, V_LB0:V_LB0 + 4] = fm_vec(inp['hg_lb_logits'][0])
    vecs[:, V_LB1:V_LB1 + 4] = fm_vec(inp['hg_lb_logits'][1])
    lbrep = np.ascontiguousarray(np.broadcast_to(np.asarray(inp['hg_lb_logits'], f32)[None], (128, 2, 512)))
    p = np.arange(128)[:, None]
    j = np.arange(128)[None, :]
    ident = (p == j).astype(f32)
    ones = np.ones((128, 128), f32)
    negU = -(p >= j).astype(f32)
    negL = -(p < j).astype(f32)
    tri = (p <= j).astype(f32)
    cc = np.arange(512)[None, :]
    masks = np.concatenate([((p + d) < cc).astype(f32) for d in (0, 128, 256, 384)], axis=1)
    cst = np.ascontiguousarray(np.concatenate([ident, ones, negU, tri, negL, masks], axis=1))
    return dict(gu1=gu(inp['ffn1_w_gate'][0], inp['ffn1_w_up'][0]), d1=dn(inp['ffn1_w_down'][0]),
                gu2=gu(inp['ffn2_w_gate'][0], inp['ffn2_w_up'][0]), d2=dn(inp['ffn2_w_down'][0]),
                wfm=wfm, wtm=wtm, vecs=vecs, lbrep=lbrep, cst=cst)


_NC_CACHE = {}


def run(inp, nft):
    f32 = np.float32
    shared = _layouts(inp)
    if nft % 2 == 0:
        npre = nmain = nft // 2
    else:
        npre, nmain = 0, nft
    key = (npre, nmain)
    if key not in _NC_CACHE:
        _NC_CACHE[key] = Builder(npre, nmain).build()
    nc = _NC_CACHE[key]
    xp = np.asarray(inp['x_prompt'], f32)
    xs = np.asarray(inp['x_sample'], f32)
    ck = np.asarray(inp['cache_sb_k'], f32)
    cv = np.asarray(inp['cache_sb_v'], f32)
    st = np.asarray(inp['state_hgrn'], f32)
    meta = np.ascontiguousarray(np.asarray(inp['meta_tokens'], f32))
    B = xp.shape[0]
    H = nmain * NT
    in_maps = []
    for c in range(8):
        m = dict(shared)
        b, half = c // 2, c % 2
        vecs = shared['vecs'].copy()
        if npre == 0:
            m['xp'] = np.ascontiguousarray(xp[b])
            m['xpre'] = np.zeros((NT, D), f32)
            m['flag'] = np.ones((128, 512), f32)
            vecs[:, V_FA], vecs[:, V_FB], vecs[:, V_DEAD] = 0.0, 1.0, 0.0
        elif half == 0:
            m['xp'] = np.ascontiguousarray(xp[b, 0:H])
            m['xpre'] = np.zeros((npre * NT, D), f32)
            m['flag'] = np.zeros((128, 512), f32)
            vecs[:, V_FA], vecs[:, V_FB], vecs[:, V_DEAD] = 1.0, 0.0, -30000.0
        else:
            m['xp'] = np.ascontiguousarray(xp[b, H:2 * H])
            m['xpre'] = np.ascontiguousarray(xp[b, 0:H])
            m['flag'] = np.ones((128, 512), f32)
            vecs[:, V_FA], vecs[:, V_FB], vecs[:, V_DEAD] = 0.0, 1.0, 0.0
        m['vecs'] = vecs
        m['xs'] = np.ascontiguousarray(xs[c])
        m['meta'] = meta
        m['ck'] = np.ascontiguousarray(ck[0, c].reshape(LC, 512))
        m['cv'] = np.ascontiguousarray(cv[0, c].reshape(LC, 512))
        m['st'] = np.ascontiguousarray(st[0, c])
        in_maps.append(m)
    res = run_bass_kernel_spmd(nc, in_maps, core_ids=list(range(8)))
    r = res.results
    if npre == 0:
        y_prompt = np.stack([r[2 * b]['yp'] for b in range(B)])
        pk = np.stack([r[2 * b]['pk'] for b in range(B)])
        pv = np.stack([r[2 * b]['pv'] for b in range(B)])
        ph = np.stack([r[2 * b]['ph'] for b in range(B)])
    else:
        y_prompt = np.stack([np.concatenate([r[2 * b]['yp'], r[2 * b + 1]['yp']], axis=0) for b in range(B)])
        pk = np.stack([np.concatenate([r[2 * b]['pk'], r[2 * b + 1]['pk'][NMETA:]], axis=0) for b in range(B)])
        pv = np.stack([np.concatenate([r[2 * b]['pv'], r[2 * b + 1]['pv'][NMETA:]], axis=0) for b in range(B)])
        ph = np.stack([r[2 * b + 1]['ph'] for b in range(B)])
    y_prompt = y_prompt.astype(f32)
    L = pk.shape[1]
    pk = pk.reshape(B, L, 8, 64)[None].astype(f32)
    pv = pv.reshape(B, L, 8, 64)[None].astype(f32)
    ph = ph[None].astype(f32)
    y_sample = np.stack([r[c]['ys'] for c in range(8)]).astype(f32)
    sk = np.stack([r[c]['sk'].reshape(TS, 8, 64) for c in range(8)])[None].astype(f32)
    sv = np.stack([r[c]['sv'].reshape(TS, 8, 64) for c in range(8)])[None].astype(f32)
    sh = np.stack([r[c]['sh'] for c in range(8)])[None].astype(f32)
    return (y_prompt, y_sample, pk, pv, ph, sk, sv, sh)


def kernel(**inputs):
    nft = np.asarray(inputs['x_prompt']).shape[1] // NT
    return run(inputs, nft)
```

```python
import numpy as np
from contextlib import ExitStack
import concourse.bass as bass
import concourse.mybir as mybir
from concourse.bass_utils import run_bass_kernel_spmd

F32 = mybir.dt.float32
BF16 = mybir.dt.bfloat16
AF = mybir.ActivationFunctionType
ALU = mybir.AluOpType

D = 1024
DFF = 2816
NMETA = 16
TS = 32
LC = 2064
EPS = 1e-6
NT = 512
ENGS = ['pe', 'act', 'dve', 'pool', 'sp']

V_N1, V_NM, V_N2, V_NF, V_BGA, V_BGB, V_HGN, V_LB0, V_LB1, V_LBV, V_OML, V_FA, V_FB, V_DEAD = 0, 8, 16, 24, 32, 40, 48, 52, 56, 60, 64, 68, 69, 70
NVEC = 71


class Sched:
    def __init__(self):
        self.prog = {e: [] for e in ENGS}
        self.cnt = {e: 0 for e in ENGS}
        self.dma_cnt = {}
        self.last_w = {}
        self.readers = {}
        self.known = {e: {} for e in ENGS}
        self.fence = {}
        self.touched = set()
        self.nwaits = 0
        import os
        self.glimit = int(os.environ.get('KOPS', '100000000'))

    def _deps(self, reads, writes):
        need = {}

        def add(src, val):
            if need.get(src, 0) < val:
                need[src] = val
        for k in list(reads) + list(writes):
            if isinstance(k, tuple) and k[0] == 'SH' and k not in self.touched:
                self.touched.add(k)
                for s, v in self.fence.items():
                    add(s, v)
        for k in reads:
            lw = self.last_w.get(k)
            if lw:
                add(*lw)
        for k in writes:
            lw = self.last_w.get(k)
            if lw:
                add(*lw)
            for s, v in self.readers.get(k, {}).items():
                add(s, v)
        return need

    def _waits(self, eng, need):
        waits = []
        for src, val in need.items():
            if src == eng and eng == 'pe':
                continue
            if self.known[eng].get(src, 0) >= val:
                continue
            self.known[eng][src] = val
            waits.append((src, val))
        self.nwaits += len(waits)
        return waits

    def _mark(self, src, val, reads, writes):
        for k in writes:
            self.last_w[k] = (src, val)
            self.readers[k] = {}
        for k in reads:
            if k in writes:
                continue
            self.readers.setdefault(k, {})[src] = val

    def op(self, eng, fn, reads=(), writes=()):
        psr = [k for k in reads if isinstance(k, tuple) and k[0] == 'ps']
        if psr:
            reads = [k for k in reads if k not in psr]
            writes = list(writes) + [k for k in psr if k not in writes]
        self.gcount = getattr(self, 'gcount', 0) + 1
        if self.gcount > self.glimit:
            return
        import sys as _sys, os as _os
        if _os.environ.get('KTRACE'):
            lo, hi = [int(x) for x in _os.environ['KTRACE'].split(',')]
            if lo <= self.gcount <= hi:
                fr = _sys._getframe(2)
                print('OP', self.gcount, eng, 'line', fr.f_lineno, 'from', fr.f_back.f_lineno, 'reads', list(reads)[:3], 'writes', list(writes))
        need = self._deps(reads, writes)
        waits = self._waits(eng, need)
        self.cnt[eng] += 1
        self._mark(eng, self.cnt[eng], reads, writes)
        self.prog[eng].append(('op', waits, fn))

    def dma(self, q, out, in_, reads, writes, sem):
        self.gcount = getattr(self, 'gcount', 0) + 1
        if self.gcount > self.glimit:
            return
        import sys as _sys, os as _os
        if _os.environ.get('KTRACE'):
            lo, hi = [int(x) for x in _os.environ['KTRACE'].split(',')]
            if lo <= self.gcount <= hi:
                fr = _sys._getframe(1)
                print('DMA', self.gcount, q, 'line', fr.f_lineno, 'sem', sem, 'writes', list(writes))
        need = self._deps(reads, writes)
        waits = self._waits(q, need)
        self.dma_cnt[sem] = self.dma_cnt.get(sem, 0) + 16
        self._mark(sem, self.dma_cnt[sem], reads, writes)
        self.prog[q].append(('dma', waits, (out, in_, sem)))

    def phase_switch(self):
        f = dict(self.fence)

        def add(s, v):
            if f.get(s, 0) < v:
                f[s] = v
        for k in list(self.last_w.keys()):
            if isinstance(k, tuple) and k[0] == 'SH':
                add(*self.last_w[k])
                del self.last_w[k]
        for k in list(self.readers.keys()):
            if isinstance(k, tuple) and k[0] == 'SH':
                for s, v in self.readers[k].items():
                    add(s, v)
                del self.readers[k]
        self.fence = f
        self.touched = set()

    def final_wait(self, q):
        waits = [(s, v) for s, v in self.dma_cnt.items()]
        self.prog[q].append(('wait', waits, None))


class Builder:
    def __init__(self, npre, nmain):
        self.npre, self.nmain = npre, nmain
        nft = npre + nmain
        self.nft = nft
        self.FR = nft * NT
        self.KTW = max(NMETA + self.FR, NMETA + LC + TS + 128)
        self.NBLK = max(1 + 4 * nft, 19)
        self.s = Sched()
        self.ring_i = 0
        self.kv_i = 0
        self.xtok_i = 0

    def mm(self, out, lhsT, rhs, start, stop, reads, writes, sgc=False):
        self.s.op('pe', lambda e: e.matmul(out, lhsT=lhsT, rhs=rhs, start=start, stop=stop, skip_group_check=sgc), reads, writes)

    def tr(self, out, in_, n, reads, writes):
        ident = self.ident
        self.s.op('pe', lambda e: e.transpose(out=out, in_=in_, identity=ident[0:n, 0:n]), reads, writes)

    def act(self, out, in_, func, reads, writes, bias=None, scale=None):
        kw = {}
        if bias is not None:
            kw['bias'] = bias
        if scale is not None:
            kw['scale'] = scale
        self.s.op('act', lambda e: e.activation(out=out, in_=in_, func=func, **kw), reads, writes)

    def tt(self, eng, out, in0, in1, op, reads, writes):
        self.s.op(eng, lambda e: e.tensor_tensor(out=out, in0=in0, in1=in1, op=op), reads, writes)

    def tsc(self, eng, out, in0, s1, s2, op0, op1, reads, writes):
        if s2 is None:
            self.s.op(eng, lambda e: e.tensor_scalar(out=out, in0=in0, scalar1=s1, scalar2=None, op0=op0), reads, writes)
        else:
            self.s.op(eng, lambda e: e.tensor_scalar(out=out, in0=in0, scalar1=s1, scalar2=s2, op0=op0, op1=op1), reads, writes)

    def stt(self, eng, out, in0, scalar, in1, op0, op1, reads, writes):
        self.s.op(eng, lambda e: e.scalar_tensor_tensor(out=out, in0=in0, scalar=scalar, in1=in1, op0=op0, op1=op1), reads, writes)

    def cp(self, eng, out, in_, reads, writes):
        if eng == 'act':
            self.act(out, in_, AF.Copy, reads, writes)
        else:
            self.s.op(eng, lambda e: e.tensor_copy(out=out, in_=in_), reads, writes)

    def ring_load(self, dram_ap, nelem, rdkey):
        slot = self.ring_i % 4
        self.ring_i += 1
        out = self.ring[:, slot, 0:nelem]
        self.s.dma('sp', out, dram_ap, [rdkey], [('ring', slot)], 'ring%d' % slot)
        return slot

    def build(self):
        nc = bass.Bass("TRN2", target_bir_lowering=False)
        self.nc = nc
        nft, FR = self.nft, self.FR
        FM_ = self.nmain * NT
        FP_ = max(self.npre, 1) * NT
        dt_in = {}

        def din(name, shape):
            dt_in[name] = nc.dram_tensor(name, list(shape), F32, kind="ExternalInput").ap()
            return dt_in[name]

        def dout(name, shape):
            return nc.dram_tensor(name, list(shape), F32, kind="ExternalOutput").ap()
        self.xp = din("xp", [FM_, D])
        self.xpre = din("xpre", [FP_, D])
        self.flag_d = din("flag", [128, 512])
        self.xs = din("xs", [TS, D])
        self.meta = din("meta", [NMETA, D])
        self.ck = din("ck", [LC, 512])
        self.cv = din("cv", [LC, 512])
        self.st = din("st", [4, 128, 128])
        self.vecs_d = din("vecs", [128, NVEC])
        self.lbrep_d = din("lbrep", [128, 2, 512])
        self.cst_d = din("cst", [128, 5 * 128 + 4 * 512])
        self.w32 = {}
        self.wsz = {'gu1': 22 * 128 * 2048, 'gu2': 22 * 128 * 2048, 'd1': 2 * 8 * 128 * 1408, 'd2': 2 * 8 * 128 * 1408,
                    'wfm': 52 * 128 * 1024, 'wtm': 4 * 128 * 4096}
        for n, sz in self.wsz.items():
            self.w32[n] = din(n, [sz // 2048, 2048])
        self.yp = dout("yp", [FM_, D])
        self.ys = dout("ys", [TS, D])
        self.pk = dout("pk", [NMETA + FM_, 512])
        self.pv = dout("pv", [NMETA + FM_, 512])
        self.ph = dout("ph", [4, 128, 128])
        self.sk = dout("sk", [TS, 512])
        self.sv = dout("sv", [TS, 512])
        self.sh = dout("sh", [4, 128, 128])
        self.wbf = {n: nc.dram_tensor(n + "_bf", [sz], BF16).ap() for n, sz in self.wsz.items()}

        with ExitStack() as es:
            es.enter_context(nc.allow_low_precision("bf16 matmul operands, fp32 accumulation"))

            def sb(name, shape, dt):
                return es.enter_context(nc.sbuf_tensor(name, list(shape), dt))

            def ps(name):
                return es.enter_context(nc.psum_tensor(name, [128, 512], F32))
            self.ident = sb("ident", [128, 128], F32)
            self.onesb = sb("onesb", [128, 128], BF16)
            self.negU = sb("negU", [128, 128], BF16)
            self.negL = sb("negL", [128, 128], BF16)
            self.tri = sb("tri", [128, 128], F32)
            self.masks = sb("masks", [128, 4, 512], BF16)
            self.vecs = sb("vecs_s", [128, NVEC], F32)
            self.lb_t = sb("lb_t", [128, 2, 512], F32)
            self.xT = sb("xT", [128, 8, NT], F32)
            self.hT = sb("hT", [128, 8, NT + 64], BF16)
            self.rstd = sb("rstd", [128, NT], F32)
            self.lnt = sb("lnt", [128, NT], F32)
            self.KT = sb("KT", [128, 4, self.KTW], BF16)
            self.Vp = sb("Vp", [128, self.NBLK, 512], BF16)
            self.ring = sb("ring", [128, 4, 4096], BF16)
            self.S = sb("S", [128, 2, 4, 128], F32)
            self.Sb = sb("Sb", [128, 2, 4, 128], BF16)
            self.kvst = sb("kvst", [128, 2, 512], F32)
            self.Ssave = sb("Ssave", [128, 4, 128], F32)
            self.qT = sb("qT", [128, 4, NT], BF16)
            self.obT = sb("obT", [128, 4, NT], F32)
            self.oaT = sb("oaT", [128, 4, NT], BF16)
            SHW = 11008
            self.SH = sb("SH", [128, SHW], F32)
            self.bank = [ps("bank%d" % i) for i in range(8)]
            self.zerob = sb("zerob", [128, 128], BF16)
            self._views()
            import os
            if os.environ.get('KMEM'):
                print('sbuf bytes remaining', nc.sbuf_bytes_remaining)

            self._emit_all()

            sems = {}
            names = [e for e in ENGS if e != 'sp'] + sorted(self.s.dma_cnt.keys())
            for n in names:
                sems[n] = es.enter_context(nc.semaphore("s_" + n))
            block = es.enter_context(nc.Block())
            prog = self.s.prog

            def run(eng_obj, ename):
                for kind, waits, payload in prog[ename]:
                    for src, val in waits:
                        eng_obj.wait_ge(sems[src], val)
                    if kind == 'op':
                        ins = payload(eng_obj)
                        ins.then_inc(sems[ename], 1)
                    elif kind == 'dma':
                        out, in_, sem = payload
                        eng_obj.dma_start(out=out, in_=in_).then_inc(sems[sem], 16)

            @block.tensor
            def _(e):
                run(e, 'pe')

            @block.scalar
            def _(e):
                run(e, 'act')

            @block.vector
            def _(e):
                run(e, 'dve')

            @block.gpsimd
            def _(e):
                run(e, 'pool')

            @block.sync
            def _(e):
                run(e, 'sp')
        return nc

    def _views(self):
        SH = self.SH

        def v(off_b, nbytes, dt, pattern=None, **kw):
            a = SH[:, off_b // 4:(off_b + nbytes) // 4]
            if dt is BF16:
                a = a.bitcast(BF16)
            if pattern:
                a = a.rearrange(pattern, **kw)
            return a
        self.hid = v(0, 11264, BF16, "p (c n) -> p c n", c=11)
        self.sg = v(11264, 4096, F32, "p (c n) -> p c n", c=2)
        self.x_tok = v(15360, 8192, F32, "p (c n) -> p c n", c=2)
        self.omfT = v(0, 8192, F32, "p (c n) -> p c n", c=4)
        self.qhT = v(8192, 8192, F32, "p (c n) -> p c n", c=4)
        self.logf = v(16384, 4096, F32, "p (c n) -> p c n", c=2)
        self.omf_tm = v(20480, 4096, F32, "p (c n) -> p c n", c=2)
        self.iv = v(24576, 2048, BF16, "p (c n) -> p c n", c=2)
        self.eb = v(26624, 2048, F32, "p (a h t) -> p a h t", a=2, h=4)
        self.enb = v(28672, 2048, F32, "p (a h t) -> p a h t", a=2, h=4)
        self.qe = v(30720, 1024, BF16, "p (a h t) -> p a h t", a=2, h=4)
        self.ke = v(41984, 2048, BF16, "p (a h t) -> p a h t", a=2, h=4)
        self.enb_tm = v(32768, 4096, F32, "p (c n) -> p c n", c=2)
        self.ke_tm = v(36864, 2048, BF16, "p (c n) -> p c n", c=2)
        self.sc = v(38912, 1024, BF16, "p (a h t) -> p a h t", a=2, h=4)
        self.tmpS = v(39936, 2048, F32, "p (h t) -> p h t", h=4)
        self.e_ = v(0, 12288, F32, "p (c n) -> p c n", c=6)
        self.sp_ = v(12288, 6144, BF16, "p (c n) -> p c n", c=6)
        self.e2_ = v(18432, 6144, F32, "p (c n) -> p c n", c=3)
        self.at_ = v(24576, 6144, BF16, "p (c n) -> p c n", c=6)
        self.ckv = v(30720, 8192, F32, "p (c n) -> p c n", c=2)
        self.sgT = v(0, 8192, F32, "p (c n) -> p c n", c=4)
        self.gA = v(8192, 8192, BF16, "p (c n) -> p c n", c=8)
        self.gB = v(16384, 8192, BF16, "p (c n) -> p c n", c=8)
        self.t1 = v(24576, 2048, F32, "p (c n) -> p c n", c=1)
        self.t2 = v(26624, 2048, F32, "p (c n) -> p c n", c=1)
        self.mT = v(28672, 8192, BF16, "p (c n) -> p c n", c=8)
        self.obn = v(36864, 4096, BF16, "p (c n) -> p c n", c=4)

    def _emit_all(self):
        import os
        self.kstop = int(os.environ.get('KSTOP', '99'))
        self.prologue()
        self.tile(-1, 'meta')
        for p in range(self.npre):
            self.tile(p, 'pre')
        if self.npre > 0:
            self.state_select()
        for m in range(self.nmain):
            self.tile(self.npre + m, 'main')
        self.tile(-1, 'sample')
        self.s.final_wait('sp')

    def prologue(self):
        s = self.s
        c = self.cst_d
        o = 0
        s.dma('pool', self.ident[:], c[:, o:o + 128], [], ['ident'], 'cst'); o += 128
        s.dma('pool', self.onesb[:], c[:, o:o + 128], [], ['onesb'], 'cst'); o += 128
        s.dma('pool', self.negU[:], c[:, o:o + 128], [], ['negU'], 'cst'); o += 128
        s.dma('pool', self.tri[:], c[:, o:o + 128], [], ['tri'], 'cst'); o += 128
        s.dma('pool', self.negL[:], c[:, o:o + 128], [], ['negL'], 'cst'); o += 128
        s.dma('pool', self.masks[:], c[:, o:o + 2048].rearrange("p (d n) -> p d n", d=4), [], ['masks'], 'cst'); o += 2048
        s.dma('pool', self.vecs[:], self.vecs_d, [], ['vecs'], 'cst')
        s.dma('pool', self.lb_t[:], self.lbrep_d, [], ['lb_t'], 'cst')
        allc = ['ident', 'onesb', 'negU', 'negL', 'tri', 'masks', 'vecs', 'lb_t']
        tot = s.dma_cnt['cst']
        for k in allc:
            s.last_w[k] = ('cst', tot)
        self.s.op('dve', lambda e: e.memset(self.hT[:, :, :], 0.0), [], [('hT', kc) for kc in range(8)])
        self.s.op('dve', lambda e: e.memset(self.KT[:, :, :], 0.0), [], [('KT', c) for c in range(4)])
        self.s.op('dve', lambda e: e.memset(self.xT[:, :, :], 0.0), [], [('xT', kc) for kc in range(8)])
        self.s.op('dve', lambda e: e.memset(self.SH[:, :], 0.0), [], [('SH', 'all')])
        self.s.op('dve', lambda e: e.memset(self.zerob[:, :], 0.0), [], ['zerob'])
        pieces = [('gu1', 0, 1408), ('d1', 0, 704), ('gu1', 1408, 1408), ('d1', 704, 704),
                  ('wfm', 0, 1024), ('wtm', 0, 1024), ('wfm', 1024, 2304),
                  ('gu2', 0, 1408), ('d2', 0, 704), ('gu2', 1408, 1408), ('d2', 704, 704)]
        for i, (n, r0, nr) in enumerate(pieces):
            dst = self.wbf[n].rearrange("(r c) -> r c", c=2048)[r0:r0 + nr, :]
            s.dma('pool', dst, self.w32[n][r0:r0 + nr, :], [], [('scr', n, r0)], 'cast%d' % i)
        vv = self.vecs
        self.tt('dve', vv[:, V_LBV:V_LBV + 4], vv[:, V_LB1:V_LB1 + 4], vv[:, V_LB0:V_LB0 + 4], ALU.subtract, ['vecs'], ['vecs'])
        self.act(vv[:, V_LBV:V_LBV + 4], vv[:, V_LBV:V_LBV + 4], AF.Sigmoid, ['vecs'], ['vecs'])
        self.tsc('dve', vv[:, V_OML:V_OML + 4], vv[:, V_LBV:V_LBV + 4], -1.0, 1.0, ALU.mult, ALU.add, ['vecs'], ['vecs'])
        lt = self.lb_t
        self.tt('dve', lt[:, 0, :], lt[:, 1, :], lt[:, 0, :], ALU.subtract, ['lb_t'], ['lb_t'])
        self.act(lt[:, 0, :], lt[:, 0, :], AF.Sigmoid, ['lb_t'], ['lb_t'])
        self.tsc('dve', lt[:, 1, :], lt[:, 0, :], -1.0, 1.0, ALU.mult, ALU.add, ['lb_t'], ['lb_t'])

    def scr_key(self, n, row2048):
        bounds = {'gu1': [0, 1408], 'gu2': [0, 1408], 'd1': [0, 704], 'd2': [0, 704], 'wfm': [0, 1024], 'wtm': [0]}[n]
        r0 = max(b for b in bounds if b <= row2048)
        return ('scr', n, r0)

    def load_gu(self, which, c):
        n = 'gu%d' % which
        ap = self.wbf[n].rearrange("(c p f) -> c p f", p=128, f=2048)[c]
        return self.ring_load(ap, 2048, self.scr_key(n, c * 128))

    def load_d(self, which, half, o):
        n = 'd%d' % which
        ap = self.wbf[n].rearrange("(h o p f) -> h o p f", h=2, o=8, p=128)[half, o]
        return self.ring_load(ap, 1408, self.scr_key(n, (half * 8 + o) * 128 * 1408 // 2048))

    def load_fm(self, c0, ncnk):
        slot = self.ring_i % 4
        self.ring_i += 1
        src = self.wbf['wfm'].rearrange("(c p f) -> c p f", p=128, f=1024)
        for i in range(ncnk):
            self.s.dma('sp', self.ring[:, slot, i * 1024:(i + 1) * 1024], src[c0 + i], [self.scr_key('wfm', c0 * 64)], [('ring', slot)], 'ring%d' % slot)
        return slot

    def load_tm(self, g):
        ap = self.wbf['wtm'].rearrange("(g p f) -> g p f", p=128, f=4096)[g]
        return self.ring_load(ap, 4096, ('scr', 'wtm', 0))

    def state_select(self):
        S0 = self.S[:, 0, :, :]
        self.tsc('dve', self.Ssave[:, :, :], self.Ssave[:, :, :], self.vecs[:, V_FA:V_FA + 1], None, ALU.mult, None, ['Ssave', 'vecs'], ['Ssave'])
        self.stt('dve', S0, S0, self.vecs[:, V_FB:V_FB + 1], self.Ssave[:, :, :], ALU.mult, ALU.add, [('S', 0), 'Ssave', 'vecs'], [('S', 0)])
        for hh in range(4):
            self.cp('pool', self.Sb[:, 0, hh, :], self.S[:, 0, hh, :], [('S', 0)], [('Sb', 0, hh)])

    def tile(self, f, mode='extra'):
        self.mode = mode
        self.m = f - self.npre if mode == 'main' else f
        if mode == 'meta':
            n = NMETA
            segs = [('meta', 0, NMETA)]
        elif mode == 'sample':
            n = TS
            segs = [('sample', 0, TS)]
        else:
            n = NT
            segs = [('frames', 0, NT)]
        self.n = n
        self.f = f
        self.segs = segs
        s = self.s
        ks = self.kstop if f < 0 else 99
        s.phase_switch()
        self.load_x()
        if mode in ('pre', 'meta'):
            self.norm(V_N1)
            self.ffn(1)
            self.norm(V_NM)
            s.phase_switch()
            self.w_in()
            self.hgrn()
            return
        if ks < 2: return
        self.norm(V_N1)
        if ks < 3: return
        self.ffn(1)
        if ks < 4: return
        self.norm(V_NM)
        s.phase_switch()
        self.w_in()
        if ks < 5 or ks == 41: return
        self.hgrn()
        if ks < 6: return
        s.phase_switch()
        self.attn()
        if ks < 7: return
        s.phase_switch()
        self.post()
        if ks < 8: return
        s.phase_switch()
        self.norm(V_N2)
        self.ffn(2)
        self.final()

    def load_x(self):
        n, f = self.n, self.f
        if f < 0:
            blocks = [(0, n)]
        else:
            blocks = [(j * 128, 128) for j in range(4)]
        for bi, (c0, nb) in enumerate(blocks):
            slot = self.xtok_i % 2
            self.xtok_i += 1
            xk = ('SH', 'x_tok', slot)
            if f < 0:
                self.s.dma('sp', self.x_tok[0:n, slot, :], self.xs if self.mode == 'sample' else self.meta, [], [xk], 'xin%d' % slot)
            else:
                r0 = self.m * NT + c0
                srcx = self.xpre if self.mode == 'pre' else self.xp
                self.s.dma('sp', self.x_tok[:, slot, :], srcx[r0:r0 + 128, :], [], [xk], 'xin%d' % slot)
            for kc in range(8):
                bk = self.bank[kc % 2]
                self.tr(bk[:, 0:nb], self.x_tok[0:nb, slot, kc * 128:(kc + 1) * 128], nb, [xk, 'ident'], [('ps', kc % 2)])
                eng = 'dve' if kc % 2 == 0 else 'act'
                self.cp(eng, self.xT[:, kc, c0:c0 + nb], bk[:, 0:nb], [('ps', kc % 2)], [('xT', kc)])

    def norm(self, gcol, inplace=False):
        n = self.n
        for kc in range(8):
            self.act(self.hT[:, kc, 0:n], self.xT[:, kc, 0:n], AF.Square, [('xT', kc)], [('hT', kc)])
        bk = self.bank[7]
        for kc in range(8):
            self.mm(bk[:, 0:n], self.onesb[:, :], self.hT[:, kc, 0:n], kc == 0, kc == 7, [('hT', kc), 'onesb'], [('ps', 7)])
        self.act(self.lnt[:, 0:n], bk[:, 0:n], AF.Ln, [('ps', 7)], ['lnt'], bias=EPS, scale=1.0 / D)
        self.act(self.rstd[:, 0:n], self.lnt[:, 0:n], AF.Exp, ['lnt'], ['rstd'], scale=-0.5)
        for kc in range(8):
            eng = 'dve'
            if inplace:
                self.stt(eng, self.xT[:, kc, 0:n], self.xT[:, kc, 0:n], self.vecs[:, gcol + kc:gcol + kc + 1], self.rstd[:, 0:n],
                         ALU.mult, ALU.mult, [('xT', kc), 'rstd', 'vecs'], [('xT', kc)])
            else:
                self.stt(eng, self.hT[:, kc, 0:n], self.xT[:, kc, 0:n], self.vecs[:, gcol + kc:gcol + kc + 1], self.rstd[:, 0:n],
                         ALU.mult, ALU.mult, [('xT', kc), 'rstd', 'vecs'], [('hT', kc)])

    def ffn(self, which):
        n = self.n
        hk = [('hT', kc) for kc in range(8)]
        it = 0
        gun, dnn = 'gu%d' % which, 'd%d' % which
        gsrc = self.wbf[gun].rearrange("(c p f) -> p c f", p=128, f=2048)
        dsrc = self.wbf[dnn].rearrange("(h o p f) -> h p o f", h=2, o=8, p=128)
        for half in range(2):
            for cc0 in range(0, 11, 2):
                ncnk = min(2, 11 - cc0)
                c = half * 11 + cc0
                slot = self.ring_i % 4
                self.ring_i += 1
                self.s.dma('sp', self.ring[:, slot, 0:ncnk * 2048].rearrange("p (c f) -> p c f", c=ncnk), gsrc[:, c:c + ncnk, :],
                           [self.scr_key(gun, c * 128)], [('ring', slot)], 'ring%d' % slot)
                for ci in range(ncnk):
                    cc = cc0 + ci
                    w = self.ring[:, slot, ci * 2048:(ci + 1) * 2048].rearrange("p (g k n) -> p g k n", g=2, k=8)
                    gb, ub = it % 2, 2 + it % 2
                    for kc in range(8):
                        self.mm(self.bank[gb][:, 0:n], w[:, 0, kc, :], self.hT[:, kc, 0:n], kc == 0, kc == 7, hk + [('ring', slot)], [('ps', gb)])
                    for kc in range(8):
                        self.mm(self.bank[ub][:, 0:n], w[:, 1, kc, :], self.hT[:, kc, 0:n], kc == 0, kc == 7, hk + [('ring', slot)], [('ps', ub)])
                    sgk = ('SH', 'sg', it % 2)
                    self.act(self.sg[:, it % 2, 0:n], self.bank[gb][:, 0:n], AF.Silu, [('ps', gb)], [sgk])
                    self.tt('dve', self.hid[:, cc, 0:n], self.sg[:, it % 2, 0:n], self.bank[ub][:, 0:n], ALU.mult, [sgk, ('ps', ub)], [('SH', 'hid', cc)])
                    it += 1
            for o0 in range(0, 8, 2):
                slot = self.ring_i % 4
                self.ring_i += 1
                self.s.dma('sp', self.ring[:, slot, 0:2 * 1408].rearrange("p (c f) -> p c f", c=2), dsrc[half, :, o0:o0 + 2, :],
                           [self.scr_key(dnn, (half * 8 + o0) * 128 * 1408 // 2048)], [('ring', slot)], 'ring%d' % slot)
                for oi in range(2):
                    o = o0 + oi
                    w = self.ring[:, slot, oi * 1408:(oi + 1) * 1408].rearrange("p (c n) -> p c n", c=11)
                    db = 4 + o % 2
                    for cc in range(11):
                        self.mm(self.bank[db][:, 0:n], w[:, cc, :], self.hid[:, cc, 0:n], cc == 0, cc == 10, [('SH', 'hid', cc), ('ring', slot)], [('ps', db)])
                    self.stt('dve', self.xT[:, o, 0:n], self.bank[db][:, 0:n], 0.5, self.xT[:, o, 0:n], ALU.mult, ALU.add, [('ps', db), ('xT', o)], [('xT', o)])

    def kcol(self, seg):
        if seg == 'sample':
            return NMETA + LC
        if seg == 'meta':
            return 0
        return NMETA + self.f * NT

    def w_in(self):
        n = self.n
        hk = [('hT', kc) for kc in range(8)]
        bi = 0
        for g4 in ([1] if self.mode in ('pre', 'meta') else range(4)):
            slot = self.load_fm(g4 * 4, 4)
            w = self.ring[:, slot, 0:4096].rearrange("p (c k n) -> p c k n", c=4, k=8)
            for ci in range(4):
                b = bi % 4
                bi += 1
                bk = self.bank[b]
                for kc in range(8):
                    self.mm(bk[:, 0:n], w[:, ci, kc, :], self.hT[:, kc, 0:n], kc == 0, kc == 7, hk + [('ring', slot)], [('ps', b)])
                if g4 == 0:
                    self.act(self.qT[:, ci, 0:n], bk[:, 0:n], AF.Copy, [('ps', b)], [('qT', ci)], scale=0.125)
                elif g4 == 1:
                    for (sname, c0, ns) in self.segs:
                        kc0 = self.kcol(sname)
                        self.cp('dve', self.KT[:, ci, kc0:kc0 + ns], bk[:, c0:c0 + ns], [('ps', b)], [('KT', ci)])
                elif g4 == 2:
                    self.act(self.lnt[:, 0:n], bk[:, 0:n], AF.Sigmoid, [('ps', b)], ['lnt'], scale=-1.0)
                    self.tsc('dve', self.omfT[:, ci, 0:n], self.lnt[:, 0:n], self.vecs[:, V_OML + ci:V_OML + ci + 1], None, ALU.mult, None,
                             ['lnt', 'vecs'], [('SH', 'omfT', ci)])
                else:
                    self.cp('act', self.qhT[:, ci, 0:n], bk[:, 0:n], [('ps', b)], [('SH', 'qhT', ci)])
        if self.kstop == 41:
            return
        if self.mode == 'sample':
            blocks = [('sample', 0, TS, self.sk, self.sv, 0, 18)]
        elif self.mode == 'meta':
            blocks = [('meta', 0, NMETA, self.pk, self.pv, 0, 0)]
        else:
            blocks = [('frames', j * 128, 128, self.pk, self.pv, NMETA + self.m * NT + j * 128, 1 + 4 * self.f + j) for j in range(4)]
        for g in ([1] if self.mode == 'pre' else range(2)):
            slot = self.load_tm(g)
            w = self.ring[:, slot, 0:4096].rearrange("p (k n) -> p k n", k=8)
            for (sname, c0, nb, okd, ovd, r0, vblk) in blocks:
                b = 4 + bi % 4
                bi += 1
                bk = self.bank[b]
                for kc in range(8):
                    self.mm(bk[:, :], self.hT[:, kc, c0:c0 + 128], w[:, kc, :], kc == 0, kc == 7, hk + [('ring', slot)], [('ps', b)])
                if self.mode == 'pre':
                    self.cp('act', self.Vp[0:nb, vblk, :], bk[0:nb, :], [('ps', b)], [('Vp', vblk)])
                    continue
                ks = self.kv_i % 2
                self.kv_i += 1
                self.cp('dve', self.kvst[0:nb, ks, :], bk[0:nb, :], [('ps', b)], [('kvst', ks)])
                dst = (okd if g == 0 else ovd)[r0:r0 + nb, :]
                import os
                if not os.environ.get('NOKV'):
                    self.s.dma(os.environ.get('KVQ', 'pool'), dst, self.kvst[0:nb, ks, :], [('kvst', ks)], [], 'kvo%d' % ks)
                if g == 1:
                    self.cp('act', self.Vp[0:nb, vblk, :], bk[0:nb, :], [('ps', b)], [('Vp', vblk)])

    def hgrn(self):
        s = self.s
        hk = [('hT', kc) for kc in range(8)]
        slz = self.load_tm(2)
        sli = self.load_tm(3)
        wz = self.ring[:, slz, 0:4096].rearrange("p (k n) -> p k n", k=8)
        wi = self.ring[:, sli, 0:4096].rearrange("p (k n) -> p k n", k=8)
        if self.mode == 'sample':
            chunks = [(0, TS, 1)]
            self.s.dma('pool', self.S[:, 1, :, :], self.st.rearrange("h k v -> k h v"), [], [('S', 1)], 'stin')
            for hh in range(4):
                self.cp('pool', self.Sb[:, 1, hh, :], self.S[:, 1, hh, :], [('S', 1)], [('Sb', 1, hh)])
        elif self.mode == 'meta':
            chunks = [(0, NMETA, 0)]
            self.s.op('dve', lambda e: e.memset(self.S[:, 0, :, :], 0.0), [], [('S', 0)])
            self.s.op('dve', lambda e: e.memset(self.Sb[:, 0, :, :], 0.0), [], [('Sb', 0, hh) for hh in range(4)])
        else:
            chunks = [(i * 64, 64, 0) for i in range(8)]
        for ci, (c0, T, si) in enumerate(chunks):
            par = ci % 2
            bA, bB, bC, bD, bE, bF = self.bank[0], self.bank[1], self.bank[2 + par], self.bank[4], self.bank[5], self.bank[6 + par]
            kC, kF = ('ps', 2 + par), ('ps', 6 + par)
            for kc in range(8):
                self.mm(bA[:, :], self.hT[:, kc, c0:c0 + 128], wz[:, kc, :], kc == 0, kc == 7, hk + [('ring', slz)], [('ps', 0)])
            for kc in range(8):
                self.mm(bB[:, :], self.hT[:, kc, c0:c0 + 128], wi[:, kc, :], kc == 0, kc == 7, hk + [('ring', sli)], [('ps', 1)])
            k_omf, k_logf, k_iv = ('SH', 'omf_tm', par), ('SH', 'logf', par), ('SH', 'iv', par)
            self.act(self.omf_tm[0:T, par, :], bA[0:T, :], AF.Sigmoid, [('ps', 0)], [k_omf], scale=-1.0)
            self.tt('dve', self.omf_tm[0:T, par, :], self.omf_tm[0:T, par, :], self.lb_t[0:T, 1, :], ALU.mult, [k_omf, 'lb_t'], [k_omf])
            self.act(self.logf[0:T, par, :], self.omf_tm[0:T, par, :], AF.Ln, [k_omf], [k_logf], bias=1.0, scale=-1.0)
            self.cp('act', self.iv[0:T, par, :], bB[0:T, :], [('ps', 1)], [k_iv])
            cheap = self.mode in ('pre', 'meta')
            if not cheap:
                for hh in range(4):
                    self.mm(bC[:, hh * 64:hh * 64 + T], self.logf[0:T, par, hh * 128:(hh + 1) * 128], self.tri[0:T, 0:T], True, True,
                            [k_logf, 'tri'], [kC])
            self.mm(bD[:, :], self.tri[0:T, 0:128], self.logf[0:T, par, :], True, True, [k_logf, 'tri'], [('ps', 4)])
            k_enbt, k_ket = ('SH', 'enb_tm', par), ('SH', 'ke_tm', par)
            self.act(self.enb_tm[0:T, par, :], bD[0:T, :], AF.Exp, [('ps', 4)], [k_enbt], scale=-1.0)
            self.tt('dve', self.ke_tm[0:T, par, :], self.omf_tm[0:T, par, :], self.enb_tm[0:T, par, :], ALU.mult, [k_omf, k_enbt], [k_ket])
            if cheap:
                for hh in range(4):
                    self.mm(bC[:, hh * 64:hh * 64 + 2], self.logf[0:T, par, hh * 128:(hh + 1) * 128], self.tri[0:T, 126:128], True, True,
                            [k_logf, 'tri'], [kC])
                for hh in range(4):
                    self.act(self.eb[:, par, hh, T - 1:T], bC[:, hh * 64:hh * 64 + 1], AF.Exp, [kC], [('SH', 'eb', par, hh)])
            else:
                for hh in range(4):
                    k_eb, k_enb, k_qe, k_ke = ('SH', 'eb', par, hh), ('SH', 'enb', par, hh), ('SH', 'qe', par, hh), ('SH', 'ke', par, hh)
                    self.act(self.eb[:, par, hh, 0:T], bC[:, hh * 64:hh * 64 + T], AF.Exp, [kC], [k_eb])
                    self.act(self.enb[:, par, hh, 0:T], bC[:, hh * 64:hh * 64 + T], AF.Exp, [kC], [k_enb], scale=-1.0)
                    self.tt('dve', self.qe[:, par, hh, 0:T], self.qhT[:, hh, c0:c0 + T], self.eb[:, par, hh, 0:T], ALU.mult,
                            [('SH', 'qhT', hh), k_eb], [k_qe])
                    self.tt('pool', self.ke[:, par, hh, 0:T], self.omfT[:, hh, c0:c0 + T], self.enb[:, par, hh, 0:T], ALU.mult,
                            [('SH', 'omfT', hh), k_enb], [k_ke])
                for hh in range(4):
                    k_qe, k_ke = ('SH', 'qe', par, hh), ('SH', 'ke', par, hh)
                    self.mm(bC[:, 256 + hh * 64:256 + hh * 64 + T], self.ke[:, par, hh, 0:128], self.qe[:, par, hh, 0:T], True, True,
                            [k_qe, k_ke], [kC])
                for hh in range(4):
                    k_sc = ('SH', 'sc', par, hh)
                    self.tt('dve', self.sc[0:T, par, hh, 0:T], bC[0:T, 256 + hh * 64:256 + hh * 64 + T], self.tri[0:T, 0:T], ALU.mult,
                            [kC, 'tri'], [k_sc])
                for hh in range(4):
                    k_sc, k_qe = ('SH', 'sc', par, hh), ('SH', 'qe', par, hh)
                    self.mm(bE[:, hh * 64:hh * 64 + T], self.iv[0:T, par, hh * 128:(hh + 1) * 128], self.sc[0:T, par, hh, 0:T], True, False,
                            [k_iv, k_sc], [('ps', 5)])
                    self.mm(bE[:, hh * 64:hh * 64 + T], self.Sb[:, si, hh, :], self.qe[:, par, hh, 0:T], False, True,
                            [('Sb', si, hh), k_qe], [('ps', 5)])
                for hh in range(4):
                    self.cp('act' if hh % 2 == 0 else 'dve', self.obT[:, hh, c0:c0 + T], bE[:, hh * 64:hh * 64 + T], [('ps', 5)], [('obT', hh)])
            for hh in range(4):
                self.mm(bF[:, hh * 128:(hh + 1) * 128], self.ke_tm[0:T, par, hh * 128:(hh + 1) * 128], self.iv[0:T, par, hh * 128:(hh + 1) * 128],
                        True, True, [k_ket, k_iv], [kF])
            for hh in range(4):
                k_eb = ('SH', 'eb', par, hh)
                ebl = self.eb[:, par, hh, T - 1:T]
                self.tsc('dve', self.tmpS[:, hh, :], self.S[:, si, hh, :], ebl, None, ALU.mult, None, [('S', si), k_eb], [('SH', 'tmpS', hh)])
                self.stt('dve', self.S[:, si, hh, :], bF[:, hh * 128:(hh + 1) * 128], ebl, self.tmpS[:, hh, :], ALU.mult, ALU.add,
                         [kF, k_eb, ('SH', 'tmpS', hh)], [('S', si)])
                self.cp('pool', self.Sb[:, si, hh, :], self.S[:, si, hh, :], [('S', si)], [('Sb', si, hh)])
            if self.f < 0 and si == 1:
                self.s.dma('pool', self.sh.rearrange("h k v -> k h v"), self.S[:, 1, :, :], [('S', 1)], [], 'sout')
            if self.f < 0 and si == 0 and self.npre > 0:
                self.cp('pool', self.Ssave[:, :, :], self.S[:, 0, :, :], [('S', 0)], ['Ssave'])
            if self.mode == 'main' and self.f == self.nft - 1 and ci == len(chunks) - 1:
                self.s.dma('pool', self.ph.rearrange("h k v -> k h v"), self.S[:, 0, :, :], [('S', 0)], [], 'sout')

    def attn(self):
        f = self.f
        if f < 0:
            nb_c = (LC + 127) // 128
            for b in range(nb_c):
                r0 = b * 128
                kn = min(128, LC - r0)
                slot = b % 2
                ck_key = ('SH', 'ckv', slot)
                self.s.dma('pool', self.ckv[0:kn, slot, 0:512], self.ck[r0:r0 + kn, :], [], [ck_key], 'ckin%d' % slot)
                self.s.dma('pool', self.ckv[0:kn, slot, 512:1024], self.cv[r0:r0 + kn, :], [], [ck_key], 'ckin%d' % slot)
                bk = self.bank[b % 2]
                for ch in range(4):
                    self.tr(bk[:, ch * 128:ch * 128 + kn], self.ckv[0:kn, slot, ch * 128:(ch + 1) * 128], kn, [ck_key, 'ident'], [('ps', b % 2)])
                for ch in range(4):
                    self.cp('dve' if ch % 2 == 0 else 'act', self.KT[:, ch, NMETA + r0:NMETA + r0 + kn], bk[:, ch * 128:ch * 128 + kn],
                            [('ps', b % 2)], [('KT', ch)])
                self.cp('pool', self.Vp[0:kn, 1 + b, :], self.ckv[0:kn, slot, 512:1024], [ck_key], [('Vp', 1 + b)])
            blocks = [(18, TS, NMETA + LC, 0)]
            blocks.append((1 + nb_c - 1, LC - 128 * (nb_c - 1), NMETA + 128 * (nb_c - 1), None))
            for b in range(nb_c - 2, -1, -1):
                blocks.append((1 + b, 128, NMETA + 128 * b, None))
            self.attn_job(0, TS, blocks)
        else:
            blocks = []
            for m in range(3, -1, -1):
                fb = 4 * f + m
                blocks.append((1 + fb, 128, NMETA + 128 * fb, 128 * m))
            for fb in range(4 * f - 1, -1, -1):
                blocks.append((1 + fb, 128, NMETA + 128 * fb, 'flag' if fb < 4 * self.npre else None))
            blocks.append((0, NMETA, 0, None))
            self.attn_job(0, NT, blocks)

    def attn_job(self, q0, nq, blocks):
        nblk = len(blocks)
        for grp in ([0, 1, 2], [3, 4, 5], [6, 7]):
            S_ = len(grp)
            nseq = nblk * S_

            def emitZ(idx):
                k, si = divmod(idx, S_)
                h = grp[si]
                vblk, kn, kcol, md = blocks[k]
                ch, pb = h // 2, 64 * (h % 2)
                zb = idx % 2
                self.mm(self.bank[zb][:, 0:nq], self.KT[pb:pb + 64, ch, kcol:kcol + 128], self.qT[pb:pb + 64, ch, q0:q0 + nq], True, True,
                        [('KT', ch), ('qT', ch)], [('ps', zb)])
            for idx in range(min(2, nseq)):
                emitZ(idx)
            for k in range(nblk):
                vblk, kn, kcol, md = blocks[k]
                par = k % 2
                for si, h in enumerate(grp):
                    idx = k * S_ + si
                    zb = idx % 2
                    ke_, ks_ = ('SH', 'e', si, par), ('SH', 'sp', si, par)
                    if md == 'flag':
                        self.act(self.e_[0:kn, si * 2 + par, 0:nq], self.bank[zb][0:kn, 0:nq], AF.Exp, [('ps', zb), 'vecs'], [ke_],
                                 bias=self.vecs[0:kn, V_DEAD:V_DEAD + 1])
                    else:
                        self.act(self.e_[0:kn, si * 2 + par, 0:nq], self.bank[zb][0:kn, 0:nq], AF.Exp, [('ps', zb)], [ke_])
                    if idx + 2 < nseq:
                        emitZ(idx + 2)
                    if md is not None and md != 'flag':
                        self.tt('pool', self.e_[0:kn, si * 2 + par, 0:nq], self.e_[0:kn, si * 2 + par, 0:nq], self.masks[0:kn, md // 128, 0:nq],
                                ALU.mult, [ke_, 'masks'], [ke_])
                    self.act(self.sp_[0:kn, si * 2 + par, 0:nq], self.e_[0:kn, si * 2 + par, 0:nq], AF.Ln, [ke_], [ks_], bias=1.0)
                for si, h in enumerate(grp):
                    ks_ = ('SH', 'sp', si, par)
                    self.mm(self.bank[2 + si][:, 0:nq], self.negU[0:kn, :], self.sp_[0:kn, si * 2 + par, 0:nq], k == 0, False,
                            [ks_, 'negU'], [('ps', 2 + si)], sgc=True)
                for si, h in enumerate(grp):
                    ke_, k2_, ka_ = ('SH', 'e', si, par), ('SH', 'e2', si), ('SH', 'at', si, par)
                    self.act(self.e2_[0:kn, si, 0:nq], self.bank[2 + si][0:kn, 0:nq], AF.Exp, [('ps', 2 + si)], [k2_])
                    self.tt('dve', self.at_[0:kn, si * 2 + par, 0:nq], self.e2_[0:kn, si, 0:nq], self.e_[0:kn, si * 2 + par, 0:nq], ALU.mult,
                            [k2_, ke_], [ka_])
                for si, h in enumerate(grp):
                    ks_, ka_ = ('SH', 'sp', si, par), ('SH', 'at', si, par)
                    self.mm(self.bank[2 + si][:, 0:nq], self.negL[0:kn, :], self.sp_[0:kn, si * 2 + par, 0:nq], False, k == nblk - 1,
                            [ks_, 'negL'], [('ps', 2 + si)], sgc=True)
                    vlo = h * 64 if h % 2 == 0 else (h - 1) * 64
                    self.mm(self.bank[5 + si][:, 0:nq], self.Vp[0:kn, vblk, vlo:vlo + 128], self.at_[0:kn, si * 2 + par, 0:nq], k == 0, k == nblk - 1,
                            [ka_, ('Vp', vblk)], [('ps', 5 + si)])
            for si, h in enumerate(grp):
                ch, pb = h // 2, 64 * (h % 2)
                self.cp('dve' if si % 2 == 0 else 'act', self.oaT[pb:pb + 64, ch, q0:q0 + nq], self.bank[5 + si][pb:pb + 64, 0:nq],
                        [('ps', 5 + si)], [('oaT', ch, pb)])

    def post(self):
        n = self.n
        hk = [('hT', kc) for kc in range(8)]
        bi = 0
        for ld in range(5):
            slot = self.load_fm(16 + ld * 4, 4)
            w = self.ring[:, slot, 0:4096].rearrange("p (c k n) -> p c k n", c=4, k=8)
            for ci in range(4):
                cidx = ld * 4 + ci
                b = bi % 4
                bi += 1
                bk = self.bank[b]
                for kc in range(8):
                    self.mm(bk[:, 0:n], w[:, ci, kc, :], self.hT[:, kc, 0:n], kc == 0, kc == 7, hk + [('ring', slot)], [('ps', b)])
                if cidx < 4:
                    self.act(self.sgT[:, cidx, 0:n], bk[:, 0:n], AF.Silu, [('ps', b)], [('SH', 'sgT', cidx)])
                elif cidx < 12:
                    o = cidx - 4
                    self.act(self.gA[:, o, 0:n], bk[:, 0:n], AF.Sigmoid, [('ps', b), 'vecs'], [('SH', 'gA', o)], bias=self.vecs[:, V_BGA + o:V_BGA + o + 1])
                else:
                    o = cidx - 12
                    self.act(self.gB[:, o, 0:n], bk[:, 0:n], AF.Sigmoid, [('ps', b), 'vecs'], [('SH', 'gB', o)], bias=self.vecs[:, V_BGB + o:V_BGB + o + 1])
        for hh in range(4):
            kq = ('SH', 'mT', hh)
            self.act(self.mT[:, hh, 0:n], self.obT[:, hh, 0:n], AF.Square, [('obT', hh)], [kq])
            self.mm(self.bank[4][:, 0:n], self.onesb[:, :], self.mT[:, hh, 0:n], True, True, [kq, 'onesb'], [('ps', 4)])
            self.act(self.lnt[:, 0:n], self.bank[4][:, 0:n], AF.Ln, [('ps', 4)], ['lnt'], bias=EPS, scale=1.0 / 128)
            self.act(self.rstd[:, 0:n], self.lnt[:, 0:n], AF.Exp, ['lnt'], ['rstd'], scale=-0.5)
            k1 = ('SH', 't1', 0)
            self.stt('dve', self.t1[:, 0, 0:n], self.obT[:, hh, 0:n], self.vecs[:, V_HGN + hh:V_HGN + hh + 1], self.rstd[:, 0:n], ALU.mult, ALU.mult,
                     [('obT', hh), 'vecs', 'rstd'], [k1])
            self.tt('pool', self.obn[:, hh, 0:n], self.t1[:, 0, 0:n], self.sgT[:, hh, 0:n], ALU.mult, [k1, ('SH', 'sgT', hh)], [('SH', 'obn', hh)])
        oak = [('oaT', c, pb) for c in range(4) for pb in (0, 64)]
        obk = [('SH', 'obn', c) for c in range(4)]
        for ld in range(2):
            slot = self.load_fm(36 + ld * 4, 4)
            w = self.ring[:, slot, 0:4096].rearrange("p (c k n) -> p c k n", c=4, k=8)
            for ci in range(4):
                o = ld * 4 + ci
                ba, bb = o % 2, 2 + o % 2
                for c in range(4):
                    self.mm(self.bank[ba][:, 0:n], w[:, ci, c, :], self.oaT[:, c, 0:n], c == 0, c == 3, oak + [('ring', slot)], [('ps', ba)])
                for c in range(4):
                    self.mm(self.bank[bb][:, 0:n], w[:, ci, 4 + c, :], self.obn[:, c, 0:n], c == 0, c == 3, obk + [('ring', slot)], [('ps', bb)])
                k1, k2 = ('SH', 't1', 0), ('SH', 't2', 0)
                self.tt('dve', self.t1[:, 0, 0:n], self.gA[:, o, 0:n], self.bank[ba][:, 0:n], ALU.mult, [('SH', 'gA', o), ('ps', ba)], [k1])
                self.tt('dve', self.t2[:, 0, 0:n], self.gB[:, o, 0:n], self.bank[bb][:, 0:n], ALU.mult, [('SH', 'gB', o), ('ps', bb)], [k2])
                self.tt('pool', self.mT[:, o, 0:n], self.t1[:, 0, 0:n], self.t2[:, 0, 0:n], ALU.add, [k1, k2], [('SH', 'mT', o)])
        mk = [('SH', 'mT', c) for c in range(8)]
        for ld in range(2):
            slot = self.load_fm(44 + ld * 4, 4)
            w = self.ring[:, slot, 0:4096].rearrange("p (c k n) -> p c k n", c=4, k=8)
            for ci in range(4):
                o = ld * 4 + ci
                b = 4 + o % 2
                for c in range(8):
                    self.mm(self.bank[b][:, 0:n], w[:, ci, c, :], self.mT[:, c, 0:n], c == 0, c == 7, mk + [('ring', slot)], [('ps', b)])
                self.tt('dve', self.xT[:, o, 0:n], self.xT[:, o, 0:n], self.bank[b][:, 0:n], ALU.add, [('xT', o), ('ps', b)], [('xT', o)])

    def final(self):
        n, f = self.n, self.f
        self.norm(V_NF, inplace=True)
        if f < 0:
            blocks = [(0, n, self.ys, 0, TS)]
            assert self.mode == 'sample'
        else:
            blocks = [(j * 128, 128, self.yp, self.m * NT + j * 128, 128) for j in range(4)]
        for (c0, nb, dst, r0, nout) in blocks:
            slot = self.xtok_i % 2
            self.xtok_i += 1
            xk = ('SH', 'x_tok', slot)
            for half in range(2):
                bk = self.bank[half]
                for q in range(4):
                    kc = half * 4 + q
                    self.tr(bk[:, q * 128:(q + 1) * 128], self.xT[:, kc, c0:c0 + 128], 128, [('xT', kc), 'ident'], [('ps', half)])
                self.cp('dve' if half == 0 else 'act', self.x_tok[0:nb, slot, half * 512:(half + 1) * 512], bk[0:nb, :], [('ps', half)], [xk])
            self.s.dma('pool', dst[r0:r0 + nout, :], self.x_tok[0:nout, slot, :], [xk], [], 'yout%d' % slot)


def _fix_kt_keys(b):
    pass


def _layouts(inp):
    f32 = np.float32

    def fm_vec(v):
        v = np.asarray(v, f32).reshape(-1, 128)
        return v.T

    def gu(wg, wu):
        a = np.asarray(wg, f32).reshape(8, 128, 22, 128).transpose(2, 1, 0, 3)
        b = np.asarray(wu, f32).reshape(8, 128, 22, 128).transpose(2, 1, 0, 3)
        return np.ascontiguousarray(np.stack([a, b], axis=2)).reshape(-1, 2048)

    def dn(wd):
        a = np.asarray(wd, f32).reshape(2, 11, 128, 8, 128).transpose(0, 3, 2, 1, 4)
        return np.ascontiguousarray(a).reshape(-1, 2048)
    w_in = np.asarray(inp['w_in'][0], f32)
    cols = np.concatenate([np.arange(0, 512), np.arange(512, 1024), np.arange(1536, 2048), np.arange(2560, 3072),
                           np.arange(3072, 3584), np.arange(3584, 4608), np.arange(4608, 5632)])
    fm = w_in[:, cols].reshape(8, 128, 36, 128).transpose(2, 1, 0, 3)
    wa = np.asarray(inp['w_branch_a'][0], f32).reshape(4, 128, 8, 128).transpose(2, 1, 0, 3)
    wb = np.asarray(inp['w_branch_b'][0], f32).reshape(4, 128, 8, 128).transpose(2, 1, 0, 3)
    wab = np.concatenate([wa, wb], axis=2)
    wo = np.asarray(inp['w_out'][0], f32).reshape(8, 128, 8, 128).transpose(2, 1, 0, 3)
    wfm = np.ascontiguousarray(np.concatenate([fm, wab, wo], axis=0)).reshape(-1, 2048)
    wtm = np.ascontiguousarray(w_in[:, 512:2560].reshape(8, 128, 4, 512).transpose(2, 1, 0, 3)).reshape(-1, 2048)
    vecs = np.zeros((128, NVEC), f32)
    vecs[:, V_N1:V_N1 + 8] = fm_vec(inp['ffn1_norm'][0])
    vecs[:, V_NM:V_NM + 8] = fm_vec(inp['mix_norm'][0])
    vecs[:, V_N2:V_N2 + 8] = fm_vec(inp['ffn2_norm'][0])
    vecs[:, V_NF:V_NF + 8] = fm_vec(inp['final_norm'])
    vecs[:, V_BGA:V_BGA + 16] = fm_vec(inp['b_gate'][0])
    vecs[:, V_HGN:V_HGN + 4] = fm_vec(inp['hg_out_norm'][0])
    vecs[:, V_LB0:V_LB0 + 4] = fm_vec(inp['hg_lb_logits'][0])
    vecs[:, V_LB1:V_LB1 + 4] = fm_vec(inp['hg_lb_logits'][1])
    lbrep = np.ascontiguousarray(np.broadcast_to(np.asarray(inp['hg_lb_logits'], f32)[None], (128, 2, 512)))
    p = np.arange(128)[:, None]
    j = np.arange(128)[None, :]
    ident = (p == j).astype(f32)
    ones = np.ones((128, 128), f32)
    negU = -(p >= j).astype(f32)
    negL = -(p < j).astype(f32)
    tri = (p <= j).astype(f32)
    cc = np.arange(512)[None, :]
    masks = np.concatenate([((p + d) < cc).astype(f32) for d in (0, 128, 256, 384)], axis=1)
    cst = np.ascontiguousarray(np.concatenate([ident, ones, negU, tri, negL, masks], axis=1))
    return dict(gu1=gu(inp['ffn1_w_gate'][0], inp['ffn1_w_up'][0]), d1=dn(inp['ffn1_w_down'][0]),
                gu2=gu(inp['ffn2_w_gate'][0], inp['ffn2_w_up'][0]), d2=dn(inp['ffn2_w_down'][0]),
                wfm=wfm, wtm=wtm, vecs=vecs, lbrep=lbrep, cst=cst)


_NC_CACHE = {}


def run(inp, nft):
    f32 = np.float32
    shared = _layouts(inp)
    if nft % 2 == 0:
        npre = nmain = nft // 2
    else:
        npre, nmain = 0, nft
    key = (npre, nmain)
    if key not in _NC_CACHE:
        _NC_CACHE[key] = Builder(npre, nmain).build()
    nc = _NC_CACHE[key]
    xp = np.asarray(inp['x_prompt'], f32)
    xs = np.asarray(inp['x_sample'], f32)
    ck = np.asarray(inp['cache_sb_k'], f32)
    cv = np.asarray(inp['cache_sb_v'], f32)
    st = np.asarray(inp['state_hgrn'], f32)
    meta = np.ascontiguousarray(np.asarray(inp['meta_tokens'], f32))
    B = xp.shape[0]
    H = nmain * NT
    in_maps = []
    for c in range(8):
        m = dict(shared)
        b, half = c // 2, c % 2
        vecs = shared['vecs'].copy()
        if npre == 0:
            m['xp'] = np.ascontiguousarray(xp[b])
            m['xpre'] = np.zeros((NT, D), f32)
            m['flag'] = np.ones((128, 512), f32)
            vecs[:, V_FA], vecs[:, V_FB], vecs[:, V_DEAD] = 0.0, 1.0, 0.0
        elif half == 0:
            m['xp'] = np.ascontiguousarray(xp[b, 0:H])
            m['xpre'] = np.zeros((npre * NT, D), f32)
            m['flag'] = np.zeros((128, 512), f32)
            vecs[:, V_FA], vecs[:, V_FB], vecs[:, V_DEAD] = 1.0, 0.0, -30000.0
        else:
            m['xp'] = np.ascontiguousarray(xp[b, H:2 * H])
            m['xpre'] = np.ascontiguousarray(xp[b, 0:H])
            m['flag'] = np.ones((128, 512), f32)
            vecs[:, V_FA], vecs[:, V_FB], vecs[:, V_DEAD] = 0.0, 1.0, 0.0
        m['vecs'] = vecs
        m['xs'] = np.ascontiguousarray(xs[c])
        m['meta'] = meta
        m['ck'] = np.ascontiguousarray(ck[0, c].reshape(LC, 512))
        m['cv'] = np.ascontiguousarray(cv[0, c].reshape(LC, 512))
        m['st'] = np.ascontiguousarray(st[0, c])
        in_maps.append(m)
    res = run_bass_kernel_spmd(nc, in_maps, core_ids=list(range(8)))
    r = res.results
    if npre == 0:
        y_prompt = np.stack([r[2 * b]['yp'] for b in range(B)])
        pk = np.stack([r[2 * b]['pk'] for b in range(B)])
        pv = np.stack([r[2 * b]['pv'] for b in range(B)])
        ph = np.stack([r[2 * b]['ph'] for b in range(B)])
    else:
        y_prompt = np.stack([np.concatenate([r[2 * b]['yp'], r[2 * b + 1]['yp']], axis=0) for b in range(B)])
        pk = np.stack([np.concatenate([r[2 * b]['pk'], r[2 * b + 1]['pk'][NMETA:]], axis=0) for b in range(B)])
        pv = np.stack([np.concatenate([r[2 * b]['pv'], r[2 * b + 1]['pv'][NMETA:]], axis=0) for b in range(B)])
        ph = np.stack([r[2 * b + 1]['ph'] for b in range(B)])
    y_prompt = y_prompt.astype(f32)
    L = pk.shape[1]
    pk = pk.reshape(B, L, 8, 64)[None].astype(f32)
    pv = pv.reshape(B, L, 8, 64)[None].astype(f32)
    ph = ph[None].astype(f32)
    y_sample = np.stack([r[c]['ys'] for c in range(8)]).astype(f32)
    sk = np.stack([r[c]['sk'].reshape(TS, 8, 64) for c in range(8)])[None].astype(f32)
    sv = np.stack([r[c]['sv'].reshape(TS, 8, 64) for c in range(8)])[None].astype(f32)
    sh = np.stack([r[c]['sh'] for c in range(8)])[None].astype(f32)
    return (y_prompt, y_sample, pk, pv, ph, sk, sv, sh)


def kernel(**inputs):
    nft = np.asarray(inputs['x_prompt']).shape[1] // NT
    return run(inputs, nft)
```

```python
import numpy as np
from contextlib import ExitStack
import concourse.bass as bass
import concourse.mybir as mybir
from concourse.bass_utils import run_bass_kernel_spmd

F32 = mybir.dt.float32
BF16 = mybir.dt.bfloat16
AF = mybir.ActivationFunctionType
ALU = mybir.AluOpType

D = 1024
DFF = 2816
NMETA = 16
TS = 32
LC = 2064
EPS = 1e-6
NT = 512
ENGS = ['pe', 'act', 'dve', 'pool', 'sp']

V_N1, V_NM, V_N2, V_NF, V_BGA, V_BGB, V_HGN, V_LB0, V_LB1, V_LBV, V_OML, V_FA, V_FB, V_DEAD = 0, 8, 16, 24, 32, 40, 48, 52, 56, 60, 64, 68, 69, 70
NVEC = 71


class Sched:
    def __init__(self):
        self.prog = {e: [] for e in ENGS}
        self.cnt = {e: 0 for e in ENGS}
        self.dma_cnt = {}
        self.last_w = {}
        self.readers = {}
        self.known = {e: {} for e in ENGS}
        self.fence = {}
        self.touched = set()
        self.nwaits = 0
        import os
        self.glimit = int(os.environ.get('KOPS', '100000000'))

    def _deps(self, reads, writes):
        need = {}

        def add(src, val):
            if need.get(src, 0) < val:
                need[src] = val
        for k in list(reads) + list(writes):
            if isinstance(k, tuple) and k[0] == 'SH' and k not in self.touched:
                self.touched.add(k)
                for s, v in self.fence.items():
                    add(s, v)
        for k in reads:
            lw = self.last_w.get(k)
            if lw:
                add(*lw)
        for k in writes:
            lw = self.last_w.get(k)
            if lw:
                add(*lw)
            for s, v in self.readers.get(k, {}).items():
                add(s, v)
        return need

    def _waits(self, eng, need):
        waits = []
        for src, val in need.items():
            if src == eng and eng == 'pe':
                continue
            if self.known[eng].get(src, 0) >= val:
                continue
            self.known[eng][src] = val
            waits.append((src, val))
        self.nwaits += len(waits)
        return waits

    def _mark(self, src, val, reads, writes):
        for k in writes:
            self.last_w[k] = (src, val)
            self.readers[k] = {}
        for k in reads:
            if k in writes:
                continue
            self.readers.setdefault(k, {})[src] = val

    def op(self, eng, fn, reads=(), writes=()):
        psr = [k for k in reads if isinstance(k, tuple) and k[0] == 'ps']
        if psr:
            reads = [k for k in reads if k not in psr]
            writes = list(writes) + [k for k in psr if k not in writes]
        self.gcount = getattr(self, 'gcount', 0) + 1
        if self.gcount > self.glimit:
            return
        import sys as _sys, os as _os
        if _os.environ.get('KTRACE'):
            lo, hi = [int(x) for x in _os.environ['KTRACE'].split(',')]
            if lo <= self.gcount <= hi:
                fr = _sys._getframe(2)
                print('OP', self.gcount, eng, 'line', fr.f_lineno, 'from', fr.f_back.f_lineno, 'reads', list(reads)[:3], 'writes', list(writes))
        need = self._deps(reads, writes)
        waits = self._waits(eng, need)
        self.cnt[eng] += 1
        self._mark(eng, self.cnt[eng], reads, writes)
        self.prog[eng].append(('op', waits, fn))

    def dma(self, q, out, in_, reads, writes, sem):
        self.gcount = getattr(self, 'gcount', 0) + 1
        if self.gcount > self.glimit:
            return
        import sys as _sys, os as _os
        if _os.environ.get('KTRACE'):
            lo, hi = [int(x) for x in _os.environ['KTRACE'].split(',')]
            if lo <= self.gcount <= hi:
                fr = _sys._getframe(1)
                print('DMA', self.gcount, q, 'line', fr.f_lineno, 'sem', sem, 'writes', list(writes))
        need = self._deps(reads, writes)
        waits = self._waits(q, need)
        self.dma_cnt[sem] = self.dma_cnt.get(sem, 0) + 16
        self._mark(sem, self.dma_cnt[sem], reads, writes)
        self.prog[q].append(('dma', waits, (out, in_, sem)))

    def phase_switch(self):
        f = dict(self.fence)

        def add(s, v):
            if f.get(s, 0) < v:
                f[s] = v
        for k in list(self.last_w.keys()):
            if isinstance(k, tuple) and k[0] == 'SH':
                add(*self.last_w[k])
                del self.last_w[k]
        for k in list(self.readers.keys()):
            if isinstance(k, tuple) and k[0] == 'SH':
                for s, v in self.readers[k].items():
                    add(s, v)
                del self.readers[k]
        self.fence = f
        self.touched = set()

    def final_wait(self, q):
        waits = [(s, v) for s, v in self.dma_cnt.items()]
        self.prog[q].append(('wait', waits, None))


class Builder:
    def __init__(self, npre, nmain):
        self.npre, self.nmain = npre, nmain
        nft = npre + nmain
        self.nft = nft
        self.FR = nft * NT
        self.KTW = max(NMETA + self.FR, NMETA + LC + TS + 128)
        self.NBLK = max(1 + 4 * nft, 19)
        self.s = Sched()
        self.ring_i = 0
        self.kv_i = 0
        self.xtok_i = 0

    def mm(self, out, lhsT, rhs, start, stop, reads, writes, sgc=False):
        self.s.op('pe', lambda e: e.matmul(out, lhsT=lhsT, rhs=rhs, start=start, stop=stop, skip_group_check=sgc), reads, writes)

    def tr(self, out, in_, n, reads, writes):
        ident = self.ident
        self.s.op('pe', lambda e: e.transpose(out=out, in_=in_, identity=ident[0:n, 0:n]), reads, writes)

    def act(self, out, in_, func, reads, writes, bias=None, scale=None):
        kw = {}
        if bias is not None:
            kw['bias'] = bias
        if scale is not None:
            kw['scale'] = scale
        self.s.op('act', lambda e: e.activation(out=out, in_=in_, func=func, **kw), reads, writes)

    def tt(self, eng, out, in0, in1, op, reads, writes):
        self.s.op(eng, lambda e: e.tensor_tensor(out=out, in0=in0, in1=in1, op=op), reads, writes)

    def tsc(self, eng, out, in0, s1, s2, op0, op1, reads, writes):
        if s2 is None:
            self.s.op(eng, lambda e: e.tensor_scalar(out=out, in0=in0, scalar1=s1, scalar2=None, op0=op0), reads, writes)
        else:
            self.s.op(eng, lambda e: e.tensor_scalar(out=out, in0=in0, scalar1=s1, scalar2=s2, op0=op0, op1=op1), reads, writes)

    def stt(self, eng, out, in0, scalar, in1, op0, op1, reads, writes):
        self.s.op(eng, lambda e: e.scalar_tensor_tensor(out=out, in0=in0, scalar=scalar, in1=in1, op0=op0, op1=op1), reads, writes)

    def cp(self, eng, out, in_, reads, writes):
        if eng == 'act':
            self.act(out, in_, AF.Copy, reads, writes)
        else:
            self.s.op(eng, lambda e: e.tensor_copy(out=out, in_=in_), reads, writes)

    def ring_load(self, dram_ap, nelem, rdkey):
        slot = self.ring_i % 4
        self.ring_i += 1
        out = self.ring[:, slot, 0:nelem]
        self.s.dma('sp', out, dram_ap, [rdkey], [('ring', slot)], 'ring%d' % slot)
        return slot

    def build(self):
        nc = bass.Bass("TRN2", target_bir_lowering=False)
        self.nc = nc
        nft, FR = self.nft, self.FR
        FM_ = self.nmain * NT
        FP_ = max(self.npre, 1) * NT
        dt_in = {}

        def din(name, shape):
            dt_in[name] = nc.dram_tensor(name, list(shape), F32, kind="ExternalInput").ap()
            return dt_in[name]

        def dout(name, shape):
            return nc.dram_tensor(name, list(shape), F32, kind="ExternalOutput").ap()
        self.xp = din("xp", [FM_, D])
        self.xpre = din("xpre", [FP_, D])
        self.flag_d = din("flag", [128, 512])
        self.xs = din("xs", [TS, D])
        self.meta = din("meta", [NMETA, D])
        self.ck = din("ck", [LC, 512])
        self.cv = din("cv", [LC, 512])
        self.st = din("st", [4, 128, 128])
        self.vecs_d = din("vecs", [128, NVEC])
        self.lbrep_d = din("lbrep", [128, 2, 512])
        self.cst_d = din("cst", [128, 5 * 128 + 4 * 512])
        self.w32 = {}
        self.wsz = {'gu1': 22 * 128 * 2048, 'gu2': 22 * 128 * 2048, 'd1': 2 * 8 * 128 * 1408, 'd2': 2 * 8 * 128 * 1408,
                    'wfm': 52 * 128 * 1024, 'wtm': 4 * 128 * 4096}
        for n, sz in self.wsz.items():
            self.w32[n] = din(n, [sz // 2048, 2048])
        self.yp = dout("yp", [FM_, D])
        self.ys = dout("ys", [TS, D])
        self.pk = dout("pk", [NMETA + FM_, 512])
        self.pv = dout("pv", [NMETA + FM_, 512])
        self.ph = dout("ph", [4, 128, 128])
        self.sk = dout("sk", [TS, 512])
        self.sv = dout("sv", [TS, 512])
        self.sh = dout("sh", [4, 128, 128])
        self.wbf = {n: nc.dram_tensor(n + "_bf", [sz], BF16).ap() for n, sz in self.wsz.items()}

        with ExitStack() as es:
            es.enter_context(nc.allow_low_precision("bf16 matmul operands, fp32 accumulation"))

            def sb(name, shape, dt):
                return es.enter_context(nc.sbuf_tensor(name, list(shape), dt))

            def ps(name):
                return es.enter_context(nc.psum_tensor(name, [128, 512], F32))
            self.ident = sb("ident", [128, 128], F32)
            self.onesb = sb("onesb", [128, 128], BF16)
            self.negU = sb("negU", [128, 128], BF16)
            self.negL = sb("negL", [128, 128], BF16)
            self.tri = sb("tri", [128, 128], F32)
            self.masks = sb("masks", [128, 4, 512], BF16)
            self.vecs = sb("vecs_s", [128, NVEC], F32)
            self.lb_t = sb("lb_t", [128, 2, 512], F32)
            self.xT = sb("xT", [128, 8, NT], F32)
            self.hT = sb("hT", [128, 8, NT + 64], BF16)
            self.rstd = sb("rstd", [128, NT], F32)
            self.lnt = sb("lnt", [128, NT], F32)
            self.KT = sb("KT", [128, 4, self.KTW], BF16)
            self.Vp = sb("Vp", [128, self.NBLK, 512], BF16)
            self.ring = sb("ring", [128, 4, 4096], BF16)
            self.S = sb("S", [128, 2, 4, 128], F32)
            self.Sb = sb("Sb", [128, 2, 4, 128], BF16)
            self.kvst = sb("kvst", [128, 2, 512], F32)
            self.Ssave = sb("Ssave", [128, 4, 128], F32)
            self.qT = sb("qT", [128, 4, NT], BF16)
            self.obT = sb("obT", [128, 4, NT], F32)
            self.oaT = sb("oaT", [128, 4, NT], BF16)
            SHW = 11008
            self.SH = sb("SH", [128, SHW], F32)
            self.bank = [ps("bank%d" % i) for i in range(8)]
            self.zerob = sb("zerob", [128, 128], BF16)
            self._views()
            import os
            if os.environ.get('KMEM'):
                print('sbuf bytes remaining', nc.sbuf_bytes_remaining)

            self._emit_all()

            sems = {}
            names = [e for e in ENGS if e != 'sp'] + sorted(self.s.dma_cnt.keys())
            for n in names:
                sems[n] = es.enter_context(nc.semaphore("s_" + n))
            block = es.enter_context(nc.Block())
            prog = self.s.prog

            def run(eng_obj, ename):
                for kind, waits, payload in prog[ename]:
                    for src, val in waits:
                        eng_obj.wait_ge(sems[src], val)
                    if kind == 'op':
                        ins = payload(eng_obj)
                        ins.then_inc(sems[ename], 1)
                    elif kind == 'dma':
                        out, in_, sem = payload
                        eng_obj.dma_start(out=out, in_=in_).then_inc(sems[sem], 16)

            @block.tensor
            def _(e):
                run(e, 'pe')

            @block.scalar
            def _(e):
                run(e, 'act')

            @block.vector
            def _(e):
                run(e, 'dve')

            @block.gpsimd
            def _(e):
                run(e, 'pool')

            @block.sync
            def _(e):
                run(e, 'sp')
        return nc

    def _views(self):
        SH = self.SH

        def v(off_b, nbytes, dt, pattern=None, **kw):
            a = SH[:, off_b // 4:(off_b + nbytes) // 4]
            if dt is BF16:
                a = a.bitcast(BF16)
            if pattern:
                a = a.rearrange(pattern, **kw)
            return a
        self.hid = v(0, 11264, BF16, "p (c n) -> p c n", c=11)
        self.sg = v(11264, 4096, F32, "p (c n) -> p c n", c=2)
        self.x_tok = v(15360, 8192, F32, "p (c n) -> p c n", c=2)
        self.omfT = v(0, 8192, F32, "p (c n) -> p c n", c=4)
        self.qhT = v(8192, 8192, F32, "p (c n) -> p c n", c=4)
        self.logf = v(16384, 4096, F32, "p (c n) -> p c n", c=2)
        self.omf_tm = v(20480, 4096, F32, "p (c n) -> p c n", c=2)
        self.iv = v(24576, 2048, BF16, "p (c n) -> p c n", c=2)
        self.eb = v(26624, 2048, F32, "p (a h t) -> p a h t", a=2, h=4)
        self.enb = v(28672, 2048, F32, "p (a h t) -> p a h t", a=2, h=4)
        self.qe = v(30720, 1024, BF16, "p (a h t) -> p a h t", a=2, h=4)
        self.ke = v(41984, 2048, BF16, "p (a h t) -> p a h t", a=2, h=4)
        self.enb_tm = v(32768, 4096, F32, "p (c n) -> p c n", c=2)
        self.ke_tm = v(36864, 2048, BF16, "p (c n) -> p c n", c=2)
        self.sc = v(38912, 1024, BF16, "p (a h t) -> p a h t", a=2, h=4)
        self.tmpS = v(39936, 2048, F32, "p (h t) -> p h t", h=4)
        self.e_ = v(0, 12288, F32, "p (c n) -> p c n", c=6)
        self.sp_ = v(12288, 6144, BF16, "p (c n) -> p c n", c=6)
        self.e2_ = v(18432, 6144, F32, "p (c n) -> p c n", c=3)
        self.at_ = v(24576, 6144, BF16, "p (c n) -> p c n", c=6)
        self.ckv = v(30720, 8192, F32, "p (c n) -> p c n", c=2)
        self.sgT = v(0, 8192, F32, "p (c n) -> p c n", c=4)
        self.gA = v(8192, 8192, BF16, "p (c n) -> p c n", c=8)
        self.gB = v(16384, 8192, BF16, "p (c n) -> p c n", c=8)
        self.t1 = v(24576, 2048, F32, "p (c n) -> p c n", c=1)
        self.t2 = v(26624, 2048, F32, "p (c n) -> p c n", c=1)
        self.mT = v(28672, 8192, BF16, "p (c n) -> p c n", c=8)
        self.obn = v(36864, 4096, BF16, "p (c n) -> p c n", c=4)

    def _emit_all(self):
        import os
        self.kstop = int(os.environ.get('KSTOP', '99'))
        self.prologue()
        self.tile(-1, 'meta')
        self.casts([('wfm', 1024, 2304), ('gu2', 0, 1408), ('d2', 0, 704), ('gu2', 1408, 1408), ('d2', 704, 704)], 6)
        for p in range(self.npre):
            self.tile(p, 'pre')
        if self.npre > 0:
            self.state_select()
        for m in range(self.nmain):
            self.tile(self.npre + m, 'main')
        self.tile(-1, 'sample')
        self.s.final_wait('sp')

    def prologue(self):
        s = self.s
        c = self.cst_d
        o = 0
        s.dma('pool', self.ident[:], c[:, o:o + 128], [], ['ident'], 'cst'); o += 128
        s.dma('pool', self.onesb[:], c[:, o:o + 128], [], ['onesb'], 'cst'); o += 128
        s.dma('pool', self.negU[:], c[:, o:o + 128], [], ['negU'], 'cst'); o += 128
        s.dma('pool', self.tri[:], c[:, o:o + 128], [], ['tri'], 'cst'); o += 128
        s.dma('pool', self.negL[:], c[:, o:o + 128], [], ['negL'], 'cst'); o += 128
        s.dma('pool', self.masks[:], c[:, o:o + 2048].rearrange("p (d n) -> p d n", d=4), [], ['masks'], 'cst'); o += 2048
        s.dma('pool', self.vecs[:], self.vecs_d, [], ['vecs'], 'cst')
        s.dma('pool', self.lb_t[:], self.lbrep_d, [], ['lb_t'], 'cst')
        allc = ['ident', 'onesb', 'negU', 'negL', 'tri', 'masks', 'vecs', 'lb_t']
        tot = s.dma_cnt['cst']
        for k in allc:
            s.last_w[k] = ('cst', tot)
        self.s.op('dve', lambda e: e.memset(self.hT[:, :, :], 0.0), [], [('hT', kc) for kc in range(8)])
        self.s.op('dve', lambda e: e.memset(self.KT[:, :, :], 0.0), [], [('KT', c) for c in range(4)])
        self.s.op('dve', lambda e: e.memset(self.xT[:, :, :], 0.0), [], [('xT', kc) for kc in range(8)])
        self.s.op('dve', lambda e: e.memset(self.SH[:, :], 0.0), [], [('SH', 'all')])
        self.s.op('dve', lambda e: e.memset(self.zerob[:, :], 0.0), [], ['zerob'])
        self.casts([('gu1', 0, 1408), ('d1', 0, 704), ('gu1', 1408, 1408), ('d1', 704, 704), ('wfm', 0, 1024), ('wtm', 0, 1024)], 0)
        vv = self.vecs
        self.tt('dve', vv[:, V_LBV:V_LBV + 4], vv[:, V_LB1:V_LB1 + 4], vv[:, V_LB0:V_LB0 + 4], ALU.subtract, ['vecs'], ['vecs'])
        self.act(vv[:, V_LBV:V_LBV + 4], vv[:, V_LBV:V_LBV + 4], AF.Sigmoid, ['vecs'], ['vecs'])
        self.tsc('dve', vv[:, V_OML:V_OML + 4], vv[:, V_LBV:V_LBV + 4], -1.0, 1.0, ALU.mult, ALU.add, ['vecs'], ['vecs'])
        lt = self.lb_t
        self.tt('dve', lt[:, 0, :], lt[:, 1, :], lt[:, 0, :], ALU.subtract, ['lb_t'], ['lb_t'])
        self.act(lt[:, 0, :], lt[:, 0, :], AF.Sigmoid, ['lb_t'], ['lb_t'])
        self.tsc('dve', lt[:, 1, :], lt[:, 0, :], -1.0, 1.0, ALU.mult, ALU.add, ['lb_t'], ['lb_t'])

    def casts(self, pieces, i0):
        for i, (n, r0, nr) in enumerate(pieces):
            dst = self.wbf[n].rearrange("(r c) -> r c", c=2048)[r0:r0 + nr, :]
            self.s.dma('pool', dst, self.w32[n][r0:r0 + nr, :], [], [('scr', n, r0)], 'cast%d' % (i0 + i))

    def scr_key(self, n, row2048):
        bounds = {'gu1': [0, 1408], 'gu2': [0, 1408], 'd1': [0, 704], 'd2': [0, 704], 'wfm': [0, 1024], 'wtm': [0]}[n]
        r0 = max(b for b in bounds if b <= row2048)
        return ('scr', n, r0)

    def load_gu(self, which, c):
        n = 'gu%d' % which
        ap = self.wbf[n].rearrange("(c p f) -> c p f", p=128, f=2048)[c]
        return self.ring_load(ap, 2048, self.scr_key(n, c * 128))

    def load_d(self, which, half, o):
        n = 'd%d' % which
        ap = self.wbf[n].rearrange("(h o p f) -> h o p f", h=2, o=8, p=128)[half, o]
        return self.ring_load(ap, 1408, self.scr_key(n, (half * 8 + o) * 128 * 1408 // 2048))

    def load_fm(self, c0, ncnk):
        slot = self.ring_i % 4
        self.ring_i += 1
        src = self.wbf['wfm'].rearrange("(c p f) -> c p f", p=128, f=1024)
        for i in range(ncnk):
            self.s.dma('sp', self.ring[:, slot, i * 1024:(i + 1) * 1024], src[c0 + i], [self.scr_key('wfm', c0 * 64)], [('ring', slot)], 'ring%d' % slot)
        return slot

    def load_tm(self, g):
        ap = self.wbf['wtm'].rearrange("(g p f) -> g p f", p=128, f=4096)[g]
        return self.ring_load(ap, 4096, ('scr', 'wtm', 0))

    def state_select(self):
        S0 = self.S[:, 0, :, :]
        self.tsc('dve', self.Ssave[:, :, :], self.Ssave[:, :, :], self.vecs[:, V_FA:V_FA + 1], None, ALU.mult, None, ['Ssave', 'vecs'], ['Ssave'])
        self.stt('dve', S0, S0, self.vecs[:, V_FB:V_FB + 1], self.Ssave[:, :, :], ALU.mult, ALU.add, [('S', 0), 'Ssave', 'vecs'], [('S', 0)])
        for hh in range(4):
            self.cp('pool', self.Sb[:, 0, hh, :], self.S[:, 0, hh, :], [('S', 0)], [('Sb', 0, hh)])

    def tile(self, f, mode='extra'):
        self.mode = mode
        self.m = f - self.npre if mode == 'main' else f
        if mode == 'meta':
            n = NMETA
            segs = [('meta', 0, NMETA)]
        elif mode == 'sample':
            n = TS
            segs = [('sample', 0, TS)]
        else:
            n = NT
            segs = [('frames', 0, NT)]
        self.n = n
        self.f = f
        self.segs = segs
        s = self.s
        ks = self.kstop if f < 0 else 99
        s.phase_switch()
        self.load_x()
        if mode in ('pre', 'meta'):
            self.norm(V_N1)
            self.ffn(1)
            self.norm(V_NM)
            s.phase_switch()
            self.w_in()
            self.hgrn()
            return
        if ks < 2: return
        self.norm(V_N1)
        if ks < 3: return
        self.ffn(1)
        if ks < 4: return
        self.norm(V_NM)
        s.phase_switch()
        self.w_in()
        if ks < 5 or ks == 41: return
        self.hgrn()
        if ks < 6: return
        s.phase_switch()
        self.attn()
        if ks < 7: return
        s.phase_switch()
        self.post()
        if ks < 8: return
        s.phase_switch()
        self.norm(V_N2)
        self.ffn(2)
        self.final()

    def load_x(self):
        n, f = self.n, self.f
        if f < 0:
            blocks = [(0, n)]
        else:
            blocks = [(j * 128, 128) for j in range(4)]
        for bi, (c0, nb) in enumerate(blocks):
            slot = self.xtok_i % 2
            self.xtok_i += 1
            xk = ('SH', 'x_tok', slot)
            if f < 0:
                self.s.dma('sp', self.x_tok[0:n, slot, :], self.xs if self.mode == 'sample' else self.meta, [], [xk], 'xin%d' % slot)
            else:
                r0 = self.m * NT + c0
                srcx = self.xpre if self.mode == 'pre' else self.xp
                self.s.dma('sp', self.x_tok[:, slot, :], srcx[r0:r0 + 128, :], [], [xk], 'xin%d' % slot)
            for kc in range(8):
                bk = self.bank[kc % 2]
                self.tr(bk[:, 0:nb], self.x_tok[0:nb, slot, kc * 128:(kc + 1) * 128], nb, [xk, 'ident'], [('ps', kc % 2)])
                eng = 'dve' if kc % 2 == 0 else 'act'
                self.cp(eng, self.xT[:, kc, c0:c0 + nb], bk[:, 0:nb], [('ps', kc % 2)], [('xT', kc)])

    def norm(self, gcol, inplace=False):
        n = self.n
        for kc in range(8):
            self.act(self.hT[:, kc, 0:n], self.xT[:, kc, 0:n], AF.Square, [('xT', kc)], [('hT', kc)])
        bk = self.bank[7]
        for kc in range(8):
            self.mm(bk[:, 0:n], self.onesb[:, :], self.hT[:, kc, 0:n], kc == 0, kc == 7, [('hT', kc), 'onesb'], [('ps', 7)])
        self.act(self.lnt[:, 0:n], bk[:, 0:n], AF.Ln, [('ps', 7)], ['lnt'], bias=EPS, scale=1.0 / D)
        self.act(self.rstd[:, 0:n], self.lnt[:, 0:n], AF.Exp, ['lnt'], ['rstd'], scale=-0.5)
        for kc in range(8):
            eng = 'dve'
            if inplace:
                self.stt(eng, self.xT[:, kc, 0:n], self.xT[:, kc, 0:n], self.vecs[:, gcol + kc:gcol + kc + 1], self.rstd[:, 0:n],
                         ALU.mult, ALU.mult, [('xT', kc), 'rstd', 'vecs'], [('xT', kc)])
            else:
                self.stt(eng, self.hT[:, kc, 0:n], self.xT[:, kc, 0:n], self.vecs[:, gcol + kc:gcol + kc + 1], self.rstd[:, 0:n],
                         ALU.mult, ALU.mult, [('xT', kc), 'rstd', 'vecs'], [('hT', kc)])

    def ffn(self, which):
        n = self.n
        hk = [('hT', kc) for kc in range(8)]
        it = 0
        gun, dnn = 'gu%d' % which, 'd%d' % which
        gsrc = self.wbf[gun].rearrange("(c p f) -> p c f", p=128, f=2048)
        dsrc = self.wbf[dnn].rearrange("(h o p f) -> h p o f", h=2, o=8, p=128)
        for half in range(2):
            for cc0 in range(0, 11, 2):
                ncnk = min(2, 11 - cc0)
                c = half * 11 + cc0
                slot = self.ring_i % 4
                self.ring_i += 1
                self.s.dma('sp', self.ring[:, slot, 0:ncnk * 2048].rearrange("p (c f) -> p c f", c=ncnk), gsrc[:, c:c + ncnk, :],
                           [self.scr_key(gun, c * 128)], [('ring', slot)], 'ring%d' % slot)
                for ci in range(ncnk):
                    cc = cc0 + ci
                    w = self.ring[:, slot, ci * 2048:(ci + 1) * 2048].rearrange("p (g k n) -> p g k n", g=2, k=8)
                    gb, ub = it % 2, 2 + it % 2
                    for kc in range(8):
                        self.mm(self.bank[gb][:, 0:n], w[:, 0, kc, :], self.hT[:, kc, 0:n], kc == 0, kc == 7, hk + [('ring', slot)], [('ps', gb)])
                    for kc in range(8):
                        self.mm(self.bank[ub][:, 0:n], w[:, 1, kc, :], self.hT[:, kc, 0:n], kc == 0, kc == 7, hk + [('ring', slot)], [('ps', ub)])
                    sgk = ('SH', 'sg', it % 2)
                    self.act(self.sg[:, it % 2, 0:n], self.bank[gb][:, 0:n], AF.Silu, [('ps', gb)], [sgk])
                    self.tt('dve', self.hid[:, cc, 0:n], self.sg[:, it % 2, 0:n], self.bank[ub][:, 0:n], ALU.mult, [sgk, ('ps', ub)], [('SH', 'hid', cc)])
                    it += 1
            for o0 in range(0, 8, 2):
                slot = self.ring_i % 4
                self.ring_i += 1
                self.s.dma('sp', self.ring[:, slot, 0:2 * 1408].rearrange("p (c f) -> p c f", c=2), dsrc[half, :, o0:o0 + 2, :],
                           [self.scr_key(dnn, (half * 8 + o0) * 128 * 1408 // 2048)], [('ring', slot)], 'ring%d' % slot)
                for oi in range(2):
                    o = o0 + oi
                    w = self.ring[:, slot, oi * 1408:(oi + 1) * 1408].rearrange("p (c n) -> p c n", c=11)
                    db = 4 + o % 2
                    for cc in range(11):
                        self.mm(self.bank[db][:, 0:n], w[:, cc, :], self.hid[:, cc, 0:n], cc == 0, cc == 10, [('SH', 'hid', cc), ('ring', slot)], [('ps', db)])
                    self.stt('dve', self.xT[:, o, 0:n], self.bank[db][:, 0:n], 0.5, self.xT[:, o, 0:n], ALU.mult, ALU.add, [('ps', db), ('xT', o)], [('xT', o)])

    def kcol(self, seg):
        if seg == 'sample':
            return NMETA + LC
        if seg == 'meta':
            return 0
        return NMETA + self.f * NT

    def w_in(self):
        n = self.n
        hk = [('hT', kc) for kc in range(8)]
        bi = 0
        for g4 in ([1] if self.mode in ('pre', 'meta') else range(4)):
            slot = self.load_fm(g4 * 4, 4)
            w = self.ring[:, slot, 0:4096].rearrange("p (c k n) -> p c k n", c=4, k=8)
            for ci in range(4):
                b = bi % 4
                bi += 1
                bk = self.bank[b]
                for kc in range(8):
                    self.mm(bk[:, 0:n], w[:, ci, kc, :], self.hT[:, kc, 0:n], kc == 0, kc == 7, hk + [('ring', slot)], [('ps', b)])
                if g4 == 0:
                    self.act(self.qT[:, ci, 0:n], bk[:, 0:n], AF.Copy, [('ps', b)], [('qT', ci)], scale=0.125)
                elif g4 == 1:
                    for (sname, c0, ns) in self.segs:
                        kc0 = self.kcol(sname)
                        self.cp('dve', self.KT[:, ci, kc0:kc0 + ns], bk[:, c0:c0 + ns], [('ps', b)], [('KT', ci)])
                elif g4 == 2:
                    self.act(self.lnt[:, 0:n], bk[:, 0:n], AF.Sigmoid, [('ps', b)], ['lnt'], scale=-1.0)
                    self.tsc('dve', self.omfT[:, ci, 0:n], self.lnt[:, 0:n], self.vecs[:, V_OML + ci:V_OML + ci + 1], None, ALU.mult, None,
                             ['lnt', 'vecs'], [('SH', 'omfT', ci)])
                else:
                    self.cp('act', self.qhT[:, ci, 0:n], bk[:, 0:n], [('ps', b)], [('SH', 'qhT', ci)])
        if self.kstop == 41:
            return
        if self.mode == 'sample':
            blocks = [('sample', 0, TS, self.sk, self.sv, 0, 18)]
        elif self.mode == 'meta':
            blocks = [('meta', 0, NMETA, self.pk, self.pv, 0, 0)]
        else:
            blocks = [('frames', j * 128, 128, self.pk, self.pv, NMETA + self.m * NT + j * 128, 1 + 4 * self.f + j) for j in range(4)]
        for g in ([1] if self.mode == 'pre' else range(2)):
            slot = self.load_tm(g)
            w = self.ring[:, slot, 0:4096].rearrange("p (k n) -> p k n", k=8)
            for (sname, c0, nb, okd, ovd, r0, vblk) in blocks:
                b = 4 + bi % 4
                bi += 1
                bk = self.bank[b]
                for kc in range(8):
                    self.mm(bk[:, :], self.hT[:, kc, c0:c0 + 128], w[:, kc, :], kc == 0, kc == 7, hk + [('ring', slot)], [('ps', b)])
                if self.mode == 'pre':
                    self.cp('act', self.Vp[0:nb, vblk, :], bk[0:nb, :], [('ps', b)], [('Vp', vblk)])
                    continue
                ks = self.kv_i % 2
                self.kv_i += 1
                self.cp('dve', self.kvst[0:nb, ks, :], bk[0:nb, :], [('ps', b)], [('kvst', ks)])
                dst = (okd if g == 0 else ovd)[r0:r0 + nb, :]
                import os
                if not os.environ.get('NOKV'):
                    self.s.dma(os.environ.get('KVQ', 'pool'), dst, self.kvst[0:nb, ks, :], [('kvst', ks)], [], 'kvo%d' % ks)
                if g == 1:
                    self.cp('act', self.Vp[0:nb, vblk, :], bk[0:nb, :], [('ps', b)], [('Vp', vblk)])

    def hgrn(self):
        s = self.s
        hk = [('hT', kc) for kc in range(8)]
        slz = self.load_tm(2)
        sli = self.load_tm(3)
        wz = self.ring[:, slz, 0:4096].rearrange("p (k n) -> p k n", k=8)
        wi = self.ring[:, sli, 0:4096].rearrange("p (k n) -> p k n", k=8)
        if self.mode == 'sample':
            chunks = [(0, TS, 1)]
            self.s.dma('pool', self.S[:, 1, :, :], self.st.rearrange("h k v -> k h v"), [], [('S', 1)], 'stin')
            for hh in range(4):
                self.cp('pool', self.Sb[:, 1, hh, :], self.S[:, 1, hh, :], [('S', 1)], [('Sb', 1, hh)])
        elif self.mode == 'meta':
            chunks = [(0, NMETA, 0)]
            self.s.op('dve', lambda e: e.memset(self.S[:, 0, :, :], 0.0), [], [('S', 0)])
            self.s.op('dve', lambda e: e.memset(self.Sb[:, 0, :, :], 0.0), [], [('Sb', 0, hh) for hh in range(4)])
        else:
            chunks = [(i * 64, 64, 0) for i in range(8)]
        def _chunk(ci, c0, T, si):
            par = ci % 2
            bA, bB, bC, bD, bE, bF = self.bank[0], self.bank[1], self.bank[2 + par], self.bank[4], self.bank[5], self.bank[6 + par]
            kC, kF = ('ps', 2 + par), ('ps', 6 + par)
            for kc in range(8):
                self.mm(bA[:, :], self.hT[:, kc, c0:c0 + 128], wz[:, kc, :], kc == 0, kc == 7, hk + [('ring', slz)], [('ps', 0)])
            for kc in range(8):
                self.mm(bB[:, :], self.hT[:, kc, c0:c0 + 128], wi[:, kc, :], kc == 0, kc == 7, hk + [('ring', sli)], [('ps', 1)])
            k_omf, k_logf, k_iv = ('SH', 'omf_tm', par), ('SH', 'logf', par), ('SH', 'iv', par)
            self.act(self.omf_tm[0:T, par, :], bA[0:T, :], AF.Sigmoid, [('ps', 0)], [k_omf], scale=-1.0)
            self.tt('dve', self.omf_tm[0:T, par, :], self.omf_tm[0:T, par, :], self.lb_t[0:T, 1, :], ALU.mult, [k_omf, 'lb_t'], [k_omf])
            self.act(self.logf[0:T, par, :], self.omf_tm[0:T, par, :], AF.Ln, [k_omf], [k_logf], bias=1.0, scale=-1.0)
            self.cp('act', self.iv[0:T, par, :], bB[0:T, :], [('ps', 1)], [k_iv])
            cheap = self.mode in ('pre', 'meta')
            if not cheap:
                for hh in range(4):
                    self.mm(bC[:, hh * 64:hh * 64 + T], self.logf[0:T, par, hh * 128:(hh + 1) * 128], self.tri[0:T, 0:T], True, True,
                            [k_logf, 'tri'], [kC])
            self.mm(bD[:, :], self.tri[0:T, 0:128], self.logf[0:T, par, :], True, True, [k_logf, 'tri'], [('ps', 4)])
            k_enbt, k_ket = ('SH', 'enb_tm', par), ('SH', 'ke_tm', par)
            self.act(self.enb_tm[0:T, par, :], bD[0:T, :], AF.Exp, [('ps', 4)], [k_enbt], scale=-1.0)
            self.tt('dve', self.ke_tm[0:T, par, :], self.omf_tm[0:T, par, :], self.enb_tm[0:T, par, :], ALU.mult, [k_omf, k_enbt], [k_ket])
            if cheap:
                for hh in range(4):
                    self.mm(bC[:, hh * 64:hh * 64 + 2], self.logf[0:T, par, hh * 128:(hh + 1) * 128], self.tri[0:T, 126:128], True, True,
                            [k_logf, 'tri'], [kC])
                for hh in range(4):
                    self.act(self.eb[:, par, hh, T - 1:T], bC[:, hh * 64:hh * 64 + 1], AF.Exp, [kC], [('SH', 'eb', par, hh)])
                yield
            else:
                for hh in range(4):
                    k_eb, k_enb, k_qe, k_ke = ('SH', 'eb', par, hh), ('SH', 'enb', par, hh), ('SH', 'qe', par, hh), ('SH', 'ke', par, hh)
                    self.act(self.eb[:, par, hh, 0:T], bC[:, hh * 64:hh * 64 + T], AF.Exp, [kC], [k_eb])
                    self.act(self.enb[:, par, hh, 0:T], bC[:, hh * 64:hh * 64 + T], AF.Exp, [kC], [k_enb], scale=-1.0)
                    self.tt('dve', self.qe[:, par, hh, 0:T], self.qhT[:, hh, c0:c0 + T], self.eb[:, par, hh, 0:T], ALU.mult,
                            [('SH', 'qhT', hh), k_eb], [k_qe])
                    self.tt('pool', self.ke[:, par, hh, 0:T], self.omfT[:, hh, c0:c0 + T], self.enb[:, par, hh, 0:T], ALU.mult,
                            [('SH', 'omfT', hh), k_enb], [k_ke])
                yield
                for hh in range(4):
                    k_qe, k_ke = ('SH', 'qe', par, hh), ('SH', 'ke', par, hh)
                    self.mm(bC[:, 256 + hh * 64:256 + hh * 64 + T], self.ke[:, par, hh, 0:128], self.qe[:, par, hh, 0:T], True, True,
                            [k_qe, k_ke], [kC])
                for hh in range(4):
                    k_sc = ('SH', 'sc', par, hh)
                    self.tt('dve', self.sc[0:T, par, hh, 0:T], bC[0:T, 256 + hh * 64:256 + hh * 64 + T], self.tri[0:T, 0:T], ALU.mult,
                            [kC, 'tri'], [k_sc])
                for hh in range(4):
                    k_sc, k_qe = ('SH', 'sc', par, hh), ('SH', 'qe', par, hh)
                    self.mm(bE[:, hh * 64:hh * 64 + T], self.iv[0:T, par, hh * 128:(hh + 1) * 128], self.sc[0:T, par, hh, 0:T], True, False,
                            [k_iv, k_sc], [('ps', 5)])
                    self.mm(bE[:, hh * 64:hh * 64 + T], self.Sb[:, si, hh, :], self.qe[:, par, hh, 0:T], False, True,
                            [('Sb', si, hh), k_qe], [('ps', 5)])
                for hh in range(4):
                    self.cp('act' if hh % 2 == 0 else 'dve', self.obT[:, hh, c0:c0 + T], bE[:, hh * 64:hh * 64 + T], [('ps', 5)], [('obT', hh)])
            for hh in range(4):
                self.mm(bF[:, hh * 128:(hh + 1) * 128], self.ke_tm[0:T, par, hh * 128:(hh + 1) * 128], self.iv[0:T, par, hh * 128:(hh + 1) * 128],
                        True, True, [k_ket, k_iv], [kF])
            for hh in range(4):
                k_eb = ('SH', 'eb', par, hh)
                ebl = self.eb[:, par, hh, T - 1:T]
                self.tsc('dve', self.tmpS[:, hh, :], self.S[:, si, hh, :], ebl, None, ALU.mult, None, [('S', si), k_eb], [('SH', 'tmpS', hh)])
                self.stt('dve', self.S[:, si, hh, :], bF[:, hh * 128:(hh + 1) * 128], ebl, self.tmpS[:, hh, :], ALU.mult, ALU.add,
                         [kF, k_eb, ('SH', 'tmpS', hh)], [('S', si)])
                self.cp('pool', self.Sb[:, si, hh, :], self.S[:, si, hh, :], [('S', si)], [('Sb', si, hh)])
            if self.f < 0 and si == 1:
                self.s.dma('pool', self.sh.rearrange("h k v -> k h v"), self.S[:, 1, :, :], [('S', 1)], [], 'sout')
            if self.f < 0 and si == 0 and self.npre > 0:
                self.cp('pool', self.Ssave[:, :, :], self.S[:, 0, :, :], [('S', 0)], ['Ssave'])
            if self.mode == 'main' and self.f == self.nft - 1 and ci == len(chunks) - 1:
                self.s.dma('pool', self.ph.rearrange("h k v -> k h v"), self.S[:, 0, :, :], [('S', 0)], [], 'sout')

        gens = [_chunk(ci, *ch) for ci, ch in enumerate(chunks)]
        next(gens[0])
        for i in range(len(gens)):
            if i + 1 < len(gens):
                next(gens[i + 1])
            for _ in gens[i]:
                pass

    def attn(self):
        f = self.f
        if f < 0:
            nb_c = (LC + 127) // 128
            for b in range(nb_c):
                r0 = b * 128
                kn = min(128, LC - r0)
                slot = b % 2
                ck_key = ('SH', 'ckv', slot)
                self.s.dma('sp', self.ckv[0:kn, slot, 0:512], self.ck[r0:r0 + kn, :], [], [ck_key], 'ckin%d' % slot)
                self.s.dma('sp', self.ckv[0:kn, slot, 512:1024], self.cv[r0:r0 + kn, :], [], [ck_key], 'ckin%d' % slot)
                bk = self.bank[b % 2]
                for ch in range(4):
                    self.tr(bk[:, ch * 128:ch * 128 + kn], self.ckv[0:kn, slot, ch * 128:(ch + 1) * 128], kn, [ck_key, 'ident'], [('ps', b % 2)])
                for ch in range(4):
                    self.cp('dve' if ch % 2 == 0 else 'act', self.KT[:, ch, NMETA + r0:NMETA + r0 + kn], bk[:, ch * 128:ch * 128 + kn],
                            [('ps', b % 2)], [('KT', ch)])
                self.cp('pool', self.Vp[0:kn, 1 + b, :], self.ckv[0:kn, slot, 512:1024], [ck_key], [('Vp', 1 + b)])
            blocks = [(18, TS, NMETA + LC, 0)]
            blocks.append((1 + nb_c - 1, LC - 128 * (nb_c - 1), NMETA + 128 * (nb_c - 1), None))
            for b in range(nb_c - 2, -1, -1):
                blocks.append((1 + b, 128, NMETA + 128 * b, None))
            self.attn_job(0, TS, blocks)
        else:
            blocks = []
            for m in range(3, -1, -1):
                fb = 4 * f + m
                blocks.append((1 + fb, 128, NMETA + 128 * fb, 128 * m))
            for fb in range(4 * f - 1, -1, -1):
                blocks.append((1 + fb, 128, NMETA + 128 * fb, 'flag' if fb < 4 * self.npre else None))
            blocks.append((0, NMETA, 0, None))
            self.attn_job(0, NT, blocks)

    def attn_job(self, q0, nq, blocks):
        nblk = len(blocks)
        for grp in ([0, 1, 2], [3, 4, 5], [6, 7]):
            S_ = len(grp)
            nseq = nblk * S_

            def emitZ(idx):
                k, si = divmod(idx, S_)
                h = grp[si]
                vblk, kn, kcol, md = blocks[k]
                ch, pb = h // 2, 64 * (h % 2)
                zb = idx % 2
                self.mm(self.bank[zb][:, 0:nq], self.KT[pb:pb + 64, ch, kcol:kcol + 128], self.qT[pb:pb + 64, ch, q0:q0 + nq], True, True,
                        [('KT', ch), ('qT', ch)], [('ps', zb)])
            for idx in range(min(2, nseq)):
                emitZ(idx)
            for k in range(nblk):
                vblk, kn, kcol, md = blocks[k]
                par = k % 2
                for si, h in enumerate(grp):
                    idx = k * S_ + si
                    zb = idx % 2
                    ke_, ks_ = ('SH', 'e', si, par), ('SH', 'sp', si, par)
                    if md == 'flag':
                        self.act(self.e_[0:kn, si * 2 + par, 0:nq], self.bank[zb][0:kn, 0:nq], AF.Exp, [('ps', zb), 'vecs'], [ke_],
                                 bias=self.vecs[0:kn, V_DEAD:V_DEAD + 1])
                    else:
                        self.act(self.e_[0:kn, si * 2 + par, 0:nq], self.bank[zb][0:kn, 0:nq], AF.Exp, [('ps', zb)], [ke_])
                    if idx + 2 < nseq:
                        emitZ(idx + 2)
                    if md is not None and md != 'flag':
                        self.tt('pool', self.e_[0:kn, si * 2 + par, 0:nq], self.e_[0:kn, si * 2 + par, 0:nq], self.masks[0:kn, md // 128, 0:nq],
                                ALU.mult, [ke_, 'masks'], [ke_])
                    self.act(self.sp_[0:kn, si * 2 + par, 0:nq], self.e_[0:kn, si * 2 + par, 0:nq], AF.Ln, [ke_], [ks_], bias=1.0)
                for si, h in enumerate(grp):
                    ks_ = ('SH', 'sp', si, par)
                    self.mm(self.bank[2 + si][:, 0:nq], self.negU[0:kn, :], self.sp_[0:kn, si * 2 + par, 0:nq], k == 0, False,
                            [ks_, 'negU'], [('ps', 2 + si)], sgc=True)
                for si, h in enumerate(grp):
                    ke_, k2_, ka_ = ('SH', 'e', si, par), ('SH', 'e2', si), ('SH', 'at', si, par)
                    self.act(self.e2_[0:kn, si, 0:nq], self.bank[2 + si][0:kn, 0:nq], AF.Exp, [('ps', 2 + si)], [k2_])
                    self.tt('dve', self.at_[0:kn, si * 2 + par, 0:nq], self.e2_[0:kn, si, 0:nq], self.e_[0:kn, si * 2 + par, 0:nq], ALU.mult,
                            [k2_, ke_], [ka_])
                for si, h in enumerate(grp):
                    ks_, ka_ = ('SH', 'sp', si, par), ('SH', 'at', si, par)
                    self.mm(self.bank[2 + si][:, 0:nq], self.negL[0:kn, :], self.sp_[0:kn, si * 2 + par, 0:nq], False, k == nblk - 1,
                            [ks_, 'negL'], [('ps', 2 + si)], sgc=True)
                    vlo = h * 64 if h % 2 == 0 else (h - 1) * 64
                    self.mm(self.bank[5 + si][:, 0:nq], self.Vp[0:kn, vblk, vlo:vlo + 128], self.at_[0:kn, si * 2 + par, 0:nq], k == 0, k == nblk - 1,
                            [ka_, ('Vp', vblk)], [('ps', 5 + si)])
            for si, h in enumerate(grp):
                ch, pb = h // 2, 64 * (h % 2)
                self.cp('dve' if si % 2 == 0 else 'act', self.oaT[pb:pb + 64, ch, q0:q0 + nq], self.bank[5 + si][pb:pb + 64, 0:nq],
                        [('ps', 5 + si)], [('oaT', ch, pb)])

    def post(self):
        n = self.n
        hk = [('hT', kc) for kc in range(8)]
        bi = 0
        for ld in range(5):
            slot = self.load_fm(16 + ld * 4, 4)
            w = self.ring[:, slot, 0:4096].rearrange("p (c k n) -> p c k n", c=4, k=8)
            for ci in range(4):
                cidx = ld * 4 + ci
                b = bi % 4
                bi += 1
                bk = self.bank[b]
                for kc in range(8):
                    self.mm(bk[:, 0:n], w[:, ci, kc, :], self.hT[:, kc, 0:n], kc == 0, kc == 7, hk + [('ring', slot)], [('ps', b)])
                if cidx < 4:
                    self.act(self.sgT[:, cidx, 0:n], bk[:, 0:n], AF.Silu, [('ps', b)], [('SH', 'sgT', cidx)])
                elif cidx < 12:
                    o = cidx - 4
                    self.act(self.gA[:, o, 0:n], bk[:, 0:n], AF.Sigmoid, [('ps', b), 'vecs'], [('SH', 'gA', o)], bias=self.vecs[:, V_BGA + o:V_BGA + o + 1])
                else:
                    o = cidx - 12
                    self.act(self.gB[:, o, 0:n], bk[:, 0:n], AF.Sigmoid, [('ps', b), 'vecs'], [('SH', 'gB', o)], bias=self.vecs[:, V_BGB + o:V_BGB + o + 1])
        for hh in range(4):
            kq = ('SH', 'mT', hh)
            self.act(self.mT[:, hh, 0:n], self.obT[:, hh, 0:n], AF.Square, [('obT', hh)], [kq])
            self.mm(self.bank[4][:, 0:n], self.onesb[:, :], self.mT[:, hh, 0:n], True, True, [kq, 'onesb'], [('ps', 4)])
            self.act(self.lnt[:, 0:n], self.bank[4][:, 0:n], AF.Ln, [('ps', 4)], ['lnt'], bias=EPS, scale=1.0 / 128)
            self.act(self.rstd[:, 0:n], self.lnt[:, 0:n], AF.Exp, ['lnt'], ['rstd'], scale=-0.5)
            k1 = ('SH', 't1', 0)
            self.stt('dve', self.t1[:, 0, 0:n], self.obT[:, hh, 0:n], self.vecs[:, V_HGN + hh:V_HGN + hh + 1], self.rstd[:, 0:n], ALU.mult, ALU.mult,
                     [('obT', hh), 'vecs', 'rstd'], [k1])
            self.tt('pool', self.obn[:, hh, 0:n], self.t1[:, 0, 0:n], self.sgT[:, hh, 0:n], ALU.mult, [k1, ('SH', 'sgT', hh)], [('SH', 'obn', hh)])
        oak = [('oaT', c, pb) for c in range(4) for pb in (0, 64)]
        obk = [('SH', 'obn', c) for c in range(4)]
        for ld in range(2):
            slot = self.load_fm(36 + ld * 4, 4)
            w = self.ring[:, slot, 0:4096].rearrange("p (c k n) -> p c k n", c=4, k=8)
            for ci in range(4):
                o = ld * 4 + ci
                ba, bb = o % 2, 2 + o % 2
                for c in range(4):
                    self.mm(self.bank[ba][:, 0:n], w[:, ci, c, :], self.oaT[:, c, 0:n], c == 0, c == 3, oak + [('ring', slot)], [('ps', ba)])
                for c in range(4):
                    self.mm(self.bank[bb][:, 0:n], w[:, ci, 4 + c, :], self.obn[:, c, 0:n], c == 0, c == 3, obk + [('ring', slot)], [('ps', bb)])
                k1, k2 = ('SH', 't1', 0), ('SH', 't2', 0)
                self.tt('dve', self.t1[:, 0, 0:n], self.gA[:, o, 0:n], self.bank[ba][:, 0:n], ALU.mult, [('SH', 'gA', o), ('ps', ba)], [k1])
                self.tt('dve', self.t2[:, 0, 0:n], self.gB[:, o, 0:n], self.bank[bb][:, 0:n], ALU.mult, [('SH', 'gB', o), ('ps', bb)], [k2])
                self.tt('pool', self.mT[:, o, 0:n], self.t1[:, 0, 0:n], self.t2[:, 0, 0:n], ALU.add, [k1, k2], [('SH', 'mT', o)])
        mk = [('SH', 'mT', c) for c in range(8)]
        for ld in range(2):
            slot = self.load_fm(44 + ld * 4, 4)
            w = self.ring[:, slot, 0:4096].rearrange("p (c k n) -> p c k n", c=4, k=8)
            for ci in range(4):
                o = ld * 4 + ci
                b = 4 + o % 2
                for c in range(8):
                    self.mm(self.bank[b][:, 0:n], w[:, ci, c, :], self.mT[:, c, 0:n], c == 0, c == 7, mk + [('ring', slot)], [('ps', b)])
                self.tt('dve', self.xT[:, o, 0:n], self.xT[:, o, 0:n], self.bank[b][:, 0:n], ALU.add, [('xT', o), ('ps', b)], [('xT', o)])

    def final(self):
        n, f = self.n, self.f
        self.norm(V_NF, inplace=True)
        if f < 0:
            blocks = [(0, n, self.ys, 0, TS)]
            assert self.mode == 'sample'
        else:
            blocks = [(j * 128, 128, self.yp, self.m * NT + j * 128, 128) for j in range(4)]
        for (c0, nb, dst, r0, nout) in blocks:
            slot = self.xtok_i % 2
            self.xtok_i += 1
            xk = ('SH', 'x_tok', slot)
            for half in range(2):
                bk = self.bank[half]
                for q in range(4):
                    kc = half * 4 + q
                    self.tr(bk[:, q * 128:(q + 1) * 128], self.xT[:, kc, c0:c0 + 128], 128, [('xT', kc), 'ident'], [('ps', half)])
                self.cp('dve' if half == 0 else 'act', self.x_tok[0:nb, slot, half * 512:(half + 1) * 512], bk[0:nb, :], [('ps', half)], [xk])
            self.s.dma('pool', dst[r0:r0 + nout, :], self.x_tok[0:nout, slot, :], [xk], [], 'yout%d' % slot)


def _fix_kt_keys(b):
    pass


def _layouts(inp):
    f32 = np.float32

    def fm_vec(v):
        v = np.asarray(v, f32).reshape(-1, 128)
        return v.T

    def gu(wg, wu):
        a = np.asarray(wg, f32).reshape(8, 128, 22, 128).transpose(2, 1, 0, 3)
        b = np.asarray(wu, f32).reshape(8, 128, 22, 128).transpose(2, 1, 0, 3)
        return np.ascontiguousarray(np.stack([a, b], axis=2)).reshape(-1, 2048)

    def dn(wd):
        a = np.asarray(wd, f32).reshape(2, 11, 128, 8, 128).transpose(0, 3, 2, 1, 4)
        return np.ascontiguousarray(a).reshape(-1, 2048)
    w_in = np.asarray(inp['w_in'][0], f32)
    cols = np.concatenate([np.arange(0, 512), np.arange(512, 1024), np.arange(1536, 2048), np.arange(2560, 3072),
                           np.arange(3072, 3584), np.arange(3584, 4608), np.arange(4608, 5632)])
    fm = w_in[:, cols].reshape(8, 128, 36, 128).transpose(2, 1, 0, 3)
    wa = np.asarray(inp['w_branch_a'][0], f32).reshape(4, 128, 8, 128).transpose(2, 1, 0, 3)
    wb = np.asarray(inp['w_branch_b'][0], f32).reshape(4, 128, 8, 128).transpose(2, 1, 0, 3)
    wab = np.concatenate([wa, wb], axis=2)
    wo = np.asarray(inp['w_out'][0], f32).reshape(8, 128, 8, 128).transpose(2, 1, 0, 3)
    wfm = np.ascontiguousarray(np.concatenate([fm, wab, wo], axis=0)).reshape(-1, 2048)
    wtm = np.ascontiguousarray(w_in[:, 512:2560].reshape(8, 128, 4, 512).transpose(2, 1, 0, 3)).reshape(-1, 2048)
    vecs = np.zeros((128, NVEC), f32)
    vecs[:, V_N1:V_N1 + 8] = fm_vec(inp['ffn1_norm'][0])
    vecs[:, V_NM:V_NM + 8] = fm_vec(inp['mix_norm'][0])
    vecs[:, V_N2:V_N2 + 8] = fm_vec(inp['ffn2_norm'][0])
    vecs[:, V_NF:V_NF + 8] = fm_vec(inp['final_norm'])
    vecs[:, V_BGA:V_BGA + 16] = fm_vec(inp['b_gate'][0])
    vecs[:, V_HGN:V_HGN + 4] = fm_vec(inp['hg_out_norm'][0])
    vecs[:, V_LB0:V_LB0 + 4] = fm_vec(inp['hg_lb_logits'][0])
    vecs[:, V_LB1:V_LB1 + 4] = fm_vec(inp['hg_lb_logits'][1])
    lbrep = np.ascontiguousarray(np.broadcast_to(np.asarray(inp['hg_lb_logits'], f32)[None], (128, 2, 512)))
    p = np.arange(128)[:, None]
    j = np.arange(128)[None, :]
    ident = (p == j).astype(f32)
    ones = np.ones((128, 128), f32)
    negU = -(p >= j).astype(f32)
    negL = -(p < j).astype(f32)
    tri = (p <= j).astype(f32)
    cc = np.arange(512)[None, :]
    masks = np.concatenate([((p + d) < cc).astype(f32) for d in (0, 128, 256, 384)], axis=1)
    cst = np.ascontiguousarray(np.concatenate([ident, ones, negU, tri, negL, masks], axis=1))
    return dict(gu1=gu(inp['ffn1_w_gate'][0], inp['ffn1_w_up'][0]), d1=dn(inp['ffn1_w_down'][0]),
                gu2=gu(inp['ffn2_w_gate'][0], inp['ffn2_w_up'][0]), d2=dn(inp['ffn2_w_down'][0]),
                wfm=wfm, wtm=wtm, vecs=vecs, lbrep=lbrep, cst=cst)


_NC_CACHE = {}


def run(inp, nft):
    f32 = np.float32
    shared = _layouts(inp)
    if nft % 2 == 0:
        npre = nmain = nft // 2
    else:
        npre, nmain = 0, nft
    key = (npre, nmain)
    if key not in _NC_CACHE:
        _NC_CACHE[key] = Builder(npre, nmain).build()
    nc = _NC_CACHE[key]
    xp = np.asarray(inp['x_prompt'], f32)
    xs = np.asarray(inp['x_sample'], f32)
    ck = np.asarray(inp['cache_sb_k'], f32)
    cv = np.asarray(inp['cache_sb_v'], f32)
    st = np.asarray(inp['state_hgrn'], f32)
    meta = np.ascontiguousarray(np.asarray(inp['meta_tokens'], f32))
    B = xp.shape[0]
    H = nmain * NT
    in_maps = []
    for c in range(8):
        m = dict(shared)
        b, half = c // 2, c % 2
        vecs = shared['vecs'].copy()
        if npre == 0:
            m['xp'] = np.ascontiguousarray(xp[b])
            m['xpre'] = np.zeros((NT, D), f32)
            m['flag'] = np.ones((128, 512), f32)
            vecs[:, V_FA], vecs[:, V_FB], vecs[:, V_DEAD] = 0.0, 1.0, 0.0
        elif half == 0:
            m['xp'] = np.ascontiguousarray(xp[b, 0:H])
            m['xpre'] = np.zeros((npre * NT, D), f32)
            m['flag'] = np.zeros((128, 512), f32)
            vecs[:, V_FA], vecs[:, V_FB], vecs[:, V_DEAD] = 1.0, 0.0, -30000.0
        else:
            m['xp'] = np.ascontiguousarray(xp[b, H:2 * H])
            m['xpre'] = np.ascontiguousarray(xp[b, 0:H])
            m['flag'] = np.ones((128, 512), f32)
            vecs[:, V_FA], vecs[:, V_FB], vecs[:, V_DEAD] = 0.0, 1.0, 0.0
        m['vecs'] = vecs
        m['xs'] = np.ascontiguousarray(xs[c])
        m['meta'] = meta
        m['ck'] = np.ascontiguousarray(ck[0, c].reshape(LC, 512))
        m['cv'] = np.ascontiguousarray(cv[0, c].reshape(LC, 512))
        m['st'] = np.ascontiguousarray(st[0, c])
        in_maps.append(m)
    res = run_bass_kernel_spmd(nc, in_maps, core_ids=list(range(8)))
    r = res.results
    if npre == 0:
        y_prompt = np.stack([r[2 * b]['yp'] for b in range(B)])
        pk = np.stack([r[2 * b]['pk'] for b in range(B)])
        pv = np.stack([r[2 * b]['pv'] for b in range(B)])
        ph = np.stack([r[2 * b]['ph'] for b in range(B)])
    else:
        y_prompt = np.stack([np.concatenate([r[2 * b]['yp'], r[2 * b + 1]['yp']], axis=0) for b in range(B)])
        pk = np.stack([np.concatenate([r[2 * b]['pk'], r[2 * b + 1]['pk'][NMETA:]], axis=0) for b in range(B)])
        pv = np.stack([np.concatenate([r[2 * b]['pv'], r[2 * b + 1]['pv'][NMETA:]], axis=0) for b in range(B)])
        ph = np.stack([r[2 * b + 1]['ph'] for b in range(B)])
    y_prompt = y_prompt.astype(f32)
    L = pk.shape[1]
    pk = pk.reshape(B, L, 8, 64)[None].astype(f32)
    pv = pv.reshape(B, L, 8, 64)[None].astype(f32)
    ph = ph[None].astype(f32)
    y_sample = np.stack([r[c]['ys'] for c in range(8)]).astype(f32)
    sk = np.stack([r[c]['sk'].reshape(TS, 8, 64) for c in range(8)])[None].astype(f32)
    sv = np.stack([r[c]['sv'].reshape(TS, 8, 64) for c in range(8)])[None].astype(f32)
    sh = np.stack([r[c]['sh'] for c in range(8)])[None].astype(f32)
    return (y_prompt, y_sample, pk, pv, ph, sk, sv, sh)


def kernel(**inputs):
    nft = np.asarray(inputs['x_prompt']).shape[1] // NT
    return run(inputs, nft)
```

```python
import numpy as np
from contextlib import ExitStack
import concourse.bass as bass
import concourse.mybir as mybir
from concourse.bass_utils import run_bass_kernel_spmd

F32 = mybir.dt.float32
BF16 = mybir.dt.bfloat16
AF = mybir.ActivationFunctionType
ALU = mybir.AluOpType

D = 1024
DFF = 2816
NMETA = 16
TS = 32
LC = 2064
EPS = 1e-6
NT = 512
ENGS = ['pe', 'act', 'dve', 'pool', 'sp']

V_N1, V_NM, V_N2, V_NF, V_BGA, V_BGB, V_HGN, V_LB0, V_LB1, V_LBV, V_OML, V_FA, V_FB, V_DEAD = 0, 8, 16, 24, 32, 40, 48, 52, 56, 60, 64, 68, 69, 70
NVEC = 71


class Sched:
    def __init__(self):
        self.prog = {e: [] for e in ENGS}
        self.cnt = {e: 0 for e in ENGS}
        self.dma_cnt = {}
        self.last_w = {}
        self.readers = {}
        self.known = {e: {} for e in ENGS}
        self.fence = {}
        self.touched = set()
        self.nwaits = 0
        import os
        self.glimit = int(os.environ.get('KOPS', '100000000'))

    def _deps(self, reads, writes):
        need = {}

        def add(src, val):
            if need.get(src, 0) < val:
                need[src] = val
        for k in list(reads) + list(writes):
            if isinstance(k, tuple) and k[0] == 'SH' and k not in self.touched:
                self.touched.add(k)
                for s, v in self.fence.items():
                    add(s, v)
        for k in reads:
            lw = self.last_w.get(k)
            if lw:
                add(*lw)
        for k in writes:
            lw = self.last_w.get(k)
            if lw:
                add(*lw)
            for s, v in self.readers.get(k, {}).items():
                add(s, v)
        return need

    def _waits(self, eng, need):
        waits = []
        for src, val in need.items():
            if src == eng and eng == 'pe':
                continue
            if self.known[eng].get(src, 0) >= val:
                continue
            self.known[eng][src] = val
            waits.append((src, val))
        self.nwaits += len(waits)
        return waits

    def _mark(self, src, val, reads, writes):
        for k in writes:
            self.last_w[k] = (src, val)
            self.readers[k] = {}
        for k in reads:
            if k in writes:
                continue
            self.readers.setdefault(k, {})[src] = val

    def op(self, eng, fn, reads=(), writes=()):
        psr = [k for k in reads if isinstance(k, tuple) and k[0] == 'ps']
        if psr:
            reads = [k for k in reads if k not in psr]
            writes = list(writes) + [k for k in psr if k not in writes]
        self.gcount = getattr(self, 'gcount', 0) + 1
        if self.gcount > self.glimit:
            return
        import sys as _sys, os as _os
        if _os.environ.get('KTRACE'):
            lo, hi = [int(x) for x in _os.environ['KTRACE'].split(',')]
            if lo <= self.gcount <= hi:
                fr = _sys._getframe(2)
                print('OP', self.gcount, eng, 'line', fr.f_lineno, 'from', fr.f_back.f_lineno, 'reads', list(reads)[:3], 'writes', list(writes))
        need = self._deps(reads, writes)
        waits = self._waits(eng, need)
        self.cnt[eng] += 1
        self._mark(eng, self.cnt[eng], reads, writes)
        self.prog[eng].append(('op', waits, fn))

    def dma(self, q, out, in_, reads, writes, sem):
        self.gcount = getattr(self, 'gcount', 0) + 1
        if self.gcount > self.glimit:
            return
        import sys as _sys, os as _os
        if _os.environ.get('KTRACE'):
            lo, hi = [int(x) for x in _os.environ['KTRACE'].split(',')]
            if lo <= self.gcount <= hi:
                fr = _sys._getframe(1)
                print('DMA', self.gcount, q, 'line', fr.f_lineno, 'sem', sem, 'writes', list(writes))
        need = self._deps(reads, writes)
        waits = self._waits(q, need)
        self.dma_cnt[sem] = self.dma_cnt.get(sem, 0) + 16
        self._mark(sem, self.dma_cnt[sem], reads, writes)
        self.prog[q].append(('dma', waits, (out, in_, sem)))

    def phase_switch(self):
        f = dict(self.fence)

        def add(s, v):
            if f.get(s, 0) < v:
                f[s] = v
        for k in list(self.last_w.keys()):
            if isinstance(k, tuple) and k[0] == 'SH':
                add(*self.last_w[k])
                del self.last_w[k]
        for k in list(self.readers.keys()):
            if isinstance(k, tuple) and k[0] == 'SH':
                for s, v in self.readers[k].items():
                    add(s, v)
                del self.readers[k]
        self.fence = f
        self.touched = set()

    def final_wait(self, q):
        waits = [(s, v) for s, v in self.dma_cnt.items()]
        self.prog[q].append(('wait', waits, None))


class Builder:
    def __init__(self, npre, nmain):
        self.npre, self.nmain = npre, nmain
        nft = npre + nmain
        self.nft = nft
        self.FR = nft * NT
        self.KTW = max(NMETA + self.FR, NMETA + LC + TS + 128)
        self.NBLK = max(1 + 4 * nft, 19)
        self.s = Sched()
        self.ring_i = 0
        self.kv_i = 0
        self.xtok_i = 0

    def mm(self, out, lhsT, rhs, start, stop, reads, writes, sgc=False):
        self.s.op('pe', lambda e: e.matmul(out, lhsT=lhsT, rhs=rhs, start=start, stop=stop, skip_group_check=sgc), reads, writes)

    def tr(self, out, in_, n, reads, writes):
        ident = self.ident
        self.s.op('pe', lambda e: e.transpose(out=out, in_=in_, identity=ident[0:n, 0:n]), reads, writes)

    def act(self, out, in_, func, reads, writes, bias=None, scale=None):
        kw = {}
        if bias is not None:
            kw['bias'] = bias
        if scale is not None:
            kw['scale'] = scale
        self.s.op('act', lambda e: e.activation(out=out, in_=in_, func=func, **kw), reads, writes)

    def tt(self, eng, out, in0, in1, op, reads, writes):
        self.s.op(eng, lambda e: e.tensor_tensor(out=out, in0=in0, in1=in1, op=op), reads, writes)

    def tsc(self, eng, out, in0, s1, s2, op0, op1, reads, writes):
        if s2 is None:
            self.s.op(eng, lambda e: e.tensor_scalar(out=out, in0=in0, scalar1=s1, scalar2=None, op0=op0), reads, writes)
        else:
            self.s.op(eng, lambda e: e.tensor_scalar(out=out, in0=in0, scalar1=s1, scalar2=s2, op0=op0, op1=op1), reads, writes)

    def stt(self, eng, out, in0, scalar, in1, op0, op1, reads, writes):
        self.s.op(eng, lambda e: e.scalar_tensor_tensor(out=out, in0=in0, scalar=scalar, in1=in1, op0=op0, op1=op1), reads, writes)

    def cp(self, eng, out, in_, reads, writes):
        if eng == 'act':
            self.act(out, in_, AF.Copy, reads, writes)
        else:
            self.s.op(eng, lambda e: e.tensor_copy(out=out, in_=in_), reads, writes)

    def ring_load(self, dram_ap, nelem, rdkey):
        slot = self.ring_i % 4
        self.ring_i += 1
        out = self.ring[:, slot, 0:nelem]
        self.s.dma('sp', out, dram_ap, [rdkey], [('ring', slot)], 'ring%d' % slot)
        return slot

    def build(self):
        nc = bass.Bass("TRN2", target_bir_lowering=False)
        self.nc = nc
        nft, FR = self.nft, self.FR
        FM_ = self.nmain * NT
        FP_ = max(self.npre, 1) * NT
        dt_in = {}

        def din(name, shape):
            dt_in[name] = nc.dram_tensor(name, list(shape), F32, kind="ExternalInput").ap()
            return dt_in[name]

        def dout(name, shape):
            return nc.dram_tensor(name, list(shape), F32, kind="ExternalOutput").ap()
        self.xp = din("xp", [FM_, D])
        self.xpre = din("xpre", [FP_, D])
        self.flag_d = din("flag", [128, 512])
        self.xs = din("xs", [TS, D])
        self.meta = din("meta", [NMETA, D])
        self.ck = din("ck", [LC, 512])
        self.cv = din("cv", [LC, 512])
        self.st = din("st", [4, 128, 128])
        self.vecs_d = din("vecs", [128, NVEC])
        self.lbrep_d = din("lbrep", [128, 2, 512])
        self.cst_d = din("cst", [128, 5 * 128 + 4 * 512])
        self.w32 = {}
        self.wsz = {'gu1': 22 * 128 * 2048, 'gu2': 22 * 128 * 2048, 'd1': 2 * 8 * 128 * 1408, 'd2': 2 * 8 * 128 * 1408,
                    'wfm': 52 * 128 * 1024, 'wtm': 4 * 128 * 4096}
        for n, sz in self.wsz.items():
            self.w32[n] = din(n, [sz // 2048, 2048])
        self.yp = dout("yp", [FM_, D])
        self.ys = dout("ys", [TS, D])
        self.pk = dout("pk", [NMETA + FM_, 512])
        self.pv = dout("pv", [NMETA + FM_, 512])
        self.ph = dout("ph", [4, 128, 128])
        self.sk = dout("sk", [TS, 512])
        self.sv = dout("sv", [TS, 512])
        self.sh = dout("sh", [4, 128, 128])
        self.wbf = {n: nc.dram_tensor(n + "_bf", [sz], BF16).ap() for n, sz in self.wsz.items()}

        with ExitStack() as es:
            es.enter_context(nc.allow_low_precision("bf16 matmul operands, fp32 accumulation"))

            def sb(name, shape, dt):
                return es.enter_context(nc.sbuf_tensor(name, list(shape), dt))

            def ps(name):
                return es.enter_context(nc.psum_tensor(name, [128, 512], F32))
            self.ident = sb("ident", [128, 128], F32)
            self.onesb = sb("onesb", [128, 128], BF16)
            self.negU = sb("negU", [128, 128], BF16)
            self.negL = sb("negL", [128, 128], BF16)
            self.tri = sb("tri", [128, 128], F32)
            self.masks = sb("masks", [128, 4, 512], BF16)
            self.vecs = sb("vecs_s", [128, NVEC], F32)
            self.lb_t = sb("lb_t", [128, 2, 512], F32)
            self.xT = sb("xT", [128, 8, NT], F32)
            self.hT = sb("hT", [128, 8, NT + 64], BF16)
            self.rstd = sb("rstd", [128, NT], F32)
            self.lnt = sb("lnt", [128, NT], F32)
            self.KT = sb("KT", [128, 4, self.KTW], BF16)
            self.Vp = sb("Vp", [128, self.NBLK, 512], BF16)
            self.ring = sb("ring", [128, 4, 4096], BF16)
            self.S = sb("S", [128, 2, 4, 128], F32)
            self.Sb = sb("Sb", [128, 2, 4, 128], BF16)
            self.kvst = sb("kvst", [128, 2, 512], F32)
            self.Ssave = sb("Ssave", [128, 4, 128], F32)
            self.qT = sb("qT", [128, 4, NT], BF16)
            self.obT = sb("obT", [128, 4, NT], F32)
            self.oaT = sb("oaT", [128, 4, NT], BF16)
            SHW = 11008
            self.SH = sb("SH", [128, SHW], F32)
            self.bank = [ps("bank%d" % i) for i in range(8)]
            self.zerob = sb("zerob", [128, 128], BF16)
            self._views()
            import os
            if os.environ.get('KMEM'):
                print('sbuf bytes remaining', nc.sbuf_bytes_remaining)

            self._emit_all()

            sems = {}
            names = [e for e in ENGS if e != 'sp'] + sorted(self.s.dma_cnt.keys())
            for n in names:
                sems[n] = es.enter_context(nc.semaphore("s_" + n))
            block = es.enter_context(nc.Block())
            prog = self.s.prog

            def run(eng_obj, ename):
                for kind, waits, payload in prog[ename]:
                    for src, val in waits:
                        eng_obj.wait_ge(sems[src], val)
                    if kind == 'op':
                        ins = payload(eng_obj)
                        ins.then_inc(sems[ename], 1)
                    elif kind == 'dma':
                        out, in_, sem = payload
                        eng_obj.dma_start(out=out, in_=in_).then_inc(sems[sem], 16)

            @block.tensor
            def _(e):
                run(e, 'pe')

            @block.scalar
            def _(e):
                run(e, 'act')

            @block.vector
            def _(e):
                run(e, 'dve')

            @block.gpsimd
            def _(e):
                run(e, 'pool')

            @block.sync
            def _(e):
                run(e, 'sp')
        return nc

    def _views(self):
        SH = self.SH

        def v(off_b, nbytes, dt, pattern=None, **kw):
            a = SH[:, off_b // 4:(off_b + nbytes) // 4]
            if dt is BF16:
                a = a.bitcast(BF16)
            if pattern:
                a = a.rearrange(pattern, **kw)
            return a
        self.hid = v(0, 11264, BF16, "p (c n) -> p c n", c=11)
        self.sg = v(11264, 4096, F32, "p (c n) -> p c n", c=2)
        self.x_tok = v(15360, 8192, F32, "p (c n) -> p c n", c=2)
        self.omfT = v(0, 8192, F32, "p (c n) -> p c n", c=4)
        self.qhT = v(8192, 8192, F32, "p (c n) -> p c n", c=4)
        self.logf = v(16384, 4096, F32, "p (c n) -> p c n", c=2)
        self.omf_tm = v(20480, 4096, F32, "p (c n) -> p c n", c=2)
        self.iv = v(24576, 2048, BF16, "p (c n) -> p c n", c=2)
        self.eb = v(26624, 2048, F32, "p (a h t) -> p a h t", a=2, h=4)
        self.enb = v(28672, 2048, F32, "p (a h t) -> p a h t", a=2, h=4)
        self.qe = v(30720, 1024, BF16, "p (a h t) -> p a h t", a=2, h=4)
        self.ke = v(41984, 2048, BF16, "p (a h t) -> p a h t", a=2, h=4)
        self.enb_tm = v(32768, 4096, F32, "p (c n) -> p c n", c=2)
        self.ke_tm = v(36864, 2048, BF16, "p (c n) -> p c n", c=2)
        self.sc = v(38912, 1024, BF16, "p (a h t) -> p a h t", a=2, h=4)
        self.tmpS = v(39936, 2048, F32, "p (h t) -> p h t", h=4)
        self.e_ = v(0, 12288, F32, "p (c n) -> p c n", c=6)
        self.sp_ = v(12288, 6144, BF16, "p (c n) -> p c n", c=6)
        self.e2_ = v(18432, 6144, F32, "p (c n) -> p c n", c=3)
        self.at_ = v(24576, 6144, BF16, "p (c n) -> p c n", c=6)
        self.ckv = v(30720, 8192, F32, "p (c n) -> p c n", c=2)
        self.sgT = v(0, 8192, F32, "p (c n) -> p c n", c=4)
        self.gA = v(8192, 8192, BF16, "p (c n) -> p c n", c=8)
        self.gB = v(16384, 8192, BF16, "p (c n) -> p c n", c=8)
        self.t1 = v(24576, 2048, F32, "p (c n) -> p c n", c=1)
        self.t2 = v(26624, 2048, F32, "p (c n) -> p c n", c=1)
        self.mT = v(28672, 8192, BF16, "p (c n) -> p c n", c=8)
        self.obn = v(36864, 4096, BF16, "p (c n) -> p c n", c=4)

    def _emit_all(self):
        import os
        self.kstop = int(os.environ.get('KSTOP', '99'))
        self.prologue()
        self.tile(-1, 'meta')
        self.casts([('wfm', 1024, 2304), ('gu2', 0, 1408), ('d2', 0, 704), ('gu2', 1408, 1408), ('d2', 704, 704)], 6)
        for p in range(self.npre):
            self.tile(p, 'pre')
        if self.npre > 0:
            self.state_select()
        for m in range(self.nmain):
            self.tile(self.npre + m, 'main')
        self.tile(-1, 'sample')
        self.s.final_wait('sp')

    def prologue(self):
        s = self.s
        c = self.cst_d
        o = 0
        s.dma('pool', self.ident[:], c[:, o:o + 128], [], ['ident'], 'cst'); o += 128
        s.dma('pool', self.onesb[:], c[:, o:o + 128], [], ['onesb'], 'cst'); o += 128
        s.dma('pool', self.negU[:], c[:, o:o + 128], [], ['negU'], 'cst'); o += 128
        s.dma('pool', self.tri[:], c[:, o:o + 128], [], ['tri'], 'cst'); o += 128
        s.dma('pool', self.negL[:], c[:, o:o + 128], [], ['negL'], 'cst'); o += 128
        s.dma('pool', self.masks[:], c[:, o:o + 2048].rearrange("p (d n) -> p d n", d=4), [], ['masks'], 'cst'); o += 2048
        s.dma('pool', self.vecs[:], self.vecs_d, [], ['vecs'], 'cst')
        s.dma('pool', self.lb_t[:], self.lbrep_d, [], ['lb_t'], 'cst')
        allc = ['ident', 'onesb', 'negU', 'negL', 'tri', 'masks', 'vecs', 'lb_t']
        tot = s.dma_cnt['cst']
        for k in allc:
            s.last_w[k] = ('cst', tot)
        self.s.op('dve', lambda e: e.memset(self.hT[:, :, :], 0.0), [], [('hT', kc) for kc in range(8)])
        self.s.op('dve', lambda e: e.memset(self.KT[:, :, :], 0.0), [], [('KT', c) for c in range(4)])
        self.s.op('dve', lambda e: e.memset(self.xT[:, :, :], 0.0), [], [('xT', kc) for kc in range(8)])
        self.s.op('dve', lambda e: e.memset(self.SH[:, :], 0.0), [], [('SH', 'all')])
        self.s.op('dve', lambda e: e.memset(self.zerob[:, :], 0.0), [], ['zerob'])
        self.casts([('gu1', 0, 1408), ('d1', 0, 704), ('gu1', 1408, 1408), ('d1', 704, 704), ('wfm', 0, 1024), ('wtm', 0, 1024)], 0)
        vv = self.vecs
        self.tt('dve', vv[:, V_LBV:V_LBV + 4], vv[:, V_LB1:V_LB1 + 4], vv[:, V_LB0:V_LB0 + 4], ALU.subtract, ['vecs'], ['vecs'])
        self.act(vv[:, V_LBV:V_LBV + 4], vv[:, V_LBV:V_LBV + 4], AF.Sigmoid, ['vecs'], ['vecs'])
        self.tsc('dve', vv[:, V_OML:V_OML + 4], vv[:, V_LBV:V_LBV + 4], -1.0, 1.0, ALU.mult, ALU.add, ['vecs'], ['vecs'])
        lt = self.lb_t
        self.tt('dve', lt[:, 0, :], lt[:, 1, :], lt[:, 0, :], ALU.subtract, ['lb_t'], ['lb_t'])
        self.act(lt[:, 0, :], lt[:, 0, :], AF.Sigmoid, ['lb_t'], ['lb_t'])
        self.tsc('dve', lt[:, 1, :], lt[:, 0, :], -1.0, 1.0, ALU.mult, ALU.add, ['lb_t'], ['lb_t'])

    def casts(self, pieces, i0):
        for i, (n, r0, nr) in enumerate(pieces):
            dst = self.wbf[n].rearrange("(r c) -> r c", c=2048)[r0:r0 + nr, :]
            self.s.dma('pool', dst, self.w32[n][r0:r0 + nr, :], [], [('scr', n, r0)], 'cast%d' % (i0 + i))

    def scr_key(self, n, row2048):
        bounds = {'gu1': [0, 1408], 'gu2': [0, 1408], 'd1': [0, 704], 'd2': [0, 704], 'wfm': [0, 1024], 'wtm': [0]}[n]
        r0 = max(b for b in bounds if b <= row2048)
        return ('scr', n, r0)

    def load_gu(self, which, c):
        n = 'gu%d' % which
        ap = self.wbf[n].rearrange("(c p f) -> c p f", p=128, f=2048)[c]
        return self.ring_load(ap, 2048, self.scr_key(n, c * 128))

    def load_d(self, which, half, o):
        n = 'd%d' % which
        ap = self.wbf[n].rearrange("(h o p f) -> h o p f", h=2, o=8, p=128)[half, o]
        return self.ring_load(ap, 1408, self.scr_key(n, (half * 8 + o) * 128 * 1408 // 2048))

    def load_fm(self, c0, ncnk):
        slot = self.ring_i % 4
        self.ring_i += 1
        src = self.wbf['wfm'].rearrange("(c p f) -> c p f", p=128, f=1024)
        for i in range(ncnk):
            self.s.dma('sp', self.ring[:, slot, i * 1024:(i + 1) * 1024], src[c0 + i], [self.scr_key('wfm', c0 * 64)], [('ring', slot)], 'ring%d' % slot)
        return slot

    def load_tm(self, g):
        ap = self.wbf['wtm'].rearrange("(g p f) -> g p f", p=128, f=4096)[g]
        return self.ring_load(ap, 4096, ('scr', 'wtm', 0))

    def state_select(self):
        S0 = self.S[:, 0, :, :]
        self.tsc('dve', self.Ssave[:, :, :], self.Ssave[:, :, :], self.vecs[:, V_FA:V_FA + 1], None, ALU.mult, None, ['Ssave', 'vecs'], ['Ssave'])
        self.stt('dve', S0, S0, self.vecs[:, V_FB:V_FB + 1], self.Ssave[:, :, :], ALU.mult, ALU.add, [('S', 0), 'Ssave', 'vecs'], [('S', 0)])
        for hh in range(4):
            self.cp('pool', self.Sb[:, 0, hh, :], self.S[:, 0, hh, :], [('S', 0)], [('Sb', 0, hh)])

    def tile(self, f, mode='extra'):
        self.mode = mode
        self.m = f - self.npre if mode == 'main' else f
        if mode == 'meta':
            n = NMETA
            segs = [('meta', 0, NMETA)]
        elif mode == 'sample':
            n = TS
            segs = [('sample', 0, TS)]
        else:
            n = NT
            segs = [('frames', 0, NT)]
        self.n = n
        self.f = f
        self.segs = segs
        s = self.s
        ks = self.kstop if f < 0 else 99
        s.phase_switch()
        self.load_x()
        if mode in ('pre', 'meta'):
            self.norm(V_N1)
            self.ffn(1)
            self.norm(V_NM)
            s.phase_switch()
            self.w_in()
            self.hgrn()
            return
        if ks < 2: return
        self.norm(V_N1)
        if ks < 3: return
        self.ffn(1)
        if ks < 4: return
        self.norm(V_NM)
        s.phase_switch()
        self.w_in()
        if ks < 5 or ks == 41: return
        self.hgrn()
        if ks < 6: return
        s.phase_switch()
        self.attn()
        if ks < 7: return
        s.phase_switch()
        self.post()
        if ks < 8: return
        s.phase_switch()
        self.norm(V_N2)
        self.ffn(2)
        self.final()

    def load_x(self):
        n, f = self.n, self.f
        if f < 0:
            blocks = [(0, n)]
        else:
            blocks = [(j * 128, 128) for j in range(4)]
        for bi, (c0, nb) in enumerate(blocks):
            slot = self.xtok_i % 2
            self.xtok_i += 1
            xk = ('SH', 'x_tok', slot)
            if f < 0:
                self.s.dma('sp', self.x_tok[0:n, slot, :], self.xs if self.mode == 'sample' else self.meta, [], [xk], 'xin%d' % slot)
            else:
                r0 = self.m * NT + c0
                srcx = self.xpre if self.mode == 'pre' else self.xp
                self.s.dma('sp', self.x_tok[:, slot, :], srcx[r0:r0 + 128, :], [], [xk], 'xin%d' % slot)
            for kc in range(8):
                bk = self.bank[kc % 2]
                self.tr(bk[:, 0:nb], self.x_tok[0:nb, slot, kc * 128:(kc + 1) * 128], nb, [xk, 'ident'], [('ps', kc % 2)])
                eng = 'dve' if kc % 2 == 0 else 'act'
                self.cp(eng, self.xT[:, kc, c0:c0 + nb], bk[:, 0:nb], [('ps', kc % 2)], [('xT', kc)])

    def norm(self, gcol, inplace=False):
        n = self.n
        for kc in range(8):
            self.act(self.hT[:, kc, 0:n], self.xT[:, kc, 0:n], AF.Square, [('xT', kc)], [('hT', kc)])
        bk = self.bank[7]
        for kc in range(8):
            self.mm(bk[:, 0:n], self.onesb[:, :], self.hT[:, kc, 0:n], kc == 0, kc == 7, [('hT', kc), 'onesb'], [('ps', 7)])
        self.act(self.lnt[:, 0:n], bk[:, 0:n], AF.Ln, [('ps', 7)], ['lnt'], bias=EPS, scale=1.0 / D)
        self.act(self.rstd[:, 0:n], self.lnt[:, 0:n], AF.Exp, ['lnt'], ['rstd'], scale=-0.5)
        for kc in range(8):
            eng = 'dve'
            if inplace:
                self.stt(eng, self.xT[:, kc, 0:n], self.xT[:, kc, 0:n], self.vecs[:, gcol + kc:gcol + kc + 1], self.rstd[:, 0:n],
                         ALU.mult, ALU.mult, [('xT', kc), 'rstd', 'vecs'], [('xT', kc)])
            else:
                self.stt(eng, self.hT[:, kc, 0:n], self.xT[:, kc, 0:n], self.vecs[:, gcol + kc:gcol + kc + 1], self.rstd[:, 0:n],
                         ALU.mult, ALU.mult, [('xT', kc), 'rstd', 'vecs'], [('hT', kc)])

    def ffn(self, which):
        n = self.n
        hk = [('hT', kc) for kc in range(8)]
        it = 0
        gun, dnn = 'gu%d' % which, 'd%d' % which
        gsrc = self.wbf[gun].rearrange("(c p f) -> p c f", p=128, f=2048)
        dsrc = self.wbf[dnn].rearrange("(h o p f) -> h p o f", h=2, o=8, p=128)
        for half in range(2):
            for cc0 in range(0, 11, 2):
                ncnk = min(2, 11 - cc0)
                c = half * 11 + cc0
                slot = self.ring_i % 4
                self.ring_i += 1
                self.s.dma('sp', self.ring[:, slot, 0:ncnk * 2048].rearrange("p (c f) -> p c f", c=ncnk), gsrc[:, c:c + ncnk, :],
                           [self.scr_key(gun, c * 128)], [('ring', slot)], 'ring%d' % slot)
                for ci in range(ncnk):
                    cc = cc0 + ci
                    w = self.ring[:, slot, ci * 2048:(ci + 1) * 2048].rearrange("p (g k n) -> p g k n", g=2, k=8)
                    gb, ub = it % 2, 2 + it % 2
                    for kc in range(8):
                        self.mm(self.bank[gb][:, 0:n], w[:, 0, kc, :], self.hT[:, kc, 0:n], kc == 0, kc == 7, hk + [('ring', slot)], [('ps', gb)])
                    for kc in range(8):
                        self.mm(self.bank[ub][:, 0:n], w[:, 1, kc, :], self.hT[:, kc, 0:n], kc == 0, kc == 7, hk + [('ring', slot)], [('ps', ub)])
                    sgk = ('SH', 'sg', it % 2)
                    self.act(self.sg[:, it % 2, 0:n], self.bank[gb][:, 0:n], AF.Silu, [('ps', gb)], [sgk])
                    self.tt('dve', self.hid[:, cc, 0:n], self.sg[:, it % 2, 0:n], self.bank[ub][:, 0:n], ALU.mult, [sgk, ('ps', ub)], [('SH', 'hid', cc)])
                    it += 1
            for o0 in range(0, 8, 2):
                slot = self.ring_i % 4
                self.ring_i += 1
                self.s.dma('sp', self.ring[:, slot, 0:2 * 1408].rearrange("p (c f) -> p c f", c=2), dsrc[half, :, o0:o0 + 2, :],
                           [self.scr_key(dnn, (half * 8 + o0) * 128 * 1408 // 2048)], [('ring', slot)], 'ring%d' % slot)
                for oi in range(2):
                    o = o0 + oi
                    w = self.ring[:, slot, oi * 1408:(oi + 1) * 1408].rearrange("p (c n) -> p c n", c=11)
                    db = 4 + o % 2
                    for cc in range(11):
                        self.mm(self.bank[db][:, 0:n], w[:, cc, :], self.hid[:, cc, 0:n], cc == 0, cc == 10, [('SH', 'hid', cc), ('ring', slot)], [('ps', db)])
                    self.stt('dve', self.xT[:, o, 0:n], self.bank[db][:, 0:n], 0.5, self.xT[:, o, 0:n], ALU.mult, ALU.add, [('ps', db), ('xT', o)], [('xT', o)])

    def kcol(self, seg):
        if seg == 'sample':
            return NMETA + LC
        if seg == 'meta':
            return 0
        return NMETA + self.f * NT

    def w_in(self):
        n = self.n
        hk = [('hT', kc) for kc in range(8)]
        bi = 0
        for g4 in ([1] if self.mode in ('pre', 'meta') else range(4)):
            slot = self.load_fm(g4 * 4, 4)
            w = self.ring[:, slot, 0:4096].rearrange("p (c k n) -> p c k n", c=4, k=8)
            for ci in range(4):
                b = bi % 4
                bi += 1
                bk = self.bank[b]
                for kc in range(8):
                    self.mm(bk[:, 0:n], w[:, ci, kc, :], self.hT[:, kc, 0:n], kc == 0, kc == 7, hk + [('ring', slot)], [('ps', b)])
                if g4 == 0:
                    self.act(self.qT[:, ci, 0:n], bk[:, 0:n], AF.Copy, [('ps', b)], [('qT', ci)], scale=0.125)
                elif g4 == 1:
                    for (sname, c0, ns) in self.segs:
                        kc0 = self.kcol(sname)
                        self.cp('dve', self.KT[:, ci, kc0:kc0 + ns], bk[:, c0:c0 + ns], [('ps', b)], [('KT', ci)])
                elif g4 == 2:
                    self.act(self.lnt[:, 0:n], bk[:, 0:n], AF.Sigmoid, [('ps', b)], ['lnt'], scale=-1.0)
                    self.tsc('dve', self.omfT[:, ci, 0:n], self.lnt[:, 0:n], self.vecs[:, V_OML + ci:V_OML + ci + 1], None, ALU.mult, None,
                             ['lnt', 'vecs'], [('SH', 'omfT', ci)])
                else:
                    self.cp('act', self.qhT[:, ci, 0:n], bk[:, 0:n], [('ps', b)], [('SH', 'qhT', ci)])
        if self.kstop == 41:
            return
        if self.mode == 'sample':
            blocks = [('sample', 0, TS, self.sk, self.sv, 0, 18)]
        elif self.mode == 'meta':
            blocks = [('meta', 0, NMETA, self.pk, self.pv, 0, 0)]
        else:
            blocks = [('frames', j * 128, 128, self.pk, self.pv, NMETA + self.m * NT + j * 128, 1 + 4 * self.f + j) for j in range(4)]
        for g in ([1] if self.mode == 'pre' else range(2)):
            slot = self.load_tm(g)
            w = self.ring[:, slot, 0:4096].rearrange("p (k n) -> p k n", k=8)
            for (sname, c0, nb, okd, ovd, r0, vblk) in blocks:
                b = 4 + bi % 4
                bi += 1
                bk = self.bank[b]
                for kc in range(8):
                    self.mm(bk[:, :], self.hT[:, kc, c0:c0 + 128], w[:, kc, :], kc == 0, kc == 7, hk + [('ring', slot)], [('ps', b)])
                if self.mode == 'pre':
                    self.cp('act', self.Vp[0:nb, vblk, :], bk[0:nb, :], [('ps', b)], [('Vp', vblk)])
                    continue
                ks = self.kv_i % 2
                self.kv_i += 1
                self.cp('dve', self.kvst[0:nb, ks, :], bk[0:nb, :], [('ps', b)], [('kvst', ks)])
                dst = (okd if g == 0 else ovd)[r0:r0 + nb, :]
                import os
                if not os.environ.get('NOKV'):
                    self.s.dma(os.environ.get('KVQ', 'pool'), dst, self.kvst[0:nb, ks, :], [('kvst', ks)], [], 'kvo%d' % ks)
                if g == 1:
                    self.cp('act', self.Vp[0:nb, vblk, :], bk[0:nb, :], [('ps', b)], [('Vp', vblk)])

    def hgrn(self):
        s = self.s
        hk = [('hT', kc) for kc in range(8)]
        slz = self.load_tm(2)
        sli = self.load_tm(3)
        wz = self.ring[:, slz, 0:4096].rearrange("p (k n) -> p k n", k=8)
        wi = self.ring[:, sli, 0:4096].rearrange("p (k n) -> p k n", k=8)
        if self.mode == 'sample':
            chunks = [(0, TS, 1)]
            self.s.dma('pool', self.S[:, 1, :, :], self.st.rearrange("h k v -> k h v"), [], [('S', 1)], 'stin')
            for hh in range(4):
                self.cp('pool', self.Sb[:, 1, hh, :], self.S[:, 1, hh, :], [('S', 1)], [('Sb', 1, hh)])
        elif self.mode == 'meta':
            chunks = [(0, NMETA, 0)]
            self.s.op('dve', lambda e: e.memset(self.S[:, 0, :, :], 0.0), [], [('S', 0)])
            self.s.op('dve', lambda e: e.memset(self.Sb[:, 0, :, :], 0.0), [], [('Sb', 0, hh) for hh in range(4)])
        else:
            chunks = [(i * 64, 64, 0) for i in range(8)]
        def _chunk(ci, c0, T, si):
            par = ci % 2
            bA, bB, bC, bD, bE, bF = self.bank[0], self.bank[1], self.bank[2 + par], self.bank[4], self.bank[5], self.bank[6 + par]
            kC, kF = ('ps', 2 + par), ('ps', 6 + par)
            for kc in range(8):
                self.mm(bA[:, :], self.hT[:, kc, c0:c0 + 128], wz[:, kc, :], kc == 0, kc == 7, hk + [('ring', slz)], [('ps', 0)])
            for kc in range(8):
                self.mm(bB[:, :], self.hT[:, kc, c0:c0 + 128], wi[:, kc, :], kc == 0, kc == 7, hk + [('ring', sli)], [('ps', 1)])
            k_omf, k_logf, k_iv = ('SH', 'omf_tm', par), ('SH', 'logf', par), ('SH', 'iv', par)
            self.act(self.omf_tm[0:T, par, :], bA[0:T, :], AF.Sigmoid, [('ps', 0)], [k_omf], scale=-1.0)
            self.tt('dve', self.omf_tm[0:T, par, :], self.omf_tm[0:T, par, :], self.lb_t[0:T, 1, :], ALU.mult, [k_omf, 'lb_t'], [k_omf])
            self.act(self.logf[0:T, par, :], self.omf_tm[0:T, par, :], AF.Ln, [k_omf], [k_logf], bias=1.0, scale=-1.0)
            self.cp('act', self.iv[0:T, par, :], bB[0:T, :], [('ps', 1)], [k_iv])
            cheap = self.mode in ('pre', 'meta')
            if not cheap:
                for hh in range(4):
                    self.mm(bC[:, hh * 64:hh * 64 + T], self.logf[0:T, par, hh * 128:(hh + 1) * 128], self.tri[0:T, 0:T], True, True,
                            [k_logf, 'tri'], [kC])
            self.mm(bD[:, :], self.tri[0:T, 0:128], self.logf[0:T, par, :], True, True, [k_logf, 'tri'], [('ps', 4)])
            k_enbt, k_ket = ('SH', 'enb_tm', par), ('SH', 'ke_tm', par)
            self.act(self.enb_tm[0:T, par, :], bD[0:T, :], AF.Exp, [('ps', 4)], [k_enbt], scale=-1.0)
            self.tt('dve', self.ke_tm[0:T, par, :], self.omf_tm[0:T, par, :], self.enb_tm[0:T, par, :], ALU.mult, [k_omf, k_enbt], [k_ket])
            if cheap:
                for hh in range(4):
                    self.mm(bC[:, hh * 64:hh * 64 + 2], self.logf[0:T, par, hh * 128:(hh + 1) * 128], self.tri[0:T, 126:128], True, True,
                            [k_logf, 'tri'], [kC])
                for hh in range(4):
                    self.act(self.eb[:, par, hh, T - 1:T], bC[:, hh * 64:hh * 64 + 1], AF.Exp, [kC], [('SH', 'eb', par, hh)])
                yield
            else:
                for hh in range(4):
                    k_eb, k_enb, k_qe, k_ke = ('SH', 'eb', par, hh), ('SH', 'enb', par, hh), ('SH', 'qe', par, hh), ('SH', 'ke', par, hh)
                    self.act(self.eb[:, par, hh, 0:T], bC[:, hh * 64:hh * 64 + T], AF.Exp, [kC], [k_eb])
                    self.act(self.enb[:, par, hh, 0:T], bC[:, hh * 64:hh * 64 + T], AF.Exp, [kC], [k_enb], scale=-1.0)
                    self.tt('dve', self.qe[:, par, hh, 0:T], self.qhT[:, hh, c0:c0 + T], self.eb[:, par, hh, 0:T], ALU.mult,
                            [('SH', 'qhT', hh), k_eb], [k_qe])
                    self.tt('pool', self.ke[:, par, hh, 0:T], self.omfT[:, hh, c0:c0 + T], self.enb[:, par, hh, 0:T], ALU.mult,
                            [('SH', 'omfT', hh), k_enb], [k_ke])
                yield
                for hh in range(4):
                    k_qe, k_ke = ('SH', 'qe', par, hh), ('SH', 'ke', par, hh)
                    self.mm(bC[:, 256 + hh * 64:256 + hh * 64 + T], self.ke[:, par, hh, 0:128], self.qe[:, par, hh, 0:T], True, True,
                            [k_qe, k_ke], [kC])
                for hh in range(4):
                    k_sc = ('SH', 'sc', par, hh)
                    self.tt('dve', self.sc[0:T, par, hh, 0:T], bC[0:T, 256 + hh * 64:256 + hh * 64 + T], self.tri[0:T, 0:T], ALU.mult,
                            [kC, 'tri'], [k_sc])
                for hh in range(4):
                    k_sc, k_qe = ('SH', 'sc', par, hh), ('SH', 'qe', par, hh)
                    self.mm(bE[:, hh * 64:hh * 64 + T], self.iv[0:T, par, hh * 128:(hh + 1) * 128], self.sc[0:T, par, hh, 0:T], True, False,
                            [k_iv, k_sc], [('ps', 5)])
                    self.mm(bE[:, hh * 64:hh * 64 + T], self.Sb[:, si, hh, :], self.qe[:, par, hh, 0:T], False, True,
                            [('Sb', si, hh), k_qe], [('ps', 5)])
                for hh in range(4):
                    self.cp('act' if hh % 2 == 0 else 'dve', self.obT[:, hh, c0:c0 + T], bE[:, hh * 64:hh * 64 + T], [('ps', 5)], [('obT', hh)])
            for hh in range(4):
                self.mm(bF[:, hh * 128:(hh + 1) * 128], self.ke_tm[0:T, par, hh * 128:(hh + 1) * 128], self.iv[0:T, par, hh * 128:(hh + 1) * 128],
                        True, True, [k_ket, k_iv], [kF])
            for hh in range(4):
                k_eb = ('SH', 'eb', par, hh)
                ebl = self.eb[:, par, hh, T - 1:T]
                self.tsc('dve', self.tmpS[:, hh, :], self.S[:, si, hh, :], ebl, None, ALU.mult, None, [('S', si), k_eb], [('SH', 'tmpS', hh)])
                self.stt('dve', self.S[:, si, hh, :], bF[:, hh * 128:(hh + 1) * 128], ebl, self.tmpS[:, hh, :], ALU.mult, ALU.add,
                         [kF, k_eb, ('SH', 'tmpS', hh)], [('S', si)])
                self.cp('pool', self.Sb[:, si, hh, :], self.S[:, si, hh, :], [('S', si)], [('Sb', si, hh)])
            if self.f < 0 and si == 1:
                self.s.dma('pool', self.sh.rearrange("h k v -> k h v"), self.S[:, 1, :, :], [('S', 1)], [], 'sout')
            if self.f < 0 and si == 0 and self.npre > 0:
                self.cp('pool', self.Ssave[:, :, :], self.S[:, 0, :, :], [('S', 0)], ['Ssave'])
            if self.mode == 'main' and self.f == self.nft - 1 and ci == len(chunks) - 1:
                self.s.dma('pool', self.ph.rearrange("h k v -> k h v"), self.S[:, 0, :, :], [('S', 0)], [], 'sout')

        gens = [_chunk(ci, *ch) for ci, ch in enumerate(chunks)]
        next(gens[0])
        for i in range(len(gens)):
            if i + 1 < len(gens):
                next(gens[i + 1])
            for _ in gens[i]:
                pass

    def attn(self):
        f = self.f
        if f < 0:
            nb_c = (LC + 127) // 128
            for b in range(nb_c):
                r0 = b * 128
                kn = min(128, LC - r0)
                slot = b % 2
                ck_key = ('SH', 'ckv', slot)
                self.s.dma('sp', self.ckv[0:kn, slot, 0:512], self.ck[r0:r0 + kn, :], [], [ck_key], 'ckin%d' % slot)
                self.s.dma('sp', self.ckv[0:kn, slot, 512:1024], self.cv[r0:r0 + kn, :], [], [ck_key], 'ckin%d' % slot)
                bk = self.bank[b % 2]
                for ch in range(4):
                    self.tr(bk[:, ch * 128:ch * 128 + kn], self.ckv[0:kn, slot, ch * 128:(ch + 1) * 128], kn, [ck_key, 'ident'], [('ps', b % 2)])
                for ch in range(4):
                    self.cp('dve' if ch % 2 == 0 else 'act', self.KT[:, ch, NMETA + r0:NMETA + r0 + kn], bk[:, ch * 128:ch * 128 + kn],
                            [('ps', b % 2)], [('KT', ch)])
                self.cp('pool', self.Vp[0:kn, 1 + b, :], self.ckv[0:kn, slot, 512:1024], [ck_key], [('Vp', 1 + b)])
            blocks = [(18, TS, NMETA + LC, 0)]
            blocks.append((1 + nb_c - 1, LC - 128 * (nb_c - 1), NMETA + 128 * (nb_c - 1), None))
            for b in range(nb_c - 2, -1, -1):
                blocks.append((1 + b, 128, NMETA + 128 * b, None))
            self.attn_job(0, TS, blocks)
        else:
            blocks = []
            for m in range(3, -1, -1):
                fb = 4 * f + m
                blocks.append((1 + fb, 128, NMETA + 128 * fb, 128 * m))
            for fb in range(4 * f - 1, -1, -1):
                blocks.append((1 + fb, 128, NMETA + 128 * fb, 'flag' if fb < 4 * self.npre else None))
            blocks.append((0, NMETA, 0, None))
            self.attn_job(0, NT, blocks)

    def attn_job(self, q0, nq, blocks):
        nblk = len(blocks)

        def col0(md):
            return md if (md is not None and md != 'flag') else 0
        for grp in ([0, 1, 2], [3, 4, 5], [6, 7]):
            S_ = len(grp)
            nseq = nblk * S_

            def emitZ(idx):
                k, si = divmod(idx, S_)
                h = grp[si]
                vblk, kn, kcol, md = blocks[k]
                cs = col0(md)
                ch, pb = h // 2, 64 * (h % 2)
                zb = idx % 2
                self.mm(self.bank[zb][:, cs:nq], self.KT[pb:pb + 64, ch, kcol:kcol + 128], self.qT[pb:pb + 64, ch, q0 + cs:q0 + nq], True, True,
                        [('KT', ch), ('qT', ch)], [('ps', zb)])
            for idx in range(min(2, nseq)):
                emitZ(idx)
            for k in range(nblk):
                vblk, kn, kcol, md = blocks[k]
                cs = col0(md)
                par = k % 2
                for si, h in enumerate(grp):
                    idx = k * S_ + si
                    zb = idx % 2
                    ke_, ks_ = ('SH', 'e', si, par), ('SH', 'sp', si, par)
                    if md == 'flag':
                        self.act(self.e_[0:kn, si * 2 + par, cs:nq], self.bank[zb][0:kn, cs:nq], AF.Exp, [('ps', zb), 'vecs'], [ke_],
                                 bias=self.vecs[0:kn, V_DEAD:V_DEAD + 1])
                    else:
                        self.act(self.e_[0:kn, si * 2 + par, cs:nq], self.bank[zb][0:kn, cs:nq], AF.Exp, [('ps', zb)], [ke_])
                    if idx + 2 < nseq:
                        emitZ(idx + 2)
                    if md is not None and md != 'flag':
                        self.tt('dve', self.e_[0:kn, si * 2 + par, cs:nq], self.e_[0:kn, si * 2 + par, cs:nq], self.masks[0:kn, md // 128, cs:nq],
                                ALU.mult, [ke_, 'masks'], [ke_])
                    self.act(self.sp_[0:kn, si * 2 + par, cs:nq], self.e_[0:kn, si * 2 + par, cs:nq], AF.Ln, [ke_], [ks_], bias=1.0)
                for si, h in enumerate(grp):
                    ks_ = ('SH', 'sp', si, par)
                    self.mm(self.bank[2 + si][:, cs:nq], self.negU[0:kn, :], self.sp_[0:kn, si * 2 + par, cs:nq], k == 0, False,
                            [ks_, 'negU'], [('ps', 2 + si)], sgc=True)
                for si, h in enumerate(grp):
                    ke_, k2_, ka_ = ('SH', 'e', si, par), ('SH', 'e2', si), ('SH', 'at', si, par)
                    self.act(self.e2_[0:kn, si, cs:nq], self.bank[2 + si][0:kn, cs:nq], AF.Exp, [('ps', 2 + si)], [k2_])
                    self.tt('dve', self.at_[0:kn, si * 2 + par, cs:nq], self.e2_[0:kn, si, cs:nq], self.e_[0:kn, si * 2 + par, cs:nq], ALU.mult,
                            [k2_, ke_], [ka_])
                for si, h in enumerate(grp):
                    ks_, ka_ = ('SH', 'sp', si, par), ('SH', 'at', si, par)
                    self.mm(self.bank[2 + si][:, cs:nq], self.negL[0:kn, :], self.sp_[0:kn, si * 2 + par, cs:nq], False, k == nblk - 1,
                            [ks_, 'negL'], [('ps', 2 + si)], sgc=True)
                    vlo = h * 64 if h % 2 == 0 else (h - 1) * 64
                    self.mm(self.bank[5 + si][:, cs:nq], self.Vp[0:kn, vblk, vlo:vlo + 128], self.at_[0:kn, si * 2 + par, cs:nq], k == 0, k == nblk - 1,
                            [ka_, ('Vp', vblk)], [('ps', 5 + si)], sgc=True)
            for si, h in enumerate(grp):
                ch, pb = h // 2, 64 * (h % 2)
                self.cp('dve' if si % 2 == 0 else 'act', self.oaT[pb:pb + 64, ch, q0:q0 + nq], self.bank[5 + si][pb:pb + 64, 0:nq],
                        [('ps', 5 + si)], [('oaT', ch, pb)])

    def post(self):
        n = self.n
        hk = [('hT', kc) for kc in range(8)]
        bi = 0
        for ld in range(5):
            slot = self.load_fm(16 + ld * 4, 4)
            w = self.ring[:, slot, 0:4096].rearrange("p (c k n) -> p c k n", c=4, k=8)
            for ci in range(4):
                cidx = ld * 4 + ci
                b = bi % 4
                bi += 1
                bk = self.bank[b]
                for kc in range(8):
                    self.mm(bk[:, 0:n], w[:, ci, kc, :], self.hT[:, kc, 0:n], kc == 0, kc == 7, hk + [('ring', slot)], [('ps', b)])
                if cidx < 4:
                    self.act(self.sgT[:, cidx, 0:n], bk[:, 0:n], AF.Silu, [('ps', b)], [('SH', 'sgT', cidx)])
                elif cidx < 12:
                    o = cidx - 4
                    self.act(self.gA[:, o, 0:n], bk[:, 0:n], AF.Sigmoid, [('ps', b), 'vecs'], [('SH', 'gA', o)], bias=self.vecs[:, V_BGA + o:V_BGA + o + 1])
                else:
                    o = cidx - 12
                    self.act(self.gB[:, o, 0:n], bk[:, 0:n], AF.Sigmoid, [('ps', b), 'vecs'], [('SH', 'gB', o)], bias=self.vecs[:, V_BGB + o:V_BGB + o + 1])
        for hh in range(4):
            kq = ('SH', 'mT', hh)
            self.act(self.mT[:, hh, 0:n], self.obT[:, hh, 0:n], AF.Square, [('obT', hh)], [kq])
            self.mm(self.bank[4][:, 0:n], self.onesb[:, :], self.mT[:, hh, 0:n], True, True, [kq, 'onesb'], [('ps', 4)])
            self.act(self.lnt[:, 0:n], self.bank[4][:, 0:n], AF.Ln, [('ps', 4)], ['lnt'], bias=EPS, scale=1.0 / 128)
            self.act(self.rstd[:, 0:n], self.lnt[:, 0:n], AF.Exp, ['lnt'], ['rstd'], scale=-0.5)
            k1 = ('SH', 't1', 0)
            self.stt('dve', self.t1[:, 0, 0:n], self.obT[:, hh, 0:n], self.vecs[:, V_HGN + hh:V_HGN + hh + 1], self.rstd[:, 0:n], ALU.mult, ALU.mult,
                     [('obT', hh), 'vecs', 'rstd'], [k1])
            self.tt('pool', self.obn[:, hh, 0:n], self.t1[:, 0, 0:n], self.sgT[:, hh, 0:n], ALU.mult, [k1, ('SH', 'sgT', hh)], [('SH', 'obn', hh)])
        oak = [('oaT', c, pb) for c in range(4) for pb in (0, 64)]
        obk = [('SH', 'obn', c) for c in range(4)]
        for ld in range(2):
            slot = self.load_fm(36 + ld * 4, 4)
            w = self.ring[:, slot, 0:4096].rearrange("p (c k n) -> p c k n", c=4, k=8)
            for ci in range(4):
                o = ld * 4 + ci
                ba, bb = o % 2, 2 + o % 2
                for c in range(4):
                    self.mm(self.bank[ba][:, 0:n], w[:, ci, c, :], self.oaT[:, c, 0:n], c == 0, c == 3, oak + [('ring', slot)], [('ps', ba)])
                for c in range(4):
                    self.mm(self.bank[bb][:, 0:n], w[:, ci, 4 + c, :], self.obn[:, c, 0:n], c == 0, c == 3, obk + [('ring', slot)], [('ps', bb)])
                k1, k2 = ('SH', 't1', 0), ('SH', 't2', 0)
                self.tt('dve', self.t1[:, 0, 0:n], self.gA[:, o, 0:n], self.bank[ba][:, 0:n], ALU.mult, [('SH', 'gA', o), ('ps', ba)], [k1])
                self.tt('dve', self.t2[:, 0, 0:n], self.gB[:, o, 0:n], self.bank[bb][:, 0:n], ALU.mult, [('SH', 'gB', o), ('ps', bb)], [k2])
                self.tt('pool', self.mT[:, o, 0:n], self.t1[:, 0, 0:n], self.t2[:, 0, 0:n], ALU.add, [k1, k2], [('SH', 'mT', o)])
        mk = [('SH', 'mT', c) for c in range(8)]
        for ld in range(2):
            slot = self.load_fm(44 + ld * 4, 4)
            w = self.ring[:, slot, 0:4096].rearrange("p (c k n) -> p c k n", c=4, k=8)
            for ci in range(4):
                o = ld * 4 + ci
                b = 4 + o % 2
                for c in range(8):
                    self.mm(self.bank[b][:, 0:n], w[:, ci, c, :], self.mT[:, c, 0:n], c == 0, c == 7, mk + [('ring', slot)], [('ps', b)])
                self.tt('dve', self.xT[:, o, 0:n], self.xT[:, o, 0:n], self.bank[b][:, 0:n], ALU.add, [('xT', o), ('ps', b)], [('xT', o)])

    def final(self):
        n, f = self.n, self.f
        self.norm(V_NF, inplace=True)
        if f < 0:
            blocks = [(0, n, self.ys, 0, TS)]
            assert self.mode == 'sample'
        else:
            blocks = [(j * 128, 128, self.yp, self.m * NT + j * 128, 128) for j in range(4)]
        for (c0, nb, dst, r0, nout) in blocks:
            slot = self.xtok_i % 2
            self.xtok_i += 1
            xk = ('SH', 'x_tok', slot)
            for half in range(2):
                bk = self.bank[half]
                for q in range(4):
                    kc = half * 4 + q
                    self.tr(bk[:, q * 128:(q + 1) * 128], self.xT[:, kc, c0:c0 + 128], 128, [('xT', kc), 'ident'], [('ps', half)])
                self.cp('dve' if half == 0 else 'act', self.x_tok[0:nb, slot, half * 512:(half + 1) * 512], bk[0:nb, :], [('ps', half)], [xk])
            self.s.dma('pool', dst[r0:r0 + nout, :], self.x_tok[0:nout, slot, :], [xk], [], 'yout%d' % slot)


def _fix_kt_keys(b):
    pass


def _layouts(inp):
    f32 = np.float32

    def fm_vec(v):
        v = np.asarray(v, f32).reshape(-1, 128)
        return v.T

    def gu(wg, wu):
        a = np.asarray(wg, f32).reshape(8, 128, 22, 128).transpose(2, 1, 0, 3)
        b = np.asarray(wu, f32).reshape(8, 128, 22, 128).transpose(2, 1, 0, 3)
        return np.ascontiguousarray(np.stack([a, b], axis=2)).reshape(-1, 2048)

    def dn(wd):
        a = np.asarray(wd, f32).reshape(2, 11, 128, 8, 128).transpose(0, 3, 2, 1, 4)
        return np.ascontiguousarray(a).reshape(-1, 2048)
    w_in = np.asarray(inp['w_in'][0], f32)
    cols = np.concatenate([np.arange(0, 512), np.arange(512, 1024), np.arange(1536, 2048), np.arange(2560, 3072),
                           np.arange(3072, 3584), np.arange(3584, 4608), np.arange(4608, 5632)])
    fm = w_in[:, cols].reshape(8, 128, 36, 128).transpose(2, 1, 0, 3)
    wa = np.asarray(inp['w_branch_a'][0], f32).reshape(4, 128, 8, 128).transpose(2, 1, 0, 3)
    wb = np.asarray(inp['w_branch_b'][0], f32).reshape(4, 128, 8, 128).transpose(2, 1, 0, 3)
    wab = np.concatenate([wa, wb], axis=2)
    wo = np.asarray(inp['w_out'][0], f32).reshape(8, 128, 8, 128).transpose(2, 1, 0, 3)
    wfm = np.ascontiguousarray(np.concatenate([fm, wab, wo], axis=0)).reshape(-1, 2048)
    wtm = np.ascontiguousarray(w_in[:, 512:2560].reshape(8, 128, 4, 512).transpose(2, 1, 0, 3)).reshape(-1, 2048)
    vecs = np.zeros((128, NVEC), f32)
    vecs[:, V_N1:V_N1 + 8] = fm_vec(inp['ffn1_norm'][0])
    vecs[:, V_NM:V_NM + 8] = fm_vec(inp['mix_norm'][0])
    vecs[:, V_N2:V_N2 + 8] = fm_vec(inp['ffn2_norm'][0])
    vecs[:, V_NF:V_NF + 8] = fm_vec(inp['final_norm'])
    vecs[:, V_BGA:V_BGA + 16] = fm_vec(inp['b_gate'][0])
    vecs[:, V_HGN:V_HGN + 4] = fm_vec(inp['hg_out_norm'][0])
    vecs[:, V_LB0:V_LB0 + 4] = fm_vec(inp['hg_lb_logits'][0])
    vecs[:, V_LB1:V_LB1 + 4] = fm_vec(inp['hg_lb_logits'][1])
    lbrep = np.ascontiguousarray(np.broadcast_to(np.asarray(inp['hg_lb_logits'], f32)[None], (128, 2, 512)))
    p = np.arange(128)[:, None]
    j = np.arange(128)[None, :]
    ident = (p == j).astype(f32)
    ones = np.ones((128, 128), f32)
    negU = -(p >= j).astype(f32)
    negL = -(p < j).astype(f32)
    tri = (p <= j).astype(f32)
    cc = np.arange(512)[None, :]
    masks = np.concatenate([((p + d) < cc).astype(f32) for d in (0, 128, 256, 384)], axis=1)
    cst = np.ascontiguousarray(np.concatenate([ident, ones, negU, tri, negL, masks], axis=1))
    return dict(gu1=gu(inp['ffn1_w_gate'][0], inp['ffn1_w_up'][0]), d1=dn(inp['ffn1_w_down'][0]),
                gu2=gu(inp['ffn2_w_gate'][0], inp['ffn2_w_up'][0]), d2=dn(inp['ffn2_w_down'][0]),
                wfm=wfm, wtm=wtm, vecs=vecs, lbrep=lbrep, cst=cst)


_NC_CACHE = {}


def run(inp, nft):
    f32 = np.float32
    shared = _layouts(inp)
    if nft % 2 == 0:
        npre = nmain = nft // 2
    else:
        npre, nmain = 0, nft
    key = (npre, nmain)
    if key not in _NC_CACHE:
        _NC_CACHE[key] = Builder(npre, nmain).build()
    nc = _NC_CACHE[key]
    xp = np.asarray(inp['x_prompt'], f32)
    xs = np.asarray(inp['x_sample'], f32)
    ck = np.asarray(inp['cache_sb_k'], f32)
    cv = np.asarray(inp['cache_sb_v'], f32)
    st = np.asarray(inp['state_hgrn'], f32)
    meta = np.ascontiguousarray(np.asarray(inp['meta_tokens'], f32))
    B = xp.shape[0]
    H = nmain * NT
    in_maps = []
    for c in range(8):
        m = dict(shared)
        b, half = c // 2, c % 2
        vecs = shared['vecs'].copy()
        if npre == 0:
            m['xp'] = np.ascontiguousarray(xp[b])
            m['xpre'] = np.zeros((NT, D), f32)
            m['flag'] = np.ones((128, 512), f32)
            vecs[:, V_FA], vecs[:, V_FB], vecs[:, V_DEAD] = 0.0, 1.0, 0.0
        elif half == 0:
            m['xp'] = np.ascontiguousarray(xp[b, 0:H])
            m['xpre'] = np.zeros((npre * NT, D), f32)
            m['flag'] = np.zeros((128, 512), f32)
            vecs[:, V_FA], vecs[:, V_FB], vecs[:, V_DEAD] = 1.0, 0.0, -30000.0
        else:
            m['xp'] = np.ascontiguousarray(xp[b, H:2 * H])
            m['xpre'] = np.ascontiguousarray(xp[b, 0:H])
            m['flag'] = np.ones((128, 512), f32)
            vecs[:, V_FA], vecs[:, V_FB], vecs[:, V_DEAD] = 0.0, 1.0, 0.0
        m['vecs'] = vecs
        m['xs'] = np.ascontiguousarray(xs[c])
        m['meta'] = meta
        m['ck'] = np.ascontiguousarray(ck[0, c].reshape(LC, 512))
        m['cv'] = np.ascontiguousarray(cv[0, c].reshape(LC, 512))
        m['st'] = np.ascontiguousarray(st[0, c])
        in_maps.append(m)
    res = run_bass_kernel_spmd(nc, in_maps, core_ids=list(range(8)))
    r = res.results
    if npre == 0:
        y_prompt = np.stack([r[2 * b]['yp'] for b in range(B)])
        pk = np.stack([r[2 * b]['pk'] for b in range(B)])
        pv = np.stack([r[2 * b]['pv'] for b in range(B)])
        ph = np.stack([r[2 * b]['ph'] for b in range(B)])
    else:
        y_prompt = np.stack([np.concatenate([r[2 * b]['yp'], r[2 * b + 1]['yp']], axis=0) for b in range(B)])
        pk = np.stack([np.concatenate([r[2 * b]['pk'], r[2 * b + 1]['pk'][NMETA:]], axis=0) for b in range(B)])
        pv = np.stack([np.concatenate([r[2 * b]['pv'], r[2 * b + 1]['pv'][NMETA:]], axis=0) for b in range(B)])
        ph = np.stack([r[2 * b + 1]['ph'] for b in range(B)])
    y_prompt = y_prompt.astype(f32)
    L = pk.shape[1]
    pk = pk.reshape(B, L, 8, 64)[None].astype(f32)
    pv = pv.reshape(B, L, 8, 64)[None].astype(f32)
    ph = ph[None].astype(f32)
    y_sample = np.stack([r[c]['ys'] for c in range(8)]).astype(f32)
    sk = np.stack([r[c]['sk'].reshape(TS, 8, 64) for c in range(8)])[None].astype(f32)
    sv = np.stack([r[c]['sv'].reshape(TS, 8, 64) for c in range(8)])[None].astype(f32)
    sh = np.stack([r[c]['sh'] for c in range(8)])[None].astype(f32)
    return (y_prompt, y_sample, pk, pv, ph, sk, sv, sh)


def kernel(**inputs):
    nft = np.asarray(inputs['x_prompt']).shape[1] // NT
    return run(inputs, nft)
```

```python
import numpy as np
from contextlib import ExitStack
import concourse.bass as bass
import concourse.mybir as mybir
from concourse.bass_utils import run_bass_kernel_spmd

F32 = mybir.dt.float32
BF16 = mybir.dt.bfloat16
AF = mybir.ActivationFunctionType
ALU = mybir.AluOpType

D = 1024
DFF = 2816
NMETA = 16
TS = 32
LC = 2064
EPS = 1e-6
NT = 512
ENGS = ['pe', 'act', 'dve', 'pool', 'sp']

V_N1, V_NM, V_N2, V_NF, V_BGA, V_BGB, V_HGN, V_LB0, V_LB1, V_LBV, V_OML, V_FA, V_FB, V_DEAD = 0, 8, 16, 24, 32, 40, 48, 52, 56, 60, 64, 68, 69, 70
NVEC = 71


class Sched:
    def __init__(self):
        self.prog = {e: [] for e in ENGS}
        self.cnt = {e: 0 for e in ENGS}
        self.dma_cnt = {}
        self.last_w = {}
        self.readers = {}
        self.known = {e: {} for e in ENGS}
        self.fence = {}
        self.touched = set()
        self.nwaits = 0
        import os
        self.glimit = int(os.environ.get('KOPS', '100000000'))

    def _deps(self, reads, writes):
        need = {}

        def add(src, val):
            if need.get(src, 0) < val:
                need[src] = val
        for k in list(reads) + list(writes):
            if isinstance(k, tuple) and k[0] == 'SH' and k not in self.touched:
                self.touched.add(k)
                for s, v in self.fence.items():
                    add(s, v)
        for k in reads:
            lw = self.last_w.get(k)
            if lw:
                add(*lw)
        for k in writes:
            lw = self.last_w.get(k)
            if lw:
                add(*lw)
            for s, v in self.readers.get(k, {}).items():
                add(s, v)
        return need

    def _waits(self, eng, need):
        waits = []
        for src, val in need.items():
            if src == eng and eng == 'pe':
                continue
            if self.known[eng].get(src, 0) >= val:
                continue
            self.known[eng][src] = val
            waits.append((src, val))
        self.nwaits += len(waits)
        return waits

    def _mark(self, src, val, reads, writes):
        for k in writes:
            self.last_w[k] = (src, val)
            self.readers[k] = {}
        for k in reads:
            if k in writes:
                continue
            self.readers.setdefault(k, {})[src] = val

    def op(self, eng, fn, reads=(), writes=()):
        psr = [k for k in reads if isinstance(k, tuple) and k[0] == 'ps']
        if psr:
            reads = [k for k in reads if k not in psr]
            writes = list(writes) + [k for k in psr if k not in writes]
        self.gcount = getattr(self, 'gcount', 0) + 1
        if self.gcount > self.glimit:
            return
        import sys as _sys, os as _os
        if _os.environ.get('KTRACE'):
            lo, hi = [int(x) for x in _os.environ['KTRACE'].split(',')]
            if lo <= self.gcount <= hi:
                fr = _sys._getframe(2)
                print('OP', self.gcount, eng, 'line', fr.f_lineno, 'from', fr.f_back.f_lineno, 'reads', list(reads)[:3], 'writes', list(writes))
        need = self._deps(reads, writes)
        waits = self._waits(eng, need)
        self.cnt[eng] += 1
        self._mark(eng, self.cnt[eng], reads, writes)
        self.prog[eng].append(('op', waits, fn))

    def dma(self, q, out, in_, reads, writes, sem):
        self.gcount = getattr(self, 'gcount', 0) + 1
        if self.gcount > self.glimit:
            return
        import sys as _sys, os as _os
        if _os.environ.get('KTRACE'):
            lo, hi = [int(x) for x in _os.environ['KTRACE'].split(',')]
            if lo <= self.gcount <= hi:
                fr = _sys._getframe(1)
                print('DMA', self.gcount, q, 'line', fr.f_lineno, 'sem', sem, 'writes', list(writes))
        need = self._deps(reads, writes)
        waits = self._waits(q, need)
        self.dma_cnt[sem] = self.dma_cnt.get(sem, 0) + 16
        self._mark(sem, self.dma_cnt[sem], reads, writes)
        self.prog[q].append(('dma', waits, (out, in_, sem)))

    def phase_switch(self):
        f = dict(self.fence)

        def add(s, v):
            if f.get(s, 0) < v:
                f[s] = v
        for k in list(self.last_w.keys()):
            if isinstance(k, tuple) and k[0] == 'SH':
                add(*self.last_w[k])
                del self.last_w[k]
        for k in list(self.readers.keys()):
            if isinstance(k, tuple) and k[0] == 'SH':
                for s, v in self.readers[k].items():
                    add(s, v)
                del self.readers[k]
        self.fence = f
        self.touched = set()

    def final_wait(self, q):
        waits = [(s, v) for s, v in self.dma_cnt.items()]
        self.prog[q].append(('wait', waits, None))


class Builder:
    def __init__(self, npre, nmain):
        self.npre, self.nmain = npre, nmain
        nft = npre + nmain
        self.nft = nft
        self.FR = nft * NT
        self.KTW = max(NMETA + self.FR, NMETA + LC + TS + 128)
        self.NBLK = max(1 + 4 * nft, 19)
        self.s = Sched()
        self.ring_i = 0
        self.kv_i = 0
        self.xtok_i = 0

    def mm(self, out, lhsT, rhs, start, stop, reads, writes, sgc=False):
        self.s.op('pe', lambda e: e.matmul(out, lhsT=lhsT, rhs=rhs, start=start, stop=stop, skip_group_check=sgc), reads, writes)

    def tr(self, out, in_, n, reads, writes):
        ident = self.ident
        self.s.op('pe', lambda e: e.transpose(out=out, in_=in_, identity=ident[0:n, 0:n]), reads, writes)

    def act(self, out, in_, func, reads, writes, bias=None, scale=None):
        kw = {}
        if bias is not None:
            kw['bias'] = bias
        if scale is not None:
            kw['scale'] = scale
        self.s.op('act', lambda e: e.activation(out=out, in_=in_, func=func, **kw), reads, writes)

    def tt(self, eng, out, in0, in1, op, reads, writes):
        self.s.op(eng, lambda e: e.tensor_tensor(out=out, in0=in0, in1=in1, op=op), reads, writes)

    def tsc(self, eng, out, in0, s1, s2, op0, op1, reads, writes):
        if s2 is None:
            self.s.op(eng, lambda e: e.tensor_scalar(out=out, in0=in0, scalar1=s1, scalar2=None, op0=op0), reads, writes)
        else:
            self.s.op(eng, lambda e: e.tensor_scalar(out=out, in0=in0, scalar1=s1, scalar2=s2, op0=op0, op1=op1), reads, writes)

    def stt(self, eng, out, in0, scalar, in1, op0, op1, reads, writes):
        self.s.op(eng, lambda e: e.scalar_tensor_tensor(out=out, in0=in0, scalar=scalar, in1=in1, op0=op0, op1=op1), reads, writes)

    def cp(self, eng, out, in_, reads, writes):
        if eng == 'act':
            self.act(out, in_, AF.Copy, reads, writes)
        else:
            self.s.op(eng, lambda e: e.tensor_copy(out=out, in_=in_), reads, writes)

    def ring_load(self, dram_ap, nelem, rdkey):
        slot = self.ring_i % 4
        self.ring_i += 1
        out = self.ring[:, slot, 0:nelem]
        self.s.dma('sp', out, dram_ap, [rdkey], [('ring', slot)], 'ring%d' % slot)
        return slot

    def build(self):
        nc = bass.Bass("TRN2", target_bir_lowering=False)
        self.nc = nc
        nft, FR = self.nft, self.FR
        FM_ = self.nmain * NT
        FP_ = max(self.npre, 1) * NT
        dt_in = {}

        def din(name, shape):
            dt_in[name] = nc.dram_tensor(name, list(shape), F32, kind="ExternalInput").ap()
            return dt_in[name]

        def dout(name, shape):
            return nc.dram_tensor(name, list(shape), F32, kind="ExternalOutput").ap()
        self.xp = din("xp", [FM_, D])
        self.xpre = din("xpre", [FP_, D])
        self.flag_d = din("flag", [128, 512])
        self.xs = din("xs", [TS, D])
        self.meta = din("meta", [NMETA, D])
        self.ck = din("ck", [LC, 512])
        self.cv = din("cv", [LC, 512])
        self.st = din("st", [4, 128, 128])
        self.vecs_d = din("vecs", [128, NVEC])
        self.lbrep_d = din("lbrep", [128, 2, 512])
        self.cst_d = din("cst", [128, 6 * 128 + 4 * 512])
        self.w32 = {}
        self.wsz = {'gu1': 22 * 128 * 2048, 'gu2': 22 * 128 * 2048, 'd1': 2 * 8 * 128 * 1408, 'd2': 2 * 8 * 128 * 1408,
                    'wfm': 52 * 128 * 1024, 'wtm': 4 * 128 * 4096}
        for n, sz in self.wsz.items():
            self.w32[n] = din(n, [sz // 2048, 2048])
        self.yp = dout("yp", [FM_, D])
        self.ys = dout("ys", [TS, D])
        self.pk = dout("pk", [NMETA + FM_, 512])
        self.pv = dout("pv", [NMETA + FM_, 512])
        self.ph = dout("ph", [4, 128, 128])
        self.sk = dout("sk", [TS, 512])
        self.sv = dout("sv", [TS, 512])
        self.sh = dout("sh", [4, 128, 128])
        self.wbf = {n: nc.dram_tensor(n + "_bf", [sz], BF16).ap() for n, sz in self.wsz.items()}

        with ExitStack() as es:
            es.enter_context(nc.allow_low_precision("bf16 matmul operands, fp32 accumulation"))

            def sb(name, shape, dt):
                return es.enter_context(nc.sbuf_tensor(name, list(shape), dt))

            def ps(name):
                return es.enter_context(nc.psum_tensor(name, [128, 512], F32))
            self.ident = sb("ident", [128, 128], F32)
            self.onesb = sb("onesb", [128, 128], BF16)
            self.negU = sb("negU", [128, 128], BF16)
            self.negL = sb("negL", [128, 128], BF16)
            self.tri = sb("tri", [128, 128], F32)
            self.masks = sb("masks", [128, 4, 512], BF16)
            self.vecs = sb("vecs_s", [128, NVEC], F32)
            self.lb_t = sb("lb_t", [128, 2, 512], F32)
            self.xT = sb("xT", [128, 8, NT], F32)
            self.hT = sb("hT", [128, 8, NT + 64], BF16)
            self.rstd = sb("rstd", [128, NT], F32)
            self.lnt = sb("lnt", [128, NT], F32)
            self.KT = sb("KT", [128, 4, self.KTW], BF16)
            self.Vp = sb("Vp", [128, self.NBLK, 512], BF16)
            self.ring = sb("ring", [128, 4, 4096], BF16)
            self.S = sb("S", [128, 2, 4, 128], F32)
            self.Sb = sb("Sb", [128, 2, 4, 128], BF16)
            self.kvst = sb("kvst", [128, 2, 512], F32)
            self.Ssave = sb("Ssave", [128, 4, 128], F32)
            self.qT = sb("qT", [128, 4, NT], BF16)
            self.obT = sb("obT", [128, 4, NT], F32)
            self.oaT = sb("oaT", [128, 4, NT], BF16)
            SHW = 11008
            self.SH = sb("SH", [128, SHW], F32)
            self.bank = [ps("bank%d" % i) for i in range(8)]
            self.trib = sb("trib", [128, 128], F32)
            self._views()
            import os
            if os.environ.get('KMEM'):
                print('sbuf bytes remaining', nc.sbuf_bytes_remaining)

            self._emit_all()

            sems = {}
            names = [e for e in ENGS if e != 'sp'] + sorted(self.s.dma_cnt.keys())
            for n in names:
                sems[n] = es.enter_context(nc.semaphore("s_" + n))
            block = es.enter_context(nc.Block())
            prog = self.s.prog

            def run(eng_obj, ename):
                for kind, waits, payload in prog[ename]:
                    for src, val in waits:
                        eng_obj.wait_ge(sems[src], val)
                    if kind == 'op':
                        ins = payload(eng_obj)
                        ins.then_inc(sems[ename], 1)
                    elif kind == 'dma':
                        out, in_, sem = payload
                        eng_obj.dma_start(out=out, in_=in_).then_inc(sems[sem], 16)

            @block.tensor
            def _(e):
                run(e, 'pe')

            @block.scalar
            def _(e):
                run(e, 'act')

            @block.vector
            def _(e):
                run(e, 'dve')

            @block.gpsimd
            def _(e):
                run(e, 'pool')

            @block.sync
            def _(e):
                run(e, 'sp')
        return nc

    def _views(self):
        SH = self.SH

        def v(off_b, nbytes, dt, pattern=None, **kw):
            a = SH[:, off_b // 4:(off_b + nbytes) // 4]
            if dt is BF16:
                a = a.bitcast(BF16)
            if pattern:
                a = a.rearrange(pattern, **kw)
            return a
        self.hid = v(0, 11264, BF16, "p (c n) -> p c n", c=11)
        self.sg = v(11264, 4096, F32, "p (c n) -> p c n", c=2)
        self.x_tok = v(15360, 8192, F32, "p (c n) -> p c n", c=2)
        self.omfT = v(0, 8192, F32, "p (c n) -> p c n", c=4)
        self.qhT = v(8192, 8192, F32, "p (c n) -> p c n", c=4)
        self.logf = v(16384, 4096, F32, "p (c n) -> p c n", c=2)
        self.omf_tm = v(20480, 4096, F32, "p (c n) -> p c n", c=2)
        self.iv = v(24576, 2048, BF16, "p (c n) -> p c n", c=2)
        self.eb = v(26624, 2048, F32, "p (a h t) -> p a h t", a=2, h=4)
        self.enb = v(28672, 2048, F32, "p (a h t) -> p a h t", a=2, h=4)
        self.qe = v(30720, 1024, BF16, "p (a h t) -> p a h t", a=2, h=4)
        self.ke = v(41984, 2048, BF16, "p (a h t) -> p a h t", a=2, h=4)
        self.enb_tm = v(32768, 4096, F32, "p (c n) -> p c n", c=2)
        self.ke_tm = v(36864, 2048, BF16, "p (c n) -> p c n", c=2)
        self.sc = v(38912, 1024, BF16, "p (a h t) -> p a h t", a=2, h=4)
        self.tmpS = v(39936, 2048, F32, "p (h t) -> p h t", h=4)
        self.e_ = v(0, 12288, F32, "p (c n) -> p c n", c=6)
        self.sp_ = v(12288, 6144, BF16, "p (c n) -> p c n", c=6)
        self.e2_ = v(18432, 6144, F32, "p (c n) -> p c n", c=3)
        self.at_ = v(24576, 6144, BF16, "p (c n) -> p c n", c=6)
        self.ckv = v(30720, 8192, F32, "p (c n) -> p c n", c=2)
        self.sgT = v(0, 8192, F32, "p (c n) -> p c n", c=4)
        self.gA = v(8192, 8192, BF16, "p (c n) -> p c n", c=8)
        self.gB = v(16384, 8192, BF16, "p (c n) -> p c n", c=8)
        self.t1 = v(24576, 2048, F32, "p (c n) -> p c n", c=1)
        self.t2 = v(26624, 2048, F32, "p (c n) -> p c n", c=1)
        self.mT = v(28672, 8192, BF16, "p (c n) -> p c n", c=8)
        self.obn = v(36864, 4096, BF16, "p (c n) -> p c n", c=4)

    def _emit_all(self):
        import os
        self.kstop = int(os.environ.get('KSTOP', '99'))
        self.prologue()
        self.tile(-1, 'meta')
        self.casts([('wfm', 1024, 2304), ('gu2', 0, 1408), ('d2', 0, 704), ('gu2', 1408, 1408), ('d2', 704, 704)], 6)
        for p in range(self.npre):
            self.tile(p, 'pre')
        if self.npre > 0:
            self.state_select()
        for m in range(self.nmain):
            self.tile(self.npre + m, 'main')
        import os
        if not os.environ.get('NOSAMPLE'):
            self.tile(-1, 'sample')
        self.s.final_wait('sp')

    def prologue(self):
        s = self.s
        c = self.cst_d
        o = 0
        s.dma('pool', self.ident[:], c[:, o:o + 128], [], ['ident'], 'cst'); o += 128
        s.dma('pool', self.onesb[:], c[:, o:o + 128], [], ['onesb'], 'cst'); o += 128
        s.dma('pool', self.negU[:], c[:, o:o + 128], [], ['negU'], 'cst'); o += 128
        s.dma('pool', self.tri[:], c[:, o:o + 128], [], ['tri'], 'cst'); o += 128
        s.dma('pool', self.negL[:], c[:, o:o + 128], [], ['negL'], 'cst'); o += 128
        s.dma('pool', self.masks[:], c[:, o:o + 2048].rearrange("p (d n) -> p d n", d=4), [], ['masks'], 'cst'); o += 2048
        s.dma('pool', self.trib[:], c[:, o:o + 128], [], ['trib'], 'cst'); o += 128
        s.dma('pool', self.vecs[:], self.vecs_d, [], ['vecs'], 'cst')
        s.dma('pool', self.lb_t[:], self.lbrep_d, [], ['lb_t'], 'cst')
        allc = ['ident', 'onesb', 'negU', 'negL', 'tri', 'masks', 'vecs', 'lb_t', 'trib']
        tot = s.dma_cnt['cst']
        for k in allc:
            s.last_w[k] = ('cst', tot)
        self.s.op('dve', lambda e: e.memset(self.hT[:, :, :], 0.0), [], [('hT', kc) for kc in range(8)])
        self.s.op('dve', lambda e: e.memset(self.KT[:, :, :], 0.0), [], [('KT', c) for c in range(4)])
        self.s.op('dve', lambda e: e.memset(self.xT[:, :, :], 0.0), [], [('xT', kc) for kc in range(8)])
        self.s.op('dve', lambda e: e.memset(self.SH[:, :], 0.0), [], [('SH', 'all')])
        self.casts([('gu1', 0, 1408), ('d1', 0, 704), ('gu1', 1408, 1408), ('d1', 704, 704), ('wfm', 0, 1024), ('wtm', 0, 1024)], 0)
        vv = self.vecs
        self.tt('dve', vv[:, V_LBV:V_LBV + 4], vv[:, V_LB1:V_LB1 + 4], vv[:, V_LB0:V_LB0 + 4], ALU.subtract, ['vecs'], ['vecs'])
        self.act(vv[:, V_LBV:V_LBV + 4], vv[:, V_LBV:V_LBV + 4], AF.Sigmoid, ['vecs'], ['vecs'])
        self.tsc('dve', vv[:, V_OML:V_OML + 4], vv[:, V_LBV:V_LBV + 4], -1.0, 1.0, ALU.mult, ALU.add, ['vecs'], ['vecs'])
        lt = self.lb_t
        self.tt('dve', lt[:, 0, :], lt[:, 1, :], lt[:, 0, :], ALU.subtract, ['lb_t'], ['lb_t'])
        self.act(lt[:, 0, :], lt[:, 0, :], AF.Sigmoid, ['lb_t'], ['lb_t'])
        self.tsc('dve', lt[:, 1, :], lt[:, 0, :], -1.0, 1.0, ALU.mult, ALU.add, ['lb_t'], ['lb_t'])

    def casts(self, pieces, i0):
        for i, (n, r0, nr) in enumerate(pieces):
            dst = self.wbf[n].rearrange("(r c) -> r c", c=2048)[r0:r0 + nr, :]
            self.s.dma('pool', dst, self.w32[n][r0:r0 + nr, :], [], [('scr', n, r0)], 'cast%d' % (i0 + i))

    def scr_key(self, n, row2048):
        bounds = {'gu1': [0, 1408], 'gu2': [0, 1408], 'd1': [0, 704], 'd2': [0, 704], 'wfm': [0, 1024], 'wtm': [0]}[n]
        r0 = max(b for b in bounds if b <= row2048)
        return ('scr', n, r0)

    def load_gu(self, which, c):
        n = 'gu%d' % which
        ap = self.wbf[n].rearrange("(c p f) -> c p f", p=128, f=2048)[c]
        return self.ring_load(ap, 2048, self.scr_key(n, c * 128))

    def load_d(self, which, half, o):
        n = 'd%d' % which
        ap = self.wbf[n].rearrange("(h o p f) -> h o p f", h=2, o=8, p=128)[half, o]
        return self.ring_load(ap, 1408, self.scr_key(n, (half * 8 + o) * 128 * 1408 // 2048))

    def load_fm(self, c0, ncnk):
        slot = self.ring_i % 4
        self.ring_i += 1
        src = self.wbf['wfm'].rearrange("(c p f) -> c p f", p=128, f=1024)
        for i in range(ncnk):
            self.s.dma('sp', self.ring[:, slot, i * 1024:(i + 1) * 1024], src[c0 + i], [self.scr_key('wfm', c0 * 64)], [('ring', slot)], 'ring%d' % slot)
        return slot

    def load_tm(self, g):
        ap = self.wbf['wtm'].rearrange("(g p f) -> g p f", p=128, f=4096)[g]
        return self.ring_load(ap, 4096, ('scr', 'wtm', 0))

    def state_select(self):
        S0 = self.S[:, 0, :, :]
        self.tsc('dve', self.Ssave[:, :, :], self.Ssave[:, :, :], self.vecs[:, V_FA:V_FA + 1], None, ALU.mult, None, ['Ssave', 'vecs'], ['Ssave'])
        self.stt('dve', S0, S0, self.vecs[:, V_FB:V_FB + 1], self.Ssave[:, :, :], ALU.mult, ALU.add, [('S', 0), 'Ssave', 'vecs'], [('S', 0)])
        for hh in range(4):
            self.cp('pool', self.Sb[:, 0, hh, :], self.S[:, 0, hh, :], [('S', 0)], [('Sb', 0, hh)])

    def tile(self, f, mode='extra'):
        self.mode = mode
        self.m = f - self.npre if mode == 'main' else f
        if mode == 'meta':
            n = NMETA
            segs = [('meta', 0, NMETA)]
        elif mode == 'sample':
            n = TS
            segs = [('sample', 0, TS)]
        else:
            n = NT
            segs = [('frames', 0, NT)]
        self.n = n
        self.f = f
        self.segs = segs
        s = self.s
        ks = self.kstop if f < 0 else 99
        s.phase_switch()
        self.load_x()
        if mode in ('pre', 'meta'):
            self.norm(V_N1)
            self.ffn(1)
            self.norm(V_NM)
            s.phase_switch()
            self.w_in()
            self.hgrn()
            return
        if ks < 2: return
        self.norm(V_N1)
        if ks < 3: return
        self.ffn(1)
        if ks < 4: return
        self.norm(V_NM)
        s.phase_switch()
        self.w_in()
        if ks < 5 or ks == 41: return
        self.hgrn()
        if ks < 6: return
        s.phase_switch()
        self.attn()
        if ks < 7: return
        s.phase_switch()
        self.post()
        if ks < 8: return
        s.phase_switch()
        self.norm(V_N2)
        self.ffn(2)
        self.final()

    def load_x(self):
        n, f = self.n, self.f
        if f < 0:
            blocks = [(0, n)]
        else:
            blocks = [(j * 128, 128) for j in range(4)]
        for bi, (c0, nb) in enumerate(blocks):
            slot = self.xtok_i % 2
            self.xtok_i += 1
            xk = ('SH', 'x_tok', slot)
            if f < 0:
                self.s.dma('sp', self.x_tok[0:n, slot, :], self.xs if self.mode == 'sample' else self.meta, [], [xk], 'xin%d' % slot)
            else:
                r0 = self.m * NT + c0
                srcx = self.xpre if self.mode == 'pre' else self.xp
                self.s.dma('sp', self.x_tok[:, slot, :], srcx[r0:r0 + 128, :], [], [xk], 'xin%d' % slot)
            for kc in range(8):
                bk = self.bank[kc % 2]
                self.tr(bk[:, 0:nb], self.x_tok[0:nb, slot, kc * 128:(kc + 1) * 128], nb, [xk, 'ident'], [('ps', kc % 2)])
                eng = 'dve' if kc % 2 == 0 else 'act'
                self.cp(eng, self.xT[:, kc, c0:c0 + nb], bk[:, 0:nb], [('ps', kc % 2)], [('xT', kc)])

    def norm(self, gcol, inplace=False):
        n = self.n
        for kc in range(8):
            self.act(self.hT[:, kc, 0:n], self.xT[:, kc, 0:n], AF.Square, [('xT', kc)], [('hT', kc)])
        bk = self.bank[7]
        for kc in range(8):
            self.mm(bk[:, 0:n], self.onesb[:, :], self.hT[:, kc, 0:n], kc == 0, kc == 7, [('hT', kc), 'onesb'], [('ps', 7)])
        self.act(self.lnt[:, 0:n], bk[:, 0:n], AF.Ln, [('ps', 7)], ['lnt'], bias=EPS, scale=1.0 / D)
        self.act(self.rstd[:, 0:n], self.lnt[:, 0:n], AF.Exp, ['lnt'], ['rstd'], scale=-0.5)
        for kc in range(8):
            eng = 'dve'
            if inplace:
                self.stt(eng, self.xT[:, kc, 0:n], self.xT[:, kc, 0:n], self.vecs[:, gcol + kc:gcol + kc + 1], self.rstd[:, 0:n],
                         ALU.mult, ALU.mult, [('xT', kc), 'rstd', 'vecs'], [('xT', kc)])
            else:
                self.stt(eng, self.hT[:, kc, 0:n], self.xT[:, kc, 0:n], self.vecs[:, gcol + kc:gcol + kc + 1], self.rstd[:, 0:n],
                         ALU.mult, ALU.mult, [('xT', kc), 'rstd', 'vecs'], [('hT', kc)])

    def ffn(self, which):
        n = self.n
        hk = [('hT', kc) for kc in range(8)]
        it = 0
        gun, dnn = 'gu%d' % which, 'd%d' % which
        gsrc = self.wbf[gun].rearrange("(c p f) -> p c f", p=128, f=2048)
        dsrc = self.wbf[dnn].rearrange("(h o p f) -> h p o f", h=2, o=8, p=128)
        for half in range(2):
            for cc0 in range(0, 11, 2):
                ncnk = min(2, 11 - cc0)
                c = half * 11 + cc0
                slot = self.ring_i % 4
                self.ring_i += 1
                self.s.dma('sp', self.ring[:, slot, 0:ncnk * 2048].rearrange("p (c f) -> p c f", c=ncnk), gsrc[:, c:c + ncnk, :],
                           [self.scr_key(gun, c * 128)], [('ring', slot)], 'ring%d' % slot)
                for ci in range(ncnk):
                    cc = cc0 + ci
                    w = self.ring[:, slot, ci * 2048:(ci + 1) * 2048].rearrange("p (g k n) -> p g k n", g=2, k=8)
                    gb, ub = it % 2, 2 + it % 2
                    for kc in range(8):
                        self.mm(self.bank[gb][:, 0:n], w[:, 0, kc, :], self.hT[:, kc, 0:n], kc == 0, kc == 7, hk + [('ring', slot)], [('ps', gb)])
                    for kc in range(8):
                        self.mm(self.bank[ub][:, 0:n], w[:, 1, kc, :], self.hT[:, kc, 0:n], kc == 0, kc == 7, hk + [('ring', slot)], [('ps', ub)])
                    sgk = ('SH', 'sg', it % 2)
                    self.act(self.sg[:, it % 2, 0:n], self.bank[gb][:, 0:n], AF.Silu, [('ps', gb)], [sgk])
                    self.tt('dve', self.hid[:, cc, 0:n], self.sg[:, it % 2, 0:n], self.bank[ub][:, 0:n], ALU.mult, [sgk, ('ps', ub)], [('SH', 'hid', cc)])
                    it += 1
            for o0 in range(0, 8, 2):
                slot = self.ring_i % 4
                self.ring_i += 1
                self.s.dma('sp', self.ring[:, slot, 0:2 * 1408].rearrange("p (c f) -> p c f", c=2), dsrc[half, :, o0:o0 + 2, :],
                           [self.scr_key(dnn, (half * 8 + o0) * 128 * 1408 // 2048)], [('ring', slot)], 'ring%d' % slot)
                for oi in range(2):
                    o = o0 + oi
                    w = self.ring[:, slot, oi * 1408:(oi + 1) * 1408].rearrange("p (c n) -> p c n", c=11)
                    db = 4 + o % 2
                    for cc in range(11):
                        self.mm(self.bank[db][:, 0:n], w[:, cc, :], self.hid[:, cc, 0:n], cc == 0, cc == 10, [('SH', 'hid', cc), ('ring', slot)], [('ps', db)])
                    self.stt('dve', self.xT[:, o, 0:n], self.bank[db][:, 0:n], 0.5, self.xT[:, o, 0:n], ALU.mult, ALU.add, [('ps', db), ('xT', o)], [('xT', o)])

    def kcol(self, seg):
        if seg == 'sample':
            return NMETA + LC
        if seg == 'meta':
            return 0
        return NMETA + self.f * NT

    def w_in(self):
        n = self.n
        hk = [('hT', kc) for kc in range(8)]
        bi = 0
        for g4 in ([1] if self.mode in ('pre', 'meta') else range(4)):
            slot = self.load_fm(g4 * 4, 4)
            w = self.ring[:, slot, 0:4096].rearrange("p (c k n) -> p c k n", c=4, k=8)
            for ci in range(4):
                b = bi % 4
                bi += 1
                bk = self.bank[b]
                for kc in range(8):
                    self.mm(bk[:, 0:n], w[:, ci, kc, :], self.hT[:, kc, 0:n], kc == 0, kc == 7, hk + [('ring', slot)], [('ps', b)])
                if g4 == 0:
                    self.act(self.qT[:, ci, 0:n], bk[:, 0:n], AF.Copy, [('ps', b)], [('qT', ci)], scale=0.125)
                elif g4 == 1:
                    for (sname, c0, ns) in self.segs:
                        kc0 = self.kcol(sname)
                        self.cp('dve', self.KT[:, ci, kc0:kc0 + ns], bk[:, c0:c0 + ns], [('ps', b)], [('KT', ci)])
                elif g4 == 2:
                    self.act(self.lnt[:, 0:n], bk[:, 0:n], AF.Sigmoid, [('ps', b)], ['lnt'], scale=-1.0)
                    self.tsc('dve', self.omfT[:, ci, 0:n], self.lnt[:, 0:n], self.vecs[:, V_OML + ci:V_OML + ci + 1], None, ALU.mult, None,
                             ['lnt', 'vecs'], [('SH', 'omfT', ci)])
                else:
                    self.cp('act', self.qhT[:, ci, 0:n], bk[:, 0:n], [('ps', b)], [('SH', 'qhT', ci)])
        if self.kstop == 41:
            return
        if self.mode == 'sample':
            blocks = [('sample', 0, TS, self.sk, self.sv, 0, 18)]
        elif self.mode == 'meta':
            blocks = [('meta', 0, NMETA, self.pk, self.pv, 0, 0)]
        else:
            blocks = [('frames', j * 128, 128, self.pk, self.pv, NMETA + self.m * NT + j * 128, 1 + 4 * self.f + j) for j in range(4)]
        for g in ([1] if self.mode == 'pre' else range(2)):
            slot = self.load_tm(g)
            w = self.ring[:, slot, 0:4096].rearrange("p (k n) -> p k n", k=8)
            for (sname, c0, nb, okd, ovd, r0, vblk) in blocks:
                b = 4 + bi % 4
                bi += 1
                bk = self.bank[b]
                for kc in range(8):
                    self.mm(bk[:, :], self.hT[:, kc, c0:c0 + 128], w[:, kc, :], kc == 0, kc == 7, hk + [('ring', slot)], [('ps', b)])
                if self.mode == 'pre':
                    self.cp('act', self.Vp[0:nb, vblk, :], bk[0:nb, :], [('ps', b)], [('Vp', vblk)])
                    continue
                ks = self.kv_i % 2
                self.kv_i += 1
                self.cp('dve', self.kvst[0:nb, ks, :], bk[0:nb, :], [('ps', b)], [('kvst', ks)])
                dst = (okd if g == 0 else ovd)[r0:r0 + nb, :]
                import os
                if not os.environ.get('NOKV'):
                    self.s.dma(os.environ.get('KVQ', 'pool'), dst, self.kvst[0:nb, ks, :], [('kvst', ks)], [], 'kvo%d' % ks)
                if g == 1:
                    self.cp('act', self.Vp[0:nb, vblk, :], bk[0:nb, :], [('ps', b)], [('Vp', vblk)])

    def hgrn(self):
        s = self.s
        hk = [('hT', kc) for kc in range(8)]
        slz = self.load_tm(2)
        sli = self.load_tm(3)
        wz = self.ring[:, slz, 0:4096].rearrange("p (k n) -> p k n", k=8)
        wi = self.ring[:, sli, 0:4096].rearrange("p (k n) -> p k n", k=8)
        if self.mode == 'sample':
            chunks = [(0, TS, 1)]
            self.s.dma('pool', self.S[:, 1, :, :], self.st.rearrange("h k v -> k h v"), [], [('S', 1)], 'stin')
            for hh in range(4):
                self.cp('pool', self.Sb[:, 1, hh, :], self.S[:, 1, hh, :], [('S', 1)], [('Sb', 1, hh)])
        elif self.mode == 'meta':
            chunks = [(0, NMETA, 0)]
            self.s.op('dve', lambda e: e.memset(self.S[:, 0, :, :], 0.0), [], [('S', 0)])
            self.s.op('dve', lambda e: e.memset(self.Sb[:, 0, :, :], 0.0), [], [('Sb', 0, hh) for hh in range(4)])
        else:
            chunks = [(i * 64, 64, 0) for i in range(8)]
        def _chunk(ci, c0, T, si):
            par = ci % 2
            paired = (T == 64)
            pp = (ci // 2) % 2 if paired else 0
            r0 = 64 * (ci % 2) if paired else 0
            r1 = r0 + T
            bA, bB, bC, bD, bE, bF = self.bank[0], self.bank[1], self.bank[2 + par], self.bank[4], self.bank[5], self.bank[6 + par]
            kC, kF = ('ps', 2 + par), ('ps', 6 + par)
            k_omf, k_logf, k_iv = ('SH', 'omf_tm', pp), ('SH', 'logf', pp), ('SH', 'iv', pp)
            k_enbt, k_ket = ('SH', 'enb_tm', pp), ('SH', 'ke_tm', pp)
            cheap = self.mode in ('pre', 'meta')
            if r0 == 0:
                TT = 128 if paired else T
                for kc in range(8):
                    self.mm(bA[:, :], self.hT[:, kc, c0:c0 + 128], wz[:, kc, :], kc == 0, kc == 7, hk + [('ring', slz)], [('ps', 0)])
                for kc in range(8):
                    self.mm(bB[:, :], self.hT[:, kc, c0:c0 + 128], wi[:, kc, :], kc == 0, kc == 7, hk + [('ring', sli)], [('ps', 1)])
                self.act(self.omf_tm[0:TT, pp, :], bA[0:TT, :], AF.Sigmoid, [('ps', 0)], [k_omf], scale=-1.0)
                self.tt('dve', self.omf_tm[0:TT, pp, :], self.omf_tm[0:TT, pp, :], self.lb_t[0:TT, 1, :], ALU.mult, [k_omf, 'lb_t'], [k_omf])
                self.act(self.logf[0:TT, pp, :], self.omf_tm[0:TT, pp, :], AF.Ln, [k_omf], [k_logf], bias=1.0, scale=-1.0)
                self.cp('act', self.iv[0:TT, pp, :], bB[0:TT, :], [('ps', 1)], [k_iv])
                self.mm(bD[:, :], self.trib[0:TT, 0:128], self.logf[0:TT, pp, :], True, True, [k_logf, 'trib'], [('ps', 4)])
                self.act(self.enb_tm[0:TT, pp, :], bD[0:TT, :], AF.Exp, [('ps', 4)], [k_enbt], scale=-1.0)
                self.tt('dve', self.ke_tm[0:TT, pp, :], self.omf_tm[0:TT, pp, :], self.enb_tm[0:TT, pp, :], ALU.mult, [k_omf, k_enbt], [k_ket])
            if cheap:
                for hh in range(4):
                    self.mm(bC[:, hh * 64:hh * 64 + 2], self.logf[r0:r1, pp, hh * 128:(hh + 1) * 128], self.tri[r0:r1, 126:128], True, True,
                            [k_logf, 'tri'], [kC])
                for hh in range(4):
                    self.act(self.eb[:, par, hh, T - 1:T], bC[:, hh * 64 + 1:hh * 64 + 2], AF.Exp, [kC], [('SH', 'eb', par, hh)])
                yield
            else:
                for hh in range(4):
                    self.mm(bC[:, hh * 64:hh * 64 + T], self.logf[r0:r1, pp, hh * 128:(hh + 1) * 128], self.tri[r0:r1, r0:r1], True, True,
                            [k_logf, 'tri'], [kC])
                for hh in range(4):
                    k_eb, k_enb, k_qe, k_ke = ('SH', 'eb', par, hh), ('SH', 'enb', par, hh), ('SH', 'qe', par, hh), ('SH', 'ke', par, hh)
                    self.act(self.eb[:, par, hh, 0:T], bC[:, hh * 64:hh * 64 + T], AF.Exp, [kC], [k_eb])
                    self.act(self.enb[:, par, hh, 0:T], bC[:, hh * 64:hh * 64 + T], AF.Exp, [kC], [k_enb], scale=-1.0)
                    self.tt('dve', self.qe[:, par, hh, 0:T], self.qhT[:, hh, c0:c0 + T], self.eb[:, par, hh, 0:T], ALU.mult,
                            [('SH', 'qhT', hh), k_eb], [k_qe])
                    self.tt('pool', self.ke[:, par, hh, r0:r1], self.omfT[:, hh, c0:c0 + T], self.enb[:, par, hh, 0:T], ALU.mult,
                            [('SH', 'omfT', hh), k_enb], [k_ke])
                yield
                for hh in range(4):
                    k_qe, k_ke = ('SH', 'qe', par, hh), ('SH', 'ke', par, hh)
                    self.mm(bC[:, 256 + hh * 64:256 + hh * 64 + T], self.ke[:, par, hh, 0:128], self.qe[:, par, hh, 0:T], True, True,
                            [k_qe, k_ke], [kC])
                for hh in range(4):
                    k_sc = ('SH', 'sc', par, hh)
                    self.tt('dve', self.sc[r0:r1, par, hh, 0:T], bC[r0:r1, 256 + hh * 64:256 + hh * 64 + T], self.tri[r0:r1, r0:r1], ALU.mult,
                            [kC, 'tri'], [k_sc])
                for hh in range(4):
                    k_sc, k_qe = ('SH', 'sc', par, hh), ('SH', 'qe', par, hh)
                    self.mm(bE[:, hh * 64:hh * 64 + T], self.iv[r0:r1, pp, hh * 128:(hh + 1) * 128], self.sc[r0:r1, par, hh, 0:T], True, False,
                            [k_iv, k_sc], [('ps', 5)])
                    self.mm(bE[:, hh * 64:hh * 64 + T], self.Sb[:, si, hh, :], self.qe[:, par, hh, 0:T], False, True,
                            [('Sb', si, hh), k_qe], [('ps', 5)])
                for hh in range(4):
                    self.cp('act' if hh % 2 == 0 else 'dve', self.obT[:, hh, c0:c0 + T], bE[:, hh * 64:hh * 64 + T], [('ps', 5)], [('obT', hh)])
            for hh in range(4):
                self.mm(bF[:, hh * 128:(hh + 1) * 128], self.ke_tm[r0:r1, pp, hh * 128:(hh + 1) * 128], self.iv[r0:r1, pp, hh * 128:(hh + 1) * 128],
                        True, True, [k_ket, k_iv], [kF])
            for hh in range(4):
                k_eb = ('SH', 'eb', par, hh)
                ebl = self.eb[:, par, hh, T - 1:T]
                self.tsc('dve', self.tmpS[:, hh, :], self.S[:, si, hh, :], ebl, None, ALU.mult, None, [('S', si), k_eb], [('SH', 'tmpS', hh)])
                self.stt('dve', self.S[:, si, hh, :], bF[:, hh * 128:(hh + 1) * 128], ebl, self.tmpS[:, hh, :], ALU.mult, ALU.add,
                         [kF, k_eb, ('SH', 'tmpS', hh)], [('S', si)])
                self.cp('pool', self.Sb[:, si, hh, :], self.S[:, si, hh, :], [('S', si)], [('Sb', si, hh)])
            if self.f < 0 and si == 1:
                self.s.dma('pool', self.sh.rearrange("h k v -> k h v"), self.S[:, 1, :, :], [('S', 1)], [], 'sout')
            if self.f < 0 and si == 0 and self.npre > 0:
                self.cp('pool', self.Ssave[:, :, :], self.S[:, 0, :, :], [('S', 0)], ['Ssave'])
            if self.mode == 'main' and self.f == self.nft - 1 and ci == len(chunks) - 1:
                self.s.dma('pool', self.ph.rearrange("h k v -> k h v"), self.S[:, 0, :, :], [('S', 0)], [], 'sout')

        gens = [_chunk(ci, *ch) for ci, ch in enumerate(chunks)]
        next(gens[0])
        for i in range(len(gens)):
            if i + 1 < len(gens):
                next(gens[i + 1])
            for _ in gens[i]:
                pass

    def attn(self):
        f = self.f
        if f < 0:
            nb_c = (LC + 127) // 128
            for b in range(nb_c):
                r0 = b * 128
                kn = min(128, LC - r0)
                slot = b % 2
                ck_key = ('SH', 'ckv', slot)
                self.s.dma('sp', self.ckv[0:kn, slot, 0:512], self.ck[r0:r0 + kn, :], [], [ck_key], 'ckin%d' % slot)
                self.s.dma('sp', self.ckv[0:kn, slot, 512:1024], self.cv[r0:r0 + kn, :], [], [ck_key], 'ckin%d' % slot)
                bk = self.bank[b % 2]
                for ch in range(4):
                    self.tr(bk[:, ch * 128:ch * 128 + kn], self.ckv[0:kn, slot, ch * 128:(ch + 1) * 128], kn, [ck_key, 'ident'], [('ps', b % 2)])
                for ch in range(4):
                    self.cp('dve' if ch % 2 == 0 else 'act', self.KT[:, ch, NMETA + r0:NMETA + r0 + kn], bk[:, ch * 128:ch * 128 + kn],
                            [('ps', b % 2)], [('KT', ch)])
                self.cp('pool', self.Vp[0:kn, 1 + b, :], self.ckv[0:kn, slot, 512:1024], [ck_key], [('Vp', 1 + b)])
            blocks = [(18, TS, NMETA + LC, 0)]
            blocks.append((1 + nb_c - 1, LC - 128 * (nb_c - 1), NMETA + 128 * (nb_c - 1), None))
            for b in range(nb_c - 2, -1, -1):
                blocks.append((1 + b, 128, NMETA + 128 * b, None))
            self.attn_job(0, TS, blocks)
        else:
            blocks = []
            for m in range(3, -1, -1):
                fb = 4 * f + m
                blocks.append((1 + fb, 128, NMETA + 128 * fb, 128 * m))
            for fb in range(4 * f - 1, -1, -1):
                blocks.append((1 + fb, 128, NMETA + 128 * fb, 'flag' if fb < 4 * self.npre else None))
            blocks.append((0, NMETA, 0, None))
            self.attn_job(0, NT, blocks)

    def attn_job(self, q0, nq, blocks):
        nblk = len(blocks)

        def col0(md):
            return md if (md is not None and md != 'flag') else 0
        for grp in ([0, 1, 2], [3, 4, 5], [6, 7]):
            S_ = len(grp)
            nseq = nblk * S_

            def emitZ(idx):
                k, si = divmod(idx, S_)
                h = grp[si]
                vblk, kn, kcol, md = blocks[k]
                cs = col0(md)
                ch, pb = h // 2, 64 * (h % 2)
                zb = idx % 2
                self.mm(self.bank[zb][:, cs:nq], self.KT[pb:pb + 64, ch, kcol:kcol + 128], self.qT[pb:pb + 64, ch, q0 + cs:q0 + nq], True, True,
                        [('KT', ch), ('qT', ch)], [('ps', zb)])
            for idx in range(min(2, nseq)):
                emitZ(idx)
            for k in range(nblk):
                vblk, kn, kcol, md = blocks[k]
                cs = col0(md)
                par = k % 2
                for si, h in enumerate(grp):
                    idx = k * S_ + si
                    zb = idx % 2
                    ke_, ks_ = ('SH', 'e', si, par), ('SH', 'sp', si, par)
                    if md == 'flag':
                        self.act(self.e_[0:kn, si * 2 + par, cs:nq], self.bank[zb][0:kn, cs:nq], AF.Exp, [('ps', zb), 'vecs'], [ke_],
                                 bias=self.vecs[0:kn, V_DEAD:V_DEAD + 1])
                    else:
                        self.act(self.e_[0:kn, si * 2 + par, cs:nq], self.bank[zb][0:kn, cs:nq], AF.Exp, [('ps', zb)], [ke_])
                    if idx + 2 < nseq:
                        emitZ(idx + 2)
                    if md is not None and md != 'flag':
                        self.tt('dve', self.e_[0:kn, si * 2 + par, cs:nq], self.e_[0:kn, si * 2 + par, cs:nq], self.masks[0:kn, md // 128, cs:nq],
                                ALU.mult, [ke_, 'masks'], [ke_])
                    self.act(self.sp_[0:kn, si * 2 + par, cs:nq], self.e_[0:kn, si * 2 + par, cs:nq], AF.Ln, [ke_], [ks_], bias=1.0)
                for si, h in enumerate(grp):
                    ks_ = ('SH', 'sp', si, par)
                    self.mm(self.bank[2 + si][:, cs:nq], self.negU[0:kn, :], self.sp_[0:kn, si * 2 + par, cs:nq], k == 0, False,
                            [ks_, 'negU'], [('ps', 2 + si)], sgc=True)
                for si, h in enumerate(grp):
                    ke_, k2_, ka_ = ('SH', 'e', si, par), ('SH', 'e2', si), ('SH', 'at', si, par)
                    self.act(self.e2_[0:kn, si, cs:nq], self.bank[2 + si][0:kn, cs:nq], AF.Exp, [('ps', 2 + si)], [k2_])
                    self.tt('dve', self.at_[0:kn, si * 2 + par, cs:nq], self.e2_[0:kn, si, cs:nq], self.e_[0:kn, si * 2 + par, cs:nq], ALU.mult,
                            [k2_, ke_], [ka_])
                for si, h in enumerate(grp):
                    ks_, ka_ = ('SH', 'sp', si, par), ('SH', 'at', si, par)
                    self.mm(self.bank[2 + si][:, cs:nq], self.negL[0:kn, :], self.sp_[0:kn, si * 2 + par, cs:nq], False, k == nblk - 1,
                            [ks_, 'negL'], [('ps', 2 + si)], sgc=True)
                    vlo = h * 64 if h % 2 == 0 else (h - 1) * 64
                    self.mm(self.bank[5 + si][:, cs:nq], self.Vp[0:kn, vblk, vlo:vlo + 128], self.at_[0:kn, si * 2 + par, cs:nq], k == 0, k == nblk - 1,
                            [ka_, ('Vp', vblk)], [('ps', 5 + si)], sgc=True)
            for si, h in enumerate(grp):
                ch, pb = h // 2, 64 * (h % 2)
                self.cp('dve' if si % 2 == 0 else 'act', self.oaT[pb:pb + 64, ch, q0:q0 + nq], self.bank[5 + si][pb:pb + 64, 0:nq],
                        [('ps', 5 + si)], [('oaT', ch, pb)])

    def post(self):
        n = self.n
        hk = [('hT', kc) for kc in range(8)]
        bi = 0
        for ld in range(5):
            slot = self.load_fm(16 + ld * 4, 4)
            w = self.ring[:, slot, 0:4096].rearrange("p (c k n) -> p c k n", c=4, k=8)
            for ci in range(4):
                cidx = ld * 4 + ci
                b = bi % 4
                bi += 1
                bk = self.bank[b]
                for kc in range(8):
                    self.mm(bk[:, 0:n], w[:, ci, kc, :], self.hT[:, kc, 0:n], kc == 0, kc == 7, hk + [('ring', slot)], [('ps', b)])
                if cidx < 4:
                    self.act(self.sgT[:, cidx, 0:n], bk[:, 0:n], AF.Silu, [('ps', b)], [('SH', 'sgT', cidx)])
                elif cidx < 12:
                    o = cidx - 4
                    self.act(self.gA[:, o, 0:n], bk[:, 0:n], AF.Sigmoid, [('ps', b), 'vecs'], [('SH', 'gA', o)], bias=self.vecs[:, V_BGA + o:V_BGA + o + 1])
                else:
                    o = cidx - 12
                    self.act(self.gB[:, o, 0:n], bk[:, 0:n], AF.Sigmoid, [('ps', b), 'vecs'], [('SH', 'gB', o)], bias=self.vecs[:, V_BGB + o:V_BGB + o + 1])
        for hh in range(4):
            kq = ('SH', 'mT', hh)
            self.act(self.mT[:, hh, 0:n], self.obT[:, hh, 0:n], AF.Square, [('obT', hh)], [kq])
            self.mm(self.bank[4][:, 0:n], self.onesb[:, :], self.mT[:, hh, 0:n], True, True, [kq, 'onesb'], [('ps', 4)])
            self.act(self.lnt[:, 0:n], self.bank[4][:, 0:n], AF.Ln, [('ps', 4)], ['lnt'], bias=EPS, scale=1.0 / 128)
            self.act(self.rstd[:, 0:n], self.lnt[:, 0:n], AF.Exp, ['lnt'], ['rstd'], scale=-0.5)
            k1 = ('SH', 't1', 0)
            self.stt('dve', self.t1[:, 0, 0:n], self.obT[:, hh, 0:n], self.vecs[:, V_HGN + hh:V_HGN + hh + 1], self.rstd[:, 0:n], ALU.mult, ALU.mult,
                     [('obT', hh), 'vecs', 'rstd'], [k1])
            self.tt('pool', self.obn[:, hh, 0:n], self.t1[:, 0, 0:n], self.sgT[:, hh, 0:n], ALU.mult, [k1, ('SH', 'sgT', hh)], [('SH', 'obn', hh)])
        oak = [('oaT', c, pb) for c in range(4) for pb in (0, 64)]
        obk = [('SH', 'obn', c) for c in range(4)]
        for ld in range(2):
            slot = self.load_fm(36 + ld * 4, 4)
            w = self.ring[:, slot, 0:4096].rearrange("p (c k n) -> p c k n", c=4, k=8)
            for ci in range(4):
                o = ld * 4 + ci
                ba, bb = o % 2, 2 + o % 2
                for c in range(4):
                    self.mm(self.bank[ba][:, 0:n], w[:, ci, c, :], self.oaT[:, c, 0:n], c == 0, c == 3, oak + [('ring', slot)], [('ps', ba)])
                for c in range(4):
                    self.mm(self.bank[bb][:, 0:n], w[:, ci, 4 + c, :], self.obn[:, c, 0:n], c == 0, c == 3, obk + [('ring', slot)], [('ps', bb)])
                k1, k2 = ('SH', 't1', 0), ('SH', 't2', 0)
                self.tt('dve', self.t1[:, 0, 0:n], self.gA[:, o, 0:n], self.bank[ba][:, 0:n], ALU.mult, [('SH', 'gA', o), ('ps', ba)], [k1])
                self.tt('dve', self.t2[:, 0, 0:n], self.gB[:, o, 0:n], self.bank[bb][:, 0:n], ALU.mult, [('SH', 'gB', o), ('ps', bb)], [k2])
                self.tt('pool', self.mT[:, o, 0:n], self.t1[:, 0, 0:n], self.t2[:, 0, 0:n], ALU.add, [k1, k2], [('SH', 'mT', o)])
        mk = [('SH', 'mT', c) for c in range(8)]
        for ld in range(2):
            slot = self.load_fm(44 + ld * 4, 4)
            w = self.ring[:, slot, 0:4096].rearrange("p (c k n) -> p c k n", c=4, k=8)
            for ci in range(4):
                o = ld * 4 + ci
                b = 4 + o % 2
                for c in range(8):
                    self.mm(self.bank[b][:, 0:n], w[:, ci, c, :], self.mT[:, c, 0:n], c == 0, c == 7, mk + [('ring', slot)], [('ps', b)])
                self.tt('dve', self.xT[:, o, 0:n], self.xT[:, o, 0:n], self.bank[b][:, 0:n], ALU.add, [('xT', o), ('ps', b)], [('xT', o)])

    def final(self):
        n, f = self.n, self.f
        self.norm(V_NF, inplace=True)
        if f < 0:
            blocks = [(0, n, self.ys, 0, TS)]
            assert self.mode == 'sample'
        else:
            blocks = [(j * 128, 128, self.yp, self.m * NT + j * 128, 128) for j in range(4)]
        for (c0, nb, dst, r0, nout) in blocks:
            slot = self.xtok_i % 2
            self.xtok_i += 1
            xk = ('SH', 'x_tok', slot)
            for half in range(2):
                bk = self.bank[half]
                for q in range(4):
                    kc = half * 4 + q
                    self.tr(bk[:, q * 128:(q + 1) * 128], self.xT[:, kc, c0:c0 + 128], 128, [('xT', kc), 'ident'], [('ps', half)])
                self.cp('dve' if half == 0 else 'act', self.x_tok[0:nb, slot, half * 512:(half + 1) * 512], bk[0:nb, :], [('ps', half)], [xk])
            self.s.dma('pool', dst[r0:r0 + nout, :], self.x_tok[0:nout, slot, :], [xk], [], 'yout%d' % slot)


def _fix_kt_keys(b):
    pass


def _layouts(inp):
    f32 = np.float32

    def fm_vec(v):
        v = np.asarray(v, f32).reshape(-1, 128)
        return v.T

    def gu(wg, wu):
        a = np.asarray(wg, f32).reshape(8, 128, 22, 128).transpose(2, 1, 0, 3)
        b = np.asarray(wu, f32).reshape(8, 128, 22, 128).transpose(2, 1, 0, 3)
        return np.ascontiguousarray(np.stack([a, b], axis=2)).reshape(-1, 2048)

    def dn(wd):
        a = np.asarray(wd, f32).reshape(2, 11, 128, 8, 128).transpose(0, 3, 2, 1, 4)
        return np.ascontiguousarray(a).reshape(-1, 2048)
    w_in = np.asarray(inp['w_in'][0], f32)
    cols = np.concatenate([np.arange(0, 512), np.arange(512, 1024), np.arange(1536, 2048), np.arange(2560, 3072),
                           np.arange(3072, 3584), np.arange(3584, 4608), np.arange(4608, 5632)])
    fm = w_in[:, cols].reshape(8, 128, 36, 128).transpose(2, 1, 0, 3)
    wa = np.asarray(inp['w_branch_a'][0], f32).reshape(4, 128, 8, 128).transpose(2, 1, 0, 3)
    wb = np.asarray(inp['w_branch_b'][0], f32).reshape(4, 128, 8, 128).transpose(2, 1, 0, 3)
    wab = np.concatenate([wa, wb], axis=2)
    wo = np.asarray(inp['w_out'][0], f32).reshape(8, 128, 8, 128).transpose(2, 1, 0, 3)
    wfm = np.ascontiguousarray(np.concatenate([fm, wab, wo], axis=0)).reshape(-1, 2048)
    wtm = np.ascontiguousarray(w_in[:, 512:2560].reshape(8, 128, 4, 512).transpose(2, 1, 0, 3)).reshape(-1, 2048)
    vecs = np.zeros((128, NVEC), f32)
    vecs[:, V_N1:V_N1 + 8] = fm_vec(inp['ffn1_norm'][0])
    vecs[:, V_NM:V_NM + 8] = fm_vec(inp['mix_norm'][0])
    vecs[:, V_N2:V_N2 + 8] = fm_vec(inp['ffn2_norm'][0])
    vecs[:, V_NF:V_NF + 8] = fm_vec(inp['final_norm'])
    vecs[:, V_BGA:V_BGA + 16] = fm_vec(inp['b_gate'][0])
    vecs[:, V_HGN:V_HGN + 4] = fm_vec(inp['hg_out_norm'][0])
    vecs[:, V_LB0:V_LB0 + 4] = fm_vec(inp['hg_lb_logits'][0])
    vecs[:, V_LB1:V_LB1 + 4] = fm_vec(inp['hg_lb_logits'][1])
    lbrep = np.ascontiguousarray(np.broadcast_to(np.asarray(inp['hg_lb_logits'], f32)[None], (128, 2, 512)))
    p = np.arange(128)[:, None]
    j = np.arange(128)[None, :]
    ident = (p == j).astype(f32)
    ones = np.ones((128, 128), f32)
    negU = -(p >= j).astype(f32)
    negL = -(p < j).astype(f32)
    tri = (p <= j).astype(f32)
    cc = np.arange(512)[None, :]
    masks = np.concatenate([((p + d) < cc).astype(f32) for d in (0, 128, 256, 384)], axis=1)
    trib = ((p <= j) & ((p // 64) == (j // 64))).astype(f32)
    cst = np.ascontiguousarray(np.concatenate([ident, ones, negU, tri, negL, masks, trib], axis=1))
    return dict(gu1=gu(inp['ffn1_w_gate'][0], inp['ffn1_w_up'][0]), d1=dn(inp['ffn1_w_down'][0]),
                gu2=gu(inp['ffn2_w_gate'][0], inp['ffn2_w_up'][0]), d2=dn(inp['ffn2_w_down'][0]),
                wfm=wfm, wtm=wtm, vecs=vecs, lbrep=lbrep, cst=cst)


_NC_CACHE = {}


def run(inp, nft):
    f32 = np.float32
    shared = _layouts(inp)
    if nft % 2 == 0:
        npre = nmain = nft // 2
    else:
        npre, nmain = 0, nft
    key = (npre, nmain)
    if key not in _NC_CACHE:
        _NC_CACHE[key] = Builder(npre, nmain).build()
    nc = _NC_CACHE[key]
    xp = np.asarray(inp['x_prompt'], f32)
    xs = np.asarray(inp['x_sample'], f32)
    ck = np.asarray(inp['cache_sb_k'], f32)
    cv = np.asarray(inp['cache_sb_v'], f32)
    st = np.asarray(inp['state_hgrn'], f32)
    meta = np.ascontiguousarray(np.asarray(inp['meta_tokens'], f32))
    B = xp.shape[0]
    H = nmain * NT
    in_maps = []
    for c in range(8):
        m = dict(shared)
        b, half = c // 2, c % 2
        vecs = shared['vecs'].copy()
        if npre == 0:
            m['xp'] = np.ascontiguousarray(xp[b])
            m['xpre'] = np.zeros((NT, D), f32)
            m['flag'] = np.ones((128, 512), f32)
            vecs[:, V_FA], vecs[:, V_FB], vecs[:, V_DEAD] = 0.0, 1.0, 0.0
        elif half == 0:
            m['xp'] = np.ascontiguousarray(xp[b, 0:H])
            m['xpre'] = np.zeros((npre * NT, D), f32)
            m['flag'] = np.zeros((128, 512), f32)
            vecs[:, V_FA], vecs[:, V_FB], vecs[:, V_DEAD] = 1.0, 0.0, -30000.0
        else:
            m['xp'] = np.ascontiguousarray(xp[b, H:2 * H])
            m['xpre'] = np.ascontiguousarray(xp[b, 0:H])
            m['flag'] = np.ones((128, 512), f32)
            vecs[:, V_FA], vecs[:, V_FB], vecs[:, V_DEAD] = 0.0, 1.0, 0.0
        m['vecs'] = vecs
        m['xs'] = np.ascontiguousarray(xs[c])
        m['meta'] = meta
        m['ck'] = np.ascontiguousarray(ck[0, c].reshape(LC, 512))
        m['cv'] = np.ascontiguousarray(cv[0, c].reshape(LC, 512))
        m['st'] = np.ascontiguousarray(st[0, c])
        in_maps.append(m)
    res = run_bass_kernel_spmd(nc, in_maps, core_ids=list(range(8)))
    r = res.results
    if npre == 0:
        y_prompt = np.stack([r[2 * b]['yp'] for b in range(B)])
        pk = np.stack([r[2 * b]['pk'] for b in range(B)])
        pv = np.stack([r[2 * b]['pv'] for b in range(B)])
        ph = np.stack([r[2 * b]['ph'] for b in range(B)])
    else:
        y_prompt = np.stack([np.concatenate([r[2 * b]['yp'], r[2 * b + 1]['yp']], axis=0) for b in range(B)])
        pk = np.stack([np.concatenate([r[2 * b]['pk'], r[2 * b + 1]['pk'][NMETA:]], axis=0) for b in range(B)])
        pv = np.stack([np.concatenate([r[2 * b]['pv'], r[2 * b + 1]['pv'][NMETA:]], axis=0) for b in range(B)])
        ph = np.stack([r[2 * b + 1]['ph'] for b in range(B)])
    y_prompt = y_prompt.astype(f32)
    L = pk.shape[1]
    pk = pk.reshape(B, L, 8, 64)[None].astype(f32)
    pv = pv.reshape(B, L, 8, 64)[None].astype(f32)
    ph = ph[None].astype(f32)
    y_sample = np.stack([r[c]['ys'] for c in range(8)]).astype(f32)
    sk = np.stack([r[c]['sk'].reshape(TS, 8, 64) for c in range(8)])[None].astype(f32)
    sv = np.stack([r[c]['sv'].reshape(TS, 8, 64) for c in range(8)])[None].astype(f32)
    sh = np.stack([r[c]['sh'] for c in range(8)])[None].astype(f32)
    return (y_prompt, y_sample, pk, pv, ph, sk, sv, sh)


def kernel(**inputs):
    nft = np.asarray(inputs['x_prompt']).shape[1] // NT
    return run(inputs, nft)
```

```python
import numpy as np
from contextlib import ExitStack
import concourse.bass as bass
import concourse.mybir as mybir
from concourse.bass_utils import run_bass_kernel_spmd

F32 = mybir.dt.float32
BF16 = mybir.dt.bfloat16
AF = mybir.ActivationFunctionType
ALU = mybir.AluOpType

D = 1024
DFF = 2816
NMETA = 16
TS = 32
LC = 2064
EPS = 1e-6
NT = 512
ENGS = ['pe', 'act', 'dve', 'pool', 'sp']

V_N1, V_NM, V_N2, V_NF, V_BGA, V_BGB, V_HGN, V_LB0, V_LB1, V_LBV, V_OML, V_FA, V_FB, V_DEAD = 0, 8, 16, 24, 32, 40, 48, 52, 56, 60, 64, 68, 69, 70
NVEC = 71


class Sched:
    def __init__(self):
        self.prog = {e: [] for e in ENGS}
        self.cnt = {e: 0 for e in ENGS}
        self.dma_cnt = {}
        self.last_w = {}
        self.readers = {}
        self.known = {e: {} for e in ENGS}
        self.fence = {}
        self.touched = set()
        self.nwaits = 0
        import os
        self.glimit = int(os.environ.get('KOPS', '100000000'))

    def _deps(self, reads, writes):
        need = {}

        def add(src, val):
            if need.get(src, 0) < val:
                need[src] = val
        for k in list(reads) + list(writes):
            if isinstance(k, tuple) and k[0] == 'SH' and k not in self.touched:
                self.touched.add(k)
                for s, v in self.fence.items():
                    add(s, v)
        for k in reads:
            lw = self.last_w.get(k)
            if lw:
                add(*lw)
        for k in writes:
            lw = self.last_w.get(k)
            if lw:
                add(*lw)
            for s, v in self.readers.get(k, {}).items():
                add(s, v)
        return need

    def _waits(self, eng, need):
        waits = []
        for src, val in need.items():
            if src == eng and eng == 'pe':
                continue
            if self.known[eng].get(src, 0) >= val:
                continue
            self.known[eng][src] = val
            waits.append((src, val))
        self.nwaits += len(waits)
        return waits

    def _mark(self, src, val, reads, writes):
        for k in writes:
            self.last_w[k] = (src, val)
            self.readers[k] = {}
        for k in reads:
            if k in writes:
                continue
            self.readers.setdefault(k, {})[src] = val

    def op(self, eng, fn, reads=(), writes=()):
        psr = [k for k in reads if isinstance(k, tuple) and k[0] == 'ps']
        if psr:
            reads = [k for k in reads if k not in psr]
            writes = list(writes) + [k for k in psr if k not in writes]
        self.gcount = getattr(self, 'gcount', 0) + 1
        if self.gcount > self.glimit:
            return
        import sys as _sys, os as _os
        if _os.environ.get('KTRACE'):
            lo, hi = [int(x) for x in _os.environ['KTRACE'].split(',')]
            if lo <= self.gcount <= hi:
                fr = _sys._getframe(2)
                print('OP', self.gcount, eng, 'line', fr.f_lineno, 'from', fr.f_back.f_lineno, 'reads', list(reads)[:3], 'writes', list(writes))
        need = self._deps(reads, writes)
        waits = self._waits(eng, need)
        self.cnt[eng] += 1
        self._mark(eng, self.cnt[eng], reads, writes)
        self.prog[eng].append(('op', waits, fn))

    def dma(self, q, out, in_, reads, writes, sem):
        self.gcount = getattr(self, 'gcount', 0) + 1
        if self.gcount > self.glimit:
            return
        import sys as _sys, os as _os
        if _os.environ.get('KTRACE'):
            lo, hi = [int(x) for x in _os.environ['KTRACE'].split(',')]
            if lo <= self.gcount <= hi:
                fr = _sys._getframe(1)
                print('DMA', self.gcount, q, 'line', fr.f_lineno, 'sem', sem, 'writes', list(writes))
        need = self._deps(reads, writes)
        waits = self._waits(q, need)
        self.dma_cnt[sem] = self.dma_cnt.get(sem, 0) + 16
        self._mark(sem, self.dma_cnt[sem], reads, writes)
        self.prog[q].append(('dma', waits, (out, in_, sem)))

    def phase_switch(self):
        f = dict(self.fence)

        def add(s, v):
            if f.get(s, 0) < v:
                f[s] = v
        for k in list(self.last_w.keys()):
            if isinstance(k, tuple) and k[0] == 'SH':
                add(*self.last_w[k])
                del self.last_w[k]
        for k in list(self.readers.keys()):
            if isinstance(k, tuple) and k[0] == 'SH':
                for s, v in self.readers[k].items():
                    add(s, v)
                del self.readers[k]
        self.fence = f
        self.touched = set()

    def final_wait(self, q):
        waits = [(s, v) for s, v in self.dma_cnt.items()]
        self.prog[q].append(('wait', waits, None))


class Builder:
    def __init__(self, npre, nmain):
        self.npre, self.nmain = npre, nmain
        nft = npre + nmain
        self.nft = nft
        self.FR = nft * NT
        self.KTW = max(NMETA + self.FR, NMETA + LC + TS + 128)
        self.NBLK = max(1 + 4 * nft, 19)
        self.s = Sched()
        self.ring_i = 0
        self.kv_i = 0
        self.xtok_i = 0

    def mm(self, out, lhsT, rhs, start, stop, reads, writes, sgc=False):
        self.s.op('pe', lambda e: e.matmul(out, lhsT=lhsT, rhs=rhs, start=start, stop=stop, skip_group_check=sgc), reads, writes)

    def tr(self, out, in_, n, reads, writes):
        ident = self.ident
        self.s.op('pe', lambda e: e.transpose(out=out, in_=in_, identity=ident[0:n, 0:n]), reads, writes)

    def act(self, out, in_, func, reads, writes, bias=None, scale=None):
        kw = {}
        if bias is not None:
            kw['bias'] = bias
        if scale is not None:
            kw['scale'] = scale
        self.s.op('act', lambda e: e.activation(out=out, in_=in_, func=func, **kw), reads, writes)

    def tt(self, eng, out, in0, in1, op, reads, writes):
        self.s.op(eng, lambda e: e.tensor_tensor(out=out, in0=in0, in1=in1, op=op), reads, writes)

    def tsc(self, eng, out, in0, s1, s2, op0, op1, reads, writes):
        if s2 is None:
            self.s.op(eng, lambda e: e.tensor_scalar(out=out, in0=in0, scalar1=s1, scalar2=None, op0=op0), reads, writes)
        else:
            self.s.op(eng, lambda e: e.tensor_scalar(out=out, in0=in0, scalar1=s1, scalar2=s2, op0=op0, op1=op1), reads, writes)

    def stt(self, eng, out, in0, scalar, in1, op0, op1, reads, writes):
        self.s.op(eng, lambda e: e.scalar_tensor_tensor(out=out, in0=in0, scalar=scalar, in1=in1, op0=op0, op1=op1), reads, writes)

    def cp(self, eng, out, in_, reads, writes):
        if eng == 'act':
            self.act(out, in_, AF.Copy, reads, writes)
        else:
            self.s.op(eng, lambda e: e.tensor_copy(out=out, in_=in_), reads, writes)

    def ring_load(self, dram_ap, nelem, rdkey):
        slot = self.ring_i % 4
        self.ring_i += 1
        out = self.ring[:, slot, 0:nelem]
        self.s.dma('sp', out, dram_ap, [rdkey], [('ring', slot)], 'ring%d' % slot)
        return slot

    def build(self):
        nc = bass.Bass("TRN2", target_bir_lowering=False)
        self.nc = nc
        nft, FR = self.nft, self.FR
        FM_ = self.nmain * NT
        FP_ = max(self.npre, 1) * NT
        dt_in = {}

        def din(name, shape):
            dt_in[name] = nc.dram_tensor(name, list(shape), F32, kind="ExternalInput").ap()
            return dt_in[name]

        def dout(name, shape):
            return nc.dram_tensor(name, list(shape), F32, kind="ExternalOutput").ap()
        self.xp = din("xp", [FM_, D])
        self.xpre = din("xpre", [FP_, D])
        self.flag_d = din("flag", [128, 512])
        self.xs = din("xs", [TS, D])
        self.meta = din("meta", [NMETA, D])
        self.ck = din("ck", [LC, 512])
        self.cv = din("cv", [LC, 512])
        self.st = din("st", [4, 128, 128])
        self.vecs_d = din("vecs", [128, NVEC])
        self.lbrep_d = din("lbrep", [128, 2, 512])
        self.cst_d = din("cst", [128, 6 * 128 + 4 * 512])
        self.w32 = {}
        self.wsz = {'gu1': 22 * 128 * 2048, 'gu2': 22 * 128 * 2048, 'd1': 2 * 8 * 128 * 1408, 'd2': 2 * 8 * 128 * 1408,
                    'wfm': 52 * 128 * 1024, 'wtm': 4 * 128 * 4096}
        for n, sz in self.wsz.items():
            self.w32[n] = din(n, [sz // 2048, 2048])
        self.yp = dout("yp", [FM_, D])
        self.ys = dout("ys", [TS, D])
        self.pk = dout("pk", [NMETA + FM_, 512])
        self.pv = dout("pv", [NMETA + FM_, 512])
        self.ph = dout("ph", [4, 128, 128])
        self.sk = dout("sk", [TS, 512])
        self.sv = dout("sv", [TS, 512])
        self.sh = dout("sh", [4, 128, 128])
        self.wbf = {n: nc.dram_tensor(n + "_bf", [sz], BF16).ap() for n, sz in self.wsz.items()}

        with ExitStack() as es:
            es.enter_context(nc.allow_low_precision("bf16 matmul operands, fp32 accumulation"))

            def sb(name, shape, dt):
                return es.enter_context(nc.sbuf_tensor(name, list(shape), dt))

            def ps(name):
                return es.enter_context(nc.psum_tensor(name, [128, 512], F32))
            self.ident = sb("ident", [128, 128], F32)
            self.onesb = sb("onesb", [128, 128], BF16)
            self.negU = sb("negU", [128, 128], BF16)
            self.negL = sb("negL", [128, 128], BF16)
            self.tri = sb("tri", [128, 128], F32)
            self.masks = sb("masks", [128, 4, 512], BF16)
            self.vecs = sb("vecs_s", [128, NVEC], F32)
            self.lb_t = sb("lb_t", [128, 2, 512], F32)
            self.xT = sb("xT", [128, 8, NT], F32)
            self.hT = sb("hT", [128, 8, NT + 64], BF16)
            self.rstd = sb("rstd", [128, NT], F32)
            self.lnt = sb("lnt", [128, NT], F32)
            self.KT = sb("KT", [128, 4, self.KTW], BF16)
            self.Vp = sb("Vp", [128, self.NBLK, 512], BF16)
            self.ring = sb("ring", [128, 4, 4096], BF16)
            self.S = sb("S", [128, 2, 4, 128], F32)
            self.Sb = sb("Sb", [128, 2, 4, 128], BF16)
            self.kvst = sb("kvst", [128, 2, 512], F32)
            self.Ssave = sb("Ssave", [128, 4, 128], F32)
            self.qT = sb("qT", [128, 4, NT], BF16)
            self.obT = sb("obT", [128, 4, NT], F32)
            self.oaT = sb("oaT", [128, 4, NT], BF16)
            SHW = 11008
            self.SH = sb("SH", [128, SHW], F32)
            self.bank = [ps("bank%d" % i) for i in range(8)]
            self.trib = sb("trib", [128, 128], F32)
            self._views()
            import os
            if os.environ.get('KMEM'):
                print('sbuf bytes remaining', nc.sbuf_bytes_remaining)

            self._emit_all()

            sems = {}
            names = [e for e in ENGS if e != 'sp'] + sorted(self.s.dma_cnt.keys())
            for n in names:
                sems[n] = es.enter_context(nc.semaphore("s_" + n))
            block = es.enter_context(nc.Block())
            prog = self.s.prog

            def run(eng_obj, ename):
                for kind, waits, payload in prog[ename]:
                    for src, val in waits:
                        eng_obj.wait_ge(sems[src], val)
                    if kind == 'op':
                        ins = payload(eng_obj)
                        ins.then_inc(sems[ename], 1)
                    elif kind == 'dma':
                        out, in_, sem = payload
                        eng_obj.dma_start(out=out, in_=in_).then_inc(sems[sem], 16)

            @block.tensor
            def _(e):
                run(e, 'pe')

            @block.scalar
            def _(e):
                run(e, 'act')

            @block.vector
            def _(e):
                run(e, 'dve')

            @block.gpsimd
            def _(e):
                run(e, 'pool')

            @block.sync
            def _(e):
                run(e, 'sp')
        return nc

    def _views(self):
        SH = self.SH

        def v(off_b, nbytes, dt, pattern=None, **kw):
            a = SH[:, off_b // 4:(off_b + nbytes) // 4]
            if dt is BF16:
                a = a.bitcast(BF16)
            if pattern:
                a = a.rearrange(pattern, **kw)
            return a
        self.hid = v(0, 11264, BF16, "p (c n) -> p c n", c=11)
        self.sg = v(11264, 4096, F32, "p (c n) -> p c n", c=2)
        self.x_tok = v(15360, 8192, F32, "p (c n) -> p c n", c=2)
        self.omfT = v(0, 8192, F32, "p (c n) -> p c n", c=4)
        self.qhT = v(8192, 8192, F32, "p (c n) -> p c n", c=4)
        self.logf = v(16384, 4096, F32, "p (c n) -> p c n", c=2)
        self.omf_tm = v(20480, 4096, F32, "p (c n) -> p c n", c=2)
        self.iv = v(24576, 2048, BF16, "p (c n) -> p c n", c=2)
        self.eb = v(26624, 2048, F32, "p (a h t) -> p a h t", a=2, h=4)
        self.enb = v(28672, 2048, F32, "p (a h t) -> p a h t", a=2, h=4)
        self.qe = v(30720, 1024, BF16, "p (a h t) -> p a h t", a=2, h=4)
        self.ke = v(41984, 2048, BF16, "p (a h t) -> p a h t", a=2, h=4)
        self.enb_tm = v(32768, 4096, F32, "p (c n) -> p c n", c=2)
        self.ke_tm = v(36864, 2048, BF16, "p (c n) -> p c n", c=2)
        self.sc = v(38912, 1024, BF16, "p (a h t) -> p a h t", a=2, h=4)
        self.tmpS = v(39936, 2048, F32, "p (h t) -> p h t", h=4)
        self.e_ = v(0, 12288, F32, "p (c n) -> p c n", c=6)
        self.sp_ = v(12288, 6144, BF16, "p (c n) -> p c n", c=6)
        self.e2_ = v(18432, 6144, F32, "p (c n) -> p c n", c=3)
        self.at_ = v(24576, 6144, BF16, "p (c n) -> p c n", c=6)
        self.ckv = v(30720, 8192, F32, "p (c n) -> p c n", c=2)
        self.sgT = v(0, 8192, F32, "p (c n) -> p c n", c=4)
        self.gA = v(8192, 8192, BF16, "p (c n) -> p c n", c=8)
        self.gB = v(16384, 8192, BF16, "p (c n) -> p c n", c=8)
        self.t1 = v(24576, 2048, F32, "p (c n) -> p c n", c=1)
        self.t2 = v(26624, 2048, F32, "p (c n) -> p c n", c=1)
        self.mT = v(28672, 8192, BF16, "p (c n) -> p c n", c=8)
        self.obn = v(36864, 4096, BF16, "p (c n) -> p c n", c=4)

    def _emit_all(self):
        import os
        self.kstop = int(os.environ.get('KSTOP', '99'))
        self.prologue()
        self.tile(-1, 'meta')
        self.casts([('wfm', 1024, 2304), ('gu2', 0, 1408), ('d2', 0, 704), ('gu2', 1408, 1408), ('d2', 704, 704)], 6)
        for p in range(self.npre):
            self.tile(p, 'pre')
        if self.npre > 0:
            self.state_select()
        for m in range(self.nmain):
            self.tile(self.npre + m, 'main')
        import os
        if not os.environ.get('NOSAMPLE'):
            self.tile(-1, 'sample')
        self.s.final_wait('sp')

    def prologue(self):
        s = self.s
        c = self.cst_d
        o = 0
        s.dma('pool', self.ident[:], c[:, o:o + 128], [], ['ident'], 'cst'); o += 128
        s.dma('pool', self.onesb[:], c[:, o:o + 128], [], ['onesb'], 'cst'); o += 128
        s.dma('pool', self.negU[:], c[:, o:o + 128], [], ['negU'], 'cst'); o += 128
        s.dma('pool', self.tri[:], c[:, o:o + 128], [], ['tri'], 'cst'); o += 128
        s.dma('pool', self.negL[:], c[:, o:o + 128], [], ['negL'], 'cst'); o += 128
        s.dma('pool', self.masks[:], c[:, o:o + 2048].rearrange("p (d n) -> p d n", d=4), [], ['masks'], 'cst'); o += 2048
        s.dma('pool', self.trib[:], c[:, o:o + 128], [], ['trib'], 'cst'); o += 128
        s.dma('pool', self.vecs[:], self.vecs_d, [], ['vecs'], 'cst')
        s.dma('pool', self.lb_t[:], self.lbrep_d, [], ['lb_t'], 'cst')
        allc = ['ident', 'onesb', 'negU', 'negL', 'tri', 'masks', 'vecs', 'lb_t', 'trib']
        tot = s.dma_cnt['cst']
        for k in allc:
            s.last_w[k] = ('cst', tot)
        self.s.op('dve', lambda e: e.memset(self.hT[:, :, :], 0.0), [], [('hT', kc) for kc in range(8)])
        self.s.op('dve', lambda e: e.memset(self.KT[:, :, :], 0.0), [], [('KT', c) for c in range(4)])
        self.s.op('dve', lambda e: e.memset(self.xT[:, :, :], 0.0), [], [('xT', kc) for kc in range(8)])
        self.s.op('dve', lambda e: e.memset(self.SH[:, :], 0.0), [], [('SH', 'all')])
        self.casts([('gu1', 0, 1408), ('d1', 0, 704), ('gu1', 1408, 1408), ('d1', 704, 704), ('wfm', 0, 1024), ('wtm', 0, 1024)], 0)
        vv = self.vecs
        self.tt('dve', vv[:, V_LBV:V_LBV + 4], vv[:, V_LB1:V_LB1 + 4], vv[:, V_LB0:V_LB0 + 4], ALU.subtract, ['vecs'], ['vecs'])
        self.act(vv[:, V_LBV:V_LBV + 4], vv[:, V_LBV:V_LBV + 4], AF.Sigmoid, ['vecs'], ['vecs'])
        self.tsc('dve', vv[:, V_OML:V_OML + 4], vv[:, V_LBV:V_LBV + 4], -1.0, 1.0, ALU.mult, ALU.add, ['vecs'], ['vecs'])
        lt = self.lb_t
        self.tt('dve', lt[:, 0, :], lt[:, 1, :], lt[:, 0, :], ALU.subtract, ['lb_t'], ['lb_t'])
        self.act(lt[:, 0, :], lt[:, 0, :], AF.Sigmoid, ['lb_t'], ['lb_t'])
        self.tsc('dve', lt[:, 1, :], lt[:, 0, :], -1.0, 1.0, ALU.mult, ALU.add, ['lb_t'], ['lb_t'])

    def casts(self, pieces, i0):
        for i, (n, r0, nr) in enumerate(pieces):
            dst = self.wbf[n].rearrange("(r c) -> r c", c=2048)[r0:r0 + nr, :]
            self.s.dma('pool', dst, self.w32[n][r0:r0 + nr, :], [], [('scr', n, r0)], 'cast%d' % (i0 + i))

    def scr_key(self, n, row2048):
        bounds = {'gu1': [0, 1408], 'gu2': [0, 1408], 'd1': [0, 704], 'd2': [0, 704], 'wfm': [0, 1024], 'wtm': [0]}[n]
        r0 = max(b for b in bounds if b <= row2048)
        return ('scr', n, r0)

    def load_gu(self, which, c):
        n = 'gu%d' % which
        ap = self.wbf[n].rearrange("(c p f) -> c p f", p=128, f=2048)[c]
        return self.ring_load(ap, 2048, self.scr_key(n, c * 128))

    def load_d(self, which, half, o):
        n = 'd%d' % which
        ap = self.wbf[n].rearrange("(h o p f) -> h o p f", h=2, o=8, p=128)[half, o]
        return self.ring_load(ap, 1408, self.scr_key(n, (half * 8 + o) * 128 * 1408 // 2048))

    def load_fm(self, c0, ncnk):
        slot = self.ring_i % 4
        self.ring_i += 1
        src = self.wbf['wfm'].rearrange("(c p f) -> p c f", p=128, f=1024)[:, c0:c0 + ncnk, :]
        out = self.ring[:, slot, 0:ncnk * 1024].rearrange("p (c f) -> p c f", c=ncnk)
        self.s.dma('sp', out, src, [self.scr_key('wfm', c0 * 64)], [('ring', slot)], 'ring%d' % slot)
        return slot

    def load_tm(self, g):
        ap = self.wbf['wtm'].rearrange("(g p f) -> g p f", p=128, f=4096)[g]
        return self.ring_load(ap, 4096, ('scr', 'wtm', 0))

    def state_select(self):
        S0 = self.S[:, 0, :, :]
        self.tsc('dve', self.Ssave[:, :, :], self.Ssave[:, :, :], self.vecs[:, V_FA:V_FA + 1], None, ALU.mult, None, ['Ssave', 'vecs'], ['Ssave'])
        self.stt('dve', S0, S0, self.vecs[:, V_FB:V_FB + 1], self.Ssave[:, :, :], ALU.mult, ALU.add, [('S', 0), 'Ssave', 'vecs'], [('S', 0)])
        for hh in range(4):
            self.cp('pool', self.Sb[:, 0, hh, :], self.S[:, 0, hh, :], [('S', 0)], [('Sb', 0, hh)])

    def tile(self, f, mode='extra'):
        self.mode = mode
        self.m = f - self.npre if mode == 'main' else f
        if mode == 'meta':
            n = NMETA
            segs = [('meta', 0, NMETA)]
        elif mode == 'sample':
            n = TS
            segs = [('sample', 0, TS)]
        else:
            n = NT
            segs = [('frames', 0, NT)]
        self.n = n
        self.f = f
        self.segs = segs
        s = self.s
        ks = self.kstop if f < 0 else 99
        s.phase_switch()
        self.load_x()
        if mode in ('pre', 'meta'):
            self.norm(V_N1)
            self.ffn(1)
            self.norm(V_NM)
            s.phase_switch()
            self.w_in()
            self.hgrn()
            return
        if ks < 2: return
        self.norm(V_N1)
        if ks < 3: return
        self.ffn(1)
        if ks < 4: return
        self.norm(V_NM)
        s.phase_switch()
        self.w_in()
        if ks < 5 or ks == 41: return
        self.hgrn()
        if ks < 6: return
        s.phase_switch()
        self.attn()
        if ks < 7: return
        s.phase_switch()
        self.post()
        if ks < 8: return
        s.phase_switch()
        self.norm(V_N2)
        self.ffn(2)
        self.final()

    def load_x(self):
        n, f = self.n, self.f
        if f < 0:
            blocks = [(0, n)]
        else:
            blocks = [(j * 128, 128) for j in range(4)]
        for bi, (c0, nb) in enumerate(blocks):
            slot = self.xtok_i % 2
            self.xtok_i += 1
            xk = ('SH', 'x_tok', slot)
            if f < 0:
                self.s.dma('sp', self.x_tok[0:n, slot, :], self.xs if self.mode == 'sample' else self.meta, [], [xk], 'xin%d' % slot)
            else:
                r0 = self.m * NT + c0
                srcx = self.xpre if self.mode == 'pre' else self.xp
                self.s.dma('sp', self.x_tok[:, slot, :], srcx[r0:r0 + 128, :], [], [xk], 'xin%d' % slot)
            for kc in range(8):
                bk = self.bank[kc % 2]
                self.tr(bk[:, 0:nb], self.x_tok[0:nb, slot, kc * 128:(kc + 1) * 128], nb, [xk, 'ident'], [('ps', kc % 2)])
                eng = 'dve' if kc % 2 == 0 else 'act'
                self.cp(eng, self.xT[:, kc, c0:c0 + nb], bk[:, 0:nb], [('ps', kc % 2)], [('xT', kc)])

    def norm(self, gcol, inplace=False):
        n = self.n
        for kc in range(8):
            self.act(self.hT[:, kc, 0:n], self.xT[:, kc, 0:n], AF.Square, [('xT', kc)], [('hT', kc)])
        bk = self.bank[7]
        for kc in range(8):
            self.mm(bk[:, 0:n], self.onesb[:, :], self.hT[:, kc, 0:n], kc == 0, kc == 7, [('hT', kc), 'onesb'], [('ps', 7)])
        self.act(self.lnt[:, 0:n], bk[:, 0:n], AF.Ln, [('ps', 7)], ['lnt'], bias=EPS, scale=1.0 / D)
        self.act(self.rstd[:, 0:n], self.lnt[:, 0:n], AF.Exp, ['lnt'], ['rstd'], scale=-0.5)
        for kc in range(8):
            eng = 'dve'
            if inplace:
                self.stt(eng, self.xT[:, kc, 0:n], self.xT[:, kc, 0:n], self.vecs[:, gcol + kc:gcol + kc + 1], self.rstd[:, 0:n],
                         ALU.mult, ALU.mult, [('xT', kc), 'rstd', 'vecs'], [('xT', kc)])
            else:
                self.stt(eng, self.hT[:, kc, 0:n], self.xT[:, kc, 0:n], self.vecs[:, gcol + kc:gcol + kc + 1], self.rstd[:, 0:n],
                         ALU.mult, ALU.mult, [('xT', kc), 'rstd', 'vecs'], [('hT', kc)])

    def ffn(self, which):
        n = self.n
        hk = [('hT', kc) for kc in range(8)]
        it = 0
        gun, dnn = 'gu%d' % which, 'd%d' % which
        gsrc = self.wbf[gun].rearrange("(c p f) -> p c f", p=128, f=2048)
        dsrc = self.wbf[dnn].rearrange("(h o p f) -> h p o f", h=2, o=8, p=128)
        for half in range(2):
            for cc0 in range(0, 11, 2):
                ncnk = min(2, 11 - cc0)
                c = half * 11 + cc0
                slot = self.ring_i % 4
                self.ring_i += 1
                self.s.dma('sp', self.ring[:, slot, 0:ncnk * 2048].rearrange("p (c f) -> p c f", c=ncnk), gsrc[:, c:c + ncnk, :],
                           [self.scr_key(gun, c * 128)], [('ring', slot)], 'ring%d' % slot)
                for ci in range(ncnk):
                    cc = cc0 + ci
                    w = self.ring[:, slot, ci * 2048:(ci + 1) * 2048].rearrange("p (g k n) -> p g k n", g=2, k=8)
                    gb, ub = it % 2, 2 + it % 2
                    for kc in range(8):
                        self.mm(self.bank[gb][:, 0:n], w[:, 0, kc, :], self.hT[:, kc, 0:n], kc == 0, kc == 7, hk + [('ring', slot)], [('ps', gb)])
                    for kc in range(8):
                        self.mm(self.bank[ub][:, 0:n], w[:, 1, kc, :], self.hT[:, kc, 0:n], kc == 0, kc == 7, hk + [('ring', slot)], [('ps', ub)])
                    sgk = ('SH', 'sg', it % 2)
                    self.act(self.sg[:, it % 2, 0:n], self.bank[gb][:, 0:n], AF.Silu, [('ps', gb)], [sgk])
                    self.tt('dve', self.hid[:, cc, 0:n], self.sg[:, it % 2, 0:n], self.bank[ub][:, 0:n], ALU.mult, [sgk, ('ps', ub)], [('SH', 'hid', cc)])
                    it += 1
            for o0 in range(0, 8, 2):
                slot = self.ring_i % 4
                self.ring_i += 1
                self.s.dma('sp', self.ring[:, slot, 0:2 * 1408].rearrange("p (c f) -> p c f", c=2), dsrc[half, :, o0:o0 + 2, :],
                           [self.scr_key(dnn, (half * 8 + o0) * 128 * 1408 // 2048)], [('ring', slot)], 'ring%d' % slot)
                for oi in range(2):
                    o = o0 + oi
                    w = self.ring[:, slot, oi * 1408:(oi + 1) * 1408].rearrange("p (c n) -> p c n", c=11)
                    db = 4 + o % 2
                    for cc in range(11):
                        self.mm(self.bank[db][:, 0:n], w[:, cc, :], self.hid[:, cc, 0:n], cc == 0, cc == 10, [('SH', 'hid', cc), ('ring', slot)], [('ps', db)])
                    self.stt('dve', self.xT[:, o, 0:n], self.bank[db][:, 0:n], 0.5, self.xT[:, o, 0:n], ALU.mult, ALU.add, [('ps', db), ('xT', o)], [('xT', o)])

    def kcol(self, seg):
        if seg == 'sample':
            return NMETA + LC
        if seg == 'meta':
            return 0
        return NMETA + self.f * NT

    def w_in(self):
        n = self.n
        hk = [('hT', kc) for kc in range(8)]
        bi = 0
        for g4 in ([1] if self.mode in ('pre', 'meta') else range(4)):
            slot = self.load_fm(g4 * 4, 4)
            w = self.ring[:, slot, 0:4096].rearrange("p (c k n) -> p c k n", c=4, k=8)
            for ci in range(4):
                b = bi % 4
                bi += 1
                bk = self.bank[b]
                for kc in range(8):
                    self.mm(bk[:, 0:n], w[:, ci, kc, :], self.hT[:, kc, 0:n], kc == 0, kc == 7, hk + [('ring', slot)], [('ps', b)])
                if g4 == 0:
                    self.act(self.qT[:, ci, 0:n], bk[:, 0:n], AF.Copy, [('ps', b)], [('qT', ci)], scale=0.125)
                elif g4 == 1:
                    for (sname, c0, ns) in self.segs:
                        kc0 = self.kcol(sname)
                        self.cp('dve', self.KT[:, ci, kc0:kc0 + ns], bk[:, c0:c0 + ns], [('ps', b)], [('KT', ci)])
                elif g4 == 2:
                    self.act(self.lnt[:, 0:n], bk[:, 0:n], AF.Sigmoid, [('ps', b)], ['lnt'], scale=-1.0)
                    self.tsc('dve', self.omfT[:, ci, 0:n], self.lnt[:, 0:n], self.vecs[:, V_OML + ci:V_OML + ci + 1], None, ALU.mult, None,
                             ['lnt', 'vecs'], [('SH', 'omfT', ci)])
                else:
                    self.cp('act', self.qhT[:, ci, 0:n], bk[:, 0:n], [('ps', b)], [('SH', 'qhT', ci)])
        if self.kstop == 41:
            return
        if self.mode == 'sample':
            blocks = [('sample', 0, TS, self.sk, self.sv, 0, 18)]
        elif self.mode == 'meta':
            blocks = [('meta', 0, NMETA, self.pk, self.pv, 0, 0)]
        else:
            blocks = [('frames', j * 128, 128, self.pk, self.pv, NMETA + self.m * NT + j * 128, 1 + 4 * self.f + j) for j in range(4)]
        for g in ([1] if self.mode == 'pre' else range(2)):
            slot = self.load_tm(g)
            w = self.ring[:, slot, 0:4096].rearrange("p (k n) -> p k n", k=8)
            for (sname, c0, nb, okd, ovd, r0, vblk) in blocks:
                b = 4 + bi % 4
                bi += 1
                bk = self.bank[b]
                for kc in range(8):
                    self.mm(bk[:, :], self.hT[:, kc, c0:c0 + 128], w[:, kc, :], kc == 0, kc == 7, hk + [('ring', slot)], [('ps', b)])
                if self.mode == 'pre':
                    self.cp('act', self.Vp[0:nb, vblk, :], bk[0:nb, :], [('ps', b)], [('Vp', vblk)])
                    continue
                ks = self.kv_i % 2
                self.kv_i += 1
                self.cp('dve', self.kvst[0:nb, ks, :], bk[0:nb, :], [('ps', b)], [('kvst', ks)])
                dst = (okd if g == 0 else ovd)[r0:r0 + nb, :]
                import os
                if not os.environ.get('NOKV'):
                    self.s.dma(os.environ.get('KVQ', 'pool'), dst, self.kvst[0:nb, ks, :], [('kvst', ks)], [], 'kvo%d' % ks)
                if g == 1:
                    self.cp('act', self.Vp[0:nb, vblk, :], bk[0:nb, :], [('ps', b)], [('Vp', vblk)])

    def hgrn(self):
        s = self.s
        hk = [('hT', kc) for kc in range(8)]
        slz = self.load_tm(2)
        sli = self.load_tm(3)
        wz = self.ring[:, slz, 0:4096].rearrange("p (k n) -> p k n", k=8)
        wi = self.ring[:, sli, 0:4096].rearrange("p (k n) -> p k n", k=8)
        if self.mode == 'sample':
            chunks = [(0, TS, 1)]
            self.s.dma('pool', self.S[:, 1, :, :], self.st.rearrange("h k v -> k h v"), [], [('S', 1)], 'stin')
            for hh in range(4):
                self.cp('pool', self.Sb[:, 1, hh, :], self.S[:, 1, hh, :], [('S', 1)], [('Sb', 1, hh)])
        elif self.mode == 'meta':
            chunks = [(0, NMETA, 0)]
            self.s.op('dve', lambda e: e.memset(self.S[:, 0, :, :], 0.0), [], [('S', 0)])
            self.s.op('dve', lambda e: e.memset(self.Sb[:, 0, :, :], 0.0), [], [('Sb', 0, hh) for hh in range(4)])
        else:
            chunks = [(i * 64, 64, 0) for i in range(8)]
        def _chunk(ci, c0, T, si):
            par = ci % 2
            paired = (T == 64)
            pp = (ci // 2) % 2 if paired else 0
            r0 = 64 * (ci % 2) if paired else 0
            r1 = r0 + T
            bA, bB, bC, bD, bE, bF = self.bank[0], self.bank[1], self.bank[2 + par], self.bank[4], self.bank[5], self.bank[6 + par]
            kC, kF = ('ps', 2 + par), ('ps', 6 + par)
            k_omf, k_logf, k_iv = ('SH', 'omf_tm', pp), ('SH', 'logf', pp), ('SH', 'iv', pp)
            k_enbt, k_ket = ('SH', 'enb_tm', pp), ('SH', 'ke_tm', pp)
            cheap = self.mode in ('pre', 'meta')
            if r0 == 0:
                TT = 128 if paired else T
                for kc in range(8):
                    self.mm(bA[:, :], self.hT[:, kc, c0:c0 + 128], wz[:, kc, :], kc == 0, kc == 7, hk + [('ring', slz)], [('ps', 0)])
                for kc in range(8):
                    self.mm(bB[:, :], self.hT[:, kc, c0:c0 + 128], wi[:, kc, :], kc == 0, kc == 7, hk + [('ring', sli)], [('ps', 1)])
                self.act(self.omf_tm[0:TT, pp, :], bA[0:TT, :], AF.Sigmoid, [('ps', 0)], [k_omf], scale=-1.0)
                self.tt('dve', self.omf_tm[0:TT, pp, :], self.omf_tm[0:TT, pp, :], self.lb_t[0:TT, 1, :], ALU.mult, [k_omf, 'lb_t'], [k_omf])
                self.act(self.logf[0:TT, pp, :], self.omf_tm[0:TT, pp, :], AF.Ln, [k_omf], [k_logf], bias=1.0, scale=-1.0)
                self.cp('act', self.iv[0:TT, pp, :], bB[0:TT, :], [('ps', 1)], [k_iv])
                self.mm(bD[:, :], self.trib[0:TT, 0:128], self.logf[0:TT, pp, :], True, True, [k_logf, 'trib'], [('ps', 4)])
                self.act(self.enb_tm[0:TT, pp, :], bD[0:TT, :], AF.Exp, [('ps', 4)], [k_enbt], scale=-1.0)
                self.tt('dve', self.ke_tm[0:TT, pp, :], self.omf_tm[0:TT, pp, :], self.enb_tm[0:TT, pp, :], ALU.mult, [k_omf, k_enbt], [k_ket])
            if cheap:
                for hh in range(4):
                    self.mm(bC[:, hh * 64:hh * 64 + 2], self.logf[r0:r1, pp, hh * 128:(hh + 1) * 128], self.tri[r0:r1, 126:128], True, True,
                            [k_logf, 'tri'], [kC])
                for hh in range(4):
                    self.act(self.eb[:, par, hh, T - 1:T], bC[:, hh * 64 + 1:hh * 64 + 2], AF.Exp, [kC], [('SH', 'eb', par, hh)])
                yield
            else:
                for hh in range(4):
                    self.mm(bC[:, hh * 64:hh * 64 + T], self.logf[r0:r1, pp, hh * 128:(hh + 1) * 128], self.tri[r0:r1, r0:r1], True, True,
                            [k_logf, 'tri'], [kC])
                for hh in range(4):
                    k_eb, k_enb, k_qe, k_ke = ('SH', 'eb', par, hh), ('SH', 'enb', par, hh), ('SH', 'qe', par, hh), ('SH', 'ke', par, hh)
                    self.act(self.eb[:, par, hh, 0:T], bC[:, hh * 64:hh * 64 + T], AF.Exp, [kC], [k_eb])
                    self.act(self.enb[:, par, hh, 0:T], bC[:, hh * 64:hh * 64 + T], AF.Exp, [kC], [k_enb], scale=-1.0)
                    self.tt('dve', self.qe[:, par, hh, 0:T], self.qhT[:, hh, c0:c0 + T], self.eb[:, par, hh, 0:T], ALU.mult,
                            [('SH', 'qhT', hh), k_eb], [k_qe])
                    self.tt('pool', self.ke[:, par, hh, r0:r1], self.omfT[:, hh, c0:c0 + T], self.enb[:, par, hh, 0:T], ALU.mult,
                            [('SH', 'omfT', hh), k_enb], [k_ke])
                yield
                for hh in range(4):
                    k_qe, k_ke = ('SH', 'qe', par, hh), ('SH', 'ke', par, hh)
                    self.mm(bC[:, 256 + hh * 64:256 + hh * 64 + T], self.ke[:, par, hh, 0:128], self.qe[:, par, hh, 0:T], True, True,
                            [k_qe, k_ke], [kC])
                for hh in range(4):
                    k_sc = ('SH', 'sc', par, hh)
                    self.tt('dve', self.sc[r0:r1, par, hh, 0:T], bC[r0:r1, 256 + hh * 64:256 + hh * 64 + T], self.tri[r0:r1, r0:r1], ALU.mult,
                            [kC, 'tri'], [k_sc])
                for hh in range(4):
                    k_sc, k_qe = ('SH', 'sc', par, hh), ('SH', 'qe', par, hh)
                    self.mm(bE[:, hh * 64:hh * 64 + T], self.iv[r0:r1, pp, hh * 128:(hh + 1) * 128], self.sc[r0:r1, par, hh, 0:T], True, False,
                            [k_iv, k_sc], [('ps', 5)])
                    self.mm(bE[:, hh * 64:hh * 64 + T], self.Sb[:, si, hh, :], self.qe[:, par, hh, 0:T], False, True,
                            [('Sb', si, hh), k_qe], [('ps', 5)])
                for hh in range(4):
                    self.cp('act' if hh % 2 == 0 else 'dve', self.obT[:, hh, c0:c0 + T], bE[:, hh * 64:hh * 64 + T], [('ps', 5)], [('obT', hh)])
            for hh in range(4):
                self.mm(bF[:, hh * 128:(hh + 1) * 128], self.ke_tm[r0:r1, pp, hh * 128:(hh + 1) * 128], self.iv[r0:r1, pp, hh * 128:(hh + 1) * 128],
                        True, True, [k_ket, k_iv], [kF])
            for hh in range(4):
                k_eb = ('SH', 'eb', par, hh)
                ebl = self.eb[:, par, hh, T - 1:T]
                self.tsc('dve', self.tmpS[:, hh, :], self.S[:, si, hh, :], ebl, None, ALU.mult, None, [('S', si), k_eb], [('SH', 'tmpS', hh)])
                self.stt('dve', self.S[:, si, hh, :], bF[:, hh * 128:(hh + 1) * 128], ebl, self.tmpS[:, hh, :], ALU.mult, ALU.add,
                         [kF, k_eb, ('SH', 'tmpS', hh)], [('S', si)])
                self.cp('pool', self.Sb[:, si, hh, :], self.S[:, si, hh, :], [('S', si)], [('Sb', si, hh)])
            if self.f < 0 and si == 1:
                self.s.dma('pool', self.sh.rearrange("h k v -> k h v"), self.S[:, 1, :, :], [('S', 1)], [], 'sout')
            if self.f < 0 and si == 0 and self.npre > 0:
                self.cp('pool', self.Ssave[:, :, :], self.S[:, 0, :, :], [('S', 0)], ['Ssave'])
            if self.mode == 'main' and self.f == self.nft - 1 and ci == len(chunks) - 1:
                self.s.dma('pool', self.ph.rearrange("h k v -> k h v"), self.S[:, 0, :, :], [('S', 0)], [], 'sout')

        gens = [_chunk(ci, *ch) for ci, ch in enumerate(chunks)]
        next(gens[0])
        for i in range(len(gens)):
            if i + 1 < len(gens):
                next(gens[i + 1])
            for _ in gens[i]:
                pass

    def attn(self):
        f = self.f
        if f < 0:
            nb_c = (LC + 127) // 128
            for b in range(nb_c):
                r0 = b * 128
                kn = min(128, LC - r0)
                slot = b % 2
                ck_key = ('SH', 'ckv', slot)
                self.s.dma('sp', self.ckv[0:kn, slot, 0:512], self.ck[r0:r0 + kn, :], [], [ck_key], 'ckin%d' % slot)
                self.s.dma('sp', self.ckv[0:kn, slot, 512:1024], self.cv[r0:r0 + kn, :], [], [ck_key], 'ckin%d' % slot)
                bk = self.bank[b % 2]
                for ch in range(4):
                    self.tr(bk[:, ch * 128:ch * 128 + kn], self.ckv[0:kn, slot, ch * 128:(ch + 1) * 128], kn, [ck_key, 'ident'], [('ps', b % 2)])
                for ch in range(4):
                    self.cp('dve' if ch % 2 == 0 else 'act', self.KT[:, ch, NMETA + r0:NMETA + r0 + kn], bk[:, ch * 128:ch * 128 + kn],
                            [('ps', b % 2)], [('KT', ch)])
                self.cp('pool', self.Vp[0:kn, 1 + b, :], self.ckv[0:kn, slot, 512:1024], [ck_key], [('Vp', 1 + b)])
            blocks = [(18, TS, NMETA + LC, 0)]
            blocks.append((1 + nb_c - 1, LC - 128 * (nb_c - 1), NMETA + 128 * (nb_c - 1), None))
            for b in range(nb_c - 2, -1, -1):
                blocks.append((1 + b, 128, NMETA + 128 * b, None))
            self.attn_job(0, TS, blocks)
        else:
            blocks = []
            for m in range(3, -1, -1):
                fb = 4 * f + m
                blocks.append((1 + fb, 128, NMETA + 128 * fb, 128 * m))
            for fb in range(4 * f - 1, -1, -1):
                blocks.append((1 + fb, 128, NMETA + 128 * fb, 'flag' if fb < 4 * self.npre else None))
            blocks.append((0, NMETA, 0, None))
            self.attn_job(0, NT, blocks)

    def attn_job(self, q0, nq, blocks):
        nblk = len(blocks)

        def col0(md):
            return md if (md is not None and md != 'flag') else 0
        for grp in ([0, 1, 2], [3, 4, 5], [6, 7]):
            S_ = len(grp)
            nseq = nblk * S_

            def emitZ(idx):
                k, si = divmod(idx, S_)
                h = grp[si]
                vblk, kn, kcol, md = blocks[k]
                cs = col0(md)
                ch, pb = h // 2, 64 * (h % 2)
                zb = idx % 2
                self.mm(self.bank[zb][:, cs:nq], self.KT[pb:pb + 64, ch, kcol:kcol + 128], self.qT[pb:pb + 64, ch, q0 + cs:q0 + nq], True, True,
                        [('KT', ch), ('qT', ch)], [('ps', zb)])
            for idx in range(min(2, nseq)):
                emitZ(idx)
            for k in range(nblk):
                vblk, kn, kcol, md = blocks[k]
                cs = col0(md)
                par = k % 2
                for si, h in enumerate(grp):
                    idx = k * S_ + si
                    zb = idx % 2
                    ke_, ks_ = ('SH', 'e', si, par), ('SH', 'sp', si, par)
                    if md == 'flag':
                        self.act(self.e_[0:kn, si * 2 + par, cs:nq], self.bank[zb][0:kn, cs:nq], AF.Exp, [('ps', zb), 'vecs'], [ke_],
                                 bias=self.vecs[0:kn, V_DEAD:V_DEAD + 1])
                    else:
                        self.act(self.e_[0:kn, si * 2 + par, cs:nq], self.bank[zb][0:kn, cs:nq], AF.Exp, [('ps', zb)], [ke_])
                    if idx + 2 < nseq:
                        emitZ(idx + 2)
                    if md is not None and md != 'flag':
                        self.tt('dve', self.e_[0:kn, si * 2 + par, cs:nq], self.e_[0:kn, si * 2 + par, cs:nq], self.masks[0:kn, md // 128, cs:nq],
                                ALU.mult, [ke_, 'masks'], [ke_])
                    self.act(self.sp_[0:kn, si * 2 + par, cs:nq], self.e_[0:kn, si * 2 + par, cs:nq], AF.Ln, [ke_], [ks_], bias=1.0)
                for si, h in enumerate(grp):
                    ks_ = ('SH', 'sp', si, par)
                    self.mm(self.bank[2 + si][:, cs:nq], self.negU[0:kn, :], self.sp_[0:kn, si * 2 + par, cs:nq], k == 0, False,
                            [ks_, 'negU'], [('ps', 2 + si)], sgc=True)
                for si, h in enumerate(grp):
                    ke_, k2_, ka_ = ('SH', 'e', si, par), ('SH', 'e2', si), ('SH', 'at', si, par)
                    self.act(self.e2_[0:kn, si, cs:nq], self.bank[2 + si][0:kn, cs:nq], AF.Exp, [('ps', 2 + si)], [k2_])
                    self.tt('dve', self.at_[0:kn, si * 2 + par, cs:nq], self.e2_[0:kn, si, cs:nq], self.e_[0:kn, si * 2 + par, cs:nq], ALU.mult,
                            [k2_, ke_], [ka_])
                for si, h in enumerate(grp):
                    ks_, ka_ = ('SH', 'sp', si, par), ('SH', 'at', si, par)
                    self.mm(self.bank[2 + si][:, cs:nq], self.negL[0:kn, :], self.sp_[0:kn, si * 2 + par, cs:nq], False, k == nblk - 1,
                            [ks_, 'negL'], [('ps', 2 + si)], sgc=True)
                    vlo = h * 64 if h % 2 == 0 else (h - 1) * 64
                    self.mm(self.bank[5 + si][:, cs:nq], self.Vp[0:kn, vblk, vlo:vlo + 128], self.at_[0:kn, si * 2 + par, cs:nq], k == 0, k == nblk - 1,
                            [ka_, ('Vp', vblk)], [('ps', 5 + si)], sgc=True)
            for si, h in enumerate(grp):
                ch, pb = h // 2, 64 * (h % 2)
                self.cp('dve' if si % 2 == 0 else 'act', self.oaT[pb:pb + 64, ch, q0:q0 + nq], self.bank[5 + si][pb:pb + 64, 0:nq],
                        [('ps', 5 + si)], [('oaT', ch, pb)])

    def post(self):
        n = self.n
        hk = [('hT', kc) for kc in range(8)]
        bi = 0
        for ld in range(5):
            slot = self.load_fm(16 + ld * 4, 4)
            w = self.ring[:, slot, 0:4096].rearrange("p (c k n) -> p c k n", c=4, k=8)
            for ci in range(4):
                cidx = ld * 4 + ci
                b = bi % 4
                bi += 1
                bk = self.bank[b]
                for kc in range(8):
                    self.mm(bk[:, 0:n], w[:, ci, kc, :], self.hT[:, kc, 0:n], kc == 0, kc == 7, hk + [('ring', slot)], [('ps', b)])
                if cidx < 4:
                    self.act(self.sgT[:, cidx, 0:n], bk[:, 0:n], AF.Silu, [('ps', b)], [('SH', 'sgT', cidx)])
                elif cidx < 12:
                    o = cidx - 4
                    self.act(self.gA[:, o, 0:n], bk[:, 0:n], AF.Sigmoid, [('ps', b), 'vecs'], [('SH', 'gA', o)], bias=self.vecs[:, V_BGA + o:V_BGA + o + 1])
                else:
                    o = cidx - 12
                    self.act(self.gB[:, o, 0:n], bk[:, 0:n], AF.Sigmoid, [('ps', b), 'vecs'], [('SH', 'gB', o)], bias=self.vecs[:, V_BGB + o:V_BGB + o + 1])
        for hh in range(4):
            kq = ('SH', 'mT', hh)
            self.act(self.mT[:, hh, 0:n], self.obT[:, hh, 0:n], AF.Square, [('obT', hh)], [kq])
            self.mm(self.bank[4][:, 0:n], self.onesb[:, :], self.mT[:, hh, 0:n], True, True, [kq, 'onesb'], [('ps', 4)])
            self.act(self.lnt[:, 0:n], self.bank[4][:, 0:n], AF.Ln, [('ps', 4)], ['lnt'], bias=EPS, scale=1.0 / 128)
            self.act(self.rstd[:, 0:n], self.lnt[:, 0:n], AF.Exp, ['lnt'], ['rstd'], scale=-0.5)
            k1 = ('SH', 't1', 0)
            self.stt('dve', self.t1[:, 0, 0:n], self.obT[:, hh, 0:n], self.vecs[:, V_HGN + hh:V_HGN + hh + 1], self.rstd[:, 0:n], ALU.mult, ALU.mult,
                     [('obT', hh), 'vecs', 'rstd'], [k1])
            self.tt('pool', self.obn[:, hh, 0:n], self.t1[:, 0, 0:n], self.sgT[:, hh, 0:n], ALU.mult, [k1, ('SH', 'sgT', hh)], [('SH', 'obn', hh)])
        oak = [('oaT', c, pb) for c in range(4) for pb in (0, 64)]
        obk = [('SH', 'obn', c) for c in range(4)]
        for ld in range(2):
            slot = self.load_fm(36 + ld * 4, 4)
            w = self.ring[:, slot, 0:4096].rearrange("p (c k n) -> p c k n", c=4, k=8)
            for ci in range(4):
                o = ld * 4 + ci
                ba, bb = o % 2, 2 + o % 2
                for c in range(4):
                    self.mm(self.bank[ba][:, 0:n], w[:, ci, c, :], self.oaT[:, c, 0:n], c == 0, c == 3, oak + [('ring', slot)], [('ps', ba)])
                for c in range(4):
                    self.mm(self.bank[bb][:, 0:n], w[:, ci, 4 + c, :], self.obn[:, c, 0:n], c == 0, c == 3, obk + [('ring', slot)], [('ps', bb)])
                k1, k2 = ('SH', 't1', 0), ('SH', 't2', 0)
                self.tt('dve', self.t1[:, 0, 0:n], self.gA[:, o, 0:n], self.bank[ba][:, 0:n], ALU.mult, [('SH', 'gA', o), ('ps', ba)], [k1])
                self.tt('dve', self.t2[:, 0, 0:n], self.gB[:, o, 0:n], self.bank[bb][:, 0:n], ALU.mult, [('SH', 'gB', o), ('ps', bb)], [k2])
                self.tt('pool', self.mT[:, o, 0:n], self.t1[:, 0, 0:n], self.t2[:, 0, 0:n], ALU.add, [k1, k2], [('SH', 'mT', o)])
        mk = [('SH', 'mT', c) for c in range(8)]
        for ld in range(2):
            slot = self.load_fm(44 + ld * 4, 4)
            w = self.ring[:, slot, 0:4096].rearrange("p (c k n) -> p c k n", c=4, k=8)
            for ci in range(4):
                o = ld * 4 + ci
                b = 4 + o % 2
                for c in range(8):
                    self.mm(self.bank[b][:, 0:n], w[:, ci, c, :], self.mT[:, c, 0:n], c == 0, c == 7, mk + [('ring', slot)], [('ps', b)])
                self.tt('dve', self.xT[:, o, 0:n], self.xT[:, o, 0:n], self.bank[b][:, 0:n], ALU.add, [('xT', o), ('ps', b)], [('xT', o)])

    def final(self):
        n, f = self.n, self.f
        self.norm(V_NF, inplace=True)
        if f < 0:
            blocks = [(0, n, self.ys, 0, TS)]
            assert self.mode == 'sample'
        else:
            blocks = [(j * 128, 128, self.yp, self.m * NT + j * 128, 128) for j in range(4)]
        for (c0, nb, dst, r0, nout) in blocks:
            slot = self.xtok_i % 2
            self.xtok_i += 1
            xk = ('SH', 'x_tok', slot)
            for half in range(2):
                bk = self.bank[half]
                for q in range(4):
                    kc = half * 4 + q
                    self.tr(bk[:, q * 128:(q + 1) * 128], self.xT[:, kc, c0:c0 + 128], 128, [('xT', kc), 'ident'], [('ps', half)])
                self.cp('dve' if half == 0 else 'act', self.x_tok[0:nb, slot, half * 512:(half + 1) * 512], bk[0:nb, :], [('ps', half)], [xk])
            self.s.dma('pool', dst[r0:r0 + nout, :], self.x_tok[0:nout, slot, :], [xk], [], 'yout%d' % slot)


def _fix_kt_keys(b):
    pass


def _layouts(inp):
    f32 = np.float32

    def fm_vec(v):
        v = np.asarray(v, f32).reshape(-1, 128)
        return v.T

    def gu(wg, wu):
        a = np.asarray(wg, f32).reshape(8, 128, 22, 128).transpose(2, 1, 0, 3)
        b = np.asarray(wu, f32).reshape(8, 128, 22, 128).transpose(2, 1, 0, 3)
        return np.ascontiguousarray(np.stack([a, b], axis=2)).reshape(-1, 2048)

    def dn(wd):
        a = np.asarray(wd, f32).reshape(2, 11, 128, 8, 128).transpose(0, 3, 2, 1, 4)
        return np.ascontiguousarray(a).reshape(-1, 2048)
    w_in = np.asarray(inp['w_in'][0], f32)
    cols = np.concatenate([np.arange(0, 512), np.arange(512, 1024), np.arange(1536, 2048), np.arange(2560, 3072),
                           np.arange(3072, 3584), np.arange(3584, 4608), np.arange(4608, 5632)])
    fm = w_in[:, cols].reshape(8, 128, 36, 128).transpose(2, 1, 0, 3)
    wa = np.asarray(inp['w_branch_a'][0], f32).reshape(4, 128, 8, 128).transpose(2, 1, 0, 3)
    wb = np.asarray(inp['w_branch_b'][0], f32).reshape(4, 128, 8, 128).transpose(2, 1, 0, 3)
    wab = np.concatenate([wa, wb], axis=2)
    wo = np.asarray(inp['w_out'][0], f32).reshape(8, 128, 8, 128).transpose(2, 1, 0, 3)
    wfm = np.ascontiguousarray(np.concatenate([fm, wab, wo], axis=0)).reshape(-1, 2048)
    wtm = np.ascontiguousarray(w_in[:, 512:2560].reshape(8, 128, 4, 512).transpose(2, 1, 0, 3)).reshape(-1, 2048)
    vecs = np.zeros((128, NVEC), f32)
    vecs[:, V_N1:V_N1 + 8] = fm_vec(inp['ffn1_norm'][0])
    vecs[:, V_NM:V_NM + 8] = fm_vec(inp['mix_norm'][0])
    vecs[:, V_N2:V_N2 + 8] = fm_vec(inp['ffn2_norm'][0])
    vecs[:, V_NF:V_NF + 8] = fm_vec(inp['final_norm'])
    vecs[:, V_BGA:V_BGA + 16] = fm_vec(inp['b_gate'][0])
    vecs[:, V_HGN:V_HGN + 4] = fm_vec(inp['hg_out_norm'][0])
    vecs[:, V_LB0:V_LB0 + 4] = fm_vec(inp['hg_lb_logits'][0])
    vecs[:, V_LB1:V_LB1 + 4] = fm_vec(inp['hg_lb_logits'][1])
    lbrep = np.ascontiguousarray(np.broadcast_to(np.asarray(inp['hg_lb_logits'], f32)[None], (128, 2, 512)))
    p = np.arange(128)[:, None]
    j = np.arange(128)[None, :]
    ident = (p == j).astype(f32)
    ones = np.ones((128, 128), f32)
    negU = -(p >= j).astype(f32)
    negL = -(p < j).astype(f32)
    tri = (p <= j).astype(f32)
    cc = np.arange(512)[None, :]
    masks = np.concatenate([((p + d) < cc).astype(f32) for d in (0, 128, 256, 384)], axis=1)
    trib = ((p <= j) & ((p // 64) == (j // 64))).astype(f32)
    cst = np.ascontiguousarray(np.concatenate([ident, ones, negU, tri, negL, masks, trib], axis=1))
    return dict(gu1=gu(inp['ffn1_w_gate'][0], inp['ffn1_w_up'][0]), d1=dn(inp['ffn1_w_down'][0]),
                gu2=gu(inp['ffn2_w_gate'][0], inp['ffn2_w_up'][0]), d2=dn(inp['ffn2_w_down'][0]),
                wfm=wfm, wtm=wtm, vecs=vecs, lbrep=lbrep, cst=cst)


_NC_CACHE = {}


def run(inp, nft):
    f32 = np.float32
    shared = _layouts(inp)
    if nft % 2 == 0:
        npre = nmain = nft // 2
    else:
        npre, nmain = 0, nft
    key = (npre, nmain)
    if key not in _NC_CACHE:
        _NC_CACHE[key] = Builder(npre, nmain).build()
    nc = _NC_CACHE[key]
    xp = np.asarray(inp['x_prompt'], f32)
    xs = np.asarray(inp['x_sample'], f32)
    ck = np.asarray(inp['cache_sb_k'], f32)
    cv = np.asarray(inp['cache_sb_v'], f32)
    st = np.asarray(inp['state_hgrn'], f32)
    meta = np.ascontiguousarray(np.asarray(inp['meta_tokens'], f32))
    B = xp.shape[0]
    H = nmain * NT
    in_maps = []
    for c in range(8):
        m = dict(shared)
        b, half = c // 2, c % 2
        vecs = shared['vecs'].copy()
        if npre == 0:
            m['xp'] = np.ascontiguousarray(xp[b])
            m['xpre'] = np.zeros((NT, D), f32)
            m['flag'] = np.ones((128, 512), f32)
            vecs[:, V_FA], vecs[:, V_FB], vecs[:, V_DEAD] = 0.0, 1.0, 0.0
        elif half == 0:
            m['xp'] = np.ascontiguousarray(xp[b, 0:H])
            m['xpre'] = np.zeros((npre * NT, D), f32)
            m['flag'] = np.zeros((128, 512), f32)
            vecs[:, V_FA], vecs[:, V_FB], vecs[:, V_DEAD] = 1.0, 0.0, -30000.0
        else:
            m['xp'] = np.ascontiguousarray(xp[b, H:2 * H])
            m['xpre'] = np.ascontiguousarray(xp[b, 0:H])
            m['flag'] = np.ones((128, 512), f32)
            vecs[:, V_FA], vecs[:, V_FB], vecs[:, V_DEAD] = 0.0, 1.0, 0.0
        m['vecs'] = vecs
        m['xs'] = np.ascontiguousarray(xs[c])
        m['meta'] = meta
        m['ck'] = np.ascontiguousarray(ck[0, c].reshape(LC, 512))
        m['cv'] = np.ascontiguousarray(cv[0, c].reshape(LC, 512))
        m['st'] = np.ascontiguousarray(st[0, c])
        in_maps.append(m)
    res = run_bass_kernel_spmd(nc, in_maps, core_ids=list(range(8)))
    r = res.results
    if npre == 0:
        y_prompt = np.stack([r[2 * b]['yp'] for b in range(B)])
        pk = np.stack([r[2 * b]['pk'] for b in range(B)])
        pv = np.stack([r[2 * b]['pv'] for b in range(B)])
        ph = np.stack([r[2 * b]['ph'] for b in range(B)])
    else:
        y_prompt = np.stack([np.concatenate([r[2 * b]['yp'], r[2 * b + 1]['yp']], axis=0) for b in range(B)])
        pk = np.stack([np.concatenate([r[2 * b]['pk'], r[2 * b + 1]['pk'][NMETA:]], axis=0) for b in range(B)])
        pv = np.stack([np.concatenate([r[2 * b]['pv'], r[2 * b + 1]['pv'][NMETA:]], axis=0) for b in range(B)])
        ph = np.stack([r[2 * b + 1]['ph'] for b in range(B)])
    y_prompt = y_prompt.astype(f32)
    L = pk.shape[1]
    pk = pk.reshape(B, L, 8, 64)[None].astype(f32)
    pv = pv.reshape(B, L, 8, 64)[None].astype(f32)
    ph = ph[None].astype(f32)
    y_sample = np.stack([r[c]['ys'] for c in range(8)]).astype(f32)
    sk = np.stack([r[c]['sk'].reshape(TS, 8, 64) for c in range(8)])[None].astype(f32)
    sv = np.stack([r[c]['sv'].reshape(TS, 8, 64) for c in range(8)])[None].astype(f32)
    sh = np.stack([r[c]['sh'] for c in range(8)])[None].astype(f32)
    return (y_prompt, y_sample, pk, pv, ph, sk, sv, sh)


def kernel(**inputs):
    nft = np.asarray(inputs['x_prompt']).shape[1] // NT
    return run(inputs, nft)
```
